# Optimizing a Trainium2 kernel written in Bass

```python
import math
import jax, jax.numpy as jnp
from jax import lax
import numpy as np

D_MODEL = 1024
BATCH = 2
SEQ = 8192
DEPTH = 4

N_MIXERS = 4
D_FF = 4 * D_MODEL
RMS_EPS = 1e-6
CONV_KERNEL = 31
SHORT_CONV = 3
FOX_HEADS = 16
FOX_HEAD_DIM = D_MODEL // FOX_HEADS
Q_BLOCK = 128
GLA_HEADS = 4
GLA_DK = D_MODEL // 2
GLA_DV = D_MODEL
GLA_DK_HEAD = GLA_DK // GLA_HEADS
GLA_DV_HEAD = GLA_DV // GLA_HEADS
GLA_GATE_RANK = 16
GLA_TAU = 16.0
GLA_CHUNK = 64

kernel_name = "interleaved_conv_fox_gla_hybrid"


def _layers_of(m):
    return len(range(m, DEPTH, N_MIXERS))


def rms_norm(x, g):
    x32 = x.astype(jnp.float32)
    y = x32 * lax.rsqrt(jnp.mean(x32 * x32, axis=-1, keepdims=True) + RMS_EPS)
    return (y * g.astype(jnp.float32)).astype(x.dtype)


def layer_norm(x, g, b):
    x32 = x.astype(jnp.float32)
    mu = jnp.mean(x32, axis=-1, keepdims=True)
    xc = x32 - mu
    y = xc * lax.rsqrt(jnp.mean(xc * xc, axis=-1, keepdims=True) + RMS_EPS)
    return (y * g.astype(jnp.float32) + b.astype(jnp.float32)).astype(x.dtype)


def conformer_conv(x, w_in, b_in, conv_w, conv_b, ln_g, ln_b, w_out, b_out):
    a, g = jnp.split(x @ w_in + b_in, 2, axis=-1)
    h = a * jax.nn.sigmoid(g)
    h = lax.conv_general_dilated(
        h, conv_w[:, None, :].astype(h.dtype), window_strides=(1,),
        padding=[(CONV_KERNEL - 1, 0)],
        dimension_numbers=("NWC", "WIO", "NWC"),
        feature_group_count=D_MODEL) + conv_b
    h = jax.nn.silu(layer_norm(h, ln_g, ln_b))
    return h @ w_out + b_out


def short_gated_conv(x, w_in, conv_w, w_out):
    S = x.shape[1]
    gate_b, gate_c, h = jnp.split(x @ w_in, 3, axis=-1)
    u = gate_c * h
    up = jnp.pad(u, ((0, 0), (SHORT_CONV - 1, 0), (0, 0)))
    conv = conv_w[0] * up[:, 0:S]
    for k in range(1, SHORT_CONV):
        conv = conv + conv_w[k] * up[:, k:k + S]
    return (gate_b * conv) @ w_out


def forgetting_attention(x, w_qkv, q_g, k_g, w_f, b_f, w_out):
    Bsz, S, _ = x.shape
    q, k, v = jnp.split(x @ w_qkv, 3, axis=-1)
    heads = lambda t: t.reshape(Bsz, S, FOX_HEADS, FOX_HEAD_DIM).transpose(0, 2, 1, 3)
    q = rms_norm(heads(q), q_g)
    k = rms_norm(heads(k), k_g)
    v = heads(v)
    log_f = jax.nn.log_sigmoid((x @ w_f + b_f).astype(jnp.float32))
    c = jnp.cumsum(log_f, axis=1).transpose(0, 2, 1)
    n_blk = S // Q_BLOCK
    qb = q.reshape(Bsz, FOX_HEADS, n_blk, Q_BLOCK, FOX_HEAD_DIM).transpose(2, 0, 1, 3, 4)
    cb = c.reshape(Bsz, FOX_HEADS, n_blk, Q_BLOCK).transpose(2, 0, 1, 3)
    starts = jnp.arange(n_blk, dtype=jnp.int32) * Q_BLOCK
    key_pos = jnp.arange(S, dtype=jnp.int32)
    scale = FOX_HEAD_DIM ** -0.5

    def block(args):
        q_i, c_i, start = args
        s = jnp.einsum("bhqd,bhkd->bhqk", q_i, k).astype(jnp.float32) * scale
        s = s + c_i[..., :, None] - c[:, :, None, :]
        q_pos = start + jnp.arange(Q_BLOCK, dtype=jnp.int32)
        s = jnp.where(key_pos[None, :] <= q_pos[:, None], s, -jnp.inf)
        p = jax.nn.softmax(s, axis=-1)
        return jnp.einsum("bhqk,bhkd->bhqd", p.astype(v.dtype), v)

    o = lax.map(block, (qb, cb, starts))
    o = o.transpose(1, 0, 3, 2, 4).reshape(Bsz, S, D_MODEL)
    return o @ w_out


def gated_linear_attention(x, w_in, w_g1, w_g2, b_g, o_g, w_out):
    Bsz, S, _ = x.shape
    q, k, v, r = jnp.split(x @ w_in, [GLA_DK, 2 * GLA_DK, 2 * GLA_DK + GLA_DV], axis=-1)
    log_a = jax.nn.log_sigmoid(((x @ w_g1) @ w_g2 + b_g).astype(jnp.float32)) / GLA_TAU
    nc = S // GLA_CHUNK

    def chunks(t, dh):
        return t.astype(jnp.float32).reshape(Bsz, nc, GLA_CHUNK, GLA_HEADS, dh).transpose(1, 0, 3, 2, 4)

    qc = chunks(q, GLA_DK_HEAD) * (GLA_DK_HEAD ** -0.5)
    kc = chunks(k, GLA_DK_HEAD)
    vc = chunks(v, GLA_DV_HEAD)
    b = jnp.cumsum(chunks(log_a, GLA_DK_HEAD), axis=-2)
    b_last = b[..., -1:, :]
    q_t = qc * jnp.exp(b)
    k_t = kc * jnp.exp(-b)
    k_dec = kc * jnp.exp(b_last - b)
    causal = jnp.tril(jnp.ones((GLA_CHUNK, GLA_CHUNK), dtype=bool))
    att = jnp.where(causal, jnp.einsum("nbhtd,nbhsd->nbhts", q_t, k_t), 0.0)
    o_intra = jnp.einsum("nbhts,nbhse->nbhte", att, vc)

    def step(state, inp):
        q_i, kd_i, v_i, dl_i = inp
        o_i = jnp.einsum("bhtd,bhde->bhte", q_i, state)
        state = state * jnp.exp(dl_i)[..., None] + jnp.einsum("bhsd,bhse->bhde", kd_i, v_i)
        return state, o_i

    s0 = jnp.zeros((Bsz, GLA_HEADS, GLA_DK_HEAD, GLA_DV_HEAD), jnp.float32)
    _, o_inter = lax.scan(step, s0, (q_t, k_dec, vc, b_last[..., 0, :]))
    o = rms_norm(o_intra + o_inter, o_g)
    o = o.transpose(1, 0, 3, 2, 4).reshape(Bsz, S, GLA_DV).astype(x.dtype)
    return (o * jax.nn.silu(r)) @ w_out


def squared_relu_mlp(x, w1, w2):
    return jnp.square(jax.nn.relu(x @ w1)) @ w2


def setup_inputs(seed: int = 0) -> dict:
    key = jax.random.key(seed)
    ks = iter(jax.random.split(key, 40))
    f32 = jnp.float32

    def nrm(shape, scale):
        return jax.random.normal(next(ks), shape, f32) * scale

    def gain(shape):
        return 1.0 + nrm(shape, 0.05)

    nA, nB, nC, nD = (_layers_of(m) for m in range(N_MIXERS))
    D = D_MODEL
    return {
        "x": nrm((BATCH, SEQ, D), 1.0),
        "mix_norm": gain((DEPTH, D)),
        "mlp_norm": gain((DEPTH, D)),
        "mlp_w1": nrm((DEPTH, D, D_FF), D ** -0.5),
        "mlp_w2": nrm((DEPTH, D_FF, D), D_FF ** -0.5),
        "a_w_in": nrm((nA, D, 2 * D), D ** -0.5),
        "a_b_in": nrm((nA, 2 * D), 0.02),
        "a_conv_w": nrm((nA, CONV_KERNEL, D), CONV_KERNEL ** -0.5),
        "a_conv_b": nrm((nA, D), 0.02),
        "a_ln_g": gain((nA, D)),
        "a_ln_b": nrm((nA, D), 0.02),
        "a_w_out": nrm((nA, D, D), D ** -0.5),
        "a_b_out": nrm((nA, D), 0.02),
        "b_w_in": nrm((nB, D, 3 * D), D ** -0.5),
        "b_conv_w": nrm((nB, SHORT_CONV, D), SHORT_CONV ** -0.5),
        "b_w_out": nrm((nB, D, D), D ** -0.5),
        "c_w_qkv": nrm((nC, D, 3 * D), D ** -0.5),
        "c_q_norm": gain((nC, FOX_HEAD_DIM)),
        "c_k_norm": gain((nC, FOX_HEAD_DIM)),
        "c_w_f": nrm((nC, D, FOX_HEADS), 0.1 * D ** -0.5),
        "c_b_f": 2.0 + nrm((nC, FOX_HEADS), 1.0),
        "c_w_out": nrm((nC, D, D), D ** -0.5),
        "d_w_in": nrm((nD, D, 2 * GLA_DK + 2 * GLA_DV), D ** -0.5),
        "d_w_g1": nrm((nD, D, GLA_GATE_RANK), D ** -0.5),
        "d_w_g2": nrm((nD, GLA_GATE_RANK, GLA_DK), GLA_GATE_RANK ** -0.5),
        "d_b_g": nrm((nD, GLA_DK), 0.02),
        "d_o_norm": gain((nD, GLA_DV_HEAD)),
        "d_w_out": nrm((nD, GLA_DV, D), GLA_DV ** -0.5),
    }


def reference(x, mix_norm, mlp_norm, mlp_w1, mlp_w2,
              a_w_in, a_b_in, a_conv_w, a_conv_b, a_ln_g, a_ln_b, a_w_out, a_b_out,
              b_w_in, b_conv_w, b_w_out,
              c_w_qkv, c_q_norm, c_k_norm, c_w_f, c_b_f, c_w_out,
              d_w_in, d_w_g1, d_w_g2, d_b_g, d_o_norm, d_w_out):
    for i in range(DEPTH):
        m, j = i % N_MIXERS, i // N_MIXERS
        h = rms_norm(x, mix_norm[i])
        if m == 0:
            y = conformer_conv(h, a_w_in[j], a_b_in[j], a_conv_w[j], a_conv_b[j],
                               a_ln_g[j], a_ln_b[j], a_w_out[j], a_b_out[j])
        elif m == 1:
            y = short_gated_conv(h, b_w_in[j], b_conv_w[j], b_w_out[j])
        elif m == 2:
            y = forgetting_attention(h, c_w_qkv[j], c_q_norm[j], c_k_norm[j],
                                     c_w_f[j], c_b_f[j], c_w_out[j])
        else:
            y = gated_linear_attention(h, d_w_in[j], d_w_g1[j], d_w_g2[j], d_b_g[j],
                                       d_o_norm[j], d_w_out[j])
        x = x + y
        x = x + squared_relu_mlp(rms_norm(x, mlp_norm[i]), mlp_w1[i], mlp_w2[i])
    return x
```

```python
import numpy as np
import concourse.bass as bass
import concourse.mybir as mybir
from concourse.bass_utils import run_bass_kernel_spmd
from contextlib import ExitStack

F32 = mybir.dt.float32
BF16 = mybir.dt.bfloat16
AF = mybir.ActivationFunctionType
ALU = mybir.AluOpType

D = 1024
KC = 8
SEG = 1024
HALO = 32
NMAIN = 2 * SEG
NT = NMAIN + 2 * HALO
EPS = 1e-6
SLABW = 4096
NSLOT = 4
ARENA_W = 53200


class Tl:
    __slots__ = ("name", "lastw", "readers")

    def __init__(self, name):
        self.name = name
        self.lastw = None
        self.readers = []


class Op:
    __slots__ = ("eng", "fn", "deps", "dma", "sig", "waits", "idx", "need", "cc")

    def __init__(self, eng, fn, deps, dma, idx):
        self.cc = False
        self.eng = eng
        self.fn = fn
        self.deps = deps
        self.dma = dma
        self.sig = None
        self.waits = []
        self.idx = idx
        self.need = False


class Sch:
    ENGS = ("pe", "act", "dve", "pool", "sp")
    DMAK = 8

    def __init__(self, nc):
        self.nc = nc
        self.ops = []
        self.last = {e: None for e in self.ENGS}
        self.dmas_open = []

    def add(self, eng, fn, r=(), w=(), dma=False, cc=False):
        idx = len(self.ops)
        deps = set()
        for t in r:
            if t.lastw is not None:
                deps.add(t.lastw)
        for t in w:
            if t.lastw is not None:
                deps.add(t.lastw)
            deps.update(t.readers)
        for t in r:
            t.readers.append(idx)
        for t in w:
            t.lastw = idx
            t.readers = []
        deps.discard(idx)
        op = Op(eng, fn, deps, dma, idx)
        op.cc = cc
        self.ops.append(op)
        self.last[eng] = idx
        if dma or cc:
            self.dmas_open.append(idx)
        return idx

    def barrier(self):
        lasts = [v for v in self.last.values() if v is not None] + list(self.dmas_open)
        self.dmas_open = []
        for e in self.ENGS:
            idx = len(self.ops)
            op = Op(e, None, set(lasts), False, idx)
            self.ops.append(op)
            self.last[e] = idx

    def finalize(self, stack):
        nc = self.nc
        ops = self.ops
        for op in ops:
            keep = set()
            for d in op.deps:
                od = ops[d]
                if od.fn is None:
                    if od.eng == op.eng:
                        continue
                    keep.add(d)
                    continue
                if od.eng == "pe" and op.eng == "pe" and not od.dma:
                    continue
                keep.add(d)
            op.deps = keep
            for d in keep:
                ops[d].need = True
        csem = {e: stack.enter_context(nc.semaphore("c_" + e)) for e in ("pe", "act", "dve", "pool", "sp")}
        dsem = {e: [stack.enter_context(nc.semaphore("d_%s%d" % (e, i))) for i in range(self.DMAK)]
                for e in ("sp", "pool")}
        ccount = {e: 0 for e in csem}
        dcount = {e: 0 for e in dsem}
        ccsem = stack.enter_context(nc.semaphore("cc_sem"))
        ncc = 0
        for op in ops:
            if op.cc:
                ncc += 1
                op.sig = (ccsem, ncc, None)
            elif op.dma:
                j = dcount[op.eng]
                dcount[op.eng] += 1
                sem = dsem[op.eng][j % self.DMAK]
                op.sig = (sem, 16 * (j // self.DMAK + 1), 16)
                if j >= self.DMAK:
                    op.waits.append((sem, 16 * (j // self.DMAK)))
            elif op.need:
                ccount[op.eng] += 1
                op.sig = (csem[op.eng], ccount[op.eng], 1)
        known = {e: {} for e in self.ENGS}
        for op in ops:
            kn = known[op.eng]
            ws = {}
            for (sem, val) in op.waits:
                ws[sem.num] = (sem, max(val, ws.get(sem.num, (None, 0))[1]))
            for d in op.deps:
                sem, val, _ = ops[d].sig
                if ws.get(sem.num, (None, 0))[1] < val:
                    ws[sem.num] = (sem, val)
            out = []
            for num, (sem, val) in ws.items():
                if kn.get(num, 0) >= val:
                    continue
                kn[num] = val
                out.append((sem, val))
            op.waits = out
        self.per_eng = {e: [op for op in ops if op.eng == e] for e in self.ENGS}

    def emit(self, eng_name, e):
        n = 0
        for op in self.per_eng[eng_name]:
            for (sem, val) in op.waits:
                e.wait_ge(sem, val)
            if op.fn is None:
                if op.sig is not None:
                    e.nop().then_inc(op.sig[0], op.sig[2])
                continue
            ins = op.fn(e)
            n += 1
            if op.sig is not None:
                if op.sig[2] is None:
                    ins.then_inc(op.sig[0])
                else:
                    ins.then_inc(op.sig[0], op.sig[2])
        return n


def slab_from(W, cols):
    sub = W[:, cols]
    return np.ascontiguousarray(sub.reshape(8, 128, 512).transpose(1, 0, 2).reshape(128, SLABW))


def colvec(v, nch):
    return np.ascontiguousarray(np.asarray(v, np.float32).reshape(nch, 128).T)


class PP:
    def __init__(self):
        self.cols = []
        self.off = {}
        self.n = 0

    def put(self, name, arr):
        arr = np.asarray(arr, np.float32)
        assert arr.shape[0] == 128
        self.off[name] = (self.n, arr.shape[1])
        self.cols.append(arr)
        self.n += arr.shape[1]

    def array(self):
        return np.ascontiguousarray(np.concatenate(self.cols, axis=1))


def seg_tokens(j):
    a = np.arange(j * SEG, (j + 1) * SEG)
    b = np.arange((7 - j) * SEG, (8 - j) * SEG)
    return a, b


def core_token_index(j):
    a, b = seg_tokens(j)
    ha = np.arange(j * SEG - HALO, j * SEG)
    hb = np.arange((7 - j) * SEG - HALO, (7 - j) * SEG)
    return np.concatenate([a, b, ha, hb])


class Prog:
    def __init__(self, n_slabs, npp, out_specs, in_specs, nslot=NSLOT):
        self.nslot = nslot
        self.nc = nc = bass.Bass("TRN2", target_bir_lowering=False)
        self.st = ExitStack()
        self.s = Sch(nc)
        self.wall = nc.dram_tensor("wall", [n_slabs, 128, SLABW], F32, kind="ExternalInput").ap()
        self.ppd = nc.dram_tensor("pp", [128, npp], F32, kind="ExternalInput").ap()
        self.ins = {}
        for name, shape, dt in in_specs:
            self.ins[name] = nc.dram_tensor(name, list(shape), dt, kind="ExternalInput").ap()
        self.outs = {}
        for name, shape, dt in out_specs:
            self.outs[name] = nc.dram_tensor(name, list(shape), dt, kind="ExternalOutput").ap()
        self.npp = npp
        self.n_slabs = n_slabs
        self.slab_i = 0
        self.slab_issued = 0
        self.out_tiles = []
        self.arena = None

    def sb(self, name, shape, dt):
        if self.arena is None:
            self.arena = self.st.enter_context(self.nc.sbuf_tensor("arena", [128, ARENA_W], F32))
            self.top = 0
        n = 1
        for d in shape[1:]:
            n *= d
        words = n if dt == F32 else (n + 1) // 2
        off = self.top
        self.top += words
        assert self.top <= ARENA_W, ("SBUF arena overflow", name, self.top)
        v = self.arena[:, off:off + words]
        if dt != F32:
            v = v.bitcast(dt)[:, 0:n]
        v = v[0:shape[0]]
        if len(shape) == 3:
            v = v.rearrange("p (a b) -> p a b", a=shape[1])
        elif len(shape) == 4:
            v = v.rearrange("p (a b c) -> p a b c", a=shape[1], b=shape[2])
        return v

    def mark(self):
        return self.top

    def release(self, mark):
        self.top = mark
        self.s.barrier()

    def setup_common(self):
        nc, s = self.nc, self.s
        self.pp = self.sb("pp_sb", [128, self.npp], F32)
        self.t_pp = Tl("pp")
        s.add("sp", lambda e: e.dma_start(out=self.pp[:, :], in_=self.ppd[:, :]), w=[self.t_pp], dma=True)
        self.wring = [self.sb("wslot%d" % i, [128, SLABW], BF16) for i in range(self.nslot)]
        self.t_w = [Tl("w%d" % i) for i in range(self.nslot)]
        self.ps = [self.st.enter_context(nc.psum_tensor("ps%d" % i, [128, 512], F32)) for i in range(8)]
        self.t_ps = [Tl("ps%d" % i) for i in range(8)]
        self.ps_rr = {}
        self.onesD = self.sb("onesD", [128, 128], BF16)
        self.t_const = Tl("const")
        s.add("dve", lambda e: e.memset(self.onesD[:, :], 1.0 / D), w=[self.t_const])
        self.ident_f = self.sb("ident_f", [128, 128], F32)
        self.ident_b = self.sb("ident_b", [128, 128], BF16)
        s.add("sp", lambda e: e.dma_start(out=self.ident_f[:, :], in_=self.ins["ident"][:, :]),
              w=[self.t_const], dma=True)
        s.add("dve", lambda e: e.tensor_copy(out=self.ident_b[:, :], in_=self.ident_f[:, :]),
              r=[self.t_const], w=[self.t_const])

    def psum(self, group, banks):
        i = self.ps_rr.get(group, 0)
        self.ps_rr[group] = i + 1
        b = banks[i % len(banks)]
        return self.ps[b], self.t_ps[b]

    def ppc(self, name, k=0, n=1):
        off, w = self.pp_off[name]
        return self.pp[:, off + k: off + k + n]

    def _issue_slab(self):
        i = self.slab_issued
        if i >= self.n_slabs:
            return
        self.slab_issued += 1
        slot = i % self.nslot
        dst = self.wring[slot]
        src = self.wall[i]
        self.s.add("pool", lambda e: e.dma_start(out=dst[:, :], in_=src), w=[self.t_w[slot]], dma=True)

    def next_slab(self, hold=1):
        i = self.slab_i
        self.slab_i += 1
        while self.slab_issued < min(self.n_slabs, i + self.nslot - (hold - 1)):
            self._issue_slab()
        slot = i % self.nslot
        return self.wring[slot], self.t_w[slot]

    def mm(self, ps_ap, t_ps, lhsT, rhs, start, stop, r):
        self.s.add("pe", lambda e: e.matmul(ps_ap, lhsT, rhs, start=start, stop=stop),
                   r=list(r) + ([] if start else [t_ps]), w=[t_ps])

    def act(self, out, in_, func, r, w, bias=None, scale=None):
        kw = {}
        if bias is not None:
            kw["bias"] = bias
        if scale is not None:
            kw["scale"] = scale
        self.s.add("act", lambda e: e.activation(out, in_, func, **kw), r=r, w=w)

    def tt(self, eng, out, in0, in1, op, r, w):
        self.s.add(eng, lambda e: e.tensor_tensor(out, in0, in1, op), r=r, w=w)

    def ts(self, eng, out, in0, s1, s2, op0, op1, r, w):
        if op1 is None:
            self.s.add(eng, lambda e: e.tensor_scalar(out, in0, s1, None, op0), r=r, w=w)
        else:
            self.s.add(eng, lambda e: e.tensor_scalar(out, in0, s1, s2, op0, op1), r=r, w=w)

    def stt(self, out, in0, scalar, in1, op0, op1, r, w):
        self.s.add("dve", lambda e: e.scalar_tensor_tensor(out, in0, scalar, in1, op0, op1), r=r, w=w)

    def rmsnorm(self, xk, t_xk, gname, outk, t_outk, w):
        sq, t_sq = self.sq, self.t_sq
        for k in range(KC):
            self.act(sq[:, k, :w], xk[k], AF.Square, r=[t_xk[k]], w=[t_sq[k]])
        ps, t_ps = self.psum("stat", [4, 5])
        for k in range(KC):
            self.mm(ps[:, :w], t_ps, self.onesD[:, :], sq[:, k, :w], k == 0, k == KC - 1,
                    r=[t_sq[k], self.t_const])
        self.act(self.rt[:, :w], ps[:, :w], AF.Ln, r=[t_ps, self.t_const], w=[self.t_rt],
                 bias=self.epsc[:, 0:1])
        self.act(self.rstd[:, :w], self.rt[:, :w], AF.Exp, r=[self.t_rt], w=[self.t_rstd], scale=-0.5)
        for k in range(KC):
            self.stt(outk[k], xk[k], self.ppc(gname, k), self.rstd[:, :w], ALU.mult, ALU.mult,
                     r=[t_xk[k], self.t_rstd, self.t_pp], w=[t_outk[k]])

    def finish(self):
        nc, s = self.nc, self.s
        s.add("sp", None, r=self.out_tiles)
        s.finalize(self.st)
        with nc.Block() as block:
            @block.tensor
            def _(e):
                s.emit("pe", e)

            @block.scalar
            def _(e):
                s.emit("act", e)

            @block.vector
            def _(e):
                s.emit("dve", e)

            @block.gpsimd
            def _(e):
                s.emit("pool", e)

            @block.sync
            def _(e):
                s.emit("sp", e)
        self.st.close()
        return nc


TILES5 = [(0, 512), (512, 512), (1024, 512), (1536, 512), (2048, 64)]


def mlp_cols(q, s):
    return np.arange(q * 1024 + s * 512, q * 1024 + (s + 1) * 512)


def build_wall_stage1(inp):
    slabs = []
    w = inp["a_w_in"][0]
    for s4 in range(4):
        cols = np.concatenate([np.arange(2 * s4 * 128, (2 * s4 + 2) * 128),
                               1024 + np.arange(2 * s4 * 128, (2 * s4 + 2) * 128)])
        slabs.append(slab_from(w, cols))
    w = inp["a_w_out"][0]
    for s2 in range(2):
        slabs.append(slab_from(w, np.arange(s2 * 512, (s2 + 1) * 512)))
    slabs += mlp_slabs(inp, 0)
    w = inp["b_w_in"][0]
    for s4 in range(4):
        cols = np.concatenate([1024 + np.arange(2 * s4 * 128, (2 * s4 + 2) * 128),
                               2048 + np.arange(2 * s4 * 128, (2 * s4 + 2) * 128)])
        slabs.append(slab_from(w, cols))
    for s2 in range(2):
        slabs.append(slab_from(w, np.arange(s2 * 512, (s2 + 1) * 512)))
    w = inp["b_w_out"][0]
    for s2 in range(2):
        slabs.append(slab_from(w, np.arange(s2 * 512, (s2 + 1) * 512)))
    slabs += mlp_slabs(inp, 1)
    return slabs


def mlp_slabs(inp, l):
    slabs = []
    w1 = inp["mlp_w1"][l]
    w2 = inp["mlp_w2"][l]
    for q in range(4):
        for s in range(2):
            slabs.append(slab_from(w1, mlp_cols(q, s)))
        for s in range(2):
            slabs.append(slab_from(w2[q * 1024:(q + 1) * 1024], np.arange(s * 512, (s + 1) * 512)))
    return slabs


def build_pp_stage1(inp):
    pp = PP()
    for l in range(4):
        pp.put("mixn%d" % l, colvec(inp["mix_norm"][l], 8))
        pp.put("mlpn%d" % l, colvec(inp["mlp_norm"][l], 8))
    pp.put("a_b_in", colvec(inp["a_b_in"][0], 16))
    cw = inp["a_conv_w"][0]
    pp.put("a_conv_w", np.ascontiguousarray(cw.T.reshape(8, 128, 31).transpose(1, 0, 2).reshape(128, 248)))
    pp.put("a_conv_b", colvec(inp["a_conv_b"][0], 8))
    pp.put("a_ln_g", colvec(inp["a_ln_g"][0], 8))
    pp.put("a_ln_b", colvec(inp["a_ln_b"][0], 8))
    pp.put("a_b_out", colvec(inp["a_b_out"][0], 8))
    bw = inp["b_conv_w"][0]
    pp.put("b_conv_w", np.ascontiguousarray(bw.T.reshape(8, 128, 3).transpose(1, 0, 2).reshape(128, 24)))
    return pp


def mlp_block(P, l, x, t_x, tiles, xn, t_xn, hq, t_hq):
    for ti, (c0, w) in enumerate(tiles):
        P.rmsnorm([x[:, k, c0:c0 + w] for k in range(KC)], [t_x[k][ti] for k in range(KC)], "mlpn%d" % l,
                  [xn[:, k, c0:c0 + w] for k in range(KC)], [t_xn[k][ti] for k in range(KC)], w)
    for q in range(4):
        for s in range(2):
            wt, t_wt = P.next_slab()
            for ti, (c0, w) in enumerate(tiles):
                for n in range(4):
                    ps, t_ps = P.psum("acc", [0, 1, 2, 3])
                    for k in range(KC):
                        P.mm(ps[:, :w], t_ps, wt[:, k * 512 + n * 128: k * 512 + (n + 1) * 128],
                             xn[:, k, c0:c0 + w], k == 0, k == KC - 1, r=[t_wt, t_xn[k][ti]])
                    tmp, t_tmp = P.tmpf()
                    P.act(tmp[:, :w], ps[:, :w], AF.Relu, r=[t_ps], w=[t_tmp])
                    hn = s * 4 + n
                    P.tt("dve", hq[:, hn, c0:c0 + w], tmp[:, :w], tmp[:, :w], ALU.mult,
                         r=[t_tmp], w=[t_hq[hn][ti]])
        for s in range(2):
            wt, t_wt = P.next_slab()
            for ti, (c0, w) in enumerate(tiles):
                for n in range(4):
                    ps, t_ps = P.psum("acc", [0, 1, 2, 3])
                    for k in range(KC):
                        P.mm(ps[:, :w], t_ps, wt[:, k * 512 + n * 128: k * 512 + (n + 1) * 128],
                             hq[:, k, c0:c0 + w], k == 0, k == KC - 1, r=[t_wt, t_hq[k][ti]])
                    on = s * 4 + n
                    P.tt("dve", x[:, on, c0:c0 + w], ps[:, :w], x[:, on, c0:c0 + w], ALU.add,
                         r=[t_ps, t_x[on][ti]], w=[t_x[on][ti]])


def setup_state(P, ncols, tiles):
    s = P.s
    P.setup_common()
    ntl = len(tiles)
    P.tiles = tiles
    P.x = P.sb("xres", [128, KC, ncols], F32)
    P.t_x = [[Tl("x%d_%d" % (k, ti)) for ti in range(ntl)] for k in range(KC)]
    P.sq = P.sb("sq", [128, KC, 512], BF16)
    P.t_sq = [Tl("sq%d" % k) for k in range(KC)]
    P.rt = P.sb("rt", [128, 512], F32)
    P.t_rt = Tl("rt")
    P.rstd = P.sb("rstd", [128, 512], F32)
    P.t_rstd = Tl("rstd")
    P.epsc = P.sb("epsc", [128, 1], F32)
    s.add("dve", lambda e: e.memset(P.epsc[:, :], EPS), w=[P.t_const])
    tmps = [P.sb("tmpf%d" % i, [128, 512], F32) for i in range(3)]
    t_tmps = [Tl("tmpf%d" % i) for i in range(3)]
    rr = [0]

    def tmpf():
        i = rr[0] % 3
        rr[0] += 1
        return tmps[i], t_tmps[i]
    P.tmpf = tmpf


def alloc_R(P, ncols, ntl):
    P.R1 = P.sb("R1", [128, KC, ncols], BF16)
    P.t_R1 = [[Tl("r1_%d_%d" % (k, ti)) for ti in range(ntl)] for k in range(KC)]
    P.R2 = P.sb("R2", [128, KC, ncols], BF16)
    P.t_R2 = [[Tl("r2_%d_%d" % (k, ti)) for ti in range(ntl)] for k in range(KC)]


def load_x(P, xt_ap, ncols):
    s = P.s
    xt_v = xt_ap.rearrange("(k p) t -> p k t", p=128)
    for k in range(KC):
        s.add("sp", (lambda k: lambda e: e.dma_start(out=P.x[:, k, :], in_=xt_v[:, k, :]))(k),
              w=[P.t_x[k][ti] for ti in range(len(P.tiles))], dma=True)


def store_x(P, xo_ap, ncols=NMAIN):
    s = P.s
    xo_v = xo_ap.rearrange("(k p) t -> p k t", p=128)
    for k in range(KC):
        t_o = Tl("out%d" % k)
        s.add("sp", (lambda k: lambda e: e.dma_start(out=xo_v[:, k, :], in_=P.x[:, k, 0:ncols]))(k),
              r=[P.t_x[k][ti] for ti in range(4)], w=[t_o], dma=True)
        P.out_tiles.append(t_o)


def phase_L0(P):
    s = P.s
    x, t_x, tiles = P.x, P.t_x, P.tiles
    R1, t_R1, R2, t_R2 = P.R1, P.t_R1, P.R2, P.t_R2
    hm, t_hm = P.hm, P.t_hm
    m0 = P.mark()
    xn, t_xn = R1, t_R1
    for ti, (c0, w) in enumerate(tiles):
        P.rmsnorm([x[:, k, c0:c0 + w] for k in range(KC)], [t_x[k][ti] for k in range(KC)], "mixn0",
                  [xn[:, k, c0:c0 + w] for k in range(KC)], [t_xn[k][ti] for k in range(KC)], w)
    hc = R2.rearrange("p k (s t) -> p k s t", s=2)
    t_hc = t_R2

    def hc_dst(c, ti):
        if ti < 4:
            seg, half = ti // 2, ti % 2
            return hc[:, c, seg, HALO + half * 512: HALO + half * 512 + 512]
        return hc[:, c, :, 0:HALO]

    for s4 in range(4):
        wt, t_wt = P.next_slab()
        for ti, (c0, w) in enumerate(tiles):
            for cl in range(2):
                c = 2 * s4 + cl
                psa, t_psa = P.psum("acc", [0, 1, 2, 3])
                psg, t_psg = P.psum("acc", [0, 1, 2, 3])
                for k in range(KC):
                    P.mm(psa[:, :w], t_psa, wt[:, k * 512 + cl * 128: k * 512 + (cl + 1) * 128],
                         xn[:, k, c0:c0 + w], k == 0, k == KC - 1, r=[t_wt, t_xn[k][ti]])
                for k in range(KC):
                    P.mm(psg[:, :w], t_psg, wt[:, k * 512 + (2 + cl) * 128: k * 512 + (3 + cl) * 128],
                         xn[:, k, c0:c0 + w], k == 0, k == KC - 1, r=[t_wt, t_xn[k][ti]])
                sg, t_sg = P.tmpf()
                P.act(sg[:, :w], psg[:, :w], AF.Sigmoid, r=[t_psg, P.t_pp], w=[t_sg],
                      bias=P.ppc("a_b_in", 8 + c))
                if ti == 4:
                    P.tt("dve", sg[:, :w], sg[:, :w], hm[:, :], ALU.mult, r=[t_sg, t_hm], w=[t_sg])
                    src_a = psa[:, :w].rearrange("p (s t) -> p s t", s=2)
                    src_g = sg[:, :w].rearrange("p (s t) -> p s t", s=2)
                else:
                    src_a = psa[:, :w]
                    src_g = sg[:, :w]
                P.stt(hc_dst(c, ti), src_a, P.ppc("a_b_in", c), src_g, ALU.add, ALU.mult,
                      r=[t_psa, t_sg, P.t_pp], w=[t_hc[c][ti]])
    dgs = [P.sb("dg%d" % i, [128, 31 * 128], BF16) for i in range(2)]
    t_dgs = [Tl("dg%d" % i) for i in range(2)]
    yall, t_yall = R1, t_R1
    for c in range(KC):
        dg, t_dg = dgs[c % 2], t_dgs[c % 2]
        for k in range(31):
            P.ts("dve", dg[:, k * 128:(k + 1) * 128], P.ident_f[:, :], P.ppc("a_conv_w", c * 31 + k), None,
                 ALU.mult, None, r=[P.t_const, P.t_pp], w=[t_dg])
        for ti, (c0, w) in enumerate(tiles):
            ps, t_ps = P.psum("conv", [6, 7])
            if ti < 4:
                seg, half = ti // 2, ti % 2
                rd = [t_dg, t_hc[c][ti], t_hc[c][ti - 1 if half else 4]]
                for k in range(31):
                    b0 = 2 + half * 512 + k
                    P.mm(ps[:, :512], t_ps, dg[:, k * 128:(k + 1) * 128], hc[:, c, seg, b0:b0 + 512],
                         k == 0, k == 30, r=rd)
                P.act(yall[:, c, c0:c0 + w], ps[:, :w], AF.Identity, r=[t_ps, P.t_pp], w=[t_yall[c][ti]],
                      bias=P.ppc("a_conv_b", c))
            else:
                s.add("dve", (lambda c: lambda e: e.memset(yall[:, c, NMAIN:NT], 0.0))(c), w=[t_yall[c][4]])
                pv = ps[:, 0:4].rearrange("p (s t) -> p s t", s=2)
                for k in range(31):
                    P.mm(pv, t_ps, dg[:, k * 128:(k + 1) * 128], hc[:, c, :, k:k + 2],
                         k == 0, k == 30, r=[t_dg, t_hc[c][4]])
                yv = yall[:, c, NMAIN:NT].rearrange("p (s t) -> p s t", s=2)[:, :, 30:32]
                P.act(yv, pv, AF.Identity, r=[t_ps, P.t_pp], w=[t_yall[c][4]], bias=P.ppc("a_conv_b", c))
    sall, t_sall = R2, t_R2
    mu_sb = P.sb("mu_sb", [128, 512], F32)
    t_mu = Tl("mu")
    m2 = P.sb("m2", [128, 512], F32)
    t_m2 = Tl("m2")
    for ti, (c0, w) in enumerate(tiles):
        for k in range(KC):
            P.act(P.sq[:, k, :w], yall[:, k, c0:c0 + w], AF.Square, r=[t_yall[k][ti]], w=[P.t_sq[k]])
        psm, t_psm = P.psum("stat", [4, 5])
        pss, t_pss = P.psum("stat", [4, 5])
        for k in range(KC):
            P.mm(psm[:, :w], t_psm, P.onesD[:, :], yall[:, k, c0:c0 + w], k == 0, k == KC - 1,
                 r=[t_yall[k][ti], P.t_const])
        for k in range(KC):
            P.mm(pss[:, :w], t_pss, P.onesD[:, :], P.sq[:, k, :w], k == 0, k == KC - 1,
                 r=[P.t_sq[k], P.t_const])
        P.act(mu_sb[:, :w], psm[:, :w], AF.Identity, r=[t_psm], w=[t_mu])
        P.tt("dve", m2[:, :w], mu_sb[:, :w], mu_sb[:, :w], ALU.mult, r=[t_mu], w=[t_m2])
        P.tt("dve", m2[:, :w], pss[:, :w], m2[:, :w], ALU.subtract, r=[t_pss, t_m2], w=[t_m2])
        P.act(P.rt[:, :w], m2[:, :w], AF.Ln, r=[t_m2, P.t_const], w=[P.t_rt], bias=P.epsc[:, 0:1])
        P.act(P.rstd[:, :w], P.rt[:, :w], AF.Exp, r=[P.t_rt], w=[P.t_rstd], scale=-0.5)
        for k in range(KC):
            z, t_z = P.tmpf()
            P.tt("dve", z[:, :w], yall[:, k, c0:c0 + w], mu_sb[:, :w], ALU.subtract,
                 r=[t_yall[k][ti], t_mu], w=[t_z])
            P.tt("dve", z[:, :w], z[:, :w], P.rstd[:, :w], ALU.mult, r=[t_z, P.t_rstd], w=[t_z])
            P.act(sall[:, k, c0:c0 + w], z[:, :w], AF.Silu, r=[t_z, P.t_pp], w=[t_sall[k][ti]],
                  bias=P.ppc("a_ln_b", k), scale=P.ppc("a_ln_g", k))
    for s2 in range(2):
        wt, t_wt = P.next_slab()
        for ti, (c0, w) in enumerate(tiles):
            for n in range(4):
                ps, t_ps = P.psum("acc", [0, 1, 2, 3])
                for k in range(KC):
                    P.mm(ps[:, :w], t_ps, wt[:, k * 512 + n * 128: k * 512 + (n + 1) * 128],
                         sall[:, k, c0:c0 + w], k == 0, k == KC - 1, r=[t_wt, t_sall[k][ti]])
                on = s2 * 4 + n
                P.stt(x[:, on, c0:c0 + w], ps[:, :w], P.ppc("a_b_out", on), x[:, on, c0:c0 + w],
                      ALU.add, ALU.add, r=[t_ps, P.t_pp, t_x[on][ti]], w=[t_x[on][ti]])
    P.release(m0)
    mlp_block(P, 0, x, t_x, tiles, R1, t_R1, R2, t_R2)


def phase_L1(P):
    s = P.s
    x, t_x, tiles = P.x, P.t_x, P.tiles
    R1, t_R1, R2, t_R2 = P.R1, P.t_R1, P.R2, P.t_R2
    hm, t_hm = P.hm, P.t_hm
    xn, t_xn = R1, t_R1
    for ti, (c0, w) in enumerate(tiles):
        P.rmsnorm([x[:, k, c0:c0 + w] for k in range(KC)], [t_x[k][ti] for k in range(KC)], "mixn1",
                  [xn[:, k, c0:c0 + w] for k in range(KC)], [t_xn[k][ti] for k in range(KC)], w)
    ub = R2.rearrange("p k (s t) -> p k s t", s=2)
    t_ub = t_R2

    def ub_dst(c, ti):
        if ti < 4:
            seg, half = ti // 2, ti % 2
            return ub[:, c, seg, HALO + half * 512: HALO + half * 512 + 512]
        return ub[:, c, :, 0:HALO]
    for s4 in range(4):
        wt, t_wt = P.next_slab()
        for ti, (c0, w) in enumerate(tiles):
            for cl in range(2):
                c = 2 * s4 + cl
                psc, t_psc = P.psum("acc", [0, 1, 2, 3])
                psh, t_psh = P.psum("acc", [0, 1, 2, 3])
                for k in range(KC):
                    P.mm(psc[:, :w], t_psc, wt[:, k * 512 + cl * 128: k * 512 + (cl + 1) * 128],
                         xn[:, k, c0:c0 + w], k == 0, k == KC - 1, r=[t_wt, t_xn[k][ti]])
                for k in range(KC):
                    P.mm(psh[:, :w], t_psh, wt[:, k * 512 + (2 + cl) * 128: k * 512 + (3 + cl) * 128],
                         xn[:, k, c0:c0 + w], k == 0, k == KC - 1, r=[t_wt, t_xn[k][ti]])
                gc, t_gc = P.tmpf()
                P.act(gc[:, :w], psc[:, :w], AF.Identity, r=[t_psc], w=[t_gc])
                if ti == 4:
                    P.tt("dve", gc[:, :w], gc[:, :w], hm[:, :], ALU.mult, r=[t_gc, t_hm], w=[t_gc])
                    src_h = psh[:, :w].rearrange("p (s t) -> p s t", s=2)
                    src_c = gc[:, :w].rearrange("p (s t) -> p s t", s=2)
                else:
                    src_h = psh[:, :w]
                    src_c = gc[:, :w]
                P.tt("dve", ub_dst(c, ti), src_h, src_c, ALU.mult, r=[t_psh, t_gc], w=[t_ub[c][ti]])
    def conv3(c, ti):
        seg, half = ti // 2, ti % 2
        acc, t_acc = P.tmpf()
        base = HALO + half * 512
        rd = [t_ub[c][ti], t_ub[c][ti - 1 if half else 4], P.t_pp]
        P.ts("dve", acc[:, :], ub[:, c, seg, base - 2: base - 2 + 512], P.ppc("b_conv_w", c * 3 + 0), None,
             ALU.mult, None, r=rd, w=[t_acc])
        P.stt(acc[:, :], ub[:, c, seg, base - 1: base - 1 + 512], P.ppc("b_conv_w", c * 3 + 1), acc[:, :],
              ALU.mult, ALU.add, r=rd + [t_acc], w=[t_acc])
        P.stt(ub[:, c, seg, base: base + 512], ub[:, c, seg, base: base + 512],
              P.ppc("b_conv_w", c * 3 + 2), acc[:, :], ALU.mult, ALU.add,
              r=rd + [t_acc], w=[t_ub[c][ti]])
    mtiles = tiles[:4]
    for s2 in range(2):
        wt, t_wt = P.next_slab()
        for ti in (1, 0, 3, 2):
            (c0, w) = mtiles[ti]
            seg, half = ti // 2, ti % 2
            base = HALO + half * 512
            for n in range(4):
                c = s2 * 4 + n
                conv3(c, ti)
                ps, t_ps = P.psum("acc", [0, 1, 2, 3])
                for k in range(KC):
                    P.mm(ps[:, :w], t_ps, wt[:, k * 512 + n * 128: k * 512 + (n + 1) * 128],
                         xn[:, k, c0:c0 + w], k == 0, k == KC - 1, r=[t_wt, t_xn[k][ti]])
                P.tt("dve", ub[:, c, seg, base:base + 512], ps[:, :w], ub[:, c, seg, base:base + 512], ALU.mult,
                     r=[t_ps, t_ub[c][ti]], w=[t_ub[c][ti]])
    for s2 in range(2):
        wt, t_wt = P.next_slab()
        for ti, (c0, w) in enumerate(mtiles):
            seg, half = ti // 2, ti % 2
            base = HALO + half * 512
            for n in range(4):
                ps, t_ps = P.psum("acc", [0, 1, 2, 3])
                for k in range(KC):
                    P.mm(ps[:, :w], t_ps, wt[:, k * 512 + n * 128: k * 512 + (n + 1) * 128],
                         ub[:, k, seg, base:base + 512], k == 0, k == KC - 1, r=[t_wt, t_ub[k][ti]])
                on = s2 * 4 + n
                P.tt("dve", x[:, on, c0:c0 + w], ps[:, :w], x[:, on, c0:c0 + w], ALU.add,
                     r=[t_ps, t_x[on][ti]], w=[t_x[on][ti]])
    mlp_block(P, 1, x, t_x, mtiles, R1, t_R1, R2, t_R2)


NH = 16
HD = 64
VW = 16 * 65
NEG = -60000.0


def seg_loc(i):
    return (i, 0) if i < 4 else (7 - i, 1)


def phase_L2pre(P, qd, kdst, vdst, cd, t_kd_hp, t_vd_hp):
    s = P.s
    x, t_x = P.x, P.t_x
    mt = P.tiles[:4]
    xn, t_xn = P.R1, P.t_R1
    for ti, (c0, w) in enumerate(mt):
        P.rmsnorm([x[:, k, c0:c0 + w] for k in range(KC)], [t_x[k][ti] for k in range(KC)], "mixn2",
                  [xn[:, k, c0:c0 + w] for k in range(KC)], [t_xn[k][ti] for k in range(KC)], w)
    m0 = P.mark()
    t_c = Tl("l2c")
    s.barrier()
    r2flat = P.R2.rearrange("p k t -> p (k t)")
    r2top = [0]

    def r2alloc(shape, dt):
        n = shape[1]
        ne = n if dt == BF16 else 2 * n
        off = r2top[0]
        r2top[0] += ne
        assert r2top[0] <= KC * NT
        v = r2flat[:, off:off + ne]
        if dt == F32:
            v = v.bitcast(F32)
        return v[0:shape[0]]
    wf = P.sb("wf_bf", [128, KC, 16], BF16)
    s.add("dve", lambda e: e.tensor_copy(out=wf.rearrange("p k h -> p (k h)"), in_=P.ppc("c_w_f", 0, 128)),
          r=[P.t_pp], w=[t_c])
    nbf = P.sb("nbf", [16, 1], F32)
    P.ts("dve", nbf[:, :], P.ppc("c_b_f")[0:16], -1.0, None, ALU.mult, None, r=[P.t_pp], w=[t_c])
    one1 = P.sb("one1", [128, 1], F32)
    s.add("dve", lambda e: e.memset(one1[:, :], 1.0), w=[t_c])
    lbuf = r2alloc([16, NMAIN], F32)
    t_l = Tl("lbuf")
    ones16 = r2alloc([16, SEG], F32)
    s.add("dve", lambda e: e.memset(ones16[:, :], 1.0), w=[t_c])
    for ti, (c0, w) in enumerate(mt):
        ps, t_ps = P.psum("stat", [4, 5])
        for k in range(KC):
            P.mm(ps[0:16, :w], t_ps, wf[:, k, :], xn[:, k, c0:c0 + w], k == 0, k == KC - 1,
                 r=[t_c, t_xn[k][ti]])
        et, t_et = P.tmpf()
        P.act(et[0:16, :w], ps[0:16, :w], AF.Exp, r=[t_ps, t_c], w=[t_et], bias=nbf[:, 0:1], scale=-1.0)
        P.act(lbuf[:, c0:c0 + w], et[0:16, :w], AF.Ln, r=[t_et, t_c], w=[t_l], bias=one1[0:16, 0:1])
    cl = r2alloc([16, NMAIN], F32)
    t_cl = Tl("cl")
    for sg in range(2):
        s.add("dve", (lambda sg: lambda e: e.tensor_tensor_scan(
            cl[:, sg * SEG:(sg + 1) * SEG], ones16[:, :], lbuf[:, sg * SEG:(sg + 1) * SEG], 0.0,
            ALU.mult, ALU.subtract))(sg), r=[t_l, t_c], w=[t_cl])
    t_cd = Tl("cd")
    s.add("sp", lambda e: e.dma_start(out=cd[:, :], in_=cl[:, :]), r=[t_cl], w=[t_cd], dma=True)
    P.out_tiles.append(t_cd)
    aq = lbuf
    hi = r2alloc([16, NMAIN], BF16)
    lo = r2alloc([16, NMAIN], BF16)
    t_aq = Tl("aq")
    for ti, (c0, w) in enumerate(mt):
        P.ts("dve", aq[:, c0:c0 + w], cl[:, c0:c0 + w], cl[:, c0:c0 + 1], None, ALU.subtract, None,
             r=[t_cl, t_l], w=[t_aq, t_l])
    s.add("dve", lambda e: e.tensor_copy(out=hi[:, :], in_=aq[:, :]), r=[t_aq], w=[t_aq])
    P.tt("dve", lo[:, :], aq[:, :], hi[:, :], ALU.subtract, r=[t_aq], w=[t_aq])
    t_qd = Tl("qd")
    s.add("sp", lambda e: e.dma_start(out=qd[:, 64, :], in_=hi[:, :]), r=[t_aq], w=[t_qd], dma=True)
    s.add("sp", lambda e: e.dma_start(out=qd[:, 65, :], in_=lo[:, :]), r=[t_aq], w=[t_qd], dma=True)
    ones64 = P.sb("ones64", [64, 64], BF16)
    s.add("dve", lambda e: e.memset(ones64[:, :], 1.0 / HD), w=[t_c])
    gq = P.sb("gq", [64, 1], F32)
    P.ts("dve", gq[:, :], P.ppc("c_q_norm")[0:64], HD ** -0.5, None, ALU.mult, None, r=[P.t_pp], w=[t_c])
    stq = [P.sb("stq%d" % i, [64, NMAIN], BF16) for i in range(2)]
    stk = [P.sb("stk%d" % i, [66, NMAIN], BF16) for i in range(2)]
    t_stq = [Tl("stq%d" % i) for i in range(2)]
    t_stk = [Tl("stk%d" % i) for i in range(2)]
    for i in range(2):
        s.add("dve", (lambda i: lambda e: e.memset(stk[i][64:66, :], 1.0))(i), w=[t_stk[i]])
    sqh = [P.sb("sqh%d" % i, [64, 512], BF16) for i in range(2)]
    t_sqh = [Tl("sqh%d" % i) for i in range(2)]
    t_kd = Tl("kd")
    cnt = 0
    pending = [None]

    def finish_unit(which, h, stg, t_stg, gcol, ps, t_ps, c0, w, is_last):
        sq_, t_sq_ = sqh[finish_unit.cnt % 2], t_sqh[finish_unit.cnt % 2]
        finish_unit.cnt += 1
        P.act(sq_[:, :w], ps[0:64, :w], AF.Square, r=[t_ps], w=[t_sq_])
        ps2, t_ps2 = P.psum("stat", [4, 5])
        P.mm(ps2[0:64, :w], t_ps2, ones64[:, :], sq_[:, :w], True, True, r=[t_sq_, t_c])
        P.act(P.rt[0:64, :w], ps2[0:64, :w], AF.Ln, r=[t_ps2, P.t_const], w=[P.t_rt],
              bias=P.epsc[0:64, 0:1])
        P.act(P.rstd[0:64, :w], P.rt[0:64, :w], AF.Exp, r=[P.t_rt], w=[P.t_rstd], scale=-0.5)
        P.stt(stg[0:64, c0:c0 + w], ps[0:64, :w], gcol, P.rstd[0:64, :w], ALU.mult, ALU.mult,
              r=[t_ps, P.t_rstd, t_c, P.t_pp], w=[t_stg])
        if is_last:
            if which == 0:
                s.add("sp", lambda e: e.dma_start(out=qd[h, 0:64, :], in_=stg[0:64, :]),
                      r=[t_stg], w=[t_qd], dma=True)
            else:
                s.add("sp", lambda e: e.dma_start(out=kdst(h)[0:64, :], in_=stg[0:64, :]),
                      r=[t_stg], w=[t_kd_hp[h // 2]], dma=True)
                s.add("sp", lambda e: e.dma_start(out=kdst(h)[64:66, :], in_=stg[64:66, :]),
                      r=[t_stg], w=[t_kd_hp[h // 2]], dma=True)
    finish_unit.cnt = 0
    for which in range(2):
        for sl in range(2):
            wt, t_wt = P.next_slab()
            for hl in range(8):
                h = sl * 8 + hl
                if which == 0:
                    stg, t_stg = stq[h % 2], t_stq[h % 2]
                    gcol = gq[:, 0:1]
                else:
                    stg, t_stg = stk[h % 2], t_stk[h % 2]
                    gcol = P.ppc("c_k_norm")[0:64]
                for ti, (c0, w) in enumerate(mt):
                    ps, t_ps = P.psum("acc", [0, 1, 2, 3])
                    for k in range(KC):
                        P.mm(ps[0:64, :w], t_ps, wt[:, k * 512 + hl * 64: k * 512 + (hl + 1) * 64],
                             xn[:, k, c0:c0 + w], k == 0, k == KC - 1, r=[t_wt, t_xn[k][ti]])
                    if pending[0] is not None:
                        finish_unit(*pending[0])
                    pending[0] = (which, h, stg, t_stg, gcol, ps, t_ps, c0, w, ti == len(mt) - 1)
    finish_unit(*pending[0])
    P.t_qd = t_qd
    P.t_cd = t_cd
    vst = P.R2.rearrange("p k t -> p (k t)")[:, 0:16 * VW].rearrange("p (h kt c) -> p h kt c", h=16, kt=16)
    t_vst = Tl("vst")
    s.barrier()
    for k in range(KC):
        for ti in range(len(P.t_R2[k])):
            P.t_R2[k][ti] = t_vst
    s.add("dve", lambda e: e.memset(vst[:, :, :, 64:65], 1.0), w=[t_vst])
    ev = 0
    for vs in range(2):
        wt, t_wt = P.next_slab()
        for tb in range(16):
            ps, t_ps = P.psum("acc", [0, 1, 2, 3])
            ti = tb // 4
            for k in range(KC):
                P.mm(ps[:, :], t_ps, xn[:, k, tb * 128:(tb + 1) * 128], wt[:, k * 512:(k + 1) * 512],
                     k == 0, k == KC - 1, r=[t_wt, t_xn[k][ti]])
            dst = vst[:, vs * 8:(vs + 1) * 8, tb, 0:64]
            srcv = ps[:, :].rearrange("p (h c) -> p h c", h=8)
            if ev % 2 == 0:
                P.act(dst, srcv, AF.Identity, r=[t_ps], w=[t_vst])
            else:
                s.add("dve", (lambda dst, srcv: lambda e: e.tensor_copy(out=dst, in_=srcv))(dst, srcv),
                      r=[t_ps], w=[t_vst])
            ev += 1
    vflat = vst.rearrange("p h kt c -> p h (kt c)")
    for hp in range(8):
        s.add("sp", (lambda hp: lambda e: e.dma_start(out=vdst(hp).rearrange("(h p) f -> p h f", h=2),
                                                      in_=vflat[:, 2 * hp:2 * hp + 2, :]))(hp),
              r=[t_vst], w=[t_vd_hp[hp]], dma=True)
    P.release(m0)


def phase_L2attn(P, qd, kown, vown, cown, kallf, vallf, call, flexmd, seld, cmaskd, dep):
    s = P.s
    x, t_x = P.x, P.t_x
    mt = P.tiles[:4]
    m0 = P.mark()
    t_k = Tl("l2a_const")
    Tt = P.sb("Tt", [16, 8], F32)
    t_T = Tl("Tt")
    for i in range(8):
        r_, part = seg_loc(i)
        col = part * SEG + SEG - 1
        s.add("sp", (lambda i, r_, col: lambda e: e.dma_start(out=Tt[:, i:i + 1], in_=call[r_, :, col:col + 1], allow_slow_non_contiguous=True))(i, r_, col),
              r=[dep["call"]], w=[t_T], dma=True)
    sel = P.sb("sel", [16, 16], F32)
    s.add("sp", lambda e: e.dma_start(out=sel[:, :], in_=seld[:, :]), w=[t_k], dma=True)
    flexm = P.sb("flexm", [128, 8], F32)
    s.add("sp", lambda e: e.dma_start(out=flexm[:, :], in_=flexmd[:, :]), w=[t_k], dma=True)
    cmask = P.sb("cmask", [128, 4, 512], BF16)
    s.add("pool", lambda e: e.dma_start(out=cmask[:, :, :], in_=cmaskd.rearrange("i p q -> p i q")), w=[t_k], dma=True)
    ones8 = P.sb("ones8", [16, 8], F32)
    s.add("dve", lambda e: e.memset(ones8[:, :], 1.0), w=[t_k])
    Pin = P.sb("Pin", [16, 8], F32)
    Pex = P.sb("Pex", [16, 8], F32)
    t_P = Tl("Pex")
    s.add("dve", lambda e: e.tensor_tensor_scan(Pin[:, :], ones8[:, :], Tt[:, :], 0.0, ALU.mult, ALU.add),
          r=[t_T, t_k], w=[t_P])
    P.tt("dve", Pex[:, :], Pin[:, :], Tt[:, :], ALU.subtract, r=[t_P, t_T], w=[t_P])
    PAB = P.sb("PAB", [16, 2], F32)
    ptmp = P.sb("ptmp", [16, 8], F32)
    for sg in range(2):
        P.tt("dve", ptmp[:, :], Pex[:, :], sel[:, sg * 8:(sg + 1) * 8], ALU.mult, r=[t_P, t_k], w=[t_P])
        s.add("dve", (lambda sg: lambda e: e.reduce_sum(PAB[:, sg:sg + 1], ptmp[:, :], mybir.AxisListType.X))(sg),
              r=[t_P], w=[t_P])
    clq = P.sb("clq", [16, 4], F32)
    t_clq = Tl("clq")
    for qt in range(4):
        s.add("sp", (lambda qt: lambda e: e.dma_start(out=clq[:, qt:qt + 1], in_=cown[:, qt * 512:qt * 512 + 1], allow_slow_non_contiguous=True))(qt),
              r=[dep["cd"]], w=[t_clq], dma=True)
    CQ = P.sb("CQ", [16, 4], F32)
    for qt in range(4):
        P.tt("dve", CQ[:, qt:qt + 1], clq[:, qt:qt + 1], PAB[:, qt // 2:qt // 2 + 1], ALU.add,
             r=[t_clq, t_P], w=[t_P])
    CQd = P.sb("CQd", [16, 64], F32)
    negI = P.sb("negI", [16, 64], F32)
    ones16c = P.sb("ones16c", [16, 128], F32)
    s.add("dve", lambda e: e.memset(ones16c[:, :], 1.0), w=[t_k])
    for qt in range(4):
        P.ts("dve", CQd[:, qt * 16:(qt + 1) * 16], P.ident_f[0:16, 0:16], CQ[:, qt:qt + 1], None, ALU.mult, None,
             r=[P.t_const, t_P], w=[t_P])
        P.ts("dve", negI[:, qt * 16:(qt + 1) * 16], P.ident_f[0:16, 0:16], -1.0, None, ALU.mult, None,
             r=[P.t_const], w=[t_k])
    biasall = P.sb("biasall", [128, 72, 64], F32)
    t_bias = Tl("biasall")
    cgk = [P.sb("cgk%d" % i, [16, SEG], F32) for i in range(2)]
    t_cgk = [Tl("cgk%d" % i) for i in range(2)]
    for ks in range(9):
        cg, t_cg = cgk[ks % 2], t_cgk[ks % 2]
        if ks < 7:
            r_, part = seg_loc(ks)
            src = call[r_, :, part * SEG:(part + 1) * SEG]
            pcol = Pex[:, ks:ks + 1]
        else:
            src = cown[:, (ks - 7) * SEG:(ks - 6) * SEG]
            pcol = PAB[:, ks - 7:ks - 6]
        s.add("sp", (lambda cg, src: lambda e: e.dma_start(out=cg[:, :], in_=src))(cg, src),
              r=[dep["call"], dep["cd"]], w=[t_cg], dma=True)
        P.ts("dve", cg[:, :], cg[:, :], pcol, None, ALU.add, None, r=[t_cg, t_P], w=[t_cg])
        for kt in range(8):
            ps, t_ps = P.psum("acc", [4, 5, 6, 7])
            P.mm(ps[:, 0:64], t_ps, cg[:, kt * 128:(kt + 1) * 128], negI[:, :], True, False, r=[t_cg, t_k])
            P.mm(ps[:, 0:64], t_ps, ones16c[:, :], CQd[:, :], False, True, r=[t_k, t_P])
            s.add("dve", (lambda ks, kt, ps: lambda e: e.tensor_copy(out=biasall[:, ks * 8 + kt, :], in_=ps[:, 0:64]))(ks, kt, ps),
                  r=[t_ps], w=[t_bias])
    flexbias = P.sb("flexbias", [128, 3, 8, 32], F32)
    for p in range(3):
        P.ts("dve", flexbias[:, p, :, :], biasall[:, p * 8:(p + 1) * 8, 0:32], flexm[:, p:p + 1], None,
             ALU.mult, None, r=[t_bias, t_k], w=[t_bias])
        P.stt(flexbias[:, p, :, :], biasall[:, (6 - p) * 8:(7 - p) * 8, 32:64], flexm[:, 3 + p:4 + p],
              flexbias[:, p, :, :], ALU.mult, ALU.add, r=[t_bias, t_k], w=[t_bias])
    sel65 = P.sb("sel65", [65, 64], F32)
    s.add("dve", lambda e: e.memset(sel65[:, :], 0.0), w=[t_k])
    s.add("dve", lambda e: e.memset(sel65[64:65, :], 1.0), r=[t_k], w=[t_k])
    NRING = 5
    kcs = [P.sb("kc%d" % i, [66, SEG], BF16) for i in range(NRING)]
    vcs = [P.sb("vc%d" % i, [128, 8, 65], BF16) for i in range(NRING)]
    t_kv = [Tl("kv%d" % i) for i in range(NRING)]
    qhs = [P.sb("qh%d" % i, [66, NMAIN], BF16) for i in range(2)]
    t_qh = [Tl("qh%d" % i) for i in range(2)]
    qf = P.sb("qflex", [66, SEG], BF16)
    t_qf = Tl("qflex")
    pts = [P.sb("pt%d" % i, [128, 512], BF16) for i in range(3)]
    t_pt = [Tl("pt%d" % i) for i in range(3)]
    osb = [P.sb("osb%d" % i, [65, 512], F32) for i in range(2)]
    t_osb = [Tl("osb%d" % i) for i in range(2)]
    rden = P.sb("rden", [64, 512], F32)
    t_rden = Tl("rden")
    oT = P.sb("oT", [64, 4, NMAIN], BF16)
    t_oT = [[Tl("oT%d_%d" % (hl, qt)) for qt in range(4)] for hl in range(4)]
    fsum = P.sq.rearrange("p k t -> p (k t)").bitcast(F32)[0:65, 0:2048].rearrange("p (a t) -> p a t", a=4)
    t_fs = [Tl("fsum%d" % i) for i in range(4)]
    ring_i = [0]
    pt_i = [0]
    ob_i = [0]

    def ring_next():
        i = ring_i[0] % NRING
        ring_i[0] += 1
        return kcs[i], vcs[i], t_kv[i]

    def load_chunk(h, ks):
        kc, vc, t = ring_next()
        if ks < 7:
            r_, part = seg_loc(ks)
            ksrc = kallf(r_, h)[:, part * SEG:(part + 1) * SEG]
            vsrc = vallf(r_, h)[:, part * 8 * 65:(part + 1) * 8 * 65]
            rk, rv = [dep["kall"][h // 2]], [dep["vall"][h // 2]]
        else:
            part = ks - 7
            ksrc = kown(h)[:, part * SEG:(part + 1) * SEG]
            vsrc = vown(h)[:, part * 8 * 65:(part + 1) * 8 * 65]
            rk, rv = [dep["kd"][h // 2]], [dep["vd"][h // 2]]
        s.add("sp", lambda e: e.dma_start(out=kc[0:64, :], in_=ksrc[0:64]), r=rk, w=[t], dma=True)
        s.add("sp", lambda e: e.dma_start(out=kc[64:66, :], in_=ksrc[64:66]), r=rk, w=[t], dma=True)
        s.add("sp", lambda e: e.dma_start(out=vc.rearrange("p a b -> p (a b)"), in_=vsrc), r=rv, w=[t], dma=True)
        return kc, vc, t

    SKEW = 2

    def run_blocks(blist):
        pend = []

        def emit_pv(item):
            (vc, t_kvc, kt, pt, tpt, bank, st, sp_) = item
            P.mm(P.ps[bank][0:65, :], P.t_ps[bank], vc[:, kt, :], pt[:, :], st, sp_, r=[t_kvc, tpt])
        for (kc, vc, t_kvc, kt, q_ap, t_q, bcol, mi, bank, st, sp_) in blist:
            psS, t_psS = P.psum("S", [4, 5, 6])
            P.mm(psS[:, :], t_psS, kc[0:66, kt * 128:(kt + 1) * 128], q_ap, True, True, r=[t_kvc, t_q])
            pt, tpt = pts[pt_i[0] % 3], t_pt[pt_i[0] % 3]
            pt_i[0] += 1
            if mi is None:
                P.act(pt[:, :], psS[:, :], AF.Exp, r=[t_psS, t_bias], w=[tpt], bias=bcol)
            else:
                sb_, tsb = P.tmpf()
                P.tt("dve", sb_[:, :], psS[:, :], cmask[:, mi, :], ALU.add, r=[t_psS, t_k], w=[tsb])
                P.act(pt[:, :], sb_[:, :], AF.Exp, r=[tsb, t_bias], w=[tpt], bias=bcol)
            pend.append((vc, t_kvc, kt, pt, tpt, bank, st, sp_))
            if len(pend) > SKEW:
                emit_pv(pend.pop(0))
        while pend:
            emit_pv(pend.pop(0))

    for h in range(NH):
        hl = h % 4
        qh, tq = qhs[h % 2], t_qh[h % 2]
        s.add("sp", (lambda h, qh: lambda e: e.dma_start(out=qh[0:64, :], in_=qd[h, 0:64, :]))(h, qh),
              r=[dep["qd"]], w=[tq], dma=True)
        s.add("sp", (lambda h, qh: lambda e: e.dma_start(out=qh[64:66, :], in_=qd[h, 64:66, :]))(h, qh),
              r=[dep["qd"]], w=[tq], dma=True)
        for p in range(3):
            mA, mB = flexm[:, p:p + 1], flexm[:, 3 + p:4 + p]
            kA, vA, tA = load_chunk(h, p)
            kB, vB, tB = load_chunk(h, 6 - p)
            kf, vf, tf = ring_next()
            P.ts("dve", kf[:, :], kA[:, :], mA[0:66], None, ALU.mult, None, r=[tA, t_k], w=[tf])
            P.stt(kf[:, :], kB[:, :], mB[0:66], kf[:, :], ALU.mult, ALU.add, r=[tB, t_k, tf], w=[tf])
            vff, vAf, vBf = (v_.rearrange("p a b -> p (a b)") for v_ in (vf, vA, vB))
            P.ts("dve", vff, vAf, mA, None, ALU.mult, None, r=[tA, t_k, tf], w=[tf])
            P.stt(vff, vBf, mB, vff, ALU.mult, ALU.add, r=[tB, t_k, tf], w=[tf])
            P.ts("dve", qf[:, :], qh[:, 0:SEG], mA[0:66], None, ALU.mult, None, r=[tq, t_k], w=[t_qf])
            P.stt(qf[:, :], qh[:, SEG:2 * SEG], mB[0:66], qf[:, :], ALU.mult, ALU.add, r=[tq, t_k, t_qf], w=[t_qf])
            bl = []
            for kt in range(8):
                for half in range(2):
                    bl.append((kf, vf, tf, kt, qf[0:66, half * 512:(half + 1) * 512], t_qf,
                               flexbias[:, p, kt, half * 16 + h: half * 16 + h + 1], None, half, kt == 0, kt == 7))
            run_blocks(bl)
            for half in range(2):
                for dest, m in ((0, mA), (1, mB)):
                    fi = 2 * dest + half
                    if p == 0:
                        P.ts("dve", fsum[:, fi, :], P.ps[half][0:65, :], m[0:65], None, ALU.mult, None,
                             r=[P.t_ps[half], t_k], w=[t_fs[fi]])
                    else:
                        P.stt(fsum[:, fi, :], P.ps[half][0:65, :], m[0:65], fsum[:, fi, :], ALU.mult, ALU.add,
                              r=[P.t_ps[half], t_k, t_fs[fi]], w=[t_fs[fi]])
        bl = []
        for ks in range(4):
            kc, vc, tkv = load_chunk(h, ks)
            for kt in range(8):
                for qt in (2, 3):
                    bl.append((kc, vc, tkv, kt, qh[0:66, qt * 512:(qt + 1) * 512], tq,
                               biasall[:, ks * 8 + kt, qt * 16 + h: qt * 16 + h + 1], None, qt,
                               ks == 0 and kt == 0, False))
        for sg in range(2):
            kc, vc, tkv = load_chunk(h, 7 + sg)
            for half in range(2):
                qt = 2 * sg + half
                nk = 4 * (half + 1)
                for kt in range(nk):
                    mi = (kt - 4 * half) if kt >= 4 * half else None
                    bl.append((kc, vc, tkv, kt, qh[0:66, qt * 512:(qt + 1) * 512], tq,
                               biasall[:, (7 + sg) * 8 + kt, qt * 16 + h: qt * 16 + h + 1], mi, qt,
                               sg == 0 and kt == 0, kt == nk - 1))
        run_blocks(bl)
        for qt in range(4):
            ob, tob = osb[ob_i[0] % 2], t_osb[ob_i[0] % 2]
            ob_i[0] += 1
            fi = 2 * (qt // 2) + qt % 2
            P.tt("dve", ob[:, :], P.ps[qt][0:65, :], fsum[:, fi, :], ALU.add, r=[P.t_ps[qt], t_fs[fi]], w=[tob])
            P.mm(P.ps[7][0:64, :], P.t_ps[7], sel65[:, :], ob[:, :], True, True, r=[t_k, tob])
            P.act(rden[:, :], P.ps[7][0:64, :], AF.Ln, r=[P.t_ps[7]], w=[t_rden])
            P.act(rden[:, :], rden[:, :], AF.Exp, r=[t_rden], w=[t_rden], scale=-1.0)
            P.tt("dve", oT[:, hl, qt * 512:(qt + 1) * 512], ob[0:64, :], rden[:, :], ALU.mult,
                 r=[tob, t_rden], w=[t_oT[hl][qt]])
        if hl == 3:
            wt, t_wt = P.next_slab()
            for ti, (c0, w) in enumerate(mt):
                for n in range(KC):
                    ps, t_ps = P.psum("S", [4, 5, 6])
                    for j in range(4):
                        P.mm(ps[:, :w], t_ps, wt[0:64, j * 1024 + n * 128: j * 1024 + (n + 1) * 128],
                             oT[:, j, c0:c0 + w], j == 0, j == 3, r=[t_wt, t_oT[j][ti]])
                    P.tt("dve", x[:, n, c0:c0 + w], ps[:, :w], x[:, n, c0:c0 + w], ALU.add,
                         r=[t_ps, t_x[n][ti]], w=[t_x[n][ti]])
    P.release(m0)


GH = 4
CH = 128
NCH = SEG // CH


def gla_seg(P, sg, mode, sinit=None, gout=None, dout=None, tri=None, sinit_t=None):
    s = P.s
    full = mode == "full"
    x, t_x = P.x, P.t_x
    tis = [2 * sg, 2 * sg + 1]
    m0 = P.mark()
    t_g = Tl("gla_c")
    xn = P.sb("gxn", [128, KC, SEG], BF16)
    t_xn = [[Tl("gxn%d_%d" % (k, i)) for i in range(2)] for k in range(KC)]
    for i in range(2):
        c0 = sg * SEG + i * 512
        P.rmsnorm([x[:, k, c0:c0 + 512] for k in range(KC)], [t_x[k][tis[i]] for k in range(KC)], "mixn3",
                  [xn[:, k, i * 512:(i + 1) * 512] for k in range(KC)], [t_xn[k][i] for k in range(KC)], 512)
    wg1 = P.sb("wg1", [128, KC, 16], BF16)
    s.add("dve", lambda e: e.tensor_copy(out=wg1.rearrange("p k h -> p (k h)"), in_=P.ppc("d_w_g1", 0, 128)),
          r=[P.t_pp], w=[t_g])
    wg2 = P.sb("wg2", [16, 512], BF16)
    s.add("dve", lambda e: e.tensor_copy(out=wg2[:, :], in_=P.ppc("d_w_g2", 0, 512)[0:16]), r=[P.t_pp], w=[t_g])
    nbg = P.sb("nbg", [128, 4], F32)
    P.ts("dve", nbg[:, :], P.ppc("d_b_g", 0, 4), -1.0, None, ALU.mult, None, r=[P.t_pp], w=[t_g])
    one1 = P.sb("gone1", [128, 1], F32)
    s.add("dve", lambda e: e.memset(one1[:, :], 1.0), w=[t_g])
    ones512 = P.sb("ones512", [128, 512], F32)
    s.add("dve", lambda e: e.memset(ones512[:, :], 1.0), w=[t_g])
    vtok = P.sb("vtok", [128, NCH, D], BF16)
    t_v = [Tl("vtok%d" % c) for c in range(NCH)]
    ev = 0
    for vs in range(2):
        wt, t_wt = P.next_slab()
        for c in range(NCH):
            ps, t_ps = P.psum("acc", [0, 1, 2, 3])
            for k in range(KC):
                P.mm(ps[:, :], t_ps, xn[:, k, c * CH:(c + 1) * CH], wt[:, k * 512:(k + 1) * 512],
                     k == 0, k == KC - 1, r=[t_wt, t_xn[k][c // 4]])
            dst = vtok[:, c, vs * 512:(vs + 1) * 512]
            if ev % 2 == 0:
                P.act(dst, ps[:, :], AF.Identity, r=[t_ps], w=[t_v[c]])
            else:
                s.add("dve", (lambda dst, ps: lambda e: e.tensor_copy(out=dst, in_=ps[:, :]))(dst, ps),
                      r=[t_ps], w=[t_v[c]])
            ev += 1
    g1 = P.sb("g1", [16, SEG], BF16)
    t_g1 = Tl("g1")
    for i in range(2):
        ps, t_ps = P.psum("stat", [4, 5])
        for k in range(KC):
            P.mm(ps[0:16, :], t_ps, wg1[:, k, :], xn[:, k, i * 512:(i + 1) * 512], k == 0, k == KC - 1,
                 r=[t_g, t_xn[k][i]])
        P.act(g1[:, i * 512:(i + 1) * 512], ps[0:16, :], AF.Identity, r=[t_ps], w=[t_g1])
    if full:
        wq, t_wq = P.next_slab()
    wk, t_wk = P.next_slab(hold=2 if full else 1)
    if full:
        qt_ = P.sb("gqt", [128, GH, SEG], BF16)
    kt_ = P.sb("gkt", [128, GH, SEG], BF16) if full else None
    kd_ = P.sb("gkd", [128, GH, SEG], BF16)
    t_q = [[Tl("gq%d_%d" % (h, i)) for i in range(2)] for h in range(GH)]
    t_kk = [[Tl("gk%d_%d" % (h, i)) for i in range(2)] for h in range(GH)]
    dl = P.sb("gdl", [128, GH, NCH], F32)
    t_dl = [Tl("gdl%d" % h) for h in range(GH)]
    dtot = P.sb("gdtot", [128, GH], F32)
    t_dt = Tl("gdtot")
    csb = [P.sb("gcs0", [128, 512], F32)] * 2
    t_cs = [Tl("gcs0")] * 2
    e4 = [P.sb("ge4_%d" % i, [128, 4], F32) for i in range(2)]
    ep4 = P.sb("gep4", [128, 4], F32)
    dd4 = P.sb("gdd4", [128, 4], F32)
    t_e = Tl("ge4")
    fb = [P.sb("gfb%d" % i, [128, 512], F32) for i in range(3)]
    t_fb = [Tl("gfb%d" % i) for i in range(3)]
    qscale = float(GLA_DKH ** -0.5)
    for h in range(GH):
        for i in range(2):
            lc = i * 512
            psz, t_psz = P.psum("stat", [4, 5])
            P.mm(psz[:, :], t_psz, wg2[0:16, h * 128:(h + 1) * 128], g1[0:16, lc:lc + 512], True, True,
                 r=[t_g, t_g1])
            P.act(fb[0][:, :], psz[:, :], AF.Exp, r=[t_psz, t_g], w=[t_fb[0]], bias=nbg[:, h:h + 1], scale=-1.0)
            P.act(fb[0][:, :], fb[0][:, :], AF.Ln, r=[t_fb[0], t_g], w=[t_fb[0]], bias=one1[:, 0:1])
            cs, tcs = csb[i], t_cs[i]
            init = 0.0 if i == 0 else e4[0][:, 3:4]
            s.add("dve", (lambda cs, init: lambda e: e.tensor_tensor_scan(
                cs[:, :], ones512[:, :], fb[0][:, :], init, ALU.mult, ALU.add))(cs, init),
                r=[t_fb[0], t_g] + ([t_e] if i else []), w=[tcs])
            csv = cs.rearrange("p (c t) -> p c t", t=CH)
            E4 = e4[i]
            s.add("dve", (lambda E4, csv: lambda e: e.tensor_copy(out=E4[:, :], in_=csv[:, :, CH - 1]))(E4, csv),
                  r=[tcs], w=[t_e])
            if i == 0:
                s.add("dve", lambda e: e.memset(ep4[:, 0:1], 0.0), r=[t_e], w=[t_e])
            else:
                s.add("dve", lambda e: e.tensor_copy(out=ep4[:, 0:1], in_=e4[0][:, 3:4]), r=[t_e], w=[t_e])
            s.add("dve", (lambda E4: lambda e: e.tensor_copy(out=ep4[:, 1:4], in_=E4[:, 0:3]))(E4), r=[t_e], w=[t_e])
            bbv = fb[1].rearrange("p (c t) -> p c t", t=CH)
            bb2v = fb[2].rearrange("p (c t) -> p c t", t=CH)
            s.add("dve", (lambda csv: lambda e: e.tensor_tensor(
                bbv, csv, ep4[:, :].unsqueeze(2).to_broadcast([128, 4, CH]), ALU.subtract))(csv),
                r=[tcs, t_e], w=[t_fb[1]])
            s.add("dve", (lambda csv, E4: lambda e: e.tensor_tensor(
                bb2v, E4[:, :].unsqueeze(2).to_broadcast([128, 4, CH]), csv, ALU.subtract))(csv, E4),
                r=[tcs, t_e], w=[t_fb[2]])
            P.tt("dve", dd4[:, :], E4[:, :], ep4[:, :], ALU.subtract, r=[t_e], w=[t_e])
            P.act(dl[:, h, i * 4:(i + 1) * 4], dd4[:, :], AF.Exp, r=[t_e], w=[t_dl[h]], scale=-1.0 / GLA_TAU)
            if i == 1:
                P.act(dtot[:, h:h + 1], E4[:, 3:4], AF.Exp, r=[t_e], w=[t_dt], scale=-1.0 / GLA_TAU)
            psk, t_psk = P.psum("acc", [0, 1, 2, 3])
            for k in range(KC):
                P.mm(psk[:, :], t_psk, wk[:, k * 512 + h * 128: k * 512 + (h + 1) * 128],
                     xn[:, k, lc:lc + 512], k == 0, k == KC - 1, r=[t_wk, t_xn[k][i]])
            P.act(fb[2][:, :], fb[2][:, :], AF.Exp, r=[t_fb[2]], w=[t_fb[2]], scale=-1.0 / GLA_TAU)
            P.tt("dve", kd_[:, h, lc:lc + 512], psk[:, :], fb[2][:, :], ALU.mult, r=[t_psk, t_fb[2]], w=[t_kk[h][i]])
            if full:
                P.act(fb[0][:, :], fb[1][:, :], AF.Exp, r=[t_fb[1]], w=[t_fb[0]], scale=1.0 / GLA_TAU)
                P.tt("dve", kt_[:, h, lc:lc + 512], psk[:, :], fb[0][:, :], ALU.mult,
                     r=[t_psk, t_fb[0]], w=[t_kk[h][i]])
                P.act(fb[1][:, :], fb[1][:, :], AF.Exp, r=[t_fb[1]], w=[t_fb[1]], scale=-1.0 / GLA_TAU)
                psq, t_psq = P.psum("acc", [0, 1, 2, 3])
                for k in range(KC):
                    P.mm(psq[:, :], t_psq, wq[:, k * 512 + h * 128: k * 512 + (h + 1) * 128],
                         xn[:, k, lc:lc + 512], k == 0, k == KC - 1, r=[t_wq, t_xn[k][i]])
                P.stt(qt_[:, h, lc:lc + 512], psq[:, :], qscale, fb[1][:, :], ALU.mult, ALU.mult,
                      r=[t_psq, t_fb[1]], w=[t_q[h][i]])
    Sf = [P.sb("gS%d" % h, [128, GLA_DVH], F32) for h in range(GH)]
    Sb = [P.sb("gSb%d" % h, [128, GLA_DVH], BF16) for h in range(GH)]
    t_S = [Tl("gS%d" % h) for h in range(GH)]
    t_Sb = [Tl("gSb%d" % h) for h in range(GH)]
    kdtok = [P.sb("gkdtok%d" % i, [128, CH], BF16) for i in range(2)]
    t_kdtok = [Tl("gkdtok%d" % i) for i in range(2)]
    if full:
        oall = P.sb("goall", [128, KC, SEG], BF16)
        t_o = [[Tl("go%d_%d" % (c8, i)) for i in range(2)] for c8 in range(KC)]
        attm = [P.sb("gattm%d" % i, [128, CH], BF16) for i in range(2)]
        t_attm = [Tl("gattm%d" % i) for i in range(2)]
        trisb = P.sb("gtri", [128, CH], F32)
        s.add("sp", lambda e: e.dma_start(out=trisb[:, :], in_=tri[:, :]), w=[t_g], dma=True)
    cnt = 0
    for h in range(GH):
        if full:
            s.add("sp", (lambda h: lambda e: e.dma_start(out=Sf[h][:, :], in_=sinit[h, :, :]))(h),
                  r=[sinit_t], w=[t_S[h]], dma=True)
            s.add("act", (lambda h: lambda e: e.activation(Sb[h][:, :], Sf[h][:, :], AF.Identity))(h),
                  r=[t_S[h]], w=[t_Sb[h]])
        else:
            s.add("dve", (lambda h: lambda e: e.memset(Sf[h][:, :], 0.0))(h), w=[t_S[h]])
    for c in range(NCH):
        for h in range(GH):
            i = c // 4
            cc = slice(c * CH, (c + 1) * CH)
            if full:
                psa, t_psa = P.psum("gl", [6, 7, 4, 5])
                P.mm(psa[:, 0:CH], t_psa, kt_[:, h, cc], qt_[:, h, cc], True, True, r=[t_kk[h][i], t_q[h][i]])
                am, tam = attm[cnt % 2], t_attm[cnt % 2]
                P.tt("dve", am[:, :], psa[:, 0:CH], trisb[:, :], ALU.mult, r=[t_psa, t_g], w=[tam])
                pso, t_pso = P.psum("gl", [6, 7, 4, 5])
                for ec in range(2):
                    P.mm(pso[:, ec * CH:(ec + 1) * CH], t_pso,
                         vtok[:, c, h * GLA_DVH + ec * 128: h * GLA_DVH + (ec + 1) * 128], am[:, :],
                         True, False, r=[t_v[c], tam])
                    P.mm(pso[:, ec * CH:(ec + 1) * CH], t_pso, Sb[h][:, ec * 128:(ec + 1) * 128], qt_[:, h, cc],
                         False, True, r=[t_Sb[h], t_q[h][i]])
                s.add("act", (lambda h, cc, pso: lambda e: e.activation(
                    oall[:, 2 * h:2 * h + 2, cc], pso[:, 0:2 * CH].rearrange("p (a t) -> p a t", a=2), AF.Identity))(h, cc, pso),
                    r=[t_pso], w=[t_o[2 * h][i], t_o[2 * h + 1][i]])
            pst, t_pst = P.psum("gl", [6, 7, 4, 5])
            pst_b = pst.bitcast(BF16)
            s.add("pe", (lambda pst_b, h, cc: lambda e: e.transpose(pst_b[:, 0:CH], kd_[:, h, cc], P.ident_b[:, :]))(pst_b, h, cc),
                  r=[t_kk[h][i], P.t_const], w=[t_pst])
            kt2, tkt2 = kdtok[cnt % 2], t_kdtok[cnt % 2]
            s.add("dve", (lambda kt2, pst_b: lambda e: e.tensor_copy(out=kt2[:, :], in_=pst_b[:, 0:CH]))(kt2, pst_b),
                  r=[t_pst], w=[tkt2])
            pss, t_pss = P.psum("gl", [6, 7, 4, 5])
            P.mm(pss[:, 0:GLA_DVH], t_pss, kt2[:, :], vtok[:, c, h * GLA_DVH:(h + 1) * GLA_DVH], True, True,
                 r=[tkt2, t_v[c]])
            P.stt(Sf[h][:, :], Sf[h][:, :], dl[:, h, c:c + 1], pss[:, 0:GLA_DVH], ALU.mult, ALU.add,
                  r=[t_S[h], t_dl[h], t_pss], w=[t_S[h]])
            if full and c < NCH - 1:
                s.add("act", (lambda h: lambda e: e.activation(Sb[h][:, :], Sf[h][:, :], AF.Identity))(h),
                      r=[t_S[h]], w=[t_Sb[h]])
            cnt += 1
    if not full:
        for h in range(GH):
            t_go = Tl("gout")
            s.add("sp", (lambda h: lambda e: e.dma_start(out=gout[h, :, :], in_=Sf[h][:, :]))(h),
                  r=[t_S[h]], w=[t_go], dma=True)
            P.out_tiles.append(t_go)
    if not full:
        t_do = Tl("dout")
        s.add("sp", lambda e: e.dma_start(out=dout[:, :], in_=dtot[:, :]), r=[t_dt], w=[t_do], dma=True)
        P.out_tiles.append(t_do)
        P.release(m0)
        return
    if getattr(P, "dbg", None) is not None and sg == 0:
        dd = P.dbg
        allq = [t_q[h][i] for h in range(GH) for i in range(2)]
        allk = [t_kk[h][i] for h in range(GH) for i in range(2)]
        allo = [t_o[c8][i] for c8 in range(KC) for i in range(2)]
        for nm, buf, rd in (("dbg_qt", qt_, allq), ("dbg_kt", kt_, allk), ("dbg_kd", kd_, allk),
                            ("dbg_v", vtok, t_v), ("dbg_o", oall, allo)):
            t_d = Tl(nm)
            s.add("sp", (lambda nm, buf: lambda e: e.dma_start(out=dd[nm], in_=buf))(nm, buf), r=rd, w=[t_d], dma=True)
            P.out_tiles.append(t_d)
        t_d = Tl("dbg_dl")
        s.add("sp", lambda e: e.dma_start(out=dd["dbg_dl"], in_=dl), r=t_dl, w=[t_d], dma=True)
        P.out_tiles.append(t_d)
    ones256 = P.sb("ones256", [128, 128], BF16)
    s.add("dve", lambda e: e.memset(ones256[:, :], 1.0 / GLA_DVH), w=[t_g])
    for i in range(2):
        lc = i * 512
        for h in range(GH):
            for ec in range(2):
                P.act(P.sq[:, ec, :], oall[:, 2 * h + ec, lc:lc + 512], AF.Square, r=[t_o[2 * h + ec][i]], w=[P.t_sq[ec]])
            ps, t_ps = P.psum("stat", [4, 5])
            for ec in range(2):
                P.mm(ps[:, :], t_ps, ones256[:, :], P.sq[:, ec, :], ec == 0, ec == 1, r=[P.t_sq[ec], t_g])
            P.act(P.rt[:, :], ps[:, :], AF.Ln, r=[t_ps, P.t_const], w=[P.t_rt], bias=P.epsc[:, 0:1])
            P.act(P.rstd[:, :], P.rt[:, :], AF.Exp, r=[P.t_rt], w=[P.t_rstd], scale=-0.5)
            for ec in range(2):
                P.stt(oall[:, 2 * h + ec, lc:lc + 512], oall[:, 2 * h + ec, lc:lc + 512], P.ppc("d_o_norm", ec),
                      P.rstd[:, :], ALU.mult, ALU.mult, r=[t_o[2 * h + ec][i], P.t_rstd, P.t_pp],
                      w=[t_o[2 * h + ec][i]])
    for rs in range(2):
        wt, t_wt = P.next_slab()
        for i in range(2):
            lc = i * 512
            for n in range(4):
                ps, t_ps = P.psum("acc", [0, 1, 2, 3])
                for k in range(KC):
                    P.mm(ps[:, :], t_ps, wt[:, k * 512 + n * 128: k * 512 + (n + 1) * 128], xn[:, k, lc:lc + 512],
                         k == 0, k == KC - 1, r=[t_wt, t_xn[k][i]])
                sr, t_sr = P.tmpf()
                P.act(sr[:, :], ps[:, :], AF.Silu, r=[t_ps], w=[t_sr])
                c8 = rs * 4 + n
                P.tt("dve", oall[:, c8, lc:lc + 512], oall[:, c8, lc:lc + 512], sr[:, :], ALU.mult,
                     r=[t_o[c8][i], t_sr], w=[t_o[c8][i]])
    for s2 in range(2):
        wt, t_wt = P.next_slab()
        for i in range(2):
            lc = i * 512
            c0 = sg * SEG + lc
            for n in range(4):
                ps, t_ps = P.psum("acc", [0, 1, 2, 3])
                for k in range(KC):
                    P.mm(ps[:, :], t_ps, wt[:, k * 512 + n * 128: k * 512 + (n + 1) * 128], oall[:, k, lc:lc + 512],
                         k == 0, k == KC - 1, r=[t_wt, t_o[k][i]])
                on = s2 * 4 + n
                P.tt("dve", x[:, on, c0:c0 + 512], ps[:, :], x[:, on, c0:c0 + 512], ALU.add,
                     r=[t_ps, t_x[on][tis[i]]], w=[t_x[on][tis[i]]])
    P.release(m0)


GLA_DKH = 128
GLA_DVH = 256
GLA_TAU = 16.0


def gla_prefix(P, gall, dall, sel128d, sinit, g_ap=None, d_ap=None):
    s = P.s
    m0 = P.mark()
    sel = P.sb("gsel", [128, 16], F32)
    t_c = Tl("gpre_c")
    s.add("sp", lambda e: e.dma_start(out=sel[:, :], in_=sel128d[:, :]), w=[t_c], dma=True)
    dsb = P.sb("gdall", [128, 8, GH], F32)
    if d_ap is None:
        s.add("sp", lambda e: e.dma_start(out=dsb[:, :, :], in_=dall.rearrange("s p h -> p s h")), w=[t_c], dma=True)
    else:
        for sgi in range(8):
            s.add("sp", (lambda sgi: lambda e: e.dma_start(out=dsb[:, sgi, :], in_=d_ap(sgi)))(sgi), w=[t_c], dma=True)
    t_out = Tl("sinit")
    Gs = [[P.sb("gG%d_%d" % (h, i), [128, GLA_DVH], F32) for i in range(7)] for h in range(GH)]
    t_Gs = [[Tl("gG%d_%d" % (h, i)) for i in range(7)] for h in range(GH)]
    for h in range(GH):
        for sgi in range(7):
            gsrc = gall[sgi, h, :, :] if g_ap is None else g_ap(sgi, h)
            s.add("sp", (lambda g, gsrc: lambda e: e.dma_start(out=g[:, :], in_=gsrc))(Gs[h][sgi], gsrc),
                  w=[t_Gs[h][sgi]], dma=True)
    for h in range(GH):
        R = P.sb("gR%d" % h, [128, GLA_DVH], F32)
        SA = P.sb("gSA%d" % h, [128, GLA_DVH], F32)
        SB = P.sb("gSB%d" % h, [128, GLA_DVH], F32)
        t_R, t_SA = Tl("gR"), Tl("gSAB")
        s.add("dve", (lambda R: lambda e: e.memset(R[:, :], 0.0))(R), w=[t_R])
        s.add("dve", (lambda SA: lambda e: e.memset(SA[:, :], 0.0))(SA), w=[t_SA])
        s.add("dve", (lambda SB: lambda e: e.memset(SB[:, :], 0.0))(SB), w=[t_SA])
        for sgi in range(8):
            if sgi > 0:
                P.stt(SA[:, :], R[:, :], sel[:, sgi:sgi + 1], SA[:, :], ALU.mult, ALU.add, r=[t_R, t_c, t_SA], w=[t_SA])
                P.stt(SB[:, :], R[:, :], sel[:, 8 + sgi:9 + sgi], SB[:, :], ALU.mult, ALU.add, r=[t_R, t_c, t_SA], w=[t_SA])
            if sgi < 7:
                g, tg = Gs[h][sgi], t_Gs[h][sgi]
                P.stt(R[:, :], R[:, :], dsb[:, sgi, h:h + 1], g[:, :], ALU.mult, ALU.add, r=[t_R, t_c, tg], w=[t_R])
        s.add("sp", (lambda SA, h: lambda e: e.dma_start(out=sinit[0, h, :, :], in_=SA[:, :]))(SA, h),
              r=[t_SA], w=[t_out], dma=True)
        s.add("sp", (lambda SB, h: lambda e: e.dma_start(out=sinit[1, h, :, :], in_=SB[:, :]))(SB, h),
              r=[t_SA], w=[t_out], dma=True)
    P.release(m0)
    return t_out


def std_slabs(W, col0, n):
    return [slab_from(W, np.arange(col0 + i * 512, col0 + (i + 1) * 512)) for i in range(n)]


def attn_out_slabs(W):
    out = []
    Wr = W.reshape(16, 64, 1024)
    for g in range(4):
        sl = np.zeros((128, 4, 1024), np.float32)
        sl[0:64] = Wr[4 * g:4 * g + 4].transpose(1, 0, 2)
        out.append(sl.reshape(128, SLABW))
    return out


def build_pp(inp):
    pp = build_pp_stage1(inp)
    wf = inp["c_w_f"][0]
    pp.put("c_w_f", np.ascontiguousarray(wf.reshape(8, 128, 16).transpose(1, 0, 2).reshape(128, 128)))
    col = np.zeros((128, 1), np.float32)
    col[0:16, 0] = inp["c_b_f"][0]
    pp.put("c_b_f", col)
    col = np.zeros((128, 1), np.float32)
    col[0:64, 0] = inp["c_q_norm"][0]
    pp.put("c_q_norm", col)
    col = np.zeros((128, 1), np.float32)
    col[0:64, 0] = inp["c_k_norm"][0]
    pp.put("c_k_norm", col)
    wg1 = inp["d_w_g1"][0]
    pp.put("d_w_g1", np.ascontiguousarray(wg1.reshape(8, 128, 16).transpose(1, 0, 2).reshape(128, 128)))
    a = np.zeros((128, 512), np.float32)
    a[0:16] = inp["d_w_g2"][0]
    pp.put("d_w_g2", a)
    pp.put("d_b_g", colvec(inp["d_b_g"][0], 4))
    pp.put("d_o_norm", colvec(inp["d_o_norm"][0], 2))
    return pp


def build_fused(n_slabs, pp_off, npp):
    P = Prog(n_slabs, npp,
             out_specs=[("xo", (D, NMAIN), F32)],
             in_specs=[("xt", (D, NT), F32), ("hm", (128, 64), F32), ("ident", (128, 128), F32),
                       ("flexm", (128, 8), F32), ("sel", (16, 16), F32), ("cmask", (4, 128, 512), F32),
                       ("sel128", (128, 16), F32), ("tri", (128, 128), F32)], nslot=3)
    P.pp_off = pp_off
    nc, s = P.nc, P.s
    I = P.ins
    groups = [[0, 1, 2, 3], [4, 5, 6, 7]]
    qd = nc.dram_tensor("qd", [NH, 66, NMAIN], BF16).ap()
    kd_hp = [nc.dram_tensor("kd_hp%d" % i, [2 * 66, NMAIN], BF16).ap() for i in range(8)]
    vd_hp = [nc.dram_tensor("vd_hp%d" % i, [2 * 128, VW], BF16).ap() for i in range(8)]
    kall_hp = [nc.dram_tensor("kall_hp%d" % i, [4 * 2 * 66, NMAIN], BF16).ap() for i in range(8)]
    vall_hp = [nc.dram_tensor("vall_hp%d" % i, [4 * 2 * 128, VW], BF16).ap() for i in range(8)]
    cd2 = nc.dram_tensor("cd2", [NH, NMAIN], F32).ap()
    call2 = nc.dram_tensor("call2", [4 * NH, NMAIN], F32).ap()
    gd2 = nc.dram_tensor("gd2", [2 * GH * 128, GLA_DVH], F32).ap()
    dd2 = nc.dram_tensor("dd2", [2 * 128, GH], F32).ap()
    gall2 = nc.dram_tensor("gall2", [4 * 2 * GH * 128, GLA_DVH], F32).ap()
    dall2 = nc.dram_tensor("dall2", [4 * 2 * 128, GH], F32).ap()
    sinit = nc.dram_tensor("sinit", [2, GH, 128, GLA_DVH], F32).ap()
    call = call2.rearrange("(g h) t -> g h t", g=4)
    gd = gd2.rearrange("(s h p) e -> s h p e", s=2, h=GH)
    dd = dd2.rearrange("(s p) h -> s p h", s=2)
    gall = gall2.rearrange("(g s h p) e -> g s h p e", g=4, s=2, h=GH)
    dall = dall2.rearrange("(g s p) h -> g s p h", g=4, s=2)

    def kdst(h):
        return kd_hp[h // 2][(h % 2) * 66:(h % 2 + 1) * 66, :]

    def vown(h):
        return vd_hp[h // 2][(h % 2) * 128:(h % 2 + 1) * 128, :]

    def kallf(r_, h):
        return kall_hp[h // 2][(r_ * 2 + h % 2) * 66:(r_ * 2 + h % 2 + 1) * 66, :]

    def vallf(r_, h):
        return vall_hp[h // 2][(r_ * 2 + h % 2) * 128:(r_ * 2 + h % 2 + 1) * 128, :]

    t_kd_hp = [Tl("kd_hp%d" % i) for i in range(8)]
    t_vd_hp = [Tl("vd_hp%d" % i) for i in range(8)]
    t_kall = [Tl("kall%d" % i) for i in range(8)]
    t_vall = [Tl("vall%d" % i) for i in range(8)]
    t_call = Tl("call")

    setup_state(P, NT, TILES5)
    P.hm = P.sb("hm_sb", [128, 64], F32)
    P.t_hm = Tl("hm")
    s.add("sp", lambda e: e.dma_start(out=P.hm[:, :], in_=I["hm"][:, :]), w=[P.t_hm], dma=True)
    load_x(P, I["xt"], NT)
    mR = P.mark()
    alloc_R(P, NT, 5)
    phase_L0(P)
    phase_L1(P)
    phase_L2pre(P, qd, kdst, lambda hp: vd_hp[hp], cd2, t_kd_hp, t_vd_hp)
    P.out_tiles = []

    def allgather(src2, dst2, r, w):
        s.add("pool", lambda e: e.collective_compute("AllGather", ALU.bypass, replica_groups=groups,
                                                     ins=[src2], outs=[dst2]), r=r, w=w, cc=True)
    P.release(mR)
    allgather(cd2, call2, [P.t_cd], [t_call])
    for hp in range(8):
        allgather(kd_hp[hp], kall_hp[hp], [t_kd_hp[hp]], [t_kall[hp]])
        allgather(vd_hp[hp], vall_hp[hp], [t_vd_hp[hp]], [t_vall[hp]])
    dep = {"qd": P.t_qd, "cd": P.t_cd, "call": t_call, "kd": t_kd_hp, "vd": t_vd_hp, "kall": t_kall, "vall": t_vall}
    phase_L2attn(P, qd, kdst, vown, cd2, kallf, vallf, call, I["flexm"], I["sel"], I["cmask"], dep)
    m = P.mark()
    alloc_R(P, NT, 5)
    mlp_block(P, 2, P.x, P.t_x, P.tiles[:4], P.R1, P.t_R1, P.R2, P.t_R2)
    P.release(m)
    for sg in range(2):
        gla_seg(P, sg, "scan", gout=gd[sg], dout=dd[sg])
    P.out_tiles = []
    s.barrier()
    allgather(gd2, gall2, [], [])
    allgather(dd2, dall2, [], [])
    s.barrier()
    t_si = gla_prefix(P, None, None, I["sel128"], sinit,
                      g_ap=lambda sgi, h: gall[seg_loc(sgi)[0], seg_loc(sgi)[1], h, :, :],
                      d_ap=lambda sgi: dall[seg_loc(sgi)[0], seg_loc(sgi)[1], :, :])
    for sg in range(2):
        gla_seg(P, sg, "full", sinit=sinit[sg], tri=I["tri"], sinit_t=t_si)
    m = P.mark()
    alloc_R(P, NT, 5)
    mlp_block(P, 3, P.x, P.t_x, P.tiles[:4], P.R1, P.t_R1, P.R2, P.t_R2)
    P.release(m)
    store_x(P, P.outs["xo"])
    return P.finish()


def run_fused(inp):
    x = np.asarray(inp["x"], np.float32)
    pp = build_pp(inp)
    ppa = pp.array()
    ident = np.eye(128, dtype=np.float32)
    wq = inp["c_w_qkv"][0]
    wd = inp["d_w_in"][0]
    gla_scan = std_slabs(wd, 1024, 2) + std_slabs(wd, 512, 1)
    gla_full = (std_slabs(wd, 1024, 2) + std_slabs(wd, 0, 1) + std_slabs(wd, 512, 1) + std_slabs(wd, 2048, 2)
                + std_slabs(inp["d_w_out"][0], 0, 2))
    slabs = (build_wall_stage1(inp) + std_slabs(wq, 0, 2) + std_slabs(wq, 1024, 2) + std_slabs(wq, 2048, 2)
             + attn_out_slabs(inp["c_w_out"][0]) + mlp_slabs(inp, 2) + gla_scan + gla_scan
             + gla_full + gla_full + mlp_slabs(inp, 3))
    wall = np.stack(slabs, axis=0)
    cm = causal_masks()
    tri = np.triu(np.ones((128, 128), np.float32))
    in_maps = []
    for r in range(8):
        b, j = r // 4, r % 4
        idx = core_token_index(j)
        xt = np.zeros((NT, D), np.float32)
        valid = idx >= 0
        xt[valid] = x[b, idx[valid]]
        hm, flexm, sel = percore_consts(j)
        sel128 = np.zeros((128, 16), np.float32)
        sel128[:, j] = 1.0
        sel128[:, 8 + 7 - j] = 1.0
        in_maps.append({"wall": wall, "pp": ppa, "xt": np.ascontiguousarray(xt.T), "hm": hm, "ident": ident,
                        "flexm": flexm, "sel": sel, "cmask": cm, "sel128": sel128, "tri": tri})
    nc = build_fused(wall.shape[0], pp.off, pp.n)
    res = run_bass_kernel_spmd(nc, in_maps, core_ids=list(range(8))).results
    return gather_x([np.asarray(res[r]["xo"]) for r in range(8)])


def percore_consts(j):
    hm = np.ones((128, 64), np.float32)
    if j == 0:
        hm[:, 0:32] = 0.0
    flexm = np.zeros((128, 8), np.float32)
    for p in range(3):
        flexm[:, p] = 1.0 if j > p else 0.0
        flexm[:, 3 + p] = 0.0 if j > p else 1.0
    sel = np.zeros((16, 16), np.float32)
    sel[:, j] = 1.0
    sel[:, 8 + 7 - j] = 1.0
    return hm, flexm, sel


def causal_masks():
    kl = np.arange(128)[:, None]
    ql = np.arange(512)[None, :]
    return np.stack([np.where(kl - ql <= -128 * i, 0.0, NEG).astype(np.float32) for i in range(4)], axis=0)


_CACHE = {}


def gather_x(xos):
    out = np.zeros((2, 8192, D), np.float32)
    for r in range(8):
        b, j = r // 4, r % 4
        a, bb = seg_tokens(j)
        xo = xos[r].T
        out[b, a] = xo[:SEG]
        out[b, bb] = xo[SEG:]
    return out


def kernel(**inputs):
    return run_fused(inputs)
```

```python
import numpy as np
import concourse.bass as bass
import concourse.mybir as mybir
from concourse.bass_utils import run_bass_kernel_spmd
from contextlib import ExitStack

F32 = mybir.dt.float32
BF16 = mybir.dt.bfloat16
AF = mybir.ActivationFunctionType
ALU = mybir.AluOpType

D = 1024
KC = 8
SEG = 1024
HALO = 32
NMAIN = 2 * SEG
NT = NMAIN + 2 * HALO
EPS = 1e-6
SLABW = 4096
NSLOT = 4
ARENA_W = 53200


class Tl:
    __slots__ = ("name", "lastw", "readers")

    def __init__(self, name):
        self.name = name
        self.lastw = None
        self.readers = []


class Op:
    __slots__ = ("eng", "fn", "deps", "dma", "sig", "waits", "idx", "need", "cc")

    def __init__(self, eng, fn, deps, dma, idx):
        self.cc = False
        self.eng = eng
        self.fn = fn
        self.deps = deps
        self.dma = dma
        self.sig = None
        self.waits = []
        self.idx = idx
        self.need = False


class Sch:
    ENGS = ("pe", "act", "dve", "pool", "sp")
    DMAK = 8

    def __init__(self, nc):
        self.nc = nc
        self.ops = []
        self.last = {e: None for e in self.ENGS}
        self.dmas_open = []

    def add(self, eng, fn, r=(), w=(), dma=False, cc=False):
        idx = len(self.ops)
        deps = set()
        for t in r:
            if t.lastw is not None:
                deps.add(t.lastw)
        for t in w:
            if t.lastw is not None:
                deps.add(t.lastw)
            deps.update(t.readers)
        for t in r:
            t.readers.append(idx)
        for t in w:
            t.lastw = idx
            t.readers = []
        deps.discard(idx)
        op = Op(eng, fn, deps, dma, idx)
        op.cc = cc
        self.ops.append(op)
        self.last[eng] = idx
        if dma or cc:
            self.dmas_open.append(idx)
        return idx

    def barrier(self):
        lasts = [v for v in self.last.values() if v is not None] + list(self.dmas_open)
        self.dmas_open = []
        for e in self.ENGS:
            idx = len(self.ops)
            op = Op(e, None, set(lasts), False, idx)
            self.ops.append(op)
            self.last[e] = idx

    def finalize(self, stack):
        nc = self.nc
        ops = self.ops
        for op in ops:
            keep = set()
            for d in op.deps:
                od = ops[d]
                if od.fn is None:
                    if od.eng == op.eng:
                        continue
                    keep.add(d)
                    continue
                if od.eng == "pe" and op.eng == "pe" and not od.dma:
                    continue
                keep.add(d)
            op.deps = keep
            for d in keep:
                ops[d].need = True
        csem = {e: stack.enter_context(nc.semaphore("c_" + e)) for e in ("pe", "act", "dve", "pool", "sp")}
        dsem = {e: [stack.enter_context(nc.semaphore("d_%s%d" % (e, i))) for i in range(self.DMAK)]
                for e in ("sp", "pool")}
        ccount = {e: 0 for e in csem}
        dcount = {e: 0 for e in dsem}
        ccsem = stack.enter_context(nc.semaphore("cc_sem"))
        ncc = 0
        for op in ops:
            if op.cc:
                ncc += 1
                op.sig = (ccsem, ncc, None)
            elif op.dma:
                j = dcount[op.eng]
                dcount[op.eng] += 1
                sem = dsem[op.eng][j % self.DMAK]
                op.sig = (sem, 16 * (j // self.DMAK + 1), 16)
                if j >= self.DMAK:
                    op.waits.append((sem, 16 * (j // self.DMAK)))
            elif op.need:
                ccount[op.eng] += 1
                op.sig = (csem[op.eng], ccount[op.eng], 1)
        known = {e: {} for e in self.ENGS}
        for op in ops:
            kn = known[op.eng]
            ws = {}
            for (sem, val) in op.waits:
                ws[sem.num] = (sem, max(val, ws.get(sem.num, (None, 0))[1]))
            for d in op.deps:
                sem, val, _ = ops[d].sig
                if ws.get(sem.num, (None, 0))[1] < val:
                    ws[sem.num] = (sem, val)
            out = []
            for num, (sem, val) in ws.items():
                if kn.get(num, 0) >= val:
                    continue
                kn[num] = val
                out.append((sem, val))
            op.waits = out
        self.per_eng = {e: [op for op in ops if op.eng == e] for e in self.ENGS}

    def emit(self, eng_name, e):
        n = 0
        for op in self.per_eng[eng_name]:
            for (sem, val) in op.waits:
                e.wait_ge(sem, val)
            if op.fn is None:
                if op.sig is not None:
                    e.nop().then_inc(op.sig[0], op.sig[2])
                continue
            ins = op.fn(e)
            n += 1
            if op.sig is not None:
                if op.sig[2] is None:
                    ins.then_inc(op.sig[0])
                else:
                    ins.then_inc(op.sig[0], op.sig[2])
        return n


def slab_from(W, cols):
    sub = W[:, cols]
    return np.ascontiguousarray(sub.reshape(8, 128, 512).transpose(1, 0, 2).reshape(128, SLABW))


def colvec(v, nch):
    return np.ascontiguousarray(np.asarray(v, np.float32).reshape(nch, 128).T)


class PP:
    def __init__(self):
        self.cols = []
        self.off = {}
        self.n = 0

    def put(self, name, arr):
        arr = np.asarray(arr, np.float32)
        assert arr.shape[0] == 128
        self.off[name] = (self.n, arr.shape[1])
        self.cols.append(arr)
        self.n += arr.shape[1]

    def array(self):
        return np.ascontiguousarray(np.concatenate(self.cols, axis=1))


def seg_tokens(j):
    a = np.arange(j * SEG, (j + 1) * SEG)
    b = np.arange((7 - j) * SEG, (8 - j) * SEG)
    return a, b


def core_token_index(j):
    a, b = seg_tokens(j)
    ha = np.arange(j * SEG - HALO, j * SEG)
    hb = np.arange((7 - j) * SEG - HALO, (7 - j) * SEG)
    return np.concatenate([a, b, ha, hb])


class Prog:
    def __init__(self, n_slabs, npp, out_specs, in_specs, nslot=NSLOT):
        self.nslot = nslot
        self.nc = nc = bass.Bass("TRN2", target_bir_lowering=False)
        self.st = ExitStack()
        self.s = Sch(nc)
        self.wall = nc.dram_tensor("wall", [n_slabs, 128, SLABW], F32, kind="ExternalInput").ap()
        self.ppd = nc.dram_tensor("pp", [128, npp], F32, kind="ExternalInput").ap()
        self.ins = {}
        for name, shape, dt in in_specs:
            self.ins[name] = nc.dram_tensor(name, list(shape), dt, kind="ExternalInput").ap()
        self.outs = {}
        for name, shape, dt in out_specs:
            self.outs[name] = nc.dram_tensor(name, list(shape), dt, kind="ExternalOutput").ap()
        self.npp = npp
        self.n_slabs = n_slabs
        self.slab_i = 0
        self.slab_issued = 0
        self.out_tiles = []
        self.arena = None

    def sb(self, name, shape, dt):
        if self.arena is None:
            self.arena = self.st.enter_context(self.nc.sbuf_tensor("arena", [128, ARENA_W], F32))
            self.top = 0
        n = 1
        for d in shape[1:]:
            n *= d
        words = n if dt == F32 else (n + 1) // 2
        off = self.top
        self.top += words
        assert self.top <= ARENA_W, ("SBUF arena overflow", name, self.top)
        v = self.arena[:, off:off + words]
        if dt != F32:
            v = v.bitcast(dt)[:, 0:n]
        v = v[0:shape[0]]
        if len(shape) == 3:
            v = v.rearrange("p (a b) -> p a b", a=shape[1])
        elif len(shape) == 4:
            v = v.rearrange("p (a b c) -> p a b c", a=shape[1], b=shape[2])
        return v

    def mark(self):
        return self.top

    def release(self, mark):
        self.top = mark
        self.s.barrier()

    def setup_common(self):
        nc, s = self.nc, self.s
        self.pp = self.sb("pp_sb", [128, self.npp], F32)
        self.t_pp = Tl("pp")
        s.add("sp", lambda e: e.dma_start(out=self.pp[:, :], in_=self.ppd[:, :]), w=[self.t_pp], dma=True)
        self.wring = [self.sb("wslot%d" % i, [128, SLABW], BF16) for i in range(self.nslot)]
        self.t_w = [Tl("w%d" % i) for i in range(self.nslot)]
        self.ps = [self.st.enter_context(nc.psum_tensor("ps%d" % i, [128, 512], F32)) for i in range(8)]
        self.t_ps = [Tl("ps%d" % i) for i in range(8)]
        self.ps_rr = {}
        self.onesD = self.sb("onesD", [128, 128], BF16)
        self.t_const = Tl("const")
        s.add("dve", lambda e: e.memset(self.onesD[:, :], 1.0 / D), w=[self.t_const])
        self.ident_f = self.sb("ident_f", [128, 128], F32)
        self.ident_b = self.sb("ident_b", [128, 128], BF16)
        s.add("sp", lambda e: e.dma_start(out=self.ident_f[:, :], in_=self.ins["ident"][:, :]),
              w=[self.t_const], dma=True)
        s.add("dve", lambda e: e.tensor_copy(out=self.ident_b[:, :], in_=self.ident_f[:, :]),
              r=[self.t_const], w=[self.t_const])

    def psum(self, group, banks):
        i = self.ps_rr.get(group, 0)
        self.ps_rr[group] = i + 1
        b = banks[i % len(banks)]
        return self.ps[b], self.t_ps[b]

    def ppc(self, name, k=0, n=1):
        off, w = self.pp_off[name]
        return self.pp[:, off + k: off + k + n]

    def _issue_slab(self):
        i = self.slab_issued
        if i >= self.n_slabs:
            return
        self.slab_issued += 1
        slot = i % self.nslot
        dst = self.wring[slot]
        src = self.wall[i]
        self.s.add("pool", lambda e: e.dma_start(out=dst[:, :], in_=src), w=[self.t_w[slot]], dma=True)

    def next_slab(self, hold=1):
        i = self.slab_i
        self.slab_i += 1
        while self.slab_issued < min(self.n_slabs, i + self.nslot - (hold - 1)):
            self._issue_slab()
        slot = i % self.nslot
        return self.wring[slot], self.t_w[slot]

    def mm(self, ps_ap, t_ps, lhsT, rhs, start, stop, r):
        self.s.add("pe", lambda e: e.matmul(ps_ap, lhsT, rhs, start=start, stop=stop),
                   r=list(r) + ([] if start else [t_ps]), w=[t_ps])

    def act(self, out, in_, func, r, w, bias=None, scale=None):
        kw = {}
        if bias is not None:
            kw["bias"] = bias
        if scale is not None:
            kw["scale"] = scale
        self.s.add("act", lambda e: e.activation(out, in_, func, **kw), r=r, w=w)

    def tt(self, eng, out, in0, in1, op, r, w):
        self.s.add(eng, lambda e: e.tensor_tensor(out, in0, in1, op), r=r, w=w)

    def ts(self, eng, out, in0, s1, s2, op0, op1, r, w):
        if op1 is None:
            self.s.add(eng, lambda e: e.tensor_scalar(out, in0, s1, None, op0), r=r, w=w)
        else:
            self.s.add(eng, lambda e: e.tensor_scalar(out, in0, s1, s2, op0, op1), r=r, w=w)

    def stt(self, out, in0, scalar, in1, op0, op1, r, w):
        self.s.add("dve", lambda e: e.scalar_tensor_tensor(out, in0, scalar, in1, op0, op1), r=r, w=w)

    def rmsnorm(self, xk, t_xk, gname, outk, t_outk, w):
        sq, t_sq = self.sq, self.t_sq
        for k in range(KC):
            self.act(sq[:, k, :w], xk[k], AF.Square, r=[t_xk[k]], w=[t_sq[k]])
        ps, t_ps = self.psum("stat", [4, 5])
        for k in range(KC):
            self.mm(ps[:, :w], t_ps, self.onesD[:, :], sq[:, k, :w], k == 0, k == KC - 1,
                    r=[t_sq[k], self.t_const])
        self.act(self.rt[:, :w], ps[:, :w], AF.Ln, r=[t_ps, self.t_const], w=[self.t_rt],
                 bias=self.epsc[:, 0:1])
        self.act(self.rstd[:, :w], self.rt[:, :w], AF.Exp, r=[self.t_rt], w=[self.t_rstd], scale=-0.5)
        for k in range(KC):
            self.stt(outk[k], xk[k], self.ppc(gname, k), self.rstd[:, :w], ALU.mult, ALU.mult,
                     r=[t_xk[k], self.t_rstd, self.t_pp], w=[t_outk[k]])

    def finish(self):
        nc, s = self.nc, self.s
        s.add("sp", None, r=self.out_tiles)
        s.finalize(self.st)
        with nc.Block() as block:
            @block.tensor
            def _(e):
                s.emit("pe", e)

            @block.scalar
            def _(e):
                s.emit("act", e)

            @block.vector
            def _(e):
                s.emit("dve", e)

            @block.gpsimd
            def _(e):
                s.emit("pool", e)

            @block.sync
            def _(e):
                s.emit("sp", e)
        self.st.close()
        return nc


TILES5 = [(0, 512), (512, 512), (1024, 512), (1536, 512), (2048, 64)]


def mlp_cols(q, s):
    return np.arange(q * 1024 + s * 512, q * 1024 + (s + 1) * 512)


def build_wall_stage1(inp):
    slabs = []
    w = inp["a_w_in"][0]
    for s4 in range(4):
        cols = np.concatenate([np.arange(2 * s4 * 128, (2 * s4 + 2) * 128),
                               1024 + np.arange(2 * s4 * 128, (2 * s4 + 2) * 128)])
        slabs.append(slab_from(w, cols))
    w = inp["a_w_out"][0]
    for s2 in range(2):
        slabs.append(slab_from(w, np.arange(s2 * 512, (s2 + 1) * 512)))
    slabs += mlp_slabs(inp, 0)
    w = inp["b_w_in"][0]
    for s4 in range(4):
        cols = np.concatenate([1024 + np.arange(2 * s4 * 128, (2 * s4 + 2) * 128),
                               2048 + np.arange(2 * s4 * 128, (2 * s4 + 2) * 128)])
        slabs.append(slab_from(w, cols))
    for s2 in range(2):
        slabs.append(slab_from(w, np.arange(s2 * 512, (s2 + 1) * 512)))
    w = inp["b_w_out"][0]
    for s2 in range(2):
        slabs.append(slab_from(w, np.arange(s2 * 512, (s2 + 1) * 512)))
    slabs += mlp_slabs(inp, 1)
    return slabs


def mlp_slabs(inp, l):
    slabs = []
    w1 = inp["mlp_w1"][l]
    w2 = inp["mlp_w2"][l]
    for q in range(4):
        for s in range(2):
            slabs.append(slab_from(w1, mlp_cols(q, s)))
        for s in range(2):
            slabs.append(slab_from(w2[q * 1024:(q + 1) * 1024], np.arange(s * 512, (s + 1) * 512)))
    return slabs


def build_pp_stage1(inp):
    pp = PP()
    for l in range(4):
        pp.put("mixn%d" % l, colvec(inp["mix_norm"][l], 8))
        pp.put("mlpn%d" % l, colvec(inp["mlp_norm"][l], 8))
    pp.put("a_b_in", colvec(inp["a_b_in"][0], 16))
    cw = inp["a_conv_w"][0]
    pp.put("a_conv_w", np.ascontiguousarray(cw.T.reshape(8, 128, 31).transpose(1, 0, 2).reshape(128, 248)))
    pp.put("a_conv_b", colvec(inp["a_conv_b"][0], 8))
    pp.put("a_ln_g", colvec(inp["a_ln_g"][0], 8))
    pp.put("a_ln_b", colvec(inp["a_ln_b"][0], 8))
    pp.put("a_b_out", colvec(inp["a_b_out"][0], 8))
    bw = inp["b_conv_w"][0]
    pp.put("b_conv_w", np.ascontiguousarray(bw.T.reshape(8, 128, 3).transpose(1, 0, 2).reshape(128, 24)))
    return pp


def mlp_block(P, l, x, t_x, tiles, xn, t_xn, hq, t_hq):
    for ti, (c0, w) in enumerate(tiles):
        P.rmsnorm([x[:, k, c0:c0 + w] for k in range(KC)], [t_x[k][ti] for k in range(KC)], "mlpn%d" % l,
                  [xn[:, k, c0:c0 + w] for k in range(KC)], [t_xn[k][ti] for k in range(KC)], w)
    for q in range(4):
        for s in range(2):
            wt, t_wt = P.next_slab()
            for ti, (c0, w) in enumerate(tiles):
                for n in range(4):
                    ps, t_ps = P.psum("acc", [0, 1, 2, 3])
                    for k in range(KC):
                        P.mm(ps[:, :w], t_ps, wt[:, k * 512 + n * 128: k * 512 + (n + 1) * 128],
                             xn[:, k, c0:c0 + w], k == 0, k == KC - 1, r=[t_wt, t_xn[k][ti]])
                    tmp, t_tmp = P.tmpf()
                    P.act(tmp[:, :w], ps[:, :w], AF.Relu, r=[t_ps], w=[t_tmp])
                    hn = s * 4 + n
                    P.tt("dve", hq[:, hn, c0:c0 + w], tmp[:, :w], tmp[:, :w], ALU.mult,
                         r=[t_tmp], w=[t_hq[hn][ti]])
        for s in range(2):
            wt, t_wt = P.next_slab()
            for ti, (c0, w) in enumerate(tiles):
                for n in range(4):
                    ps, t_ps = P.psum("acc", [0, 1, 2, 3])
                    for k in range(KC):
                        P.mm(ps[:, :w], t_ps, wt[:, k * 512 + n * 128: k * 512 + (n + 1) * 128],
                             hq[:, k, c0:c0 + w], k == 0, k == KC - 1, r=[t_wt, t_hq[k][ti]])
                    on = s * 4 + n
                    P.tt("dve", x[:, on, c0:c0 + w], ps[:, :w], x[:, on, c0:c0 + w], ALU.add,
                         r=[t_ps, t_x[on][ti]], w=[t_x[on][ti]])


def setup_state(P, ncols, tiles):
    s = P.s
    P.setup_common()
    ntl = len(tiles)
    P.tiles = tiles
    P.x = P.sb("xres", [128, KC, ncols], F32)
    P.t_x = [[Tl("x%d_%d" % (k, ti)) for ti in range(ntl)] for k in range(KC)]
    P.sq = P.sb("sq", [128, KC, 512], BF16)
    P.t_sq = [Tl("sq%d" % k) for k in range(KC)]
    P.rt = P.sb("rt", [128, 512], F32)
    P.t_rt = Tl("rt")
    P.rstd = P.sb("rstd", [128, 512], F32)
    P.t_rstd = Tl("rstd")
    P.epsc = P.sb("epsc", [128, 1], F32)
    s.add("dve", lambda e: e.memset(P.epsc[:, :], EPS), w=[P.t_const])
    tmps = [P.sb("tmpf%d" % i, [128, 512], F32) for i in range(3)]
    t_tmps = [Tl("tmpf%d" % i) for i in range(3)]
    rr = [0]

    def tmpf():
        i = rr[0] % 3
        rr[0] += 1
        return tmps[i], t_tmps[i]
    P.tmpf = tmpf


def alloc_R(P, ncols, ntl):
    P.R1 = P.sb("R1", [128, KC, ncols], BF16)
    P.t_R1 = [[Tl("r1_%d_%d" % (k, ti)) for ti in range(ntl)] for k in range(KC)]
    P.R2 = P.sb("R2", [128, KC, ncols], BF16)
    P.t_R2 = [[Tl("r2_%d_%d" % (k, ti)) for ti in range(ntl)] for k in range(KC)]


def load_x(P, xt_ap, ncols):
    s = P.s
    xt_v = xt_ap.rearrange("(k p) t -> p k t", p=128)
    for k in range(KC):
        s.add("sp", (lambda k: lambda e: e.dma_start(out=P.x[:, k, :], in_=xt_v[:, k, :]))(k),
              w=[P.t_x[k][ti] for ti in range(len(P.tiles))], dma=True)


def store_x(P, xo_ap, ncols=NMAIN):
    s = P.s
    xo_v = xo_ap.rearrange("(k p) t -> p k t", p=128)
    for k in range(KC):
        t_o = Tl("out%d" % k)
        s.add("sp", (lambda k: lambda e: e.dma_start(out=xo_v[:, k, :], in_=P.x[:, k, 0:ncols]))(k),
              r=[P.t_x[k][ti] for ti in range(4)], w=[t_o], dma=True)
        P.out_tiles.append(t_o)


def phase_L0(P):
    s = P.s
    x, t_x, tiles = P.x, P.t_x, P.tiles
    R1, t_R1, R2, t_R2 = P.R1, P.t_R1, P.R2, P.t_R2
    hm, t_hm = P.hm, P.t_hm
    m0 = P.mark()
    xn, t_xn = R1, t_R1
    for ti, (c0, w) in enumerate(tiles):
        P.rmsnorm([x[:, k, c0:c0 + w] for k in range(KC)], [t_x[k][ti] for k in range(KC)], "mixn0",
                  [xn[:, k, c0:c0 + w] for k in range(KC)], [t_xn[k][ti] for k in range(KC)], w)
    hc = R2.rearrange("p k (s t) -> p k s t", s=2)
    t_hc = t_R2

    def hc_dst(c, ti):
        if ti < 4:
            seg, half = ti // 2, ti % 2
            return hc[:, c, seg, HALO + half * 512: HALO + half * 512 + 512]
        return hc[:, c, :, 0:HALO]

    for s4 in range(4):
        wt, t_wt = P.next_slab()
        for ti, (c0, w) in enumerate(tiles):
            for cl in range(2):
                c = 2 * s4 + cl
                psa, t_psa = P.psum("acc", [0, 1, 2, 3])
                psg, t_psg = P.psum("acc", [0, 1, 2, 3])
                for k in range(KC):
                    P.mm(psa[:, :w], t_psa, wt[:, k * 512 + cl * 128: k * 512 + (cl + 1) * 128],
                         xn[:, k, c0:c0 + w], k == 0, k == KC - 1, r=[t_wt, t_xn[k][ti]])
                for k in range(KC):
                    P.mm(psg[:, :w], t_psg, wt[:, k * 512 + (2 + cl) * 128: k * 512 + (3 + cl) * 128],
                         xn[:, k, c0:c0 + w], k == 0, k == KC - 1, r=[t_wt, t_xn[k][ti]])
                sg, t_sg = P.tmpf()
                P.act(sg[:, :w], psg[:, :w], AF.Sigmoid, r=[t_psg, P.t_pp], w=[t_sg],
                      bias=P.ppc("a_b_in", 8 + c))
                if ti == 4:
                    P.tt("dve", sg[:, :w], sg[:, :w], hm[:, :], ALU.mult, r=[t_sg, t_hm], w=[t_sg])
                    src_a = psa[:, :w].rearrange("p (s t) -> p s t", s=2)
                    src_g = sg[:, :w].rearrange("p (s t) -> p s t", s=2)
                else:
                    src_a = psa[:, :w]
                    src_g = sg[:, :w]
                P.stt(hc_dst(c, ti), src_a, P.ppc("a_b_in", c), src_g, ALU.add, ALU.mult,
                      r=[t_psa, t_sg, P.t_pp], w=[t_hc[c][ti]])
    dgs = [P.sb("dg%d" % i, [128, 31 * 128], BF16) for i in range(2)]
    t_dgs = [Tl("dg%d" % i) for i in range(2)]
    yall, t_yall = R1, t_R1
    for c in range(KC):
        dg, t_dg = dgs[c % 2], t_dgs[c % 2]
        for k in range(31):
            P.ts("dve", dg[:, k * 128:(k + 1) * 128], P.ident_f[:, :], P.ppc("a_conv_w", c * 31 + k), None,
                 ALU.mult, None, r=[P.t_const, P.t_pp], w=[t_dg])
        for ti, (c0, w) in enumerate(tiles):
            ps, t_ps = P.psum("conv", [6, 7])
            if ti < 4:
                seg, half = ti // 2, ti % 2
                rd = [t_dg, t_hc[c][ti], t_hc[c][ti - 1 if half else 4]]
                for k in range(31):
                    b0 = 2 + half * 512 + k
                    P.mm(ps[:, :512], t_ps, dg[:, k * 128:(k + 1) * 128], hc[:, c, seg, b0:b0 + 512],
                         k == 0, k == 30, r=rd)
                P.act(yall[:, c, c0:c0 + w], ps[:, :w], AF.Identity, r=[t_ps, P.t_pp], w=[t_yall[c][ti]],
                      bias=P.ppc("a_conv_b", c))
            else:
                s.add("dve", (lambda c: lambda e: e.memset(yall[:, c, NMAIN:NT], 0.0))(c), w=[t_yall[c][4]])
                pv = ps[:, 0:4].rearrange("p (s t) -> p s t", s=2)
                for k in range(31):
                    P.mm(pv, t_ps, dg[:, k * 128:(k + 1) * 128], hc[:, c, :, k:k + 2],
                         k == 0, k == 30, r=[t_dg, t_hc[c][4]])
                yv = yall[:, c, NMAIN:NT].rearrange("p (s t) -> p s t", s=2)[:, :, 30:32]
                P.act(yv, pv, AF.Identity, r=[t_ps, P.t_pp], w=[t_yall[c][4]], bias=P.ppc("a_conv_b", c))
    sall, t_sall = R2, t_R2
    mu_sb = P.sb("mu_sb", [128, 512], F32)
    t_mu = Tl("mu")
    m2 = P.sb("m2", [128, 512], F32)
    t_m2 = Tl("m2")
    for ti, (c0, w) in enumerate(tiles):
        for k in range(KC):
            P.act(P.sq[:, k, :w], yall[:, k, c0:c0 + w], AF.Square, r=[t_yall[k][ti]], w=[P.t_sq[k]])
        psm, t_psm = P.psum("stat", [4, 5])
        pss, t_pss = P.psum("stat", [4, 5])
        for k in range(KC):
            P.mm(psm[:, :w], t_psm, P.onesD[:, :], yall[:, k, c0:c0 + w], k == 0, k == KC - 1,
                 r=[t_yall[k][ti], P.t_const])
        for k in range(KC):
            P.mm(pss[:, :w], t_pss, P.onesD[:, :], P.sq[:, k, :w], k == 0, k == KC - 1,
                 r=[P.t_sq[k], P.t_const])
        P.act(mu_sb[:, :w], psm[:, :w], AF.Identity, r=[t_psm], w=[t_mu])
        P.tt("dve", m2[:, :w], mu_sb[:, :w], mu_sb[:, :w], ALU.mult, r=[t_mu], w=[t_m2])
        P.tt("dve", m2[:, :w], pss[:, :w], m2[:, :w], ALU.subtract, r=[t_pss, t_m2], w=[t_m2])
        P.act(P.rt[:, :w], m2[:, :w], AF.Ln, r=[t_m2, P.t_const], w=[P.t_rt], bias=P.epsc[:, 0:1])
        P.act(P.rstd[:, :w], P.rt[:, :w], AF.Exp, r=[P.t_rt], w=[P.t_rstd], scale=-0.5)
        for k in range(KC):
            z, t_z = P.tmpf()
            P.tt("dve", z[:, :w], yall[:, k, c0:c0 + w], mu_sb[:, :w], ALU.subtract,
                 r=[t_yall[k][ti], t_mu], w=[t_z])
            P.tt("dve", z[:, :w], z[:, :w], P.rstd[:, :w], ALU.mult, r=[t_z, P.t_rstd], w=[t_z])
            P.act(sall[:, k, c0:c0 + w], z[:, :w], AF.Silu, r=[t_z, P.t_pp], w=[t_sall[k][ti]],
                  bias=P.ppc("a_ln_b", k), scale=P.ppc("a_ln_g", k))
    for s2 in range(2):
        wt, t_wt = P.next_slab()
        for ti, (c0, w) in enumerate(tiles):
            for n in range(4):
                ps, t_ps = P.psum("acc", [0, 1, 2, 3])
                for k in range(KC):
                    P.mm(ps[:, :w], t_ps, wt[:, k * 512 + n * 128: k * 512 + (n + 1) * 128],
                         sall[:, k, c0:c0 + w], k == 0, k == KC - 1, r=[t_wt, t_sall[k][ti]])
                on = s2 * 4 + n
                P.stt(x[:, on, c0:c0 + w], ps[:, :w], P.ppc("a_b_out", on), x[:, on, c0:c0 + w],
                      ALU.add, ALU.add, r=[t_ps, P.t_pp, t_x[on][ti]], w=[t_x[on][ti]])
    P.release(m0)
    mlp_block(P, 0, x, t_x, tiles, R1, t_R1, R2, t_R2)


def phase_L1(P):
    s = P.s
    x, t_x, tiles = P.x, P.t_x, P.tiles
    R1, t_R1, R2, t_R2 = P.R1, P.t_R1, P.R2, P.t_R2
    hm, t_hm = P.hm, P.t_hm
    xn, t_xn = R1, t_R1
    for ti, (c0, w) in enumerate(tiles):
        P.rmsnorm([x[:, k, c0:c0 + w] for k in range(KC)], [t_x[k][ti] for k in range(KC)], "mixn1",
                  [xn[:, k, c0:c0 + w] for k in range(KC)], [t_xn[k][ti] for k in range(KC)], w)
    ub = R2.rearrange("p k (s t) -> p k s t", s=2)
    t_ub = t_R2

    def ub_dst(c, ti):
        if ti < 4:
            seg, half = ti // 2, ti % 2
            return ub[:, c, seg, HALO + half * 512: HALO + half * 512 + 512]
        return ub[:, c, :, 0:HALO]
    for s4 in range(4):
        wt, t_wt = P.next_slab()
        for ti, (c0, w) in enumerate(tiles):
            for cl in range(2):
                c = 2 * s4 + cl
                psc, t_psc = P.psum("acc", [0, 1, 2, 3])
                psh, t_psh = P.psum("acc", [0, 1, 2, 3])
                for k in range(KC):
                    P.mm(psc[:, :w], t_psc, wt[:, k * 512 + cl * 128: k * 512 + (cl + 1) * 128],
                         xn[:, k, c0:c0 + w], k == 0, k == KC - 1, r=[t_wt, t_xn[k][ti]])
                for k in range(KC):
                    P.mm(psh[:, :w], t_psh, wt[:, k * 512 + (2 + cl) * 128: k * 512 + (3 + cl) * 128],
                         xn[:, k, c0:c0 + w], k == 0, k == KC - 1, r=[t_wt, t_xn[k][ti]])
                gc, t_gc = P.tmpf()
                P.act(gc[:, :w], psc[:, :w], AF.Identity, r=[t_psc], w=[t_gc])
                if ti == 4:
                    P.tt("dve", gc[:, :w], gc[:, :w], hm[:, :], ALU.mult, r=[t_gc, t_hm], w=[t_gc])
                    src_h = psh[:, :w].rearrange("p (s t) -> p s t", s=2)
                    src_c = gc[:, :w].rearrange("p (s t) -> p s t", s=2)
                else:
                    src_h = psh[:, :w]
                    src_c = gc[:, :w]
                P.tt("dve", ub_dst(c, ti), src_h, src_c, ALU.mult, r=[t_psh, t_gc], w=[t_ub[c][ti]])
    def conv3(c, ti):
        seg, half = ti // 2, ti % 2
        acc, t_acc = P.tmpf()
        base = HALO + half * 512
        rd = [t_ub[c][ti], t_ub[c][ti - 1 if half else 4], P.t_pp]
        P.ts("dve", acc[:, :], ub[:, c, seg, base - 2: base - 2 + 512], P.ppc("b_conv_w", c * 3 + 0), None,
             ALU.mult, None, r=rd, w=[t_acc])
        P.stt(acc[:, :], ub[:, c, seg, base - 1: base - 1 + 512], P.ppc("b_conv_w", c * 3 + 1), acc[:, :],
              ALU.mult, ALU.add, r=rd + [t_acc], w=[t_acc])
        P.stt(ub[:, c, seg, base: base + 512], ub[:, c, seg, base: base + 512],
              P.ppc("b_conv_w", c * 3 + 2), acc[:, :], ALU.mult, ALU.add,
              r=rd + [t_acc], w=[t_ub[c][ti]])
    mtiles = tiles[:4]
    for s2 in range(2):
        wt, t_wt = P.next_slab()
        for ti in (1, 0, 3, 2):
            (c0, w) = mtiles[ti]
            seg, half = ti // 2, ti % 2
            base = HALO + half * 512
            for n in range(4):
                c = s2 * 4 + n
                conv3(c, ti)
                ps, t_ps = P.psum("acc", [0, 1, 2, 3])
                for k in range(KC):
                    P.mm(ps[:, :w], t_ps, wt[:, k * 512 + n * 128: k * 512 + (n + 1) * 128],
                         xn[:, k, c0:c0 + w], k == 0, k == KC - 1, r=[t_wt, t_xn[k][ti]])
                P.tt("dve", ub[:, c, seg, base:base + 512], ps[:, :w], ub[:, c, seg, base:base + 512], ALU.mult,
                     r=[t_ps, t_ub[c][ti]], w=[t_ub[c][ti]])
    for s2 in range(2):
        wt, t_wt = P.next_slab()
        for ti, (c0, w) in enumerate(mtiles):
            seg, half = ti // 2, ti % 2
            base = HALO + half * 512
            for n in range(4):
                ps, t_ps = P.psum("acc", [0, 1, 2, 3])
                for k in range(KC):
                    P.mm(ps[:, :w], t_ps, wt[:, k * 512 + n * 128: k * 512 + (n + 1) * 128],
                         ub[:, k, seg, base:base + 512], k == 0, k == KC - 1, r=[t_wt, t_ub[k][ti]])
                on = s2 * 4 + n
                P.tt("dve", x[:, on, c0:c0 + w], ps[:, :w], x[:, on, c0:c0 + w], ALU.add,
                     r=[t_ps, t_x[on][ti]], w=[t_x[on][ti]])
    mlp_block(P, 1, x, t_x, mtiles, R1, t_R1, R2, t_R2)


NH = 16
HD = 64
VW = 16 * 65
NEG = -60000.0


def seg_loc(i):
    return (i, 0) if i < 4 else (7 - i, 1)


def phase_L2pre(P, qd, kdst, vdst, cd, t_kd_hp, t_vd_hp):
    s = P.s
    x, t_x = P.x, P.t_x
    mt = P.tiles[:4]
    xn, t_xn = P.R1, P.t_R1
    for ti, (c0, w) in enumerate(mt):
        P.rmsnorm([x[:, k, c0:c0 + w] for k in range(KC)], [t_x[k][ti] for k in range(KC)], "mixn2",
                  [xn[:, k, c0:c0 + w] for k in range(KC)], [t_xn[k][ti] for k in range(KC)], w)
    m0 = P.mark()
    t_c = Tl("l2c")
    s.barrier()
    r2flat = P.R2.rearrange("p k t -> p (k t)")
    r2top = [0]

    def r2alloc(shape, dt):
        n = shape[1]
        ne = n if dt == BF16 else 2 * n
        off = r2top[0]
        r2top[0] += ne
        assert r2top[0] <= KC * NT
        v = r2flat[:, off:off + ne]
        if dt == F32:
            v = v.bitcast(F32)
        return v[0:shape[0]]
    wf = P.sb("wf_bf", [128, KC, 16], BF16)
    s.add("dve", lambda e: e.tensor_copy(out=wf.rearrange("p k h -> p (k h)"), in_=P.ppc("c_w_f", 0, 128)),
          r=[P.t_pp], w=[t_c])
    nbf = P.sb("nbf", [16, 1], F32)
    P.ts("dve", nbf[:, :], P.ppc("c_b_f")[0:16], -1.0, None, ALU.mult, None, r=[P.t_pp], w=[t_c])
    one1 = P.sb("one1", [128, 1], F32)
    s.add("dve", lambda e: e.memset(one1[:, :], 1.0), w=[t_c])
    lbuf = r2alloc([16, NMAIN], F32)
    t_l = Tl("lbuf")
    ones16 = r2alloc([16, SEG], F32)
    s.add("dve", lambda e: e.memset(ones16[:, :], 1.0), w=[t_c])
    for ti, (c0, w) in enumerate(mt):
        ps, t_ps = P.psum("stat", [4, 5])
        for k in range(KC):
            P.mm(ps[0:16, :w], t_ps, wf[:, k, :], xn[:, k, c0:c0 + w], k == 0, k == KC - 1,
                 r=[t_c, t_xn[k][ti]])
        et, t_et = P.tmpf()
        P.act(et[0:16, :w], ps[0:16, :w], AF.Exp, r=[t_ps, t_c], w=[t_et], bias=nbf[:, 0:1], scale=-1.0)
        P.act(lbuf[:, c0:c0 + w], et[0:16, :w], AF.Ln, r=[t_et, t_c], w=[t_l], bias=one1[0:16, 0:1])
    cl = r2alloc([16, NMAIN], F32)
    t_cl = Tl("cl")
    for sg in range(2):
        s.add("dve", (lambda sg: lambda e: e.tensor_tensor_scan(
            cl[:, sg * SEG:(sg + 1) * SEG], ones16[:, :], lbuf[:, sg * SEG:(sg + 1) * SEG], 0.0,
            ALU.mult, ALU.subtract))(sg), r=[t_l, t_c], w=[t_cl])
    t_cd = Tl("cd")
    s.add("sp", lambda e: e.dma_start(out=cd[:, :], in_=cl[:, :]), r=[t_cl], w=[t_cd], dma=True)
    P.out_tiles.append(t_cd)
    aq = lbuf
    hi = r2alloc([16, NMAIN], BF16)
    lo = r2alloc([16, NMAIN], BF16)
    t_aq = Tl("aq")
    for ti, (c0, w) in enumerate(mt):
        P.ts("dve", aq[:, c0:c0 + w], cl[:, c0:c0 + w], cl[:, c0:c0 + 1], None, ALU.subtract, None,
             r=[t_cl, t_l], w=[t_aq, t_l])
    s.add("dve", lambda e: e.tensor_copy(out=hi[:, :], in_=aq[:, :]), r=[t_aq], w=[t_aq])
    P.tt("dve", lo[:, :], aq[:, :], hi[:, :], ALU.subtract, r=[t_aq], w=[t_aq])
    t_qd = Tl("qd")
    s.add("sp", lambda e: e.dma_start(out=qd[:, 64, :], in_=hi[:, :]), r=[t_aq], w=[t_qd], dma=True)
    s.add("sp", lambda e: e.dma_start(out=qd[:, 65, :], in_=lo[:, :]), r=[t_aq], w=[t_qd], dma=True)
    ones64 = P.sb("ones64", [64, 64], BF16)
    s.add("dve", lambda e: e.memset(ones64[:, :], 1.0 / HD), w=[t_c])
    gq = P.sb("gq", [64, 1], F32)
    P.ts("dve", gq[:, :], P.ppc("c_q_norm")[0:64], HD ** -0.5, None, ALU.mult, None, r=[P.t_pp], w=[t_c])
    stq = [P.sb("stq%d" % i, [64, NMAIN], BF16) for i in range(2)]
    stk = [P.sb("stk%d" % i, [66, NMAIN], BF16) for i in range(2)]
    t_stq = [Tl("stq%d" % i) for i in range(2)]
    t_stk = [Tl("stk%d" % i) for i in range(2)]
    for i in range(2):
        s.add("dve", (lambda i: lambda e: e.memset(stk[i][64:66, :], 1.0))(i), w=[t_stk[i]])
    sqh = [P.sb("sqh%d" % i, [64, 512], BF16) for i in range(2)]
    t_sqh = [Tl("sqh%d" % i) for i in range(2)]
    t_kd = Tl("kd")
    cnt = 0
    pending = [None]

    def finish_unit(which, h, stg, t_stg, gcol, ps, t_ps, c0, w, is_last):
        sq_, t_sq_ = sqh[finish_unit.cnt % 2], t_sqh[finish_unit.cnt % 2]
        finish_unit.cnt += 1
        P.act(sq_[:, :w], ps[0:64, :w], AF.Square, r=[t_ps], w=[t_sq_])
        ps2, t_ps2 = P.psum("stat", [4, 5])
        P.mm(ps2[0:64, :w], t_ps2, ones64[:, :], sq_[:, :w], True, True, r=[t_sq_, t_c])
        P.act(P.rt[0:64, :w], ps2[0:64, :w], AF.Ln, r=[t_ps2, P.t_const], w=[P.t_rt],
              bias=P.epsc[0:64, 0:1])
        P.act(P.rstd[0:64, :w], P.rt[0:64, :w], AF.Exp, r=[P.t_rt], w=[P.t_rstd], scale=-0.5)
        P.stt(stg[0:64, c0:c0 + w], ps[0:64, :w], gcol, P.rstd[0:64, :w], ALU.mult, ALU.mult,
              r=[t_ps, P.t_rstd, t_c, P.t_pp], w=[t_stg])
        if is_last:
            if which == 0:
                s.add("sp", lambda e: e.dma_start(out=qd[h, 0:64, :], in_=stg[0:64, :]),
                      r=[t_stg], w=[t_qd], dma=True)
            else:
                s.add("sp", lambda e: e.dma_start(out=kdst(h)[0:64, :], in_=stg[0:64, :]),
                      r=[t_stg], w=[t_kd_hp[h // 2]], dma=True)
                s.add("sp", lambda e: e.dma_start(out=kdst(h)[64:66, :], in_=stg[64:66, :]),
                      r=[t_stg], w=[t_kd_hp[h // 2]], dma=True)
    finish_unit.cnt = 0
    for which in range(2):
        for sl in range(2):
            wt, t_wt = P.next_slab()
            for hl in range(8):
                h = sl * 8 + hl
                if which == 0:
                    stg, t_stg = stq[h % 2], t_stq[h % 2]
                    gcol = gq[:, 0:1]
                else:
                    stg, t_stg = stk[h % 2], t_stk[h % 2]
                    gcol = P.ppc("c_k_norm")[0:64]
                for ti, (c0, w) in enumerate(mt):
                    ps, t_ps = P.psum("acc", [0, 1, 2, 3])
                    for k in range(KC):
                        P.mm(ps[0:64, :w], t_ps, wt[:, k * 512 + hl * 64: k * 512 + (hl + 1) * 64],
                             xn[:, k, c0:c0 + w], k == 0, k == KC - 1, r=[t_wt, t_xn[k][ti]])
                    if pending[0] is not None:
                        finish_unit(*pending[0])
                    pending[0] = (which, h, stg, t_stg, gcol, ps, t_ps, c0, w, ti == len(mt) - 1)
    finish_unit(*pending[0])
    P.t_qd = t_qd
    P.t_cd = t_cd
    vst = P.R2.rearrange("p k t -> p (k t)")[:, 0:16 * VW].rearrange("p (h kt c) -> p h kt c", h=16, kt=16)
    t_vst = Tl("vst")
    s.barrier()
    for k in range(KC):
        for ti in range(len(P.t_R2[k])):
            P.t_R2[k][ti] = t_vst
    s.add("dve", lambda e: e.memset(vst[:, :, :, 64:65], 1.0), w=[t_vst])
    ev = 0
    for vs in range(2):
        wt, t_wt = P.next_slab()
        for tb in range(16):
            ps, t_ps = P.psum("acc", [0, 1, 2, 3])
            ti = tb // 4
            for k in range(KC):
                P.mm(ps[:, :], t_ps, xn[:, k, tb * 128:(tb + 1) * 128], wt[:, k * 512:(k + 1) * 512],
                     k == 0, k == KC - 1, r=[t_wt, t_xn[k][ti]])
            dst = vst[:, vs * 8:(vs + 1) * 8, tb, 0:64]
            srcv = ps[:, :].rearrange("p (h c) -> p h c", h=8)
            if ev % 2 == 0:
                P.act(dst, srcv, AF.Identity, r=[t_ps], w=[t_vst])
            else:
                s.add("dve", (lambda dst, srcv: lambda e: e.tensor_copy(out=dst, in_=srcv))(dst, srcv),
                      r=[t_ps], w=[t_vst])
            ev += 1
    vflat = vst.rearrange("p h kt c -> p h (kt c)")
    for hp in range(8):
        s.add("sp", (lambda hp: lambda e: e.dma_start(out=vdst(hp).rearrange("(h p) f -> p h f", h=2),
                                                      in_=vflat[:, 2 * hp:2 * hp + 2, :]))(hp),
              r=[t_vst], w=[t_vd_hp[hp]], dma=True)
    P.release(m0)


def phase_L2attn(P, qd, kown, vown, cown, kallf, vallf, call, flexmd, seld, cmaskd, dep):
    s = P.s
    x, t_x = P.x, P.t_x
    mt = P.tiles[:4]
    m0 = P.mark()
    t_k = Tl("l2a_const")
    Tt = P.sb("Tt", [16, 8], F32)
    t_T = Tl("Tt")
    for i in range(8):
        r_, part = seg_loc(i)
        col = part * SEG + SEG - 1
        s.add("sp", (lambda i, r_, col: lambda e: e.dma_start(out=Tt[:, i:i + 1], in_=call[r_, :, col:col + 1], allow_slow_non_contiguous=True))(i, r_, col),
              r=[dep["call"]], w=[t_T], dma=True)
    sel = P.sb("sel", [16, 16], F32)
    s.add("sp", lambda e: e.dma_start(out=sel[:, :], in_=seld[:, :]), w=[t_k], dma=True)
    flexm = P.sb("flexm", [128, 8], F32)
    s.add("sp", lambda e: e.dma_start(out=flexm[:, :], in_=flexmd[:, :]), w=[t_k], dma=True)
    cmask = P.sb("cmask", [128, 4, 512], BF16)
    s.add("pool", lambda e: e.dma_start(out=cmask[:, :, :], in_=cmaskd.rearrange("i p q -> p i q")), w=[t_k], dma=True)
    ones8 = P.sb("ones8", [16, 8], F32)
    s.add("dve", lambda e: e.memset(ones8[:, :], 1.0), w=[t_k])
    Pin = P.sb("Pin", [16, 8], F32)
    Pex = P.sb("Pex", [16, 8], F32)
    t_P = Tl("Pex")
    s.add("dve", lambda e: e.tensor_tensor_scan(Pin[:, :], ones8[:, :], Tt[:, :], 0.0, ALU.mult, ALU.add),
          r=[t_T, t_k], w=[t_P])
    P.tt("dve", Pex[:, :], Pin[:, :], Tt[:, :], ALU.subtract, r=[t_P, t_T], w=[t_P])
    PAB = P.sb("PAB", [16, 2], F32)
    ptmp = P.sb("ptmp", [16, 8], F32)
    for sg in range(2):
        P.tt("dve", ptmp[:, :], Pex[:, :], sel[:, sg * 8:(sg + 1) * 8], ALU.mult, r=[t_P, t_k], w=[t_P])
        s.add("dve", (lambda sg: lambda e: e.reduce_sum(PAB[:, sg:sg + 1], ptmp[:, :], mybir.AxisListType.X))(sg),
              r=[t_P], w=[t_P])
    clq = P.sb("clq", [16, 4], F32)
    t_clq = Tl("clq")
    for qt in range(4):
        s.add("sp", (lambda qt: lambda e: e.dma_start(out=clq[:, qt:qt + 1], in_=cown[:, qt * 512:qt * 512 + 1], allow_slow_non_contiguous=True))(qt),
              r=[dep["cd"]], w=[t_clq], dma=True)
    CQ = P.sb("CQ", [16, 4], F32)
    for qt in range(4):
        P.tt("dve", CQ[:, qt:qt + 1], clq[:, qt:qt + 1], PAB[:, qt // 2:qt // 2 + 1], ALU.add,
             r=[t_clq, t_P], w=[t_P])
    CQd = P.sb("CQd", [16, 64], F32)
    negI = P.sb("negI", [16, 64], F32)
    ones16c = P.sb("ones16c", [16, 128], F32)
    s.add("dve", lambda e: e.memset(ones16c[:, :], 1.0), w=[t_k])
    for qt in range(4):
        P.ts("dve", CQd[:, qt * 16:(qt + 1) * 16], P.ident_f[0:16, 0:16], CQ[:, qt:qt + 1], None, ALU.mult, None,
             r=[P.t_const, t_P], w=[t_P])
        P.ts("dve", negI[:, qt * 16:(qt + 1) * 16], P.ident_f[0:16, 0:16], -1.0, None, ALU.mult, None,
             r=[P.t_const], w=[t_k])
    biasall = P.sb("biasall", [128, 72, 64], F32)
    t_bias = Tl("biasall")
    cgk = [P.sb("cgk%d" % i, [16, SEG], F32) for i in range(2)]
    t_cgk = [Tl("cgk%d" % i) for i in range(2)]
    for ks in range(9):
        cg, t_cg = cgk[ks % 2], t_cgk[ks % 2]
        if ks < 7:
            r_, part = seg_loc(ks)
            src = call[r_, :, part * SEG:(part + 1) * SEG]
            pcol = Pex[:, ks:ks + 1]
        else:
            src = cown[:, (ks - 7) * SEG:(ks - 6) * SEG]
            pcol = PAB[:, ks - 7:ks - 6]
        s.add("sp", (lambda cg, src: lambda e: e.dma_start(out=cg[:, :], in_=src))(cg, src),
              r=[dep["call"], dep["cd"]], w=[t_cg], dma=True)
        P.ts("dve", cg[:, :], cg[:, :], pcol, None, ALU.add, None, r=[t_cg, t_P], w=[t_cg])
        for kt in range(8):
            ps, t_ps = P.psum("acc", [4, 5, 6, 7])
            P.mm(ps[:, 0:64], t_ps, cg[:, kt * 128:(kt + 1) * 128], negI[:, :], True, False, r=[t_cg, t_k])
            P.mm(ps[:, 0:64], t_ps, ones16c[:, :], CQd[:, :], False, True, r=[t_k, t_P])
            s.add("dve", (lambda ks, kt, ps: lambda e: e.tensor_copy(out=biasall[:, ks * 8 + kt, :], in_=ps[:, 0:64]))(ks, kt, ps),
                  r=[t_ps], w=[t_bias])
    flexbias = P.sb("flexbias", [128, 3, 8, 32], F32)
    for p in range(3):
        P.ts("dve", flexbias[:, p, :, :], biasall[:, p * 8:(p + 1) * 8, 0:32], flexm[:, p:p + 1], None,
             ALU.mult, None, r=[t_bias, t_k], w=[t_bias])
        P.stt(flexbias[:, p, :, :], biasall[:, (6 - p) * 8:(7 - p) * 8, 32:64], flexm[:, 3 + p:4 + p],
              flexbias[:, p, :, :], ALU.mult, ALU.add, r=[t_bias, t_k], w=[t_bias])
    sel65 = P.sb("sel65", [65, 64], F32)
    s.add("dve", lambda e: e.memset(sel65[:, :], 0.0), w=[t_k])
    s.add("dve", lambda e: e.memset(sel65[64:65, :], 1.0), r=[t_k], w=[t_k])
    NRING = 4
    kcs = [P.sb("kc%d" % i, [66, SEG], BF16) for i in range(NRING)]
    vcs = [P.sb("vc%d" % i, [128, 8, 65], BF16) for i in range(NRING)]
    t_kv = [Tl("kv%d" % i) for i in range(NRING)]
    qhs = [P.sb("qh%d" % i, [66, NMAIN], BF16) for i in range(2)]
    t_qh = [Tl("qh%d" % i) for i in range(2)]
    qfb = [P.sb("qflex%d" % i, [66, SEG], BF16) for i in range(2)]
    t_qfb = [Tl("qflex%d" % i) for i in range(2)]
    kfb = [P.sb("kflex%d" % i, [66, SEG], BF16) for i in range(2)]
    vfb = [P.sb("vflex%d" % i, [128, 8, 65], BF16) for i in range(2)]
    t_kfb = [Tl("kvflex%d" % i) for i in range(2)]
    prep_i = [0]
    pts = [P.sb("pt%d" % i, [128, 512], BF16) for i in range(3)]
    t_pt = [Tl("pt%d" % i) for i in range(3)]
    osb = [P.sb("osb%d" % i, [65, 512], F32) for i in range(2)]
    t_osb = [Tl("osb%d" % i) for i in range(2)]
    rden = P.sb("rden", [64, 512], F32)
    t_rden = Tl("rden")
    oT = P.sb("oT", [64, 4, NMAIN], BF16)
    t_oT = [[Tl("oT%d_%d" % (hl, qt)) for qt in range(4)] for hl in range(4)]
    fsum = P.sq.rearrange("p k t -> p (k t)").bitcast(F32)[0:65, 0:2048].rearrange("p (a t) -> p a t", a=4)
    t_fs = [Tl("fsum%d" % i) for i in range(4)]
    ring_i = [0]
    pt_i = [0]
    ob_i = [0]

    def ring_next():
        i = ring_i[0] % NRING
        ring_i[0] += 1
        return kcs[i], vcs[i], t_kv[i]

    def load_chunk(h, ks):
        kc, vc, t = ring_next()
        if ks < 7:
            r_, part = seg_loc(ks)
            ksrc = kallf(r_, h)[:, part * SEG:(part + 1) * SEG]
            vsrc = vallf(r_, h)[:, part * 8 * 65:(part + 1) * 8 * 65]
            rk, rv = [dep["kall"][h // 2]], [dep["vall"][h // 2]]
        else:
            part = ks - 7
            ksrc = kown(h)[:, part * SEG:(part + 1) * SEG]
            vsrc = vown(h)[:, part * 8 * 65:(part + 1) * 8 * 65]
            rk, rv = [dep["kd"][h // 2]], [dep["vd"][h // 2]]
        s.add("sp", lambda e: e.dma_start(out=kc[0:64, :], in_=ksrc[0:64]), r=rk, w=[t], dma=True)
        s.add("sp", lambda e: e.dma_start(out=kc[64:66, :], in_=ksrc[64:66]), r=rk, w=[t], dma=True)
        s.add("sp", lambda e: e.dma_start(out=vc.rearrange("p a b -> p (a b)"), in_=vsrc), r=rv, w=[t], dma=True)
        return kc, vc, t

    SKEW = 2

    def run_blocks(blist):
        pend = []

        def emit_pv(item):
            (vc, t_kvc, kt, pt, tpt, bank, st, sp_) = item
            P.mm(P.ps[bank][0:65, :], P.t_ps[bank], vc[:, kt, :], pt[:, :], st, sp_, r=[t_kvc, tpt])
        for (kc, vc, t_kvc, kt, q_ap, t_q, bcol, mi, bank, st, sp_) in blist:
            psS, t_psS = P.psum("S", [4, 5, 6])
            P.mm(psS[:, :], t_psS, kc[0:66, kt * 128:(kt + 1) * 128], q_ap, True, True, r=[t_kvc, t_q])
            pt, tpt = pts[pt_i[0] % 3], t_pt[pt_i[0] % 3]
            pt_i[0] += 1
            if mi is None:
                P.act(pt[:, :], psS[:, :], AF.Exp, r=[t_psS, t_bias], w=[tpt], bias=bcol)
            else:
                sb_, tsb = P.tmpf()
                P.tt("dve", sb_[:, :], psS[:, :], cmask[:, mi, :], ALU.add, r=[t_psS, t_k], w=[tsb])
                P.act(pt[:, :], sb_[:, :], AF.Exp, r=[tsb, t_bias], w=[tpt], bias=bcol)
            pend.append((vc, t_kvc, kt, pt, tpt, bank, st, sp_))
            if len(pend) > SKEW:
                emit_pv(pend.pop(0))
        while pend:
            emit_pv(pend.pop(0))

    def load_q(h):
        qh, tq = qhs[h % 2], t_qh[h % 2]
        s.add("sp", (lambda h, qh: lambda e: e.dma_start(out=qh[0:64, :], in_=qd[h, 0:64, :]))(h, qh),
              r=[dep["qd"]], w=[tq], dma=True)
        s.add("sp", (lambda h, qh: lambda e: e.dma_start(out=qh[64:66, :], in_=qd[h, 64:66, :]))(h, qh),
              r=[dep["qd"]], w=[tq], dma=True)

    def prepare(h, p):
        i = prep_i[0] % 2
        prep_i[0] += 1
        qh, tq = qhs[h % 2], t_qh[h % 2]
        mA, mB = flexm[:, p:p + 1], flexm[:, 3 + p:4 + p]
        kA, vA, tA = load_chunk(h, p)
        kB, vB, tB = load_chunk(h, 6 - p)
        kf, vf, tf, qf, t_qf = kfb[i], vfb[i], t_kfb[i], qfb[i], t_qfb[i]
        P.ts("dve", kf[:, :], kA[:, :], mA[0:66], None, ALU.mult, None, r=[tA, t_k], w=[tf])
        P.stt(kf[:, :], kB[:, :], mB[0:66], kf[:, :], ALU.mult, ALU.add, r=[tB, t_k, tf], w=[tf])
        vff, vAf, vBf = (v_.rearrange("p a b -> p (a b)") for v_ in (vf, vA, vB))
        P.ts("dve", vff, vAf, mA, None, ALU.mult, None, r=[tA, t_k, tf], w=[tf])
        P.stt(vff, vBf, mB, vff, ALU.mult, ALU.add, r=[tB, t_k, tf], w=[tf])
        P.ts("dve", qf[:, :], qh[:, 0:SEG], mA[0:66], None, ALU.mult, None, r=[tq, t_k], w=[t_qf])
        P.stt(qf[:, :], qh[:, SEG:2 * SEG], mB[0:66], qf[:, :], ALU.mult, ALU.add, r=[tq, t_k, t_qf], w=[t_qf])
        return kf, vf, tf, qf, t_qf

    load_q(0)
    prepared = prepare(0, 0)
    for h in range(NH):
        hl = h % 4
        qh, tq = qhs[h % 2], t_qh[h % 2]
        for p in range(3):
            mA, mB = flexm[:, p:p + 1], flexm[:, 3 + p:4 + p]
            kf, vf, tf, qf, t_qf = prepared
            if p < 2:
                prepared = prepare(h, p + 1)
            elif h + 1 < NH:
                load_q(h + 1)
                prepared = prepare(h + 1, 0)
            bk = 2 * (p % 2)
            bl = []
            for kt in range(8):
                for half in range(2):
                    bl.append((kf, vf, tf, kt, qf[0:66, half * 512:(half + 1) * 512], t_qf,
                               flexbias[:, p, kt, half * 16 + h: half * 16 + h + 1], None, bk + half, kt == 0, kt == 7))
            run_blocks(bl)
            for half in range(2):
                for dest, m in ((0, mA), (1, mB)):
                    fi = 2 * dest + half
                    if p == 0:
                        P.ts("dve", fsum[:, fi, :], P.ps[bk + half][0:65, :], m[0:65], None, ALU.mult, None,
                             r=[P.t_ps[bk + half], t_k], w=[t_fs[fi]])
                    else:
                        P.stt(fsum[:, fi, :], P.ps[bk + half][0:65, :], m[0:65], fsum[:, fi, :], ALU.mult, ALU.add,
                              r=[P.t_ps[bk + half], t_k, t_fs[fi]], w=[t_fs[fi]])
        bl = []
        for ks in range(4):
            kc, vc, tkv = load_chunk(h, ks)
            for kt in range(8):
                for qt in (2, 3):
                    bl.append((kc, vc, tkv, kt, qh[0:66, qt * 512:(qt + 1) * 512], tq,
                               biasall[:, ks * 8 + kt, qt * 16 + h: qt * 16 + h + 1], None, qt,
                               ks == 0 and kt == 0, False))
        for sg in range(2):
            kc, vc, tkv = load_chunk(h, 7 + sg)
            for half in range(2):
                qt = 2 * sg + half
                nk = 4 * (half + 1)
                for kt in range(nk):
                    mi = (kt - 4 * half) if kt >= 4 * half else None
                    bl.append((kc, vc, tkv, kt, qh[0:66, qt * 512:(qt + 1) * 512], tq,
                               biasall[:, (7 + sg) * 8 + kt, qt * 16 + h: qt * 16 + h + 1], mi, qt,
                               sg == 0 and kt == 0, kt == nk - 1))
        run_blocks(bl)
        for qt in range(4):
            ob, tob = osb[ob_i[0] % 2], t_osb[ob_i[0] % 2]
            ob_i[0] += 1
            fi = 2 * (qt // 2) + qt % 2
            P.tt("dve", ob[:, :], P.ps[qt][0:65, :], fsum[:, fi, :], ALU.add, r=[P.t_ps[qt], t_fs[fi]], w=[tob])
            P.mm(P.ps[7][0:64, :], P.t_ps[7], sel65[:, :], ob[:, :], True, True, r=[t_k, tob])
            P.act(rden[:, :], P.ps[7][0:64, :], AF.Ln, r=[P.t_ps[7]], w=[t_rden])
            P.act(rden[:, :], rden[:, :], AF.Exp, r=[t_rden], w=[t_rden], scale=-1.0)
            P.tt("dve", oT[:, hl, qt * 512:(qt + 1) * 512], ob[0:64, :], rden[:, :], ALU.mult,
                 r=[tob, t_rden], w=[t_oT[hl][qt]])
        if hl == 3:
            wt, t_wt = P.next_slab()
            for ti, (c0, w) in enumerate(mt):
                for n in range(KC):
                    ps, t_ps = P.psum("S", [4, 5, 6])
                    for j in range(4):
                        P.mm(ps[:, :w], t_ps, wt[0:64, j * 1024 + n * 128: j * 1024 + (n + 1) * 128],
                             oT[:, j, c0:c0 + w], j == 0, j == 3, r=[t_wt, t_oT[j][ti]])
                    P.tt("dve", x[:, n, c0:c0 + w], ps[:, :w], x[:, n, c0:c0 + w], ALU.add,
                         r=[t_ps, t_x[n][ti]], w=[t_x[n][ti]])
    P.release(m0)


GH = 4
CH = 128
NCH = SEG // CH


def gla_seg(P, sg, mode, sinit=None, gout=None, dout=None, tri=None, sinit_t=None):
    s = P.s
    full = mode == "full"
    x, t_x = P.x, P.t_x
    tis = [2 * sg, 2 * sg + 1]
    m0 = P.mark()
    t_g = Tl("gla_c")
    xn = P.sb("gxn", [128, KC, SEG], BF16)
    t_xn = [[Tl("gxn%d_%d" % (k, i)) for i in range(2)] for k in range(KC)]
    for i in range(2):
        c0 = sg * SEG + i * 512
        P.rmsnorm([x[:, k, c0:c0 + 512] for k in range(KC)], [t_x[k][tis[i]] for k in range(KC)], "mixn3",
                  [xn[:, k, i * 512:(i + 1) * 512] for k in range(KC)], [t_xn[k][i] for k in range(KC)], 512)
    wg1 = P.sb("wg1", [128, KC, 16], BF16)
    s.add("dve", lambda e: e.tensor_copy(out=wg1.rearrange("p k h -> p (k h)"), in_=P.ppc("d_w_g1", 0, 128)),
          r=[P.t_pp], w=[t_g])
    wg2 = P.sb("wg2", [16, 512], BF16)
    s.add("dve", lambda e: e.tensor_copy(out=wg2[:, :], in_=P.ppc("d_w_g2", 0, 512)[0:16]), r=[P.t_pp], w=[t_g])
    nbg = P.sb("nbg", [128, 4], F32)
    P.ts("dve", nbg[:, :], P.ppc("d_b_g", 0, 4), -1.0, None, ALU.mult, None, r=[P.t_pp], w=[t_g])
    one1 = P.sb("gone1", [128, 1], F32)
    s.add("dve", lambda e: e.memset(one1[:, :], 1.0), w=[t_g])
    ones512 = P.sb("ones512", [128, 512], F32)
    s.add("dve", lambda e: e.memset(ones512[:, :], 1.0), w=[t_g])
    vtok = P.sb("vtok", [128, NCH, D], BF16)
    t_v = [Tl("vtok%d" % c) for c in range(NCH)]
    ev = 0
    for vs in range(2):
        wt, t_wt = P.next_slab()
        for c in range(NCH):
            ps, t_ps = P.psum("acc", [0, 1, 2, 3])
            for k in range(KC):
                P.mm(ps[:, :], t_ps, xn[:, k, c * CH:(c + 1) * CH], wt[:, k * 512:(k + 1) * 512],
                     k == 0, k == KC - 1, r=[t_wt, t_xn[k][c // 4]])
            dst = vtok[:, c, vs * 512:(vs + 1) * 512]
            if ev % 2 == 0:
                P.act(dst, ps[:, :], AF.Identity, r=[t_ps], w=[t_v[c]])
            else:
                s.add("dve", (lambda dst, ps: lambda e: e.tensor_copy(out=dst, in_=ps[:, :]))(dst, ps),
                      r=[t_ps], w=[t_v[c]])
            ev += 1
    g1 = P.sb("g1", [16, SEG], BF16)
    t_g1 = Tl("g1")
    for i in range(2):
        ps, t_ps = P.psum("stat", [4, 5])
        for k in range(KC):
            P.mm(ps[0:16, :], t_ps, wg1[:, k, :], xn[:, k, i * 512:(i + 1) * 512], k == 0, k == KC - 1,
                 r=[t_g, t_xn[k][i]])
        P.act(g1[:, i * 512:(i + 1) * 512], ps[0:16, :], AF.Identity, r=[t_ps], w=[t_g1])
    if full:
        wq, t_wq = P.next_slab()
    wk, t_wk = P.next_slab(hold=2 if full else 1)
    if full:
        qt_ = P.sb("gqt", [128, GH, SEG], BF16)
    kt_ = P.sb("gkt", [128, GH, SEG], BF16) if full else None
    kd_ = P.sb("gkd", [128, GH, SEG], BF16)
    t_q = [[Tl("gq%d_%d" % (h, i)) for i in range(2)] for h in range(GH)]
    t_kk = [[Tl("gk%d_%d" % (h, i)) for i in range(2)] for h in range(GH)]
    dl = P.sb("gdl", [128, GH, NCH], F32)
    t_dl = [Tl("gdl%d" % h) for h in range(GH)]
    dtot = P.sb("gdtot", [128, GH], F32)
    t_dt = Tl("gdtot")
    csb = [P.sb("gcs0", [128, 512], F32)] * 2
    t_cs = [Tl("gcs0")] * 2
    e4 = [P.sb("ge4_%d" % i, [128, 4], F32) for i in range(2)]
    ep4 = P.sb("gep4", [128, 4], F32)
    dd4 = P.sb("gdd4", [128, 4], F32)
    t_e = Tl("ge4")
    fb = [P.sb("gfb%d" % i, [128, 512], F32) for i in range(3)]
    t_fb = [Tl("gfb%d" % i) for i in range(3)]
    qscale = float(GLA_DKH ** -0.5)
    for h in range(GH):
        for i in range(2):
            lc = i * 512
            psz, t_psz = P.psum("stat", [4, 5])
            P.mm(psz[:, :], t_psz, wg2[0:16, h * 128:(h + 1) * 128], g1[0:16, lc:lc + 512], True, True,
                 r=[t_g, t_g1])
            P.act(fb[0][:, :], psz[:, :], AF.Exp, r=[t_psz, t_g], w=[t_fb[0]], bias=nbg[:, h:h + 1], scale=-1.0)
            P.act(fb[0][:, :], fb[0][:, :], AF.Ln, r=[t_fb[0], t_g], w=[t_fb[0]], bias=one1[:, 0:1])
            cs, tcs = csb[i], t_cs[i]
            init = 0.0 if i == 0 else e4[0][:, 3:4]
            s.add("dve", (lambda cs, init: lambda e: e.tensor_tensor_scan(
                cs[:, :], ones512[:, :], fb[0][:, :], init, ALU.mult, ALU.add))(cs, init),
                r=[t_fb[0], t_g] + ([t_e] if i else []), w=[tcs])
            csv = cs.rearrange("p (c t) -> p c t", t=CH)
            E4 = e4[i]
            s.add("dve", (lambda E4, csv: lambda e: e.tensor_copy(out=E4[:, :], in_=csv[:, :, CH - 1]))(E4, csv),
                  r=[tcs], w=[t_e])
            if i == 0:
                s.add("dve", lambda e: e.memset(ep4[:, 0:1], 0.0), r=[t_e], w=[t_e])
            else:
                s.add("dve", lambda e: e.tensor_copy(out=ep4[:, 0:1], in_=e4[0][:, 3:4]), r=[t_e], w=[t_e])
            s.add("dve", (lambda E4: lambda e: e.tensor_copy(out=ep4[:, 1:4], in_=E4[:, 0:3]))(E4), r=[t_e], w=[t_e])
            bbv = fb[1].rearrange("p (c t) -> p c t", t=CH)
            bb2v = fb[2].rearrange("p (c t) -> p c t", t=CH)
            s.add("dve", (lambda csv: lambda e: e.tensor_tensor(
                bbv, csv, ep4[:, :].unsqueeze(2).to_broadcast([128, 4, CH]), ALU.subtract))(csv),
                r=[tcs, t_e], w=[t_fb[1]])
            s.add("dve", (lambda csv, E4: lambda e: e.tensor_tensor(
                bb2v, E4[:, :].unsqueeze(2).to_broadcast([128, 4, CH]), csv, ALU.subtract))(csv, E4),
                r=[tcs, t_e], w=[t_fb[2]])
            P.tt("dve", dd4[:, :], E4[:, :], ep4[:, :], ALU.subtract, r=[t_e], w=[t_e])
            P.act(dl[:, h, i * 4:(i + 1) * 4], dd4[:, :], AF.Exp, r=[t_e], w=[t_dl[h]], scale=-1.0 / GLA_TAU)
            if i == 1:
                P.act(dtot[:, h:h + 1], E4[:, 3:4], AF.Exp, r=[t_e], w=[t_dt], scale=-1.0 / GLA_TAU)
            psk, t_psk = P.psum("acc", [0, 1, 2, 3])
            for k in range(KC):
                P.mm(psk[:, :], t_psk, wk[:, k * 512 + h * 128: k * 512 + (h + 1) * 128],
                     xn[:, k, lc:lc + 512], k == 0, k == KC - 1, r=[t_wk, t_xn[k][i]])
            P.act(fb[2][:, :], fb[2][:, :], AF.Exp, r=[t_fb[2]], w=[t_fb[2]], scale=-1.0 / GLA_TAU)
            P.tt("dve", kd_[:, h, lc:lc + 512], psk[:, :], fb[2][:, :], ALU.mult, r=[t_psk, t_fb[2]], w=[t_kk[h][i]])
            if full:
                P.act(fb[0][:, :], fb[1][:, :], AF.Exp, r=[t_fb[1]], w=[t_fb[0]], scale=1.0 / GLA_TAU)
                P.tt("dve", kt_[:, h, lc:lc + 512], psk[:, :], fb[0][:, :], ALU.mult,
                     r=[t_psk, t_fb[0]], w=[t_kk[h][i]])
                P.act(fb[1][:, :], fb[1][:, :], AF.Exp, r=[t_fb[1]], w=[t_fb[1]], scale=-1.0 / GLA_TAU)
                psq, t_psq = P.psum("acc", [0, 1, 2, 3])
                for k in range(KC):
                    P.mm(psq[:, :], t_psq, wq[:, k * 512 + h * 128: k * 512 + (h + 1) * 128],
                         xn[:, k, lc:lc + 512], k == 0, k == KC - 1, r=[t_wq, t_xn[k][i]])
                P.stt(qt_[:, h, lc:lc + 512], psq[:, :], qscale, fb[1][:, :], ALU.mult, ALU.mult,
                      r=[t_psq, t_fb[1]], w=[t_q[h][i]])
    Sf = [P.sb("gS%d" % h, [128, GLA_DVH], F32) for h in range(GH)]
    Sb = [P.sb("gSb%d" % h, [128, GLA_DVH], BF16) for h in range(GH)]
    t_S = [Tl("gS%d" % h) for h in range(GH)]
    t_Sb = [Tl("gSb%d" % h) for h in range(GH)]
    kdtok = [P.sb("gkdtok%d" % i, [128, CH], BF16) for i in range(2)]
    t_kdtok = [Tl("gkdtok%d" % i) for i in range(2)]
    if full:
        oall = P.sb("goall", [128, KC, SEG], BF16)
        t_o = [[Tl("go%d_%d" % (c8, i)) for i in range(2)] for c8 in range(KC)]
        attm = [P.sb("gattm%d" % i, [128, CH], BF16) for i in range(2)]
        t_attm = [Tl("gattm%d" % i) for i in range(2)]
        trisb = P.sb("gtri", [128, CH], F32)
        s.add("sp", lambda e: e.dma_start(out=trisb[:, :], in_=tri[:, :]), w=[t_g], dma=True)
    cnt = 0
    for h in range(GH):
        if full:
            s.add("sp", (lambda h: lambda e: e.dma_start(out=Sf[h][:, :], in_=sinit[h, :, :]))(h),
                  r=[sinit_t], w=[t_S[h]], dma=True)
            s.add("act", (lambda h: lambda e: e.activation(Sb[h][:, :], Sf[h][:, :], AF.Identity))(h),
                  r=[t_S[h]], w=[t_Sb[h]])
        else:
            s.add("dve", (lambda h: lambda e: e.memset(Sf[h][:, :], 0.0))(h), w=[t_S[h]])
    for c in range(NCH):
        for h in range(GH):
            i = c // 4
            cc = slice(c * CH, (c + 1) * CH)
            if full:
                psa, t_psa = P.psum("gl", [6, 7, 4, 5])
                P.mm(psa[:, 0:CH], t_psa, kt_[:, h, cc], qt_[:, h, cc], True, True, r=[t_kk[h][i], t_q[h][i]])
                am, tam = attm[cnt % 2], t_attm[cnt % 2]
                P.tt("dve", am[:, :], psa[:, 0:CH], trisb[:, :], ALU.mult, r=[t_psa, t_g], w=[tam])
                pso, t_pso = P.psum("gl", [6, 7, 4, 5])
                for ec in range(2):
                    P.mm(pso[:, ec * CH:(ec + 1) * CH], t_pso,
                         vtok[:, c, h * GLA_DVH + ec * 128: h * GLA_DVH + (ec + 1) * 128], am[:, :],
                         True, False, r=[t_v[c], tam])
                    P.mm(pso[:, ec * CH:(ec + 1) * CH], t_pso, Sb[h][:, ec * 128:(ec + 1) * 128], qt_[:, h, cc],
                         False, True, r=[t_Sb[h], t_q[h][i]])
                s.add("act", (lambda h, cc, pso: lambda e: e.activation(
                    oall[:, 2 * h:2 * h + 2, cc], pso[:, 0:2 * CH].rearrange("p (a t) -> p a t", a=2), AF.Identity))(h, cc, pso),
                    r=[t_pso], w=[t_o[2 * h][i], t_o[2 * h + 1][i]])
            pst, t_pst = P.psum("gl", [6, 7, 4, 5])
            pst_b = pst.bitcast(BF16)
            s.add("pe", (lambda pst_b, h, cc: lambda e: e.transpose(pst_b[:, 0:CH], kd_[:, h, cc], P.ident_b[:, :]))(pst_b, h, cc),
                  r=[t_kk[h][i], P.t_const], w=[t_pst])
            kt2, tkt2 = kdtok[cnt % 2], t_kdtok[cnt % 2]
            s.add("dve", (lambda kt2, pst_b: lambda e: e.tensor_copy(out=kt2[:, :], in_=pst_b[:, 0:CH]))(kt2, pst_b),
                  r=[t_pst], w=[tkt2])
            pss, t_pss = P.psum("gl", [6, 7, 4, 5])
            P.mm(pss[:, 0:GLA_DVH], t_pss, kt2[:, :], vtok[:, c, h * GLA_DVH:(h + 1) * GLA_DVH], True, True,
                 r=[tkt2, t_v[c]])
            P.stt(Sf[h][:, :], Sf[h][:, :], dl[:, h, c:c + 1], pss[:, 0:GLA_DVH], ALU.mult, ALU.add,
                  r=[t_S[h], t_dl[h], t_pss], w=[t_S[h]])
            if full and c < NCH - 1:
                s.add("act", (lambda h: lambda e: e.activation(Sb[h][:, :], Sf[h][:, :], AF.Identity))(h),
                      r=[t_S[h]], w=[t_Sb[h]])
            cnt += 1
    if not full:
        for h in range(GH):
            t_go = Tl("gout")
            s.add("sp", (lambda h: lambda e: e.dma_start(out=gout[h, :, :], in_=Sf[h][:, :]))(h),
                  r=[t_S[h]], w=[t_go], dma=True)
            P.out_tiles.append(t_go)
    if not full:
        t_do = Tl("dout")
        s.add("sp", lambda e: e.dma_start(out=dout[:, :], in_=dtot[:, :]), r=[t_dt], w=[t_do], dma=True)
        P.out_tiles.append(t_do)
        P.release(m0)
        return
    if getattr(P, "dbg", None) is not None and sg == 0:
        dd = P.dbg
        allq = [t_q[h][i] for h in range(GH) for i in range(2)]
        allk = [t_kk[h][i] for h in range(GH) for i in range(2)]
        allo = [t_o[c8][i] for c8 in range(KC) for i in range(2)]
        for nm, buf, rd in (("dbg_qt", qt_, allq), ("dbg_kt", kt_, allk), ("dbg_kd", kd_, allk),
                            ("dbg_v", vtok, t_v), ("dbg_o", oall, allo)):
            t_d = Tl(nm)
            s.add("sp", (lambda nm, buf: lambda e: e.dma_start(out=dd[nm], in_=buf))(nm, buf), r=rd, w=[t_d], dma=True)
            P.out_tiles.append(t_d)
        t_d = Tl("dbg_dl")
        s.add("sp", lambda e: e.dma_start(out=dd["dbg_dl"], in_=dl), r=t_dl, w=[t_d], dma=True)
        P.out_tiles.append(t_d)
    ones256 = P.sb("ones256", [128, 128], BF16)
    s.add("dve", lambda e: e.memset(ones256[:, :], 1.0 / GLA_DVH), w=[t_g])
    for i in range(2):
        lc = i * 512
        for h in range(GH):
            for ec in range(2):
                P.act(P.sq[:, ec, :], oall[:, 2 * h + ec, lc:lc + 512], AF.Square, r=[t_o[2 * h + ec][i]], w=[P.t_sq[ec]])
            ps, t_ps = P.psum("stat", [4, 5])
            for ec in range(2):
                P.mm(ps[:, :], t_ps, ones256[:, :], P.sq[:, ec, :], ec == 0, ec == 1, r=[P.t_sq[ec], t_g])
            P.act(P.rt[:, :], ps[:, :], AF.Ln, r=[t_ps, P.t_const], w=[P.t_rt], bias=P.epsc[:, 0:1])
            P.act(P.rstd[:, :], P.rt[:, :], AF.Exp, r=[P.t_rt], w=[P.t_rstd], scale=-0.5)
            for ec in range(2):
                P.stt(oall[:, 2 * h + ec, lc:lc + 512], oall[:, 2 * h + ec, lc:lc + 512], P.ppc("d_o_norm", ec),
                      P.rstd[:, :], ALU.mult, ALU.mult, r=[t_o[2 * h + ec][i], P.t_rstd, P.t_pp],
                      w=[t_o[2 * h + ec][i]])
    for rs in range(2):
        wt, t_wt = P.next_slab()
        for i in range(2):
            lc = i * 512
            for n in range(4):
                ps, t_ps = P.psum("acc", [0, 1, 2, 3])
                for k in range(KC):
                    P.mm(ps[:, :], t_ps, wt[:, k * 512 + n * 128: k * 512 + (n + 1) * 128], xn[:, k, lc:lc + 512],
                         k == 0, k == KC - 1, r=[t_wt, t_xn[k][i]])
                sr, t_sr = P.tmpf()
                P.act(sr[:, :], ps[:, :], AF.Silu, r=[t_ps], w=[t_sr])
                c8 = rs * 4 + n
                P.tt("dve", oall[:, c8, lc:lc + 512], oall[:, c8, lc:lc + 512], sr[:, :], ALU.mult,
                     r=[t_o[c8][i], t_sr], w=[t_o[c8][i]])
    for s2 in range(2):
        wt, t_wt = P.next_slab()
        for i in range(2):
            lc = i * 512
            c0 = sg * SEG + lc
            for n in range(4):
                ps, t_ps = P.psum("acc", [0, 1, 2, 3])
                for k in range(KC):
                    P.mm(ps[:, :], t_ps, wt[:, k * 512 + n * 128: k * 512 + (n + 1) * 128], oall[:, k, lc:lc + 512],
                         k == 0, k == KC - 1, r=[t_wt, t_o[k][i]])
                on = s2 * 4 + n
                P.tt("dve", x[:, on, c0:c0 + 512], ps[:, :], x[:, on, c0:c0 + 512], ALU.add,
                     r=[t_ps, t_x[on][tis[i]]], w=[t_x[on][tis[i]]])
    P.release(m0)


GLA_DKH = 128
GLA_DVH = 256
GLA_TAU = 16.0


def gla_prefix(P, gall, dall, sel128d, sinit, g_ap=None, d_ap=None):
    s = P.s
    m0 = P.mark()
    sel = P.sb("gsel", [128, 16], F32)
    t_c = Tl("gpre_c")
    s.add("sp", lambda e: e.dma_start(out=sel[:, :], in_=sel128d[:, :]), w=[t_c], dma=True)
    dsb = P.sb("gdall", [128, 8, GH], F32)
    if d_ap is None:
        s.add("sp", lambda e: e.dma_start(out=dsb[:, :, :], in_=dall.rearrange("s p h -> p s h")), w=[t_c], dma=True)
    else:
        for sgi in range(8):
            s.add("sp", (lambda sgi: lambda e: e.dma_start(out=dsb[:, sgi, :], in_=d_ap(sgi)))(sgi), w=[t_c], dma=True)
    t_out = Tl("sinit")
    Gs = [[P.sb("gG%d_%d" % (h, i), [128, GLA_DVH], F32) for i in range(7)] for h in range(GH)]
    t_Gs = [[Tl("gG%d_%d" % (h, i)) for i in range(7)] for h in range(GH)]
    for h in range(GH):
        for sgi in range(7):
            gsrc = gall[sgi, h, :, :] if g_ap is None else g_ap(sgi, h)
            s.add("sp", (lambda g, gsrc: lambda e: e.dma_start(out=g[:, :], in_=gsrc))(Gs[h][sgi], gsrc),
                  w=[t_Gs[h][sgi]], dma=True)
    for h in range(GH):
        R = P.sb("gR%d" % h, [128, GLA_DVH], F32)
        SA = P.sb("gSA%d" % h, [128, GLA_DVH], F32)
        SB = P.sb("gSB%d" % h, [128, GLA_DVH], F32)
        t_R, t_SA = Tl("gR"), Tl("gSAB")
        s.add("dve", (lambda R: lambda e: e.memset(R[:, :], 0.0))(R), w=[t_R])
        s.add("dve", (lambda SA: lambda e: e.memset(SA[:, :], 0.0))(SA), w=[t_SA])
        s.add("dve", (lambda SB: lambda e: e.memset(SB[:, :], 0.0))(SB), w=[t_SA])
        for sgi in range(8):
            if sgi > 0:
                P.stt(SA[:, :], R[:, :], sel[:, sgi:sgi + 1], SA[:, :], ALU.mult, ALU.add, r=[t_R, t_c, t_SA], w=[t_SA])
                P.stt(SB[:, :], R[:, :], sel[:, 8 + sgi:9 + sgi], SB[:, :], ALU.mult, ALU.add, r=[t_R, t_c, t_SA], w=[t_SA])
            if sgi < 7:
                g, tg = Gs[h][sgi], t_Gs[h][sgi]
                P.stt(R[:, :], R[:, :], dsb[:, sgi, h:h + 1], g[:, :], ALU.mult, ALU.add, r=[t_R, t_c, tg], w=[t_R])
        s.add("sp", (lambda SA, h: lambda e: e.dma_start(out=sinit[0, h, :, :], in_=SA[:, :]))(SA, h),
              r=[t_SA], w=[t_out], dma=True)
        s.add("sp", (lambda SB, h: lambda e: e.dma_start(out=sinit[1, h, :, :], in_=SB[:, :]))(SB, h),
              r=[t_SA], w=[t_out], dma=True)
    P.release(m0)
    return t_out


def std_slabs(W, col0, n):
    return [slab_from(W, np.arange(col0 + i * 512, col0 + (i + 1) * 512)) for i in range(n)]


def attn_out_slabs(W):
    out = []
    Wr = W.reshape(16, 64, 1024)
    for g in range(4):
        sl = np.zeros((128, 4, 1024), np.float32)
        sl[0:64] = Wr[4 * g:4 * g + 4].transpose(1, 0, 2)
        out.append(sl.reshape(128, SLABW))
    return out


def build_pp(inp):
    pp = build_pp_stage1(inp)
    wf = inp["c_w_f"][0]
    pp.put("c_w_f", np.ascontiguousarray(wf.reshape(8, 128, 16).transpose(1, 0, 2).reshape(128, 128)))
    col = np.zeros((128, 1), np.float32)
    col[0:16, 0] = inp["c_b_f"][0]
    pp.put("c_b_f", col)
    col = np.zeros((128, 1), np.float32)
    col[0:64, 0] = inp["c_q_norm"][0]
    pp.put("c_q_norm", col)
    col = np.zeros((128, 1), np.float32)
    col[0:64, 0] = inp["c_k_norm"][0]
    pp.put("c_k_norm", col)
    wg1 = inp["d_w_g1"][0]
    pp.put("d_w_g1", np.ascontiguousarray(wg1.reshape(8, 128, 16).transpose(1, 0, 2).reshape(128, 128)))
    a = np.zeros((128, 512), np.float32)
    a[0:16] = inp["d_w_g2"][0]
    pp.put("d_w_g2", a)
    pp.put("d_b_g", colvec(inp["d_b_g"][0], 4))
    pp.put("d_o_norm", colvec(inp["d_o_norm"][0], 2))
    return pp


def build_fused(n_slabs, pp_off, npp):
    P = Prog(n_slabs, npp,
             out_specs=[("xo", (D, NMAIN), F32)],
             in_specs=[("xt", (D, NT), F32), ("hm", (128, 64), F32), ("ident", (128, 128), F32),
                       ("flexm", (128, 8), F32), ("sel", (16, 16), F32), ("cmask", (4, 128, 512), F32),
                       ("sel128", (128, 16), F32), ("tri", (128, 128), F32)], nslot=3)
    P.pp_off = pp_off
    nc, s = P.nc, P.s
    I = P.ins
    groups = [[0, 1, 2, 3], [4, 5, 6, 7]]
    qd = nc.dram_tensor("qd", [NH, 66, NMAIN], BF16).ap()
    kd_hp = [nc.dram_tensor("kd_hp%d" % i, [2 * 66, NMAIN], BF16).ap() for i in range(8)]
    vd_hp = [nc.dram_tensor("vd_hp%d" % i, [2 * 128, VW], BF16).ap() for i in range(8)]
    kall_hp = [nc.dram_tensor("kall_hp%d" % i, [4 * 2 * 66, NMAIN], BF16).ap() for i in range(8)]
    vall_hp = [nc.dram_tensor("vall_hp%d" % i, [4 * 2 * 128, VW], BF16).ap() for i in range(8)]
    cd2 = nc.dram_tensor("cd2", [NH, NMAIN], F32).ap()
    call2 = nc.dram_tensor("call2", [4 * NH, NMAIN], F32).ap()
    gd2 = nc.dram_tensor("gd2", [2 * GH * 128, GLA_DVH], F32).ap()
    dd2 = nc.dram_tensor("dd2", [2 * 128, GH], F32).ap()
    gall2 = nc.dram_tensor("gall2", [4 * 2 * GH * 128, GLA_DVH], F32).ap()
    dall2 = nc.dram_tensor("dall2", [4 * 2 * 128, GH], F32).ap()
    sinit = nc.dram_tensor("sinit", [2, GH, 128, GLA_DVH], F32).ap()
    call = call2.rearrange("(g h) t -> g h t", g=4)
    gd = gd2.rearrange("(s h p) e -> s h p e", s=2, h=GH)
    dd = dd2.rearrange("(s p) h -> s p h", s=2)
    gall = gall2.rearrange("(g s h p) e -> g s h p e", g=4, s=2, h=GH)
    dall = dall2.rearrange("(g s p) h -> g s p h", g=4, s=2)

    def kdst(h):
        return kd_hp[h // 2][(h % 2) * 66:(h % 2 + 1) * 66, :]

    def vown(h):
        return vd_hp[h // 2][(h % 2) * 128:(h % 2 + 1) * 128, :]

    def kallf(r_, h):
        return kall_hp[h // 2][(r_ * 2 + h % 2) * 66:(r_ * 2 + h % 2 + 1) * 66, :]

    def vallf(r_, h):
        return vall_hp[h // 2][(r_ * 2 + h % 2) * 128:(r_ * 2 + h % 2 + 1) * 128, :]

    t_kd_hp = [Tl("kd_hp%d" % i) for i in range(8)]
    t_vd_hp = [Tl("vd_hp%d" % i) for i in range(8)]
    t_kall = [Tl("kall%d" % i) for i in range(8)]
    t_vall = [Tl("vall%d" % i) for i in range(8)]
    t_call = Tl("call")

    setup_state(P, NT, TILES5)
    P.hm = P.sb("hm_sb", [128, 64], F32)
    P.t_hm = Tl("hm")
    s.add("sp", lambda e: e.dma_start(out=P.hm[:, :], in_=I["hm"][:, :]), w=[P.t_hm], dma=True)
    load_x(P, I["xt"], NT)
    mR = P.mark()
    alloc_R(P, NT, 5)
    phase_L0(P)
    phase_L1(P)
    phase_L2pre(P, qd, kdst, lambda hp: vd_hp[hp], cd2, t_kd_hp, t_vd_hp)
    P.out_tiles = []

    def allgather(src2, dst2, r, w):
        s.add("pool", lambda e: e.collective_compute("AllGather", ALU.bypass, replica_groups=groups,
                                                     ins=[src2], outs=[dst2]), r=r, w=w, cc=True)
    P.release(mR)
    allgather(cd2, call2, [P.t_cd], [t_call])
    for hp in range(8):
        allgather(kd_hp[hp], kall_hp[hp], [t_kd_hp[hp]], [t_kall[hp]])
        allgather(vd_hp[hp], vall_hp[hp], [t_vd_hp[hp]], [t_vall[hp]])
    dep = {"qd": P.t_qd, "cd": P.t_cd, "call": t_call, "kd": t_kd_hp, "vd": t_vd_hp, "kall": t_kall, "vall": t_vall}
    phase_L2attn(P, qd, kdst, vown, cd2, kallf, vallf, call, I["flexm"], I["sel"], I["cmask"], dep)
    m = P.mark()
    alloc_R(P, NT, 5)
    mlp_block(P, 2, P.x, P.t_x, P.tiles[:4], P.R1, P.t_R1, P.R2, P.t_R2)
    P.release(m)
    for sg in range(2):
        gla_seg(P, sg, "scan", gout=gd[sg], dout=dd[sg])
    P.out_tiles = []
    s.barrier()
    allgather(gd2, gall2, [], [])
    allgather(dd2, dall2, [], [])
    s.barrier()
    t_si = gla_prefix(P, None, None, I["sel128"], sinit,
                      g_ap=lambda sgi, h: gall[seg_loc(sgi)[0], seg_loc(sgi)[1], h, :, :],
                      d_ap=lambda sgi: dall[seg_loc(sgi)[0], seg_loc(sgi)[1], :, :])
    for sg in range(2):
        gla_seg(P, sg, "full", sinit=sinit[sg], tri=I["tri"], sinit_t=t_si)
    m = P.mark()
    alloc_R(P, NT, 5)
    mlp_block(P, 3, P.x, P.t_x, P.tiles[:4], P.R1, P.t_R1, P.R2, P.t_R2)
    P.release(m)
    store_x(P, P.outs["xo"])
    return P.finish()


def run_fused(inp):
    x = np.asarray(inp["x"], np.float32)
    pp = build_pp(inp)
    ppa = pp.array()
    ident = np.eye(128, dtype=np.float32)
    wq = inp["c_w_qkv"][0]
    wd = inp["d_w_in"][0]
    gla_scan = std_slabs(wd, 1024, 2) + std_slabs(wd, 512, 1)
    gla_full = (std_slabs(wd, 1024, 2) + std_slabs(wd, 0, 1) + std_slabs(wd, 512, 1) + std_slabs(wd, 2048, 2)
                + std_slabs(inp["d_w_out"][0], 0, 2))
    slabs = (build_wall_stage1(inp) + std_slabs(wq, 0, 2) + std_slabs(wq, 1024, 2) + std_slabs(wq, 2048, 2)
             + attn_out_slabs(inp["c_w_out"][0]) + mlp_slabs(inp, 2) + gla_scan + gla_scan
             + gla_full + gla_full + mlp_slabs(inp, 3))
    wall = np.stack(slabs, axis=0)
    cm = causal_masks()
    tri = np.triu(np.ones((128, 128), np.float32))
    in_maps = []
    for r in range(8):
        b, j = r // 4, r % 4
        idx = core_token_index(j)
        xt = np.zeros((NT, D), np.float32)
        valid = idx >= 0
        xt[valid] = x[b, idx[valid]]
        hm, flexm, sel = percore_consts(j)
        sel128 = np.zeros((128, 16), np.float32)
        sel128[:, j] = 1.0
        sel128[:, 8 + 7 - j] = 1.0
        in_maps.append({"wall": wall, "pp": ppa, "xt": np.ascontiguousarray(xt.T), "hm": hm, "ident": ident,
                        "flexm": flexm, "sel": sel, "cmask": cm, "sel128": sel128, "tri": tri})
    nc = build_fused(wall.shape[0], pp.off, pp.n)
    res = run_bass_kernel_spmd(nc, in_maps, core_ids=list(range(8))).results
    return gather_x([np.asarray(res[r]["xo"]) for r in range(8)])


def percore_consts(j):
    hm = np.ones((128, 64), np.float32)
    if j == 0:
        hm[:, 0:32] = 0.0
    flexm = np.zeros((128, 8), np.float32)
    for p in range(3):
        flexm[:, p] = 1.0 if j > p else 0.0
        flexm[:, 3 + p] = 0.0 if j > p else 1.0
    sel = np.zeros((16, 16), np.float32)
    sel[:, j] = 1.0
    sel[:, 8 + 7 - j] = 1.0
    return hm, flexm, sel


def causal_masks():
    kl = np.arange(128)[:, None]
    ql = np.arange(512)[None, :]
    return np.stack([np.where(kl - ql <= -128 * i, 0.0, NEG).astype(np.float32) for i in range(4)], axis=0)


_CACHE = {}


def gather_x(xos):
    out = np.zeros((2, 8192, D), np.float32)
    for r in range(8):
        b, j = r // 4, r % 4
        a, bb = seg_tokens(j)
        xo = xos[r].T
        out[b, a] = xo[:SEG]
        out[b, bb] = xo[SEG:]
    return out


def kernel(**inputs):
    return run_fused(inputs)
```

```python
import numpy as np
import concourse.bass as bass
import concourse.mybir as mybir
from concourse.bass_utils import run_bass_kernel_spmd
from contextlib import ExitStack

F32 = mybir.dt.float32
BF16 = mybir.dt.bfloat16
AF = mybir.ActivationFunctionType
ALU = mybir.AluOpType

D = 1024
KC = 8
SEG = 1024
HALO = 32
NMAIN = 2 * SEG
NT = NMAIN + 2 * HALO
EPS = 1e-6
SLABW = 4096
NSLOT = 4
ARENA_W = 53200


class Tl:
    __slots__ = ("name", "lastw", "readers")

    def __init__(self, name):
        self.name = name
        self.lastw = None
        self.readers = []


class Op:
    __slots__ = ("eng", "fn", "deps", "dma", "sig", "waits", "idx", "need", "cc")

    def __init__(self, eng, fn, deps, dma, idx):
        self.cc = False
        self.eng = eng
        self.fn = fn
        self.deps = deps
        self.dma = dma
        self.sig = None
        self.waits = []
        self.idx = idx
        self.need = False


class Sch:
    ENGS = ("pe", "act", "dve", "pool", "sp")
    DMAK = 8

    def __init__(self, nc):
        self.nc = nc
        self.ops = []
        self.last = {e: None for e in self.ENGS}
        self.dmas_open = []

    def add(self, eng, fn, r=(), w=(), dma=False, cc=False):
        idx = len(self.ops)
        deps = set()
        for t in r:
            if t.lastw is not None:
                deps.add(t.lastw)
        for t in w:
            if t.lastw is not None:
                deps.add(t.lastw)
            deps.update(t.readers)
        for t in r:
            t.readers.append(idx)
        for t in w:
            t.lastw = idx
            t.readers = []
        deps.discard(idx)
        op = Op(eng, fn, deps, dma, idx)
        op.cc = cc
        self.ops.append(op)
        self.last[eng] = idx
        if dma or cc:
            self.dmas_open.append(idx)
        return idx

    def barrier(self):
        lasts = [v for v in self.last.values() if v is not None] + list(self.dmas_open)
        self.dmas_open = []
        for e in self.ENGS:
            idx = len(self.ops)
            op = Op(e, None, set(lasts), False, idx)
            self.ops.append(op)
            self.last[e] = idx

    def finalize(self, stack):
        nc = self.nc
        ops = self.ops
        for op in ops:
            keep = set()
            for d in op.deps:
                od = ops[d]
                if od.fn is None:
                    if od.eng == op.eng:
                        continue
                    keep.add(d)
                    continue
                if od.eng == "pe" and op.eng == "pe" and not od.dma:
                    continue
                keep.add(d)
            op.deps = keep
            for d in keep:
                ops[d].need = True
        csem = {e: stack.enter_context(nc.semaphore("c_" + e)) for e in ("pe", "act", "dve", "pool", "sp")}
        dsem = {e: [stack.enter_context(nc.semaphore("d_%s%d" % (e, i))) for i in range(self.DMAK)]
                for e in ("sp", "pool")}
        ccount = {e: 0 for e in csem}
        dcount = {e: 0 for e in dsem}
        ccsem = stack.enter_context(nc.semaphore("cc_sem"))
        ncc = 0
        for op in ops:
            if op.cc:
                ncc += 1
                op.sig = (ccsem, ncc, None)
            elif op.dma:
                j = dcount[op.eng]
                dcount[op.eng] += 1
                sem = dsem[op.eng][j % self.DMAK]
                op.sig = (sem, 16 * (j // self.DMAK + 1), 16)
                if j >= self.DMAK:
                    op.waits.append((sem, 16 * (j // self.DMAK)))
            elif op.need:
                ccount[op.eng] += 1
                op.sig = (csem[op.eng], ccount[op.eng], 1)
        known = {e: {} for e in self.ENGS}
        for op in ops:
            kn = known[op.eng]
            ws = {}
            for (sem, val) in op.waits:
                ws[sem.num] = (sem, max(val, ws.get(sem.num, (None, 0))[1]))
            for d in op.deps:
                sem, val, _ = ops[d].sig
                if ws.get(sem.num, (None, 0))[1] < val:
                    ws[sem.num] = (sem, val)
            out = []
            for num, (sem, val) in ws.items():
                if kn.get(num, 0) >= val:
                    continue
                kn[num] = val
                out.append((sem, val))
            op.waits = out
        self.per_eng = {e: [op for op in ops if op.eng == e] for e in self.ENGS}

    def emit(self, eng_name, e):
        n = 0
        for op in self.per_eng[eng_name]:
            for (sem, val) in op.waits:
                e.wait_ge(sem, val)
            if op.fn is None:
                if op.sig is not None:
                    e.nop().then_inc(op.sig[0], op.sig[2])
                continue
            ins = op.fn(e)
            n += 1
            if op.sig is not None:
                if op.sig[2] is None:
                    ins.then_inc(op.sig[0])
                else:
                    ins.then_inc(op.sig[0], op.sig[2])
        return n


def slab_from(W, cols):
    sub = W[:, cols]
    return np.ascontiguousarray(sub.reshape(8, 128, 512).transpose(1, 0, 2).reshape(128, SLABW))


def colvec(v, nch):
    return np.ascontiguousarray(np.asarray(v, np.float32).reshape(nch, 128).T)


class PP:
    def __init__(self):
        self.cols = []
        self.off = {}
        self.n = 0

    def put(self, name, arr):
        arr = np.asarray(arr, np.float32)
        assert arr.shape[0] == 128
        self.off[name] = (self.n, arr.shape[1])
        self.cols.append(arr)
        self.n += arr.shape[1]

    def array(self):
        return np.ascontiguousarray(np.concatenate(self.cols, axis=1))


def seg_tokens(j):
    a = np.arange(j * SEG, (j + 1) * SEG)
    b = np.arange((7 - j) * SEG, (8 - j) * SEG)
    return a, b


def core_token_index(j):
    a, b = seg_tokens(j)
    ha = np.arange(j * SEG - HALO, j * SEG)
    hb = np.arange((7 - j) * SEG - HALO, (7 - j) * SEG)
    return np.concatenate([a, b, ha, hb])


class Prog:
    def __init__(self, n_slabs, npp, out_specs, in_specs, nslot=NSLOT):
        self.nslot = nslot
        self.nc = nc = bass.Bass("TRN2", target_bir_lowering=False)
        self.st = ExitStack()
        self.s = Sch(nc)
        self.wall = nc.dram_tensor("wall", [n_slabs, 128, SLABW], F32, kind="ExternalInput").ap()
        self.ppd = nc.dram_tensor("pp", [128, npp], F32, kind="ExternalInput").ap()
        self.ins = {}
        for name, shape, dt in in_specs:
            self.ins[name] = nc.dram_tensor(name, list(shape), dt, kind="ExternalInput").ap()
        self.outs = {}
        for name, shape, dt in out_specs:
            self.outs[name] = nc.dram_tensor(name, list(shape), dt, kind="ExternalOutput").ap()
        self.npp = npp
        self.n_slabs = n_slabs
        self.slab_i = 0
        self.slab_issued = 0
        self.out_tiles = []
        self.arena = None

    def sb(self, name, shape, dt):
        if self.arena is None:
            self.arena = self.st.enter_context(self.nc.sbuf_tensor("arena", [128, ARENA_W], F32))
            self.top = 0
        n = 1
        for d in shape[1:]:
            n *= d
        words = n if dt == F32 else (n + 1) // 2
        off = self.top
        self.top += words
        assert self.top <= ARENA_W, ("SBUF arena overflow", name, self.top)
        v = self.arena[:, off:off + words]
        if dt != F32:
            v = v.bitcast(dt)[:, 0:n]
        v = v[0:shape[0]]
        if len(shape) == 3:
            v = v.rearrange("p (a b) -> p a b", a=shape[1])
        elif len(shape) == 4:
            v = v.rearrange("p (a b c) -> p a b c", a=shape[1], b=shape[2])
        return v

    def mark(self):
        return self.top

    def release(self, mark):
        self.top = mark
        self.s.barrier()

    def setup_common(self):
        nc, s = self.nc, self.s
        self.pp = self.sb("pp_sb", [128, self.npp], F32)
        self.t_pp = Tl("pp")
        s.add("sp", lambda e: e.dma_start(out=self.pp[:, :], in_=self.ppd[:, :]), w=[self.t_pp], dma=True)
        self.wring = [self.sb("wslot%d" % i, [128, SLABW], BF16) for i in range(self.nslot)]
        self.t_w = [Tl("w%d" % i) for i in range(self.nslot)]
        self.ps = [self.st.enter_context(nc.psum_tensor("ps%d" % i, [128, 512], F32)) for i in range(8)]
        self.t_ps = [Tl("ps%d" % i) for i in range(8)]
        self.ps_rr = {}
        self.onesD = self.sb("onesD", [128, 128], BF16)
        self.t_const = Tl("const")
        s.add("dve", lambda e: e.memset(self.onesD[:, :], 1.0 / D), w=[self.t_const])
        self.ident_f = self.sb("ident_f", [128, 128], F32)
        self.ident_b = self.sb("ident_b", [128, 128], BF16)
        s.add("sp", lambda e: e.dma_start(out=self.ident_f[:, :], in_=self.ins["ident"][:, :]),
              w=[self.t_const], dma=True)
        s.add("dve", lambda e: e.tensor_copy(out=self.ident_b[:, :], in_=self.ident_f[:, :]),
              r=[self.t_const], w=[self.t_const])

    def psum(self, group, banks):
        i = self.ps_rr.get(group, 0)
        self.ps_rr[group] = i + 1
        b = banks[i % len(banks)]
        return self.ps[b], self.t_ps[b]

    def ppc(self, name, k=0, n=1):
        off, w = self.pp_off[name]
        return self.pp[:, off + k: off + k + n]

    def _issue_slab(self):
        i = self.slab_issued
        if i >= self.n_slabs:
            return
        self.slab_issued += 1
        slot = i % self.nslot
        dst = self.wring[slot]
        src = self.wall[i]
        self.s.add("pool", lambda e: e.dma_start(out=dst[:, :], in_=src), w=[self.t_w[slot]], dma=True)

    def next_slab(self, hold=1):
        i = self.slab_i
        self.slab_i += 1
        while self.slab_issued < min(self.n_slabs, i + self.nslot - (hold - 1)):
            self._issue_slab()
        slot = i % self.nslot
        return self.wring[slot], self.t_w[slot]

    def mm(self, ps_ap, t_ps, lhsT, rhs, start, stop, r):
        self.s.add("pe", lambda e: e.matmul(ps_ap, lhsT, rhs, start=start, stop=stop),
                   r=list(r) + ([] if start else [t_ps]), w=[t_ps])

    def act(self, out, in_, func, r, w, bias=None, scale=None):
        kw = {}
        if bias is not None:
            kw["bias"] = bias
        if scale is not None:
            kw["scale"] = scale
        self.s.add("act", lambda e: e.activation(out, in_, func, **kw), r=r, w=w)

    def tt(self, eng, out, in0, in1, op, r, w):
        self.s.add(eng, lambda e: e.tensor_tensor(out, in0, in1, op), r=r, w=w)

    def ts(self, eng, out, in0, s1, s2, op0, op1, r, w):
        if op1 is None:
            self.s.add(eng, lambda e: e.tensor_scalar(out, in0, s1, None, op0), r=r, w=w)
        else:
            self.s.add(eng, lambda e: e.tensor_scalar(out, in0, s1, s2, op0, op1), r=r, w=w)

    def stt(self, out, in0, scalar, in1, op0, op1, r, w):
        self.s.add("dve", lambda e: e.scalar_tensor_tensor(out, in0, scalar, in1, op0, op1), r=r, w=w)

    def rmsnorm(self, xk, t_xk, gname, outk, t_outk, w):
        sq, t_sq = self.sq, self.t_sq
        for k in range(KC):
            self.act(sq[:, k, :w], xk[k], AF.Square, r=[t_xk[k]], w=[t_sq[k]])
        ps, t_ps = self.psum("stat", [4, 5])
        for k in range(KC):
            self.mm(ps[:, :w], t_ps, self.onesD[:, :], sq[:, k, :w], k == 0, k == KC - 1,
                    r=[t_sq[k], self.t_const])
        self.act(self.rt[:, :w], ps[:, :w], AF.Ln, r=[t_ps, self.t_const], w=[self.t_rt],
                 bias=self.epsc[:, 0:1])
        self.act(self.rstd[:, :w], self.rt[:, :w], AF.Exp, r=[self.t_rt], w=[self.t_rstd], scale=-0.5)
        for k in range(KC):
            self.stt(outk[k], xk[k], self.ppc(gname, k), self.rstd[:, :w], ALU.mult, ALU.mult,
                     r=[t_xk[k], self.t_rstd, self.t_pp], w=[t_outk[k]])

    def finish(self):
        nc, s = self.nc, self.s
        s.add("sp", None, r=self.out_tiles)
        s.finalize(self.st)
        with nc.Block() as block:
            @block.tensor
            def _(e):
                s.emit("pe", e)

            @block.scalar
            def _(e):
                s.emit("act", e)

            @block.vector
            def _(e):
                s.emit("dve", e)

            @block.gpsimd
            def _(e):
                s.emit("pool", e)

            @block.sync
            def _(e):
                s.emit("sp", e)
        self.st.close()
        return nc


TILES5 = [(0, 512), (512, 512), (1024, 512), (1536, 512), (2048, 64)]


def mlp_cols(q, s):
    return np.arange(q * 1024 + s * 512, q * 1024 + (s + 1) * 512)


def build_wall_stage1(inp):
    slabs = []
    w = inp["a_w_in"][0]
    for s4 in range(4):
        cols = np.concatenate([np.arange(2 * s4 * 128, (2 * s4 + 2) * 128),
                               1024 + np.arange(2 * s4 * 128, (2 * s4 + 2) * 128)])
        slabs.append(slab_from(w, cols))
    w = inp["a_w_out"][0]
    for s2 in range(2):
        slabs.append(slab_from(w, np.arange(s2 * 512, (s2 + 1) * 512)))
    slabs += mlp_slabs(inp, 0)
    w = inp["b_w_in"][0]
    for s4 in range(4):
        cols = np.concatenate([1024 + np.arange(2 * s4 * 128, (2 * s4 + 2) * 128),
                               2048 + np.arange(2 * s4 * 128, (2 * s4 + 2) * 128)])
        slabs.append(slab_from(w, cols))
    for s2 in range(2):
        slabs.append(slab_from(w, np.arange(s2 * 512, (s2 + 1) * 512)))
    w = inp["b_w_out"][0]
    for s2 in range(2):
        slabs.append(slab_from(w, np.arange(s2 * 512, (s2 + 1) * 512)))
    slabs += mlp_slabs(inp, 1)
    return slabs


def mlp_slabs(inp, l):
    slabs = []
    w1 = inp["mlp_w1"][l]
    w2 = inp["mlp_w2"][l]
    for q in range(4):
        for s in range(2):
            slabs.append(slab_from(w1, mlp_cols(q, s)))
        for s in range(2):
            slabs.append(slab_from(w2[q * 1024:(q + 1) * 1024], np.arange(s * 512, (s + 1) * 512)))
    return slabs


def build_pp_stage1(inp):
    pp = PP()
    for l in range(4):
        pp.put("mixn%d" % l, colvec(inp["mix_norm"][l], 8))
        pp.put("mlpn%d" % l, colvec(inp["mlp_norm"][l], 8))
    pp.put("a_b_in", colvec(inp["a_b_in"][0], 16))
    cw = inp["a_conv_w"][0]
    pp.put("a_conv_w", np.ascontiguousarray(cw.T.reshape(8, 128, 31).transpose(1, 0, 2).reshape(128, 248)))
    pp.put("a_conv_b", colvec(inp["a_conv_b"][0], 8))
    pp.put("a_ln_g", colvec(inp["a_ln_g"][0], 8))
    pp.put("a_ln_b", colvec(inp["a_ln_b"][0], 8))
    pp.put("a_b_out", colvec(inp["a_b_out"][0], 8))
    bw = inp["b_conv_w"][0]
    pp.put("b_conv_w", np.ascontiguousarray(bw.T.reshape(8, 128, 3).transpose(1, 0, 2).reshape(128, 24)))
    return pp


def mlp_block(P, l, x, t_x, tiles, xn, t_xn, hq, t_hq):
    for ti, (c0, w) in enumerate(tiles):
        P.rmsnorm([x[:, k, c0:c0 + w] for k in range(KC)], [t_x[k][ti] for k in range(KC)], "mlpn%d" % l,
                  [xn[:, k, c0:c0 + w] for k in range(KC)], [t_xn[k][ti] for k in range(KC)], w)
    for q in range(4):
        for s in range(2):
            wt, t_wt = P.next_slab()
            for ti, (c0, w) in enumerate(tiles):
                for n in range(4):
                    ps, t_ps = P.psum("acc", [0, 1, 2, 3])
                    for k in range(KC):
                        P.mm(ps[:, :w], t_ps, wt[:, k * 512 + n * 128: k * 512 + (n + 1) * 128],
                             xn[:, k, c0:c0 + w], k == 0, k == KC - 1, r=[t_wt, t_xn[k][ti]])
                    tmp, t_tmp = P.tmpf()
                    P.act(tmp[:, :w], ps[:, :w], AF.Relu, r=[t_ps], w=[t_tmp])
                    hn = s * 4 + n
                    P.tt("dve", hq[:, hn, c0:c0 + w], tmp[:, :w], tmp[:, :w], ALU.mult,
                         r=[t_tmp], w=[t_hq[hn][ti]])
        for s in range(2):
            wt, t_wt = P.next_slab()
            for ti, (c0, w) in enumerate(tiles):
                for n in range(4):
                    ps, t_ps = P.psum("acc", [0, 1, 2, 3])
                    for k in range(KC):
                        P.mm(ps[:, :w], t_ps, wt[:, k * 512 + n * 128: k * 512 + (n + 1) * 128],
                             hq[:, k, c0:c0 + w], k == 0, k == KC - 1, r=[t_wt, t_hq[k][ti]])
                    on = s * 4 + n
                    P.tt("dve", x[:, on, c0:c0 + w], ps[:, :w], x[:, on, c0:c0 + w], ALU.add,
                         r=[t_ps, t_x[on][ti]], w=[t_x[on][ti]])


def setup_state(P, ncols, tiles):
    s = P.s
    P.setup_common()
    ntl = len(tiles)
    P.tiles = tiles
    P.x = P.sb("xres", [128, KC, ncols], F32)
    P.t_x = [[Tl("x%d_%d" % (k, ti)) for ti in range(ntl)] for k in range(KC)]
    P.sq = P.sb("sq", [128, KC, 512], BF16)
    P.t_sq = [Tl("sq%d" % k) for k in range(KC)]
    P.rt = P.sb("rt", [128, 512], F32)
    P.t_rt = Tl("rt")
    P.rstd = P.sb("rstd", [128, 512], F32)
    P.t_rstd = Tl("rstd")
    P.epsc = P.sb("epsc", [128, 1], F32)
    s.add("dve", lambda e: e.memset(P.epsc[:, :], EPS), w=[P.t_const])
    tmps = [P.sb("tmpf%d" % i, [128, 512], F32) for i in range(3)]
    t_tmps = [Tl("tmpf%d" % i) for i in range(3)]
    rr = [0]

    def tmpf():
        i = rr[0] % 3
        rr[0] += 1
        return tmps[i], t_tmps[i]
    P.tmpf = tmpf


def alloc_R(P, ncols, ntl):
    P.R1 = P.sb("R1", [128, KC, ncols], BF16)
    P.t_R1 = [[Tl("r1_%d_%d" % (k, ti)) for ti in range(ntl)] for k in range(KC)]
    P.R2 = P.sb("R2", [128, KC, ncols], BF16)
    P.t_R2 = [[Tl("r2_%d_%d" % (k, ti)) for ti in range(ntl)] for k in range(KC)]


def load_x(P, xt_ap, ncols):
    s = P.s
    xt_v = xt_ap.rearrange("(k p) t -> p k t", p=128)
    for k in range(KC):
        s.add("sp", (lambda k: lambda e: e.dma_start(out=P.x[:, k, :], in_=xt_v[:, k, :]))(k),
              w=[P.t_x[k][ti] for ti in range(len(P.tiles))], dma=True)


def store_x(P, xo_ap, ncols=NMAIN):
    s = P.s
    xo_v = xo_ap.rearrange("(k p) t -> p k t", p=128)
    for k in range(KC):
        t_o = Tl("out%d" % k)
        s.add("sp", (lambda k: lambda e: e.dma_start(out=xo_v[:, k, :], in_=P.x[:, k, 0:ncols]))(k),
              r=[P.t_x[k][ti] for ti in range(4)], w=[t_o], dma=True)
        P.out_tiles.append(t_o)


def phase_L0(P):
    s = P.s
    x, t_x, tiles = P.x, P.t_x, P.tiles
    R1, t_R1, R2, t_R2 = P.R1, P.t_R1, P.R2, P.t_R2
    hm, t_hm = P.hm, P.t_hm
    m0 = P.mark()
    xn, t_xn = R1, t_R1
    for ti, (c0, w) in enumerate(tiles):
        P.rmsnorm([x[:, k, c0:c0 + w] for k in range(KC)], [t_x[k][ti] for k in range(KC)], "mixn0",
                  [xn[:, k, c0:c0 + w] for k in range(KC)], [t_xn[k][ti] for k in range(KC)], w)
    hc = R2.rearrange("p k (s t) -> p k s t", s=2)
    t_hc = t_R2

    def hc_dst(c, ti):
        if ti < 4:
            seg, half = ti // 2, ti % 2
            return hc[:, c, seg, HALO + half * 512: HALO + half * 512 + 512]
        return hc[:, c, :, 0:HALO]

    for s4 in range(4):
        wt, t_wt = P.next_slab()
        for ti, (c0, w) in enumerate(tiles):
            for cl in range(2):
                c = 2 * s4 + cl
                psa, t_psa = P.psum("acc", [0, 1, 2, 3])
                psg, t_psg = P.psum("acc", [0, 1, 2, 3])
                for k in range(KC):
                    P.mm(psa[:, :w], t_psa, wt[:, k * 512 + cl * 128: k * 512 + (cl + 1) * 128],
                         xn[:, k, c0:c0 + w], k == 0, k == KC - 1, r=[t_wt, t_xn[k][ti]])
                for k in range(KC):
                    P.mm(psg[:, :w], t_psg, wt[:, k * 512 + (2 + cl) * 128: k * 512 + (3 + cl) * 128],
                         xn[:, k, c0:c0 + w], k == 0, k == KC - 1, r=[t_wt, t_xn[k][ti]])
                sg, t_sg = P.tmpf()
                P.act(sg[:, :w], psg[:, :w], AF.Sigmoid, r=[t_psg, P.t_pp], w=[t_sg],
                      bias=P.ppc("a_b_in", 8 + c))
                if ti == 4:
                    P.tt("dve", sg[:, :w], sg[:, :w], hm[:, :], ALU.mult, r=[t_sg, t_hm], w=[t_sg])
                    src_a = psa[:, :w].rearrange("p (s t) -> p s t", s=2)
                    src_g = sg[:, :w].rearrange("p (s t) -> p s t", s=2)
                else:
                    src_a = psa[:, :w]
                    src_g = sg[:, :w]
                P.stt(hc_dst(c, ti), src_a, P.ppc("a_b_in", c), src_g, ALU.add, ALU.mult,
                      r=[t_psa, t_sg, P.t_pp], w=[t_hc[c][ti]])
    dgs = [P.sb("dg%d" % i, [128, 31 * 128], BF16) for i in range(2)]
    t_dgs = [Tl("dg%d" % i) for i in range(2)]
    yall, t_yall = R1, t_R1
    for c in range(KC):
        dg, t_dg = dgs[c % 2], t_dgs[c % 2]
        for k in range(31):
            P.ts("dve", dg[:, k * 128:(k + 1) * 128], P.ident_f[:, :], P.ppc("a_conv_w", c * 31 + k), None,
                 ALU.mult, None, r=[P.t_const, P.t_pp], w=[t_dg])
        for ti, (c0, w) in enumerate(tiles):
            ps, t_ps = P.psum("conv", [6, 7])
            if ti < 4:
                seg, half = ti // 2, ti % 2
                rd = [t_dg, t_hc[c][ti], t_hc[c][ti - 1 if half else 4]]
                for k in range(31):
                    b0 = 2 + half * 512 + k
                    P.mm(ps[:, :512], t_ps, dg[:, k * 128:(k + 1) * 128], hc[:, c, seg, b0:b0 + 512],
                         k == 0, k == 30, r=rd)
                P.act(yall[:, c, c0:c0 + w], ps[:, :w], AF.Identity, r=[t_ps, P.t_pp], w=[t_yall[c][ti]],
                      bias=P.ppc("a_conv_b", c))
            else:
                s.add("dve", (lambda c: lambda e: e.memset(yall[:, c, NMAIN:NT], 0.0))(c), w=[t_yall[c][4]])
                pv = ps[:, 0:4].rearrange("p (s t) -> p s t", s=2)
                for k in range(31):
                    P.mm(pv, t_ps, dg[:, k * 128:(k + 1) * 128], hc[:, c, :, k:k + 2],
                         k == 0, k == 30, r=[t_dg, t_hc[c][4]])
                yv = yall[:, c, NMAIN:NT].rearrange("p (s t) -> p s t", s=2)[:, :, 30:32]
                P.act(yv, pv, AF.Identity, r=[t_ps, P.t_pp], w=[t_yall[c][4]], bias=P.ppc("a_conv_b", c))
    sall, t_sall = R2, t_R2
    mu_sb = P.sb("mu_sb", [128, 512], F32)
    t_mu = Tl("mu")
    m2 = P.sb("m2", [128, 512], F32)
    t_m2 = Tl("m2")
    for ti, (c0, w) in enumerate(tiles):
        for k in range(KC):
            P.act(P.sq[:, k, :w], yall[:, k, c0:c0 + w], AF.Square, r=[t_yall[k][ti]], w=[P.t_sq[k]])
        psm, t_psm = P.psum("stat", [4, 5])
        pss, t_pss = P.psum("stat", [4, 5])
        for k in range(KC):
            P.mm(psm[:, :w], t_psm, P.onesD[:, :], yall[:, k, c0:c0 + w], k == 0, k == KC - 1,
                 r=[t_yall[k][ti], P.t_const])
        for k in range(KC):
            P.mm(pss[:, :w], t_pss, P.onesD[:, :], P.sq[:, k, :w], k == 0, k == KC - 1,
                 r=[P.t_sq[k], P.t_const])
        P.act(mu_sb[:, :w], psm[:, :w], AF.Identity, r=[t_psm], w=[t_mu])
        P.tt("dve", m2[:, :w], mu_sb[:, :w], mu_sb[:, :w], ALU.mult, r=[t_mu], w=[t_m2])
        P.tt("dve", m2[:, :w], pss[:, :w], m2[:, :w], ALU.subtract, r=[t_pss, t_m2], w=[t_m2])
        P.act(P.rt[:, :w], m2[:, :w], AF.Ln, r=[t_m2, P.t_const], w=[P.t_rt], bias=P.epsc[:, 0:1])
        P.act(P.rstd[:, :w], P.rt[:, :w], AF.Exp, r=[P.t_rt], w=[P.t_rstd], scale=-0.5)
        for k in range(KC):
            z, t_z = P.tmpf()
            P.tt("dve", z[:, :w], yall[:, k, c0:c0 + w], mu_sb[:, :w], ALU.subtract,
                 r=[t_yall[k][ti], t_mu], w=[t_z])
            P.tt("dve", z[:, :w], z[:, :w], P.rstd[:, :w], ALU.mult, r=[t_z, P.t_rstd], w=[t_z])
            P.act(sall[:, k, c0:c0 + w], z[:, :w], AF.Silu, r=[t_z, P.t_pp], w=[t_sall[k][ti]],
                  bias=P.ppc("a_ln_b", k), scale=P.ppc("a_ln_g", k))
    for s2 in range(2):
        wt, t_wt = P.next_slab()
        for ti, (c0, w) in enumerate(tiles):
            for n in range(4):
                ps, t_ps = P.psum("acc", [0, 1, 2, 3])
                for k in range(KC):
                    P.mm(ps[:, :w], t_ps, wt[:, k * 512 + n * 128: k * 512 + (n + 1) * 128],
                         sall[:, k, c0:c0 + w], k == 0, k == KC - 1, r=[t_wt, t_sall[k][ti]])
                on = s2 * 4 + n
                P.stt(x[:, on, c0:c0 + w], ps[:, :w], P.ppc("a_b_out", on), x[:, on, c0:c0 + w],
                      ALU.add, ALU.add, r=[t_ps, P.t_pp, t_x[on][ti]], w=[t_x[on][ti]])
    P.release(m0)
    mlp_block(P, 0, x, t_x, tiles, R1, t_R1, R2, t_R2)


def phase_L1(P):
    s = P.s
    x, t_x, tiles = P.x, P.t_x, P.tiles
    R1, t_R1, R2, t_R2 = P.R1, P.t_R1, P.R2, P.t_R2
    hm, t_hm = P.hm, P.t_hm
    xn, t_xn = R1, t_R1
    for ti, (c0, w) in enumerate(tiles):
        P.rmsnorm([x[:, k, c0:c0 + w] for k in range(KC)], [t_x[k][ti] for k in range(KC)], "mixn1",
                  [xn[:, k, c0:c0 + w] for k in range(KC)], [t_xn[k][ti] for k in range(KC)], w)
    ub = R2.rearrange("p k (s t) -> p k s t", s=2)
    t_ub = t_R2

    def ub_dst(c, ti):
        if ti < 4:
            seg, half = ti // 2, ti % 2
            return ub[:, c, seg, HALO + half * 512: HALO + half * 512 + 512]
        return ub[:, c, :, 0:HALO]
    for s4 in range(4):
        wt, t_wt = P.next_slab()
        for ti, (c0, w) in enumerate(tiles):
            for cl in range(2):
                c = 2 * s4 + cl
                psc, t_psc = P.psum("acc", [0, 1, 2, 3])
                psh, t_psh = P.psum("acc", [0, 1, 2, 3])
                for k in range(KC):
                    P.mm(psc[:, :w], t_psc, wt[:, k * 512 + cl * 128: k * 512 + (cl + 1) * 128],
                         xn[:, k, c0:c0 + w], k == 0, k == KC - 1, r=[t_wt, t_xn[k][ti]])
                for k in range(KC):
                    P.mm(psh[:, :w], t_psh, wt[:, k * 512 + (2 + cl) * 128: k * 512 + (3 + cl) * 128],
                         xn[:, k, c0:c0 + w], k == 0, k == KC - 1, r=[t_wt, t_xn[k][ti]])
                gc, t_gc = P.tmpf()
                P.act(gc[:, :w], psc[:, :w], AF.Identity, r=[t_psc], w=[t_gc])
                if ti == 4:
                    P.tt("dve", gc[:, :w], gc[:, :w], hm[:, :], ALU.mult, r=[t_gc, t_hm], w=[t_gc])
                    src_h = psh[:, :w].rearrange("p (s t) -> p s t", s=2)
                    src_c = gc[:, :w].rearrange("p (s t) -> p s t", s=2)
                else:
                    src_h = psh[:, :w]
                    src_c = gc[:, :w]
                P.tt("dve", ub_dst(c, ti), src_h, src_c, ALU.mult, r=[t_psh, t_gc], w=[t_ub[c][ti]])
    def conv3(c, ti):
        seg, half = ti // 2, ti % 2
        acc, t_acc = P.tmpf()
        base = HALO + half * 512
        rd = [t_ub[c][ti], t_ub[c][ti - 1 if half else 4], P.t_pp]
        P.ts("dve", acc[:, :], ub[:, c, seg, base - 2: base - 2 + 512], P.ppc("b_conv_w", c * 3 + 0), None,
             ALU.mult, None, r=rd, w=[t_acc])
        P.stt(acc[:, :], ub[:, c, seg, base - 1: base - 1 + 512], P.ppc("b_conv_w", c * 3 + 1), acc[:, :],
              ALU.mult, ALU.add, r=rd + [t_acc], w=[t_acc])
        P.stt(ub[:, c, seg, base: base + 512], ub[:, c, seg, base: base + 512],
              P.ppc("b_conv_w", c * 3 + 2), acc[:, :], ALU.mult, ALU.add,
              r=rd + [t_acc], w=[t_ub[c][ti]])
    mtiles = tiles[:4]
    for s2 in range(2):
        wt, t_wt = P.next_slab()
        for ti in (1, 0, 3, 2):
            (c0, w) = mtiles[ti]
            seg, half = ti // 2, ti % 2
            base = HALO + half * 512
            for n in range(4):
                c = s2 * 4 + n
                conv3(c, ti)
                ps, t_ps = P.psum("acc", [0, 1, 2, 3])
                for k in range(KC):
                    P.mm(ps[:, :w], t_ps, wt[:, k * 512 + n * 128: k * 512 + (n + 1) * 128],
                         xn[:, k, c0:c0 + w], k == 0, k == KC - 1, r=[t_wt, t_xn[k][ti]])
                P.tt("dve", ub[:, c, seg, base:base + 512], ps[:, :w], ub[:, c, seg, base:base + 512], ALU.mult,
                     r=[t_ps, t_ub[c][ti]], w=[t_ub[c][ti]])
    for s2 in range(2):
        wt, t_wt = P.next_slab()
        for ti, (c0, w) in enumerate(mtiles):
            seg, half = ti // 2, ti % 2
            base = HALO + half * 512
            for n in range(4):
                ps, t_ps = P.psum("acc", [0, 1, 2, 3])
                for k in range(KC):
                    P.mm(ps[:, :w], t_ps, wt[:, k * 512 + n * 128: k * 512 + (n + 1) * 128],
                         ub[:, k, seg, base:base + 512], k == 0, k == KC - 1, r=[t_wt, t_ub[k][ti]])
                on = s2 * 4 + n
                P.tt("dve", x[:, on, c0:c0 + w], ps[:, :w], x[:, on, c0:c0 + w], ALU.add,
                     r=[t_ps, t_x[on][ti]], w=[t_x[on][ti]])
    mlp_block(P, 1, x, t_x, mtiles, R1, t_R1, R2, t_R2)


NH = 16
HD = 64
VW = 16 * 65
NEG = -60000.0


def seg_loc(i):
    return (i, 0) if i < 4 else (7 - i, 1)


def phase_L2pre(P, qd, kdst, vdst, cd, t_kd_hp, t_vd_hp, on_written=None):
    s = P.s
    x, t_x = P.x, P.t_x
    mt = P.tiles[:4]
    xn, t_xn = P.R1, P.t_R1
    for ti, (c0, w) in enumerate(mt):
        P.rmsnorm([x[:, k, c0:c0 + w] for k in range(KC)], [t_x[k][ti] for k in range(KC)], "mixn2",
                  [xn[:, k, c0:c0 + w] for k in range(KC)], [t_xn[k][ti] for k in range(KC)], w)
    m0 = P.mark()
    t_c = Tl("l2c")
    s.barrier()
    r2flat = P.R2.rearrange("p k t -> p (k t)")
    r2top = [0]

    def r2alloc(shape, dt):
        n = shape[1]
        ne = n if dt == BF16 else 2 * n
        off = r2top[0]
        r2top[0] += ne
        assert r2top[0] <= KC * NT
        v = r2flat[:, off:off + ne]
        if dt == F32:
            v = v.bitcast(F32)
        return v[0:shape[0]]
    wf = P.sb("wf_bf", [128, KC, 16], BF16)
    s.add("dve", lambda e: e.tensor_copy(out=wf.rearrange("p k h -> p (k h)"), in_=P.ppc("c_w_f", 0, 128)),
          r=[P.t_pp], w=[t_c])
    nbf = P.sb("nbf", [16, 1], F32)
    P.ts("dve", nbf[:, :], P.ppc("c_b_f")[0:16], -1.0, None, ALU.mult, None, r=[P.t_pp], w=[t_c])
    one1 = P.sb("one1", [128, 1], F32)
    s.add("dve", lambda e: e.memset(one1[:, :], 1.0), w=[t_c])
    lbuf = r2alloc([16, NMAIN], F32)
    t_l = Tl("lbuf")
    ones16 = r2alloc([16, SEG], F32)
    s.add("dve", lambda e: e.memset(ones16[:, :], 1.0), w=[t_c])
    for ti, (c0, w) in enumerate(mt):
        ps, t_ps = P.psum("stat", [4, 5])
        for k in range(KC):
            P.mm(ps[0:16, :w], t_ps, wf[:, k, :], xn[:, k, c0:c0 + w], k == 0, k == KC - 1,
                 r=[t_c, t_xn[k][ti]])
        et, t_et = P.tmpf()
        P.act(et[0:16, :w], ps[0:16, :w], AF.Exp, r=[t_ps, t_c], w=[t_et], bias=nbf[:, 0:1], scale=-1.0)
        P.act(lbuf[:, c0:c0 + w], et[0:16, :w], AF.Ln, r=[t_et, t_c], w=[t_l], bias=one1[0:16, 0:1])
    cl = r2alloc([16, NMAIN], F32)
    t_cl = Tl("cl")
    for sg in range(2):
        s.add("dve", (lambda sg: lambda e: e.tensor_tensor_scan(
            cl[:, sg * SEG:(sg + 1) * SEG], ones16[:, :], lbuf[:, sg * SEG:(sg + 1) * SEG], 0.0,
            ALU.mult, ALU.subtract))(sg), r=[t_l, t_c], w=[t_cl])
    t_cd = Tl("cd")
    s.add("sp", lambda e: e.dma_start(out=cd[:, :], in_=cl[:, :]), r=[t_cl], w=[t_cd], dma=True)
    P.out_tiles.append(t_cd)
    P.t_cd = t_cd
    if on_written is not None:
        on_written("c", 0)
    aq = lbuf
    hi = r2alloc([16, NMAIN], BF16)
    lo = r2alloc([16, NMAIN], BF16)
    t_aq = Tl("aq")
    for ti, (c0, w) in enumerate(mt):
        P.ts("dve", aq[:, c0:c0 + w], cl[:, c0:c0 + w], cl[:, c0:c0 + 1], None, ALU.subtract, None,
             r=[t_cl, t_l], w=[t_aq, t_l])
    s.add("dve", lambda e: e.tensor_copy(out=hi[:, :], in_=aq[:, :]), r=[t_aq], w=[t_aq])
    P.tt("dve", lo[:, :], aq[:, :], hi[:, :], ALU.subtract, r=[t_aq], w=[t_aq])
    t_qd = Tl("qd")
    s.add("sp", lambda e: e.dma_start(out=qd[:, 64, :], in_=hi[:, :]), r=[t_aq], w=[t_qd], dma=True)
    s.add("sp", lambda e: e.dma_start(out=qd[:, 65, :], in_=lo[:, :]), r=[t_aq], w=[t_qd], dma=True)
    vst = P.R2.rearrange("p k t -> p (k t)")[:, 0:16 * VW].rearrange("p (h kt c) -> p h kt c", h=16, kt=16)
    t_vst = Tl("vst")
    s.barrier()
    for k in range(KC):
        for ti in range(len(P.t_R2[k])):
            P.t_R2[k][ti] = t_vst
    s.add("dve", lambda e: e.memset(vst[:, :, :, 64:65], 1.0), w=[t_vst])
    ev = 0
    for vs in range(2):
        wt, t_wt = P.next_slab()
        for tb in range(16):
            ps, t_ps = P.psum("acc", [0, 1, 2, 3])
            ti = tb // 4
            for k in range(KC):
                P.mm(ps[:, :], t_ps, xn[:, k, tb * 128:(tb + 1) * 128], wt[:, k * 512:(k + 1) * 512],
                     k == 0, k == KC - 1, r=[t_wt, t_xn[k][ti]])
            dst = vst[:, vs * 8:(vs + 1) * 8, tb, 0:64]
            srcv = ps[:, :].rearrange("p (h c) -> p h c", h=8)
            if ev % 2 == 0:
                P.act(dst, srcv, AF.Identity, r=[t_ps], w=[t_vst])
            else:
                s.add("dve", (lambda dst, srcv: lambda e: e.tensor_copy(out=dst, in_=srcv))(dst, srcv),
                      r=[t_ps], w=[t_vst])
            ev += 1
    vflat = vst.rearrange("p h kt c -> p h (kt c)")
    for hp in range(8):
        s.add("sp", (lambda hp: lambda e: e.dma_start(out=vdst(hp).rearrange("(h p) f -> p h f", h=2),
                                                      in_=vflat[:, 2 * hp:2 * hp + 2, :]))(hp),
              r=[t_vst], w=[t_vd_hp[hp]], dma=True)
        if on_written is not None:
            on_written("v", hp)
    ones64 = P.sb("ones64", [64, 64], BF16)
    s.add("dve", lambda e: e.memset(ones64[:, :], 1.0 / HD), w=[t_c])
    gq = P.sb("gq", [64, 1], F32)
    P.ts("dve", gq[:, :], P.ppc("c_q_norm")[0:64], HD ** -0.5, None, ALU.mult, None, r=[P.t_pp], w=[t_c])
    stq = [P.sb("stq%d" % i, [64, NMAIN], BF16) for i in range(2)]
    stk = [P.sb("stk%d" % i, [66, NMAIN], BF16) for i in range(2)]
    t_stq = [Tl("stq%d" % i) for i in range(2)]
    t_stk = [Tl("stk%d" % i) for i in range(2)]
    for i in range(2):
        s.add("dve", (lambda i: lambda e: e.memset(stk[i][64:66, :], 1.0))(i), w=[t_stk[i]])
    sqh = [P.sb("sqh%d" % i, [64, 512], BF16) for i in range(2)]
    t_sqh = [Tl("sqh%d" % i) for i in range(2)]
    t_kd = Tl("kd")
    cnt = 0
    pending = [None]

    def finish_unit(which, h, stg, t_stg, gcol, ps, t_ps, c0, w, is_last):
        sq_, t_sq_ = sqh[finish_unit.cnt % 2], t_sqh[finish_unit.cnt % 2]
        finish_unit.cnt += 1
        P.act(sq_[:, :w], ps[0:64, :w], AF.Square, r=[t_ps], w=[t_sq_])
        ps2, t_ps2 = P.psum("stat", [4, 5])
        P.mm(ps2[0:64, :w], t_ps2, ones64[:, :], sq_[:, :w], True, True, r=[t_sq_, t_c])
        P.act(P.rt[0:64, :w], ps2[0:64, :w], AF.Ln, r=[t_ps2, P.t_const], w=[P.t_rt],
              bias=P.epsc[0:64, 0:1])
        P.act(P.rstd[0:64, :w], P.rt[0:64, :w], AF.Exp, r=[P.t_rt], w=[P.t_rstd], scale=-0.5)
        P.stt(stg[0:64, c0:c0 + w], ps[0:64, :w], gcol, P.rstd[0:64, :w], ALU.mult, ALU.mult,
              r=[t_ps, P.t_rstd, t_c, P.t_pp], w=[t_stg])
        if is_last:
            if which == 0:
                s.add("sp", lambda e: e.dma_start(out=qd[h, 0:64, :], in_=stg[0:64, :]),
                      r=[t_stg], w=[t_qd], dma=True)
            else:
                s.add("sp", lambda e: e.dma_start(out=kdst(h)[0:64, :], in_=stg[0:64, :]),
                      r=[t_stg], w=[t_kd_hp[h // 2]], dma=True)
                s.add("sp", lambda e: e.dma_start(out=kdst(h)[64:66, :], in_=stg[64:66, :]),
                      r=[t_stg], w=[t_kd_hp[h // 2]], dma=True)
                if on_written is not None and h % 2 == 1:
                    on_written("k", h // 2)
    finish_unit.cnt = 0
    for which in range(2):
        for sl in range(2):
            wt, t_wt = P.next_slab()
            for hl in range(8):
                h = sl * 8 + hl
                if which == 0:
                    stg, t_stg = stq[h % 2], t_stq[h % 2]
                    gcol = gq[:, 0:1]
                else:
                    stg, t_stg = stk[h % 2], t_stk[h % 2]
                    gcol = P.ppc("c_k_norm")[0:64]
                for ti, (c0, w) in enumerate(mt):
                    ps, t_ps = P.psum("acc", [0, 1, 2, 3])
                    for k in range(KC):
                        P.mm(ps[0:64, :w], t_ps, wt[:, k * 512 + hl * 64: k * 512 + (hl + 1) * 64],
                             xn[:, k, c0:c0 + w], k == 0, k == KC - 1, r=[t_wt, t_xn[k][ti]])
                    if pending[0] is not None:
                        finish_unit(*pending[0])
                    pending[0] = (which, h, stg, t_stg, gcol, ps, t_ps, c0, w, ti == len(mt) - 1)
    finish_unit(*pending[0])
    P.t_qd = t_qd
    P.t_cd = t_cd
    P.release(m0)


def phase_L2attn(P, qd, kown, vown, cown, kallf, vallf, call, flexmd, seld, cmaskd, dep):
    s = P.s
    x, t_x = P.x, P.t_x
    mt = P.tiles[:4]
    m0 = P.mark()
    t_k = Tl("l2a_const")
    Tt = P.sb("Tt", [16, 8], F32)
    t_T = Tl("Tt")
    for i in range(8):
        r_, part = seg_loc(i)
        col = part * SEG + SEG - 1
        s.add("sp", (lambda i, r_, col: lambda e: e.dma_start(out=Tt[:, i:i + 1], in_=call[r_, :, col:col + 1], allow_slow_non_contiguous=True))(i, r_, col),
              r=[dep["call"]], w=[t_T], dma=True)
    sel = P.sb("sel", [16, 16], F32)
    s.add("sp", lambda e: e.dma_start(out=sel[:, :], in_=seld[:, :]), w=[t_k], dma=True)
    flexm = P.sb("flexm", [128, 8], F32)
    s.add("sp", lambda e: e.dma_start(out=flexm[:, :], in_=flexmd[:, :]), w=[t_k], dma=True)
    cmask = P.sb("cmask", [128, 4, 512], BF16)
    s.add("pool", lambda e: e.dma_start(out=cmask[:, :, :], in_=cmaskd.rearrange("i p q -> p i q")), w=[t_k], dma=True)
    ones8 = P.sb("ones8", [16, 8], F32)
    s.add("dve", lambda e: e.memset(ones8[:, :], 1.0), w=[t_k])
    Pin = P.sb("Pin", [16, 8], F32)
    Pex = P.sb("Pex", [16, 8], F32)
    t_P = Tl("Pex")
    s.add("dve", lambda e: e.tensor_tensor_scan(Pin[:, :], ones8[:, :], Tt[:, :], 0.0, ALU.mult, ALU.add),
          r=[t_T, t_k], w=[t_P])
    P.tt("dve", Pex[:, :], Pin[:, :], Tt[:, :], ALU.subtract, r=[t_P, t_T], w=[t_P])
    PAB = P.sb("PAB", [16, 2], F32)
    ptmp = P.sb("ptmp", [16, 8], F32)
    for sg in range(2):
        P.tt("dve", ptmp[:, :], Pex[:, :], sel[:, sg * 8:(sg + 1) * 8], ALU.mult, r=[t_P, t_k], w=[t_P])
        s.add("dve", (lambda sg: lambda e: e.reduce_sum(PAB[:, sg:sg + 1], ptmp[:, :], mybir.AxisListType.X))(sg),
              r=[t_P], w=[t_P])
    clq = P.sb("clq", [16, 4], F32)
    t_clq = Tl("clq")
    for qt in range(4):
        s.add("sp", (lambda qt: lambda e: e.dma_start(out=clq[:, qt:qt + 1], in_=cown[:, qt * 512:qt * 512 + 1], allow_slow_non_contiguous=True))(qt),
              r=[dep["cd"]], w=[t_clq], dma=True)
    CQ = P.sb("CQ", [16, 4], F32)
    for qt in range(4):
        P.tt("dve", CQ[:, qt:qt + 1], clq[:, qt:qt + 1], PAB[:, qt // 2:qt // 2 + 1], ALU.add,
             r=[t_clq, t_P], w=[t_P])
    CQd = P.sb("CQd", [16, 64], F32)
    negI = P.sb("negI", [16, 64], F32)
    ones16c = P.sb("ones16c", [16, 128], F32)
    s.add("dve", lambda e: e.memset(ones16c[:, :], 1.0), w=[t_k])
    for qt in range(4):
        P.ts("dve", CQd[:, qt * 16:(qt + 1) * 16], P.ident_f[0:16, 0:16], CQ[:, qt:qt + 1], None, ALU.mult, None,
             r=[P.t_const, t_P], w=[t_P])
        P.ts("dve", negI[:, qt * 16:(qt + 1) * 16], P.ident_f[0:16, 0:16], -1.0, None, ALU.mult, None,
             r=[P.t_const], w=[t_k])
    biasall = P.sb("biasall", [128, 72, 64], F32)
    t_bias = Tl("biasall")
    cgk = [P.sb("cgk%d" % i, [16, SEG], F32) for i in range(2)]
    t_cgk = [Tl("cgk%d" % i) for i in range(2)]
    for ks in range(9):
        cg, t_cg = cgk[ks % 2], t_cgk[ks % 2]
        if ks < 7:
            r_, part = seg_loc(ks)
            src = call[r_, :, part * SEG:(part + 1) * SEG]
            pcol = Pex[:, ks:ks + 1]
        else:
            src = cown[:, (ks - 7) * SEG:(ks - 6) * SEG]
            pcol = PAB[:, ks - 7:ks - 6]
        s.add("sp", (lambda cg, src: lambda e: e.dma_start(out=cg[:, :], in_=src))(cg, src),
              r=[dep["call"], dep["cd"]], w=[t_cg], dma=True)
        P.ts("dve", cg[:, :], cg[:, :], pcol, None, ALU.add, None, r=[t_cg, t_P], w=[t_cg])
        for kt in range(8):
            ps, t_ps = P.psum("acc", [4, 5, 6, 7])
            P.mm(ps[:, 0:64], t_ps, cg[:, kt * 128:(kt + 1) * 128], negI[:, :], True, False, r=[t_cg, t_k])
            P.mm(ps[:, 0:64], t_ps, ones16c[:, :], CQd[:, :], False, True, r=[t_k, t_P])
            s.add("dve", (lambda ks, kt, ps: lambda e: e.tensor_copy(out=biasall[:, ks * 8 + kt, :], in_=ps[:, 0:64]))(ks, kt, ps),
                  r=[t_ps], w=[t_bias])
    flexbias = P.sb("flexbias", [128, 3, 8, 32], F32)
    for p in range(3):
        P.ts("dve", flexbias[:, p, :, :], biasall[:, p * 8:(p + 1) * 8, 0:32], flexm[:, p:p + 1], None,
             ALU.mult, None, r=[t_bias, t_k], w=[t_bias])
        P.stt(flexbias[:, p, :, :], biasall[:, (6 - p) * 8:(7 - p) * 8, 32:64], flexm[:, 3 + p:4 + p],
              flexbias[:, p, :, :], ALU.mult, ALU.add, r=[t_bias, t_k], w=[t_bias])
    sel65 = P.sb("sel65", [65, 64], F32)
    s.add("dve", lambda e: e.memset(sel65[:, :], 0.0), w=[t_k])
    s.add("dve", lambda e: e.memset(sel65[64:65, :], 1.0), r=[t_k], w=[t_k])
    NRING = 4
    kcs = [P.sb("kc%d" % i, [66, SEG], BF16) for i in range(NRING)]
    vcs = [P.sb("vc%d" % i, [128, 8, 65], BF16) for i in range(NRING)]
    t_kv = [Tl("kv%d" % i) for i in range(NRING)]
    qhs = [P.sb("qh%d" % i, [66, NMAIN], BF16) for i in range(2)]
    t_qh = [Tl("qh%d" % i) for i in range(2)]
    qfb = [P.sb("qflex%d" % i, [66, SEG], BF16) for i in range(2)]
    t_qfb = [Tl("qflex%d" % i) for i in range(2)]
    kfb = [P.sb("kflex%d" % i, [66, SEG], BF16) for i in range(2)]
    vfb = [P.sb("vflex%d" % i, [128, 8, 65], BF16) for i in range(2)]
    t_kfb = [Tl("kvflex%d" % i) for i in range(2)]
    prep_i = [0]
    pts = [P.sb("pt%d" % i, [128, 512], BF16) for i in range(3)]
    t_pt = [Tl("pt%d" % i) for i in range(3)]
    osb = [P.sb("osb%d" % i, [65, 512], F32) for i in range(2)]
    t_osb = [Tl("osb%d" % i) for i in range(2)]
    rden = P.sb("rden", [64, 512], F32)
    t_rden = Tl("rden")
    oT = P.sb("oT", [64, 4, NMAIN], BF16)
    t_oT = [[Tl("oT%d_%d" % (hl, qt)) for qt in range(4)] for hl in range(4)]
    fsum = P.sq.rearrange("p k t -> p (k t)").bitcast(F32)[0:65, 0:2048].rearrange("p (a t) -> p a t", a=4)
    t_fs = [Tl("fsum%d" % i) for i in range(4)]
    ring_i = [0]
    pt_i = [0]
    ob_i = [0]

    def ring_next():
        i = ring_i[0] % NRING
        ring_i[0] += 1
        return kcs[i], vcs[i], t_kv[i]

    def load_chunk(h, ks):
        kc, vc, t = ring_next()
        if ks < 7:
            r_, part = seg_loc(ks)
            ksrc = kallf(r_, h)[:, part * SEG:(part + 1) * SEG]
            vsrc = vallf(r_, h)[:, part * 8 * 65:(part + 1) * 8 * 65]
            rk, rv = [dep["kall"][h // 2]], [dep["vall"][h // 2]]
        else:
            part = ks - 7
            ksrc = kown(h)[:, part * SEG:(part + 1) * SEG]
            vsrc = vown(h)[:, part * 8 * 65:(part + 1) * 8 * 65]
            rk, rv = [dep["kd"][h // 2]], [dep["vd"][h // 2]]
        s.add("sp", lambda e: e.dma_start(out=kc[0:64, :], in_=ksrc[0:64]), r=rk, w=[t], dma=True)
        s.add("sp", lambda e: e.dma_start(out=kc[64:66, :], in_=ksrc[64:66]), r=rk, w=[t], dma=True)
        s.add("sp", lambda e: e.dma_start(out=vc.rearrange("p a b -> p (a b)"), in_=vsrc), r=rv, w=[t], dma=True)
        return kc, vc, t

    SKEW = 2

    def run_blocks(blist):
        pend = []

        def emit_pv(item):
            (vc, t_kvc, kt, pt, tpt, bank, st, sp_) = item
            P.mm(P.ps[bank][0:65, :], P.t_ps[bank], vc[:, kt, :], pt[:, :], st, sp_, r=[t_kvc, tpt])
        for (kc, vc, t_kvc, kt, q_ap, t_q, bcol, mi, bank, st, sp_) in blist:
            psS, t_psS = P.psum("S", [4, 5, 6])
            P.mm(psS[:, :], t_psS, kc[0:66, kt * 128:(kt + 1) * 128], q_ap, True, True, r=[t_kvc, t_q])
            pt, tpt = pts[pt_i[0] % 3], t_pt[pt_i[0] % 3]
            pt_i[0] += 1
            if mi is None:
                P.act(pt[:, :], psS[:, :], AF.Exp, r=[t_psS, t_bias], w=[tpt], bias=bcol)
            else:
                sb_, tsb = P.tmpf()
                P.tt("dve", sb_[:, :], psS[:, :], cmask[:, mi, :], ALU.add, r=[t_psS, t_k], w=[tsb])
                P.act(pt[:, :], sb_[:, :], AF.Exp, r=[tsb, t_bias], w=[tpt], bias=bcol)
            pend.append((vc, t_kvc, kt, pt, tpt, bank, st, sp_))
            if len(pend) > SKEW:
                emit_pv(pend.pop(0))
        while pend:
            emit_pv(pend.pop(0))

    def load_q(h):
        qh, tq = qhs[h % 2], t_qh[h % 2]
        s.add("sp", (lambda h, qh: lambda e: e.dma_start(out=qh[0:64, :], in_=qd[h, 0:64, :]))(h, qh),
              r=[dep["qd"]], w=[tq], dma=True)
        s.add("sp", (lambda h, qh: lambda e: e.dma_start(out=qh[64:66, :], in_=qd[h, 64:66, :]))(h, qh),
              r=[dep["qd"]], w=[tq], dma=True)

    def prepare(h, p):
        i = prep_i[0] % 2
        prep_i[0] += 1
        qh, tq = qhs[h % 2], t_qh[h % 2]
        mA, mB = flexm[:, p:p + 1], flexm[:, 3 + p:4 + p]
        kA, vA, tA = load_chunk(h, p)
        kB, vB, tB = load_chunk(h, 6 - p)
        kf, vf, tf, qf, t_qf = kfb[i], vfb[i], t_kfb[i], qfb[i], t_qfb[i]
        P.ts("dve", kf[:, :], kA[:, :], mA[0:66], None, ALU.mult, None, r=[tA, t_k], w=[tf])
        P.stt(kf[:, :], kB[:, :], mB[0:66], kf[:, :], ALU.mult, ALU.add, r=[tB, t_k, tf], w=[tf])
        vff, vAf, vBf = (v_.rearrange("p a b -> p (a b)") for v_ in (vf, vA, vB))
        P.ts("dve", vff, vAf, mA, None, ALU.mult, None, r=[tA, t_k, tf], w=[tf])
        P.stt(vff, vBf, mB, vff, ALU.mult, ALU.add, r=[tB, t_k, tf], w=[tf])
        P.ts("dve", qf[:, :], qh[:, 0:SEG], mA[0:66], None, ALU.mult, None, r=[tq, t_k], w=[t_qf])
        P.stt(qf[:, :], qh[:, SEG:2 * SEG], mB[0:66], qf[:, :], ALU.mult, ALU.add, r=[tq, t_k, t_qf], w=[t_qf])
        return kf, vf, tf, qf, t_qf

    load_q(0)
    prepared = prepare(0, 0)
    for h in range(NH):
        hl = h % 4
        qh, tq = qhs[h % 2], t_qh[h % 2]
        for p in range(3):
            mA, mB = flexm[:, p:p + 1], flexm[:, 3 + p:4 + p]
            kf, vf, tf, qf, t_qf = prepared
            if p < 2:
                prepared = prepare(h, p + 1)
            elif h + 1 < NH:
                load_q(h + 1)
                prepared = prepare(h + 1, 0)
            bk = 2 * (p % 2)
            bl = []
            for kt in range(8):
                for half in range(2):
                    bl.append((kf, vf, tf, kt, qf[0:66, half * 512:(half + 1) * 512], t_qf,
                               flexbias[:, p, kt, half * 16 + h: half * 16 + h + 1], None, bk + half, kt == 0, kt == 7))
            run_blocks(bl)
            for half in range(2):
                for dest, m in ((0, mA), (1, mB)):
                    fi = 2 * dest + half
                    if p == 0:
                        P.ts("dve", fsum[:, fi, :], P.ps[bk + half][0:65, :], m[0:65], None, ALU.mult, None,
                             r=[P.t_ps[bk + half], t_k], w=[t_fs[fi]])
                    else:
                        P.stt(fsum[:, fi, :], P.ps[bk + half][0:65, :], m[0:65], fsum[:, fi, :], ALU.mult, ALU.add,
                              r=[P.t_ps[bk + half], t_k, t_fs[fi]], w=[t_fs[fi]])
        bl = []
        for ks in range(4):
            kc, vc, tkv = load_chunk(h, ks)
            for kt in range(8):
                for qt in (2, 3):
                    bl.append((kc, vc, tkv, kt, qh[0:66, qt * 512:(qt + 1) * 512], tq,
                               biasall[:, ks * 8 + kt, qt * 16 + h: qt * 16 + h + 1], None, qt,
                               ks == 0 and kt == 0, False))
        for sg in range(2):
            kc, vc, tkv = load_chunk(h, 7 + sg)
            for half in range(2):
                qt = 2 * sg + half
                nk = 4 * (half + 1)
                for kt in range(nk):
                    mi = (kt - 4 * half) if kt >= 4 * half else None
                    bl.append((kc, vc, tkv, kt, qh[0:66, qt * 512:(qt + 1) * 512], tq,
                               biasall[:, (7 + sg) * 8 + kt, qt * 16 + h: qt * 16 + h + 1], mi, qt,
                               sg == 0 and kt == 0, kt == nk - 1))
        run_blocks(bl)
        for qt in range(4):
            ob, tob = osb[ob_i[0] % 2], t_osb[ob_i[0] % 2]
            ob_i[0] += 1
            fi = 2 * (qt // 2) + qt % 2
            P.tt("dve", ob[:, :], P.ps[qt][0:65, :], fsum[:, fi, :], ALU.add, r=[P.t_ps[qt], t_fs[fi]], w=[tob])
            P.mm(P.ps[7][0:64, :], P.t_ps[7], sel65[:, :], ob[:, :], True, True, r=[t_k, tob])
            P.act(rden[:, :], P.ps[7][0:64, :], AF.Ln, r=[P.t_ps[7]], w=[t_rden])
            P.act(rden[:, :], rden[:, :], AF.Exp, r=[t_rden], w=[t_rden], scale=-1.0)
            P.tt("dve", oT[:, hl, qt * 512:(qt + 1) * 512], ob[0:64, :], rden[:, :], ALU.mult,
                 r=[tob, t_rden], w=[t_oT[hl][qt]])
        if hl == 3:
            wt, t_wt = P.next_slab()
            for ti, (c0, w) in enumerate(mt):
                for n in range(KC):
                    ps, t_ps = P.psum("S", [4, 5, 6])
                    for j in range(4):
                        P.mm(ps[:, :w], t_ps, wt[0:64, j * 1024 + n * 128: j * 1024 + (n + 1) * 128],
                             oT[:, j, c0:c0 + w], j == 0, j == 3, r=[t_wt, t_oT[j][ti]])
                    P.tt("dve", x[:, n, c0:c0 + w], ps[:, :w], x[:, n, c0:c0 + w], ALU.add,
                         r=[t_ps, t_x[n][ti]], w=[t_x[n][ti]])
    P.release(m0)


GH = 4
CH = 128
NCH = SEG // CH


def gla_seg(P, sg, mode, sinit=None, gout=None, dout=None, tri=None, sinit_t=None, pre_chunk=None):
    s = P.s
    full = mode == "full"
    x, t_x = P.x, P.t_x
    tis = [2 * sg, 2 * sg + 1]
    m0 = P.mark()
    t_g = Tl("gla_c")
    xn = P.sb("gxn", [128, KC, SEG], BF16)
    t_xn = [[Tl("gxn%d_%d" % (k, i)) for i in range(2)] for k in range(KC)]
    for i in range(2):
        c0 = sg * SEG + i * 512
        P.rmsnorm([x[:, k, c0:c0 + 512] for k in range(KC)], [t_x[k][tis[i]] for k in range(KC)], "mixn3",
                  [xn[:, k, i * 512:(i + 1) * 512] for k in range(KC)], [t_xn[k][i] for k in range(KC)], 512)
    wg1 = P.sb("wg1", [128, KC, 16], BF16)
    s.add("dve", lambda e: e.tensor_copy(out=wg1.rearrange("p k h -> p (k h)"), in_=P.ppc("d_w_g1", 0, 128)),
          r=[P.t_pp], w=[t_g])
    wg2 = P.sb("wg2", [16, 512], BF16)
    s.add("dve", lambda e: e.tensor_copy(out=wg2[:, :], in_=P.ppc("d_w_g2", 0, 512)[0:16]), r=[P.t_pp], w=[t_g])
    nbg = P.sb("nbg", [128, 4], F32)
    P.ts("dve", nbg[:, :], P.ppc("d_b_g", 0, 4), -1.0, None, ALU.mult, None, r=[P.t_pp], w=[t_g])
    one1 = P.sb("gone1", [128, 1], F32)
    s.add("dve", lambda e: e.memset(one1[:, :], 1.0), w=[t_g])
    ones512 = P.sb("ones512", [128, 512], F32)
    s.add("dve", lambda e: e.memset(ones512[:, :], 1.0), w=[t_g])
    vtok = P.sb("vtok", [128, NCH, D], BF16)
    t_v = [Tl("vtok%d" % c) for c in range(NCH)]
    ev = 0
    for vs in range(2):
        wt, t_wt = P.next_slab()
        for c in range(NCH):
            ps, t_ps = P.psum("acc", [0, 1, 2, 3])
            for k in range(KC):
                P.mm(ps[:, :], t_ps, xn[:, k, c * CH:(c + 1) * CH], wt[:, k * 512:(k + 1) * 512],
                     k == 0, k == KC - 1, r=[t_wt, t_xn[k][c // 4]])
            dst = vtok[:, c, vs * 512:(vs + 1) * 512]
            if ev % 2 == 0:
                P.act(dst, ps[:, :], AF.Identity, r=[t_ps], w=[t_v[c]])
            else:
                s.add("dve", (lambda dst, ps: lambda e: e.tensor_copy(out=dst, in_=ps[:, :]))(dst, ps),
                      r=[t_ps], w=[t_v[c]])
            ev += 1
    g1 = P.sb("g1", [16, SEG], BF16)
    t_g1 = Tl("g1")
    for i in range(2):
        ps, t_ps = P.psum("stat", [4, 5])
        for k in range(KC):
            P.mm(ps[0:16, :], t_ps, wg1[:, k, :], xn[:, k, i * 512:(i + 1) * 512], k == 0, k == KC - 1,
                 r=[t_g, t_xn[k][i]])
        P.act(g1[:, i * 512:(i + 1) * 512], ps[0:16, :], AF.Identity, r=[t_ps], w=[t_g1])
    if full:
        wq, t_wq = P.next_slab()
    wk, t_wk = P.next_slab(hold=2 if full else 1)
    if full:
        qt_ = P.sb("gqt", [128, GH, SEG], BF16)
    kt_ = P.sb("gkt", [128, GH, SEG], BF16) if full else None
    kd_ = P.sb("gkd", [128, GH, SEG], BF16)
    t_q = [[Tl("gq%d_%d" % (h, i)) for i in range(2)] for h in range(GH)]
    t_kk = [[Tl("gk%d_%d" % (h, i)) for i in range(2)] for h in range(GH)]
    dl = P.sb("gdl", [128, GH, NCH], F32)
    t_dl = [Tl("gdl%d" % h) for h in range(GH)]
    dtot = P.sb("gdtot", [128, GH], F32)
    t_dt = Tl("gdtot")
    csb = [P.sb("gcs0", [128, 512], F32)] * 2
    t_cs = [Tl("gcs0")] * 2
    e4 = [P.sb("ge4_%d" % i, [128, 4], F32) for i in range(2)]
    ep4 = P.sb("gep4", [128, 4], F32)
    dd4 = P.sb("gdd4", [128, 4], F32)
    t_e = Tl("ge4")
    fb = [P.sb("gfb%d" % i, [128, 512], F32) for i in range(3)]
    t_fb = [Tl("gfb%d" % i) for i in range(3)]
    qscale = float(GLA_DKH ** -0.5)
    for h in range(GH):
        for i in range(2):
            lc = i * 512
            psz, t_psz = P.psum("stat", [4, 5])
            P.mm(psz[:, :], t_psz, wg2[0:16, h * 128:(h + 1) * 128], g1[0:16, lc:lc + 512], True, True,
                 r=[t_g, t_g1])
            P.act(fb[0][:, :], psz[:, :], AF.Exp, r=[t_psz, t_g], w=[t_fb[0]], bias=nbg[:, h:h + 1], scale=-1.0)
            P.act(fb[0][:, :], fb[0][:, :], AF.Ln, r=[t_fb[0], t_g], w=[t_fb[0]], bias=one1[:, 0:1])
            cs, tcs = csb[i], t_cs[i]
            init = 0.0 if i == 0 else e4[0][:, 3:4]
            s.add("dve", (lambda cs, init: lambda e: e.tensor_tensor_scan(
                cs[:, :], ones512[:, :], fb[0][:, :], init, ALU.mult, ALU.add))(cs, init),
                r=[t_fb[0], t_g] + ([t_e] if i else []), w=[tcs])
            csv = cs.rearrange("p (c t) -> p c t", t=CH)
            E4 = e4[i]
            s.add("dve", (lambda E4, csv: lambda e: e.tensor_copy(out=E4[:, :], in_=csv[:, :, CH - 1]))(E4, csv),
                  r=[tcs], w=[t_e])
            if i == 0:
                s.add("dve", lambda e: e.memset(ep4[:, 0:1], 0.0), r=[t_e], w=[t_e])
            else:
                s.add("dve", lambda e: e.tensor_copy(out=ep4[:, 0:1], in_=e4[0][:, 3:4]), r=[t_e], w=[t_e])
            s.add("dve", (lambda E4: lambda e: e.tensor_copy(out=ep4[:, 1:4], in_=E4[:, 0:3]))(E4), r=[t_e], w=[t_e])
            bbv = fb[1].rearrange("p (c t) -> p c t", t=CH)
            bb2v = fb[2].rearrange("p (c t) -> p c t", t=CH)
            s.add("dve", (lambda csv: lambda e: e.tensor_tensor(
                bbv, csv, ep4[:, :].unsqueeze(2).to_broadcast([128, 4, CH]), ALU.subtract))(csv),
                r=[tcs, t_e], w=[t_fb[1]])
            s.add("dve", (lambda csv, E4: lambda e: e.tensor_tensor(
                bb2v, E4[:, :].unsqueeze(2).to_broadcast([128, 4, CH]), csv, ALU.subtract))(csv, E4),
                r=[tcs, t_e], w=[t_fb[2]])
            P.tt("dve", dd4[:, :], E4[:, :], ep4[:, :], ALU.subtract, r=[t_e], w=[t_e])
            P.act(dl[:, h, i * 4:(i + 1) * 4], dd4[:, :], AF.Exp, r=[t_e], w=[t_dl[h]], scale=-1.0 / GLA_TAU)
            if i == 1:
                P.act(dtot[:, h:h + 1], E4[:, 3:4], AF.Exp, r=[t_e], w=[t_dt], scale=-1.0 / GLA_TAU)
            psk, t_psk = P.psum("acc", [0, 1, 2, 3])
            for k in range(KC):
                P.mm(psk[:, :], t_psk, wk[:, k * 512 + h * 128: k * 512 + (h + 1) * 128],
                     xn[:, k, lc:lc + 512], k == 0, k == KC - 1, r=[t_wk, t_xn[k][i]])
            P.act(fb[2][:, :], fb[2][:, :], AF.Exp, r=[t_fb[2]], w=[t_fb[2]], scale=-1.0 / GLA_TAU)
            P.tt("dve", kd_[:, h, lc:lc + 512], psk[:, :], fb[2][:, :], ALU.mult, r=[t_psk, t_fb[2]], w=[t_kk[h][i]])
            if full:
                P.act(fb[0][:, :], fb[1][:, :], AF.Exp, r=[t_fb[1]], w=[t_fb[0]], scale=1.0 / GLA_TAU)
                P.tt("dve", kt_[:, h, lc:lc + 512], psk[:, :], fb[0][:, :], ALU.mult,
                     r=[t_psk, t_fb[0]], w=[t_kk[h][i]])
                P.act(fb[1][:, :], fb[1][:, :], AF.Exp, r=[t_fb[1]], w=[t_fb[1]], scale=-1.0 / GLA_TAU)
                psq, t_psq = P.psum("acc", [0, 1, 2, 3])
                for k in range(KC):
                    P.mm(psq[:, :], t_psq, wq[:, k * 512 + h * 128: k * 512 + (h + 1) * 128],
                         xn[:, k, lc:lc + 512], k == 0, k == KC - 1, r=[t_wq, t_xn[k][i]])
                P.stt(qt_[:, h, lc:lc + 512], psq[:, :], qscale, fb[1][:, :], ALU.mult, ALU.mult,
                      r=[t_psq, t_fb[1]], w=[t_q[h][i]])
    Sf = [P.sb("gS%d" % h, [128, GLA_DVH], F32) for h in range(GH)]
    Sb = [P.sb("gSb%d" % h, [128, GLA_DVH], BF16) for h in range(GH)]
    t_S = [Tl("gS%d" % h) for h in range(GH)]
    t_Sb = [Tl("gSb%d" % h) for h in range(GH)]
    kdtok = [P.sb("gkdtok%d" % i, [128, CH], BF16) for i in range(2)]
    t_kdtok = [Tl("gkdtok%d" % i) for i in range(2)]
    if full:
        oall = P.sb("goall", [128, KC, SEG], BF16)
        t_o = [[Tl("go%d_%d" % (c8, i)) for i in range(2)] for c8 in range(KC)]
        attm = [P.sb("gattm%d" % i, [128, CH], BF16) for i in range(2)]
        t_attm = [Tl("gattm%d" % i) for i in range(2)]
        trisb = P.sb("gtri", [128, CH], F32)
        s.add("sp", lambda e: e.dma_start(out=trisb[:, :], in_=tri[:, :]), w=[t_g], dma=True)
    cnt = 0
    if pre_chunk is not None:
        sinit_t = pre_chunk()
        P.t_sinit = sinit_t
    for h in range(GH):
        if full:
            s.add("sp", (lambda h: lambda e: e.dma_start(out=Sf[h][:, :], in_=sinit[h, :, :]))(h),
                  r=[sinit_t], w=[t_S[h]], dma=True)
            s.add("act", (lambda h: lambda e: e.activation(Sb[h][:, :], Sf[h][:, :], AF.Identity))(h),
                  r=[t_S[h]], w=[t_Sb[h]])
        else:
            s.add("dve", (lambda h: lambda e: e.memset(Sf[h][:, :], 0.0))(h), w=[t_S[h]])
    for c in range(NCH):
        for h in range(GH):
            i = c // 4
            cc = slice(c * CH, (c + 1) * CH)
            if full:
                psa, t_psa = P.psum("gl", [6, 7, 4, 5])
                P.mm(psa[:, 0:CH], t_psa, kt_[:, h, cc], qt_[:, h, cc], True, True, r=[t_kk[h][i], t_q[h][i]])
                am, tam = attm[cnt % 2], t_attm[cnt % 2]
                P.tt("dve", am[:, :], psa[:, 0:CH], trisb[:, :], ALU.mult, r=[t_psa, t_g], w=[tam])
                pso, t_pso = P.psum("gl", [6, 7, 4, 5])
                for ec in range(2):
                    P.mm(pso[:, ec * CH:(ec + 1) * CH], t_pso,
                         vtok[:, c, h * GLA_DVH + ec * 128: h * GLA_DVH + (ec + 1) * 128], am[:, :],
                         True, False, r=[t_v[c], tam])
                    P.mm(pso[:, ec * CH:(ec + 1) * CH], t_pso, Sb[h][:, ec * 128:(ec + 1) * 128], qt_[:, h, cc],
                         False, True, r=[t_Sb[h], t_q[h][i]])
                s.add("act", (lambda h, cc, pso: lambda e: e.activation(
                    oall[:, 2 * h:2 * h + 2, cc], pso[:, 0:2 * CH].rearrange("p (a t) -> p a t", a=2), AF.Identity))(h, cc, pso),
                    r=[t_pso], w=[t_o[2 * h][i], t_o[2 * h + 1][i]])
            pst, t_pst = P.psum("gl", [6, 7, 4, 5])
            pst_b = pst.bitcast(BF16)
            s.add("pe", (lambda pst_b, h, cc: lambda e: e.transpose(pst_b[:, 0:CH], kd_[:, h, cc], P.ident_b[:, :]))(pst_b, h, cc),
                  r=[t_kk[h][i], P.t_const], w=[t_pst])
            kt2, tkt2 = kdtok[cnt % 2], t_kdtok[cnt % 2]
            s.add("dve", (lambda kt2, pst_b: lambda e: e.tensor_copy(out=kt2[:, :], in_=pst_b[:, 0:CH]))(kt2, pst_b),
                  r=[t_pst], w=[tkt2])
            pss, t_pss = P.psum("gl", [6, 7, 4, 5])
            P.mm(pss[:, 0:GLA_DVH], t_pss, kt2[:, :], vtok[:, c, h * GLA_DVH:(h + 1) * GLA_DVH], True, True,
                 r=[tkt2, t_v[c]])
            P.stt(Sf[h][:, :], Sf[h][:, :], dl[:, h, c:c + 1], pss[:, 0:GLA_DVH], ALU.mult, ALU.add,
                  r=[t_S[h], t_dl[h], t_pss], w=[t_S[h]])
            if full and c < NCH - 1:
                s.add("act", (lambda h: lambda e: e.activation(Sb[h][:, :], Sf[h][:, :], AF.Identity))(h),
                      r=[t_S[h]], w=[t_Sb[h]])
            cnt += 1
    if not full:
        for h in range(GH):
            t_go = Tl("gout")
            s.add("sp", (lambda h: lambda e: e.dma_start(out=gout[h, :, :], in_=Sf[h][:, :]))(h),
                  r=[t_S[h]], w=[t_go], dma=True)
            P.out_tiles.append(t_go)
    if not full:
        t_do = Tl("dout")
        s.add("sp", lambda e: e.dma_start(out=dout[:, :], in_=dtot[:, :]), r=[t_dt], w=[t_do], dma=True)
        P.out_tiles.append(t_do)
        P.release(m0)
        return
    if getattr(P, "dbg", None) is not None and sg == 0:
        dd = P.dbg
        allq = [t_q[h][i] for h in range(GH) for i in range(2)]
        allk = [t_kk[h][i] for h in range(GH) for i in range(2)]
        allo = [t_o[c8][i] for c8 in range(KC) for i in range(2)]
        for nm, buf, rd in (("dbg_qt", qt_, allq), ("dbg_kt", kt_, allk), ("dbg_kd", kd_, allk),
                            ("dbg_v", vtok, t_v), ("dbg_o", oall, allo)):
            t_d = Tl(nm)
            s.add("sp", (lambda nm, buf: lambda e: e.dma_start(out=dd[nm], in_=buf))(nm, buf), r=rd, w=[t_d], dma=True)
            P.out_tiles.append(t_d)
        t_d = Tl("dbg_dl")
        s.add("sp", lambda e: e.dma_start(out=dd["dbg_dl"], in_=dl), r=t_dl, w=[t_d], dma=True)
        P.out_tiles.append(t_d)
    ones256 = P.sb("ones256", [128, 128], BF16)
    s.add("dve", lambda e: e.memset(ones256[:, :], 1.0 / GLA_DVH), w=[t_g])
    for i in range(2):
        lc = i * 512
        for h in range(GH):
            for ec in range(2):
                P.act(P.sq[:, ec, :], oall[:, 2 * h + ec, lc:lc + 512], AF.Square, r=[t_o[2 * h + ec][i]], w=[P.t_sq[ec]])
            ps, t_ps = P.psum("stat", [4, 5])
            for ec in range(2):
                P.mm(ps[:, :], t_ps, ones256[:, :], P.sq[:, ec, :], ec == 0, ec == 1, r=[P.t_sq[ec], t_g])
            P.act(P.rt[:, :], ps[:, :], AF.Ln, r=[t_ps, P.t_const], w=[P.t_rt], bias=P.epsc[:, 0:1])
            P.act(P.rstd[:, :], P.rt[:, :], AF.Exp, r=[P.t_rt], w=[P.t_rstd], scale=-0.5)
            for ec in range(2):
                P.stt(oall[:, 2 * h + ec, lc:lc + 512], oall[:, 2 * h + ec, lc:lc + 512], P.ppc("d_o_norm", ec),
                      P.rstd[:, :], ALU.mult, ALU.mult, r=[t_o[2 * h + ec][i], P.t_rstd, P.t_pp],
                      w=[t_o[2 * h + ec][i]])
    for rs in range(2):
        wt, t_wt = P.next_slab()
        for i in range(2):
            lc = i * 512
            for n in range(4):
                ps, t_ps = P.psum("acc", [0, 1, 2, 3])
                for k in range(KC):
                    P.mm(ps[:, :], t_ps, wt[:, k * 512 + n * 128: k * 512 + (n + 1) * 128], xn[:, k, lc:lc + 512],
                         k == 0, k == KC - 1, r=[t_wt, t_xn[k][i]])
                sr, t_sr = P.tmpf()
                P.act(sr[:, :], ps[:, :], AF.Silu, r=[t_ps], w=[t_sr])
                c8 = rs * 4 + n
                P.tt("dve", oall[:, c8, lc:lc + 512], oall[:, c8, lc:lc + 512], sr[:, :], ALU.mult,
                     r=[t_o[c8][i], t_sr], w=[t_o[c8][i]])
    for s2 in range(2):
        wt, t_wt = P.next_slab()
        for i in range(2):
            lc = i * 512
            c0 = sg * SEG + lc
            for n in range(4):
                ps, t_ps = P.psum("acc", [0, 1, 2, 3])
                for k in range(KC):
                    P.mm(ps[:, :], t_ps, wt[:, k * 512 + n * 128: k * 512 + (n + 1) * 128], oall[:, k, lc:lc + 512],
                         k == 0, k == KC - 1, r=[t_wt, t_o[k][i]])
                on = s2 * 4 + n
                P.tt("dve", x[:, on, c0:c0 + 512], ps[:, :], x[:, on, c0:c0 + 512], ALU.add,
                     r=[t_ps, t_x[on][tis[i]]], w=[t_x[on][tis[i]]])
    P.release(m0)


GLA_DKH = 128
GLA_DVH = 256
GLA_TAU = 16.0


def gla_prefix(P, sel128d, sinit, g_ap, d_ap, t_gall, t_dall):
    s = P.s
    flat = P.sq.rearrange("p k t -> p (k t)").bitcast(F32)
    R = flat[:, 0:256]
    SA = flat[:, 256:512]
    SB = flat[:, 512:768]
    G = [flat[:, 768:1024], flat[:, 1024:1280]]
    sel = flat[:, 1280:1296]
    dsb = flat[:, 1296:1328].rearrange("p (s h) -> p s h", s=8)
    t_R, t_SA, t_c = Tl("gR"), Tl("gSAB"), Tl("gpre_c")
    t_G = [Tl("gG0"), Tl("gG1")]
    allt = [t_R, t_SA, t_c] + t_G
    s.add("dve", lambda e: e.memset(flat[:, 0:1328], 0.0), w=list(P.t_sq) + allt)
    s.add("sp", lambda e: e.dma_start(out=sel, in_=sel128d[:, :]), w=[t_c], dma=True)
    for sgi in range(8):
        s.add("sp", (lambda sgi: lambda e: e.dma_start(out=dsb[:, sgi, :], in_=d_ap(sgi)))(sgi),
              r=[t_dall], w=[t_c], dma=True)
    t_out = Tl("sinit")
    gi = 0
    for h in range(GH):
        if h > 0:
            s.add("dve", lambda e: e.memset(flat[:, 0:768], 0.0), w=[t_R, t_SA])
        for sgi in range(8):
            if sgi > 0:
                P.stt(SA, R, sel[:, sgi:sgi + 1], SA, ALU.mult, ALU.add, r=[t_R, t_c, t_SA], w=[t_SA])
                P.stt(SB, R, sel[:, 8 + sgi:9 + sgi], SB, ALU.mult, ALU.add, r=[t_R, t_c, t_SA], w=[t_SA])
            if sgi < 7:
                g, tg = G[gi % 2], t_G[gi % 2]
                gi += 1
                s.add("sp", (lambda g, gsrc: lambda e: e.dma_start(out=g, in_=gsrc))(g, g_ap(sgi, h)),
                      r=[t_gall], w=[tg], dma=True)
                P.stt(R, R, dsb[:, sgi, h:h + 1], g, ALU.mult, ALU.add, r=[t_R, t_c, tg], w=[t_R])
        s.add("sp", (lambda h: lambda e: e.dma_start(out=sinit[0, h, :, :], in_=SA))(h),
              r=[t_SA], w=[t_out], dma=True)
        s.add("sp", (lambda h: lambda e: e.dma_start(out=sinit[1, h, :, :], in_=SB))(h),
              r=[t_SA], w=[t_out], dma=True)
    s.add("dve", lambda e: e.memset(flat[:, 0:16], 0.0), w=list(P.t_sq) + allt)
    return t_out


def std_slabs(W, col0, n):
    return [slab_from(W, np.arange(col0 + i * 512, col0 + (i + 1) * 512)) for i in range(n)]


def attn_out_slabs(W):
    out = []
    Wr = W.reshape(16, 64, 1024)
    for g in range(4):
        sl = np.zeros((128, 4, 1024), np.float32)
        sl[0:64] = Wr[4 * g:4 * g + 4].transpose(1, 0, 2)
        out.append(sl.reshape(128, SLABW))
    return out


def build_pp(inp):
    pp = build_pp_stage1(inp)
    wf = inp["c_w_f"][0]
    pp.put("c_w_f", np.ascontiguousarray(wf.reshape(8, 128, 16).transpose(1, 0, 2).reshape(128, 128)))
    col = np.zeros((128, 1), np.float32)
    col[0:16, 0] = inp["c_b_f"][0]
    pp.put("c_b_f", col)
    col = np.zeros((128, 1), np.float32)
    col[0:64, 0] = inp["c_q_norm"][0]
    pp.put("c_q_norm", col)
    col = np.zeros((128, 1), np.float32)
    col[0:64, 0] = inp["c_k_norm"][0]
    pp.put("c_k_norm", col)
    wg1 = inp["d_w_g1"][0]
    pp.put("d_w_g1", np.ascontiguousarray(wg1.reshape(8, 128, 16).transpose(1, 0, 2).reshape(128, 128)))
    a = np.zeros((128, 512), np.float32)
    a[0:16] = inp["d_w_g2"][0]
    pp.put("d_w_g2", a)
    pp.put("d_b_g", colvec(inp["d_b_g"][0], 4))
    pp.put("d_o_norm", colvec(inp["d_o_norm"][0], 2))
    return pp


def build_fused(n_slabs, pp_off, npp):
    P = Prog(n_slabs, npp,
             out_specs=[("xo", (D, NMAIN), F32)],
             in_specs=[("xt", (D, NT), F32), ("hm", (128, 64), F32), ("ident", (128, 128), F32),
                       ("flexm", (128, 8), F32), ("sel", (16, 16), F32), ("cmask", (4, 128, 512), F32),
                       ("sel128", (128, 16), F32), ("tri", (128, 128), F32)], nslot=3)
    P.pp_off = pp_off
    nc, s = P.nc, P.s
    I = P.ins
    groups = [[0, 1, 2, 3], [4, 5, 6, 7]]
    qd = nc.dram_tensor("qd", [NH, 66, NMAIN], BF16).ap()
    kd_hp = [nc.dram_tensor("kd_hp%d" % i, [2 * 66, NMAIN], BF16).ap() for i in range(8)]
    vd_hp = [nc.dram_tensor("vd_hp%d" % i, [2 * 128, VW], BF16).ap() for i in range(8)]
    kall_hp = [nc.dram_tensor("kall_hp%d" % i, [4 * 2 * 66, NMAIN], BF16).ap() for i in range(8)]
    vall_hp = [nc.dram_tensor("vall_hp%d" % i, [4 * 2 * 128, VW], BF16).ap() for i in range(8)]
    cd2 = nc.dram_tensor("cd2", [NH, NMAIN], F32).ap()
    call2 = nc.dram_tensor("call2", [4 * NH, NMAIN], F32).ap()
    gd2 = nc.dram_tensor("gd2", [2 * GH * 128, GLA_DVH], F32).ap()
    dd2 = nc.dram_tensor("dd2", [2 * 128, GH], F32).ap()
    gall2 = nc.dram_tensor("gall2", [4 * 2 * GH * 128, GLA_DVH], F32).ap()
    dall2 = nc.dram_tensor("dall2", [4 * 2 * 128, GH], F32).ap()
    sinit = nc.dram_tensor("sinit", [2, GH, 128, GLA_DVH], F32).ap()
    call = call2.rearrange("(g h) t -> g h t", g=4)
    gd = gd2.rearrange("(s h p) e -> s h p e", s=2, h=GH)
    dd = dd2.rearrange("(s p) h -> s p h", s=2)
    gall = gall2.rearrange("(g s h p) e -> g s h p e", g=4, s=2, h=GH)
    dall = dall2.rearrange("(g s p) h -> g s p h", g=4, s=2)

    def kdst(h):
        return kd_hp[h // 2][(h % 2) * 66:(h % 2 + 1) * 66, :]

    def vown(h):
        return vd_hp[h // 2][(h % 2) * 128:(h % 2 + 1) * 128, :]

    def kallf(r_, h):
        return kall_hp[h // 2][(r_ * 2 + h % 2) * 66:(r_ * 2 + h % 2 + 1) * 66, :]

    def vallf(r_, h):
        return vall_hp[h // 2][(r_ * 2 + h % 2) * 128:(r_ * 2 + h % 2 + 1) * 128, :]

    t_kd_hp = [Tl("kd_hp%d" % i) for i in range(8)]
    t_vd_hp = [Tl("vd_hp%d" % i) for i in range(8)]
    t_kall = [Tl("kall%d" % i) for i in range(8)]
    t_vall = [Tl("vall%d" % i) for i in range(8)]
    t_call = Tl("call")

    setup_state(P, NT, TILES5)
    P.hm = P.sb("hm_sb", [128, 64], F32)
    P.t_hm = Tl("hm")
    s.add("sp", lambda e: e.dma_start(out=P.hm[:, :], in_=I["hm"][:, :]), w=[P.t_hm], dma=True)
    load_x(P, I["xt"], NT)
    mR = P.mark()
    alloc_R(P, NT, 5)
    phase_L0(P)
    phase_L1(P)
    def allgather(src2, dst2, r, w):
        s.add("pool", lambda e: e.collective_compute("AllGather", ALU.bypass, replica_groups=groups,
                                                     ins=[src2], outs=[dst2]), r=r, w=w, cc=True)

    def on_written(kind, i):
        if kind == "c":
            allgather(cd2, call2, [P.t_cd], [t_call])
        elif kind == "k":
            allgather(kd_hp[i], kall_hp[i], [t_kd_hp[i]], [t_kall[i]])
        else:
            allgather(vd_hp[i], vall_hp[i], [t_vd_hp[i]], [t_vall[i]])
    phase_L2pre(P, qd, kdst, lambda hp: vd_hp[hp], cd2, t_kd_hp, t_vd_hp, on_written=on_written)
    P.out_tiles = []
    P.release(mR)
    dep = {"qd": P.t_qd, "cd": P.t_cd, "call": t_call, "kd": t_kd_hp, "vd": t_vd_hp, "kall": t_kall, "vall": t_vall}
    phase_L2attn(P, qd, kdst, vown, cd2, kallf, vallf, call, I["flexm"], I["sel"], I["cmask"], dep)
    m = P.mark()
    alloc_R(P, NT, 5)
    mlp_block(P, 2, P.x, P.t_x, P.tiles[:4], P.R1, P.t_R1, P.R2, P.t_R2)
    P.release(m)
    for sg in range(2):
        gla_seg(P, sg, "scan", gout=gd[sg], dout=dd[sg])
    P.out_tiles = []
    s.barrier()
    t_gall, t_dall = Tl("gall"), Tl("dall")
    allgather(gd2, gall2, [], [t_gall])
    allgather(dd2, dall2, [], [t_dall])

    def pre_chunk():
        return gla_prefix(P, I["sel128"], sinit,
                          lambda sgi, h: gall[seg_loc(sgi)[0], seg_loc(sgi)[1], h, :, :],
                          lambda sgi: dall[seg_loc(sgi)[0], seg_loc(sgi)[1], :, :], t_gall, t_dall)
    gla_seg(P, 0, "full", sinit=sinit[0], tri=I["tri"], pre_chunk=pre_chunk)
    gla_seg(P, 1, "full", sinit=sinit[1], tri=I["tri"], sinit_t=P.t_sinit)
    m = P.mark()
    alloc_R(P, NT, 5)
    mlp_block(P, 3, P.x, P.t_x, P.tiles[:4], P.R1, P.t_R1, P.R2, P.t_R2)
    P.release(m)
    store_x(P, P.outs["xo"])
    return P.finish()


def run_fused(inp):
    x = np.asarray(inp["x"], np.float32)
    pp = build_pp(inp)
    ppa = pp.array()
    ident = np.eye(128, dtype=np.float32)
    wq = inp["c_w_qkv"][0]
    wd = inp["d_w_in"][0]
    gla_scan = std_slabs(wd, 1024, 2) + std_slabs(wd, 512, 1)
    gla_full = (std_slabs(wd, 1024, 2) + std_slabs(wd, 0, 1) + std_slabs(wd, 512, 1) + std_slabs(wd, 2048, 2)
                + std_slabs(inp["d_w_out"][0], 0, 2))
    slabs = (build_wall_stage1(inp) + std_slabs(wq, 2048, 2) + std_slabs(wq, 0, 2) + std_slabs(wq, 1024, 2)
             + attn_out_slabs(inp["c_w_out"][0]) + mlp_slabs(inp, 2) + gla_scan + gla_scan
             + gla_full + gla_full + mlp_slabs(inp, 3))
    wall = np.stack(slabs, axis=0)
    cm = causal_masks()
    tri = np.triu(np.ones((128, 128), np.float32))
    in_maps = []
    for r in range(8):
        b, j = r // 4, r % 4
        idx = core_token_index(j)
        xt = np.zeros((NT, D), np.float32)
        valid = idx >= 0
        xt[valid] = x[b, idx[valid]]
        hm, flexm, sel = percore_consts(j)
        sel128 = np.zeros((128, 16), np.float32)
        sel128[:, j] = 1.0
        sel128[:, 8 + 7 - j] = 1.0
        in_maps.append({"wall": wall, "pp": ppa, "xt": np.ascontiguousarray(xt.T), "hm": hm, "ident": ident,
                        "flexm": flexm, "sel": sel, "cmask": cm, "sel128": sel128, "tri": tri})
    nc = build_fused(wall.shape[0], pp.off, pp.n)
    res = run_bass_kernel_spmd(nc, in_maps, core_ids=list(range(8))).results
    return gather_x([np.asarray(res[r]["xo"]) for r in range(8)])


def percore_consts(j):
    hm = np.ones((128, 64), np.float32)
    if j == 0:
        hm[:, 0:32] = 0.0
    flexm = np.zeros((128, 8), np.float32)
    for p in range(3):
        flexm[:, p] = 1.0 if j > p else 0.0
        flexm[:, 3 + p] = 0.0 if j > p else 1.0
    sel = np.zeros((16, 16), np.float32)
    sel[:, j] = 1.0
    sel[:, 8 + 7 - j] = 1.0
    return hm, flexm, sel


def causal_masks():
    kl = np.arange(128)[:, None]
    ql = np.arange(512)[None, :]
    return np.stack([np.where(kl - ql <= -128 * i, 0.0, NEG).astype(np.float32) for i in range(4)], axis=0)


_CACHE = {}


def gather_x(xos):
    out = np.zeros((2, 8192, D), np.float32)
    for r in range(8):
        b, j = r // 4, r % 4
        a, bb = seg_tokens(j)
        xo = xos[r].T
        out[b, a] = xo[:SEG]
        out[b, bb] = xo[SEG:]
    return out


def kernel(**inputs):
    return run_fused(inputs)
```

```python
import numpy as np
import concourse.bass as bass
import concourse.mybir as mybir
from concourse.bass_utils import run_bass_kernel_spmd
from contextlib import ExitStack

F32 = mybir.dt.float32
BF16 = mybir.dt.bfloat16
AF = mybir.ActivationFunctionType
ALU = mybir.AluOpType

D = 1024
KC = 8
SEG = 1024
HALO = 32
NMAIN = 2 * SEG
NT = NMAIN + 2 * HALO
EPS = 1e-6
SLABW = 4096
NSLOT = 4
ARENA_W = 53200


class Tl:
    __slots__ = ("name", "lastw", "readers")

    def __init__(self, name):
        self.name = name
        self.lastw = None
        self.readers = []


class Op:
    __slots__ = ("eng", "fn", "deps", "dma", "sig", "waits", "idx", "need", "cc")

    def __init__(self, eng, fn, deps, dma, idx):
        self.cc = False
        self.eng = eng
        self.fn = fn
        self.deps = deps
        self.dma = dma
        self.sig = None
        self.waits = []
        self.idx = idx
        self.need = False


class Sch:
    ENGS = ("pe", "act", "dve", "pool", "sp")
    DMAK = 8

    def __init__(self, nc):
        self.nc = nc
        self.ops = []
        self.last = {e: None for e in self.ENGS}
        self.dmas_open = []

    def add(self, eng, fn, r=(), w=(), dma=False, cc=False):
        idx = len(self.ops)
        deps = set()
        for t in r:
            if t.lastw is not None:
                deps.add(t.lastw)
        for t in w:
            if t.lastw is not None:
                deps.add(t.lastw)
            deps.update(t.readers)
        for t in r:
            t.readers.append(idx)
        for t in w:
            t.lastw = idx
            t.readers = []
        deps.discard(idx)
        op = Op(eng, fn, deps, dma, idx)
        op.cc = cc
        self.ops.append(op)
        self.last[eng] = idx
        if dma or cc:
            self.dmas_open.append(idx)
        return idx

    def barrier(self):
        lasts = [v for v in self.last.values() if v is not None] + list(self.dmas_open)
        self.dmas_open = []
        for e in self.ENGS:
            idx = len(self.ops)
            op = Op(e, None, set(lasts), False, idx)
            self.ops.append(op)
            self.last[e] = idx

    def finalize(self, stack):
        nc = self.nc
        ops = self.ops
        for op in ops:
            keep = set()
            for d in op.deps:
                od = ops[d]
                if od.fn is None:
                    if od.eng == op.eng:
                        continue
                    keep.add(d)
                    continue
                if od.eng == "pe" and op.eng == "pe" and not od.dma:
                    continue
                keep.add(d)
            op.deps = keep
            for d in keep:
                ops[d].need = True
        csem = {e: stack.enter_context(nc.semaphore("c_" + e)) for e in ("pe", "act", "dve", "pool", "sp")}
        dsem = {e: [stack.enter_context(nc.semaphore("d_%s%d" % (e, i))) for i in range(self.DMAK)]
                for e in ("sp", "pool")}
        ccount = {e: 0 for e in csem}
        dcount = {e: 0 for e in dsem}
        ccsem = stack.enter_context(nc.semaphore("cc_sem"))
        ncc = 0
        for op in ops:
            if op.cc:
                ncc += 1
                op.sig = (ccsem, ncc, None)
            elif op.dma:
                j = dcount[op.eng]
                dcount[op.eng] += 1
                sem = dsem[op.eng][j % self.DMAK]
                op.sig = (sem, 16 * (j // self.DMAK + 1), 16)
                if j >= self.DMAK:
                    op.waits.append((sem, 16 * (j // self.DMAK)))
            elif op.need:
                ccount[op.eng] += 1
                op.sig = (csem[op.eng], ccount[op.eng], 1)
        known = {e: {} for e in self.ENGS}
        for op in ops:
            kn = known[op.eng]
            ws = {}
            for (sem, val) in op.waits:
                ws[sem.num] = (sem, max(val, ws.get(sem.num, (None, 0))[1]))
            for d in op.deps:
                sem, val, _ = ops[d].sig
                if ws.get(sem.num, (None, 0))[1] < val:
                    ws[sem.num] = (sem, val)
            out = []
            for num, (sem, val) in ws.items():
                if kn.get(num, 0) >= val:
                    continue
                kn[num] = val
                out.append((sem, val))
            op.waits = out
        self.per_eng = {e: [op for op in ops if op.eng == e] for e in self.ENGS}

    def emit(self, eng_name, e):
        n = 0
        for op in self.per_eng[eng_name]:
            for (sem, val) in op.waits:
                e.wait_ge(sem, val)
            if op.fn is None:
                if op.sig is not None:
                    e.nop().then_inc(op.sig[0], op.sig[2])
                continue
            ins = op.fn(e)
            n += 1
            if op.sig is not None:
                if op.sig[2] is None:
                    ins.then_inc(op.sig[0])
                else:
                    ins.then_inc(op.sig[0], op.sig[2])
        return n


def slab_from(W, cols):
    sub = W[:, cols]
    return np.ascontiguousarray(sub.reshape(8, 128, 512).transpose(1, 0, 2).reshape(128, SLABW))


def colvec(v, nch):
    return np.ascontiguousarray(np.asarray(v, np.float32).reshape(nch, 128).T)


class PP:
    def __init__(self):
        self.cols = []
        self.off = {}
        self.n = 0

    def put(self, name, arr):
        arr = np.asarray(arr, np.float32)
        assert arr.shape[0] == 128
        self.off[name] = (self.n, arr.shape[1])
        self.cols.append(arr)
        self.n += arr.shape[1]

    def array(self):
        return np.ascontiguousarray(np.concatenate(self.cols, axis=1))


def seg_tokens(j):
    a = np.arange(j * SEG, (j + 1) * SEG)
    b = np.arange((7 - j) * SEG, (8 - j) * SEG)
    return a, b


def core_token_index(j):
    a, b = seg_tokens(j)
    ha = np.arange(j * SEG - HALO, j * SEG)
    hb = np.arange((7 - j) * SEG - HALO, (7 - j) * SEG)
    return np.concatenate([a, b, ha, hb])


class Prog:
    def __init__(self, n_slabs, npp, out_specs, in_specs, nslot=NSLOT):
        self.nslot = nslot
        self.nc = nc = bass.Bass("TRN2", target_bir_lowering=False)
        self.st = ExitStack()
        self.s = Sch(nc)
        self.wall = nc.dram_tensor("wall", [n_slabs, 128, SLABW], F32, kind="ExternalInput").ap()
        self.ppd = nc.dram_tensor("pp", [128, npp], F32, kind="ExternalInput").ap()
        self.ins = {}
        for name, shape, dt in in_specs:
            self.ins[name] = nc.dram_tensor(name, list(shape), dt, kind="ExternalInput").ap()
        self.outs = {}
        for name, shape, dt in out_specs:
            self.outs[name] = nc.dram_tensor(name, list(shape), dt, kind="ExternalOutput").ap()
        self.npp = npp
        self.n_slabs = n_slabs
        self.slab_i = 0
        self.slab_issued = 0
        self.out_tiles = []
        self.arena = None

    def sb(self, name, shape, dt):
        if self.arena is None:
            self.arena = self.st.enter_context(self.nc.sbuf_tensor("arena", [128, ARENA_W], F32))
            self.top = 0
        n = 1
        for d in shape[1:]:
            n *= d
        words = n if dt == F32 else (n + 1) // 2
        off = self.top
        self.top += words
        assert self.top <= ARENA_W, ("SBUF arena overflow", name, self.top)
        v = self.arena[:, off:off + words]
        if dt != F32:
            v = v.bitcast(dt)[:, 0:n]
        v = v[0:shape[0]]
        if len(shape) == 3:
            v = v.rearrange("p (a b) -> p a b", a=shape[1])
        elif len(shape) == 4:
            v = v.rearrange("p (a b c) -> p a b c", a=shape[1], b=shape[2])
        return v

    def mark(self):
        return self.top

    def release(self, mark):
        self.top = mark
        self.s.barrier()

    def setup_common(self):
        nc, s = self.nc, self.s
        self.pp = self.sb("pp_sb", [128, self.npp], F32)
        self.t_pp = Tl("pp")
        s.add("sp", lambda e: e.dma_start(out=self.pp[:, :], in_=self.ppd[:, :]), w=[self.t_pp], dma=True)
        self.wring = [self.sb("wslot%d" % i, [128, SLABW], BF16) for i in range(self.nslot)]
        self.t_w = [Tl("w%d" % i) for i in range(self.nslot)]
        self.ps = [self.st.enter_context(nc.psum_tensor("ps%d" % i, [128, 512], F32)) for i in range(8)]
        self.t_ps = [Tl("ps%d" % i) for i in range(8)]
        self.ps_rr = {}
        self.onesD = self.sb("onesD", [128, 128], BF16)
        self.t_const = Tl("const")
        s.add("dve", lambda e: e.memset(self.onesD[:, :], 1.0 / D), w=[self.t_const])
        self.ident_f = self.sb("ident_f", [128, 128], F32)
        self.ident_b = self.sb("ident_b", [128, 128], BF16)
        s.add("sp", lambda e: e.dma_start(out=self.ident_f[:, :], in_=self.ins["ident"][:, :]),
              w=[self.t_const], dma=True)
        s.add("dve", lambda e: e.tensor_copy(out=self.ident_b[:, :], in_=self.ident_f[:, :]),
              r=[self.t_const], w=[self.t_const])

    def psum(self, group, banks):
        i = self.ps_rr.get(group, 0)
        self.ps_rr[group] = i + 1
        b = banks[i % len(banks)]
        return self.ps[b], self.t_ps[b]

    def ppc(self, name, k=0, n=1):
        off, w = self.pp_off[name]
        return self.pp[:, off + k: off + k + n]

    def _issue_slab(self):
        i = self.slab_issued
        if i >= self.n_slabs:
            return
        self.slab_issued += 1
        slot = i % self.nslot
        dst = self.wring[slot]
        src = self.wall[i]
        self.s.add("pool", lambda e: e.dma_start(out=dst[:, :], in_=src), w=[self.t_w[slot]], dma=True)

    def next_slab(self, hold=1):
        i = self.slab_i
        self.slab_i += 1
        while self.slab_issued < min(self.n_slabs, i + self.nslot - (hold - 1)):
            self._issue_slab()
        slot = i % self.nslot
        return self.wring[slot], self.t_w[slot]

    def mm(self, ps_ap, t_ps, lhsT, rhs, start, stop, r):
        self.s.add("pe", lambda e: e.matmul(ps_ap, lhsT, rhs, start=start, stop=stop),
                   r=list(r) + ([] if start else [t_ps]), w=[t_ps])

    def act(self, out, in_, func, r, w, bias=None, scale=None):
        kw = {}
        if bias is not None:
            kw["bias"] = bias
        if scale is not None:
            kw["scale"] = scale
        self.s.add("act", lambda e: e.activation(out, in_, func, **kw), r=r, w=w)

    def tt(self, eng, out, in0, in1, op, r, w):
        self.s.add(eng, lambda e: e.tensor_tensor(out, in0, in1, op), r=r, w=w)

    def ts(self, eng, out, in0, s1, s2, op0, op1, r, w):
        if op1 is None:
            self.s.add(eng, lambda e: e.tensor_scalar(out, in0, s1, None, op0), r=r, w=w)
        else:
            self.s.add(eng, lambda e: e.tensor_scalar(out, in0, s1, s2, op0, op1), r=r, w=w)

    def stt(self, out, in0, scalar, in1, op0, op1, r, w):
        self.s.add("dve", lambda e: e.scalar_tensor_tensor(out, in0, scalar, in1, op0, op1), r=r, w=w)

    def rmsnorm(self, xk, t_xk, gname, outk, t_outk, w):
        sq, t_sq = self.sq, self.t_sq
        for k in range(KC):
            self.act(sq[:, k, :w], xk[k], AF.Square, r=[t_xk[k]], w=[t_sq[k]])
        ps, t_ps = self.psum("stat", [4, 5])
        for k in range(KC):
            self.mm(ps[:, :w], t_ps, self.onesD[:, :], sq[:, k, :w], k == 0, k == KC - 1,
                    r=[t_sq[k], self.t_const])
        self.act(self.rt[:, :w], ps[:, :w], AF.Ln, r=[t_ps, self.t_const], w=[self.t_rt],
                 bias=self.epsc[:, 0:1])
        self.act(self.rstd[:, :w], self.rt[:, :w], AF.Exp, r=[self.t_rt], w=[self.t_rstd], scale=-0.5)
        for k in range(KC):
            self.stt(outk[k], xk[k], self.ppc(gname, k), self.rstd[:, :w], ALU.mult, ALU.mult,
                     r=[t_xk[k], self.t_rstd, self.t_pp], w=[t_outk[k]])

    def finish(self):
        nc, s = self.nc, self.s
        s.add("sp", None, r=self.out_tiles)
        s.finalize(self.st)
        with nc.Block() as block:
            @block.tensor
            def _(e):
                s.emit("pe", e)

            @block.scalar
            def _(e):
                s.emit("act", e)

            @block.vector
            def _(e):
                s.emit("dve", e)

            @block.gpsimd
            def _(e):
                s.emit("pool", e)

            @block.sync
            def _(e):
                s.emit("sp", e)
        self.st.close()
        return nc


TILES5 = [(0, 512), (512, 512), (1024, 512), (1536, 512), (2048, 64)]


def mlp_cols(q, s):
    return np.arange(q * 1024 + s * 512, q * 1024 + (s + 1) * 512)


def build_wall_stage1(inp):
    slabs = []
    w = inp["a_w_in"][0]
    for s4 in range(4):
        cols = np.concatenate([np.arange(2 * s4 * 128, (2 * s4 + 2) * 128),
                               1024 + np.arange(2 * s4 * 128, (2 * s4 + 2) * 128)])
        slabs.append(slab_from(w, cols))
    w = inp["a_w_out"][0]
    for s2 in range(2):
        slabs.append(slab_from(w, np.arange(s2 * 512, (s2 + 1) * 512)))
    slabs += mlp_slabs(inp, 0)
    w = inp["b_w_in"][0]
    for s4 in range(4):
        cols = np.concatenate([1024 + np.arange(2 * s4 * 128, (2 * s4 + 2) * 128),
                               2048 + np.arange(2 * s4 * 128, (2 * s4 + 2) * 128)])
        slabs.append(slab_from(w, cols))
    for s2 in range(2):
        slabs.append(slab_from(w, np.arange(s2 * 512, (s2 + 1) * 512)))
    w = inp["b_w_out"][0]
    for s2 in range(2):
        slabs.append(slab_from(w, np.arange(s2 * 512, (s2 + 1) * 512)))
    slabs += mlp_slabs(inp, 1)
    return slabs


def mlp_slabs(inp, l):
    slabs = []
    w1 = inp["mlp_w1"][l]
    w2 = inp["mlp_w2"][l]
    for q in range(4):
        for s in range(2):
            slabs.append(slab_from(w1, mlp_cols(q, s)))
        for s in range(2):
            slabs.append(slab_from(w2[q * 1024:(q + 1) * 1024], np.arange(s * 512, (s + 1) * 512)))
    return slabs


def build_pp_stage1(inp):
    pp = PP()
    for l in range(4):
        pp.put("mixn%d" % l, colvec(inp["mix_norm"][l], 8))
        pp.put("mlpn%d" % l, colvec(inp["mlp_norm"][l], 8))
    pp.put("a_b_in", colvec(inp["a_b_in"][0], 16))
    cw = inp["a_conv_w"][0]
    pp.put("a_conv_w", np.ascontiguousarray(cw.T.reshape(8, 128, 31).transpose(1, 0, 2).reshape(128, 248)))
    pp.put("a_conv_b", colvec(inp["a_conv_b"][0], 8))
    pp.put("a_ln_g", colvec(inp["a_ln_g"][0], 8))
    pp.put("a_ln_b", colvec(inp["a_ln_b"][0], 8))
    pp.put("a_b_out", colvec(inp["a_b_out"][0], 8))
    bw = inp["b_conv_w"][0]
    pp.put("b_conv_w", np.ascontiguousarray(bw.T.reshape(8, 128, 3).transpose(1, 0, 2).reshape(128, 24)))
    return pp


def mlp_block(P, l, x, t_x, tiles, xn, t_xn, hq, t_hq):
    for ti, (c0, w) in enumerate(tiles):
        P.rmsnorm([x[:, k, c0:c0 + w] for k in range(KC)], [t_x[k][ti] for k in range(KC)], "mlpn%d" % l,
                  [xn[:, k, c0:c0 + w] for k in range(KC)], [t_xn[k][ti] for k in range(KC)], w)
    for q in range(4):
        for s in range(2):
            wt, t_wt = P.next_slab()
            for ti, (c0, w) in enumerate(tiles):
                for n in range(4):
                    ps, t_ps = P.psum("acc", [0, 1, 2, 3])
                    for k in range(KC):
                        P.mm(ps[:, :w], t_ps, wt[:, k * 512 + n * 128: k * 512 + (n + 1) * 128],
                             xn[:, k, c0:c0 + w], k == 0, k == KC - 1, r=[t_wt, t_xn[k][ti]])
                    tmp, t_tmp = P.tmpf()
                    P.act(tmp[:, :w], ps[:, :w], AF.Relu, r=[t_ps], w=[t_tmp])
                    hn = s * 4 + n
                    P.tt("dve", hq[:, hn, c0:c0 + w], tmp[:, :w], tmp[:, :w], ALU.mult,
                         r=[t_tmp], w=[t_hq[hn][ti]])
        for s in range(2):
            wt, t_wt = P.next_slab()
            for ti, (c0, w) in enumerate(tiles):
                for n in range(4):
                    ps, t_ps = P.psum("acc", [0, 1, 2, 3])
                    for k in range(KC):
                        P.mm(ps[:, :w], t_ps, wt[:, k * 512 + n * 128: k * 512 + (n + 1) * 128],
                             hq[:, k, c0:c0 + w], k == 0, k == KC - 1, r=[t_wt, t_hq[k][ti]])
                    on = s * 4 + n
                    P.tt("dve", x[:, on, c0:c0 + w], ps[:, :w], x[:, on, c0:c0 + w], ALU.add,
                         r=[t_ps, t_x[on][ti]], w=[t_x[on][ti]])


def setup_state(P, ncols, tiles):
    s = P.s
    P.setup_common()
    ntl = len(tiles)
    P.tiles = tiles
    P.x = P.sb("xres", [128, KC, ncols], F32)
    P.t_x = [[Tl("x%d_%d" % (k, ti)) for ti in range(ntl)] for k in range(KC)]
    P.sq = P.sb("sq", [128, KC, 512], BF16)
    P.t_sq = [Tl("sq%d" % k) for k in range(KC)]
    P.rt = P.sb("rt", [128, 512], F32)
    P.t_rt = Tl("rt")
    P.rstd = P.sb("rstd", [128, 512], F32)
    P.t_rstd = Tl("rstd")
    P.epsc = P.sb("epsc", [128, 1], F32)
    s.add("dve", lambda e: e.memset(P.epsc[:, :], EPS), w=[P.t_const])
    tmps = [P.sb("tmpf%d" % i, [128, 512], F32) for i in range(3)]
    t_tmps = [Tl("tmpf%d" % i) for i in range(3)]
    rr = [0]

    def tmpf():
        i = rr[0] % 3
        rr[0] += 1
        return tmps[i], t_tmps[i]
    P.tmpf = tmpf


def alloc_R(P, ncols, ntl):
    P.R1 = P.sb("R1", [128, KC, ncols], BF16)
    P.t_R1 = [[Tl("r1_%d_%d" % (k, ti)) for ti in range(ntl)] for k in range(KC)]
    P.R2 = P.sb("R2", [128, KC, ncols], BF16)
    P.t_R2 = [[Tl("r2_%d_%d" % (k, ti)) for ti in range(ntl)] for k in range(KC)]


def load_x(P, xt_ap, ncols):
    s = P.s
    xt_v = xt_ap.rearrange("(k p) t -> p k t", p=128)
    for ti, (c0, w) in enumerate(P.tiles):
        s.add("sp", (lambda c0, w: lambda e: e.dma_start(out=P.x[:, :, c0:c0 + w], in_=xt_v[:, :, c0:c0 + w]))(c0, w),
              w=[P.t_x[k][ti] for k in range(KC)], dma=True)


def store_x(P, xo_ap, ncols=NMAIN):
    s = P.s
    xo_v = xo_ap.rearrange("(k p) t -> p k t", p=128)
    for k in range(KC):
        t_o = Tl("out%d" % k)
        s.add("sp", (lambda k: lambda e: e.dma_start(out=xo_v[:, k, :], in_=P.x[:, k, 0:ncols]))(k),
              r=[P.t_x[k][ti] for ti in range(4)], w=[t_o], dma=True)
        P.out_tiles.append(t_o)


def phase_L0(P):
    s = P.s
    x, t_x, tiles = P.x, P.t_x, P.tiles
    R1, t_R1, R2, t_R2 = P.R1, P.t_R1, P.R2, P.t_R2
    hm, t_hm = P.hm, P.t_hm
    m0 = P.mark()
    xn, t_xn = R1, t_R1
    for ti, (c0, w) in enumerate(tiles):
        P.rmsnorm([x[:, k, c0:c0 + w] for k in range(KC)], [t_x[k][ti] for k in range(KC)], "mixn0",
                  [xn[:, k, c0:c0 + w] for k in range(KC)], [t_xn[k][ti] for k in range(KC)], w)
    hc = R2.rearrange("p k (s t) -> p k s t", s=2)
    t_hc = t_R2

    def hc_dst(c, ti):
        if ti < 4:
            seg, half = ti // 2, ti % 2
            return hc[:, c, seg, HALO + half * 512: HALO + half * 512 + 512]
        return hc[:, c, :, 0:HALO]

    for s4 in range(4):
        wt, t_wt = P.next_slab()
        for ti, (c0, w) in enumerate(tiles):
            for cl in range(2):
                c = 2 * s4 + cl
                psa, t_psa = P.psum("acc", [0, 1, 2, 3])
                psg, t_psg = P.psum("acc", [0, 1, 2, 3])
                for k in range(KC):
                    P.mm(psa[:, :w], t_psa, wt[:, k * 512 + cl * 128: k * 512 + (cl + 1) * 128],
                         xn[:, k, c0:c0 + w], k == 0, k == KC - 1, r=[t_wt, t_xn[k][ti]])
                for k in range(KC):
                    P.mm(psg[:, :w], t_psg, wt[:, k * 512 + (2 + cl) * 128: k * 512 + (3 + cl) * 128],
                         xn[:, k, c0:c0 + w], k == 0, k == KC - 1, r=[t_wt, t_xn[k][ti]])
                sg, t_sg = P.tmpf()
                P.act(sg[:, :w], psg[:, :w], AF.Sigmoid, r=[t_psg, P.t_pp], w=[t_sg],
                      bias=P.ppc("a_b_in", 8 + c))
                if ti == 4:
                    P.tt("dve", sg[:, :w], sg[:, :w], hm[:, :], ALU.mult, r=[t_sg, t_hm], w=[t_sg])
                    src_a = psa[:, :w].rearrange("p (s t) -> p s t", s=2)
                    src_g = sg[:, :w].rearrange("p (s t) -> p s t", s=2)
                else:
                    src_a = psa[:, :w]
                    src_g = sg[:, :w]
                P.stt(hc_dst(c, ti), src_a, P.ppc("a_b_in", c), src_g, ALU.add, ALU.mult,
                      r=[t_psa, t_sg, P.t_pp], w=[t_hc[c][ti]])
    dgs = [P.sb("dg%d" % i, [128, 31 * 128], BF16) for i in range(2)]
    t_dgs = [Tl("dg%d" % i) for i in range(2)]
    yall, t_yall = R1, t_R1
    for c in range(KC):
        dg, t_dg = dgs[c % 2], t_dgs[c % 2]
        for k in range(31):
            P.ts("dve", dg[:, k * 128:(k + 1) * 128], P.ident_f[:, :], P.ppc("a_conv_w", c * 31 + k), None,
                 ALU.mult, None, r=[P.t_const, P.t_pp], w=[t_dg])
        for ti, (c0, w) in enumerate(tiles):
            ps, t_ps = P.psum("conv", [6, 7])
            if ti < 4:
                seg, half = ti // 2, ti % 2
                rd = [t_dg, t_hc[c][ti], t_hc[c][ti - 1 if half else 4]]
                for k in range(31):
                    b0 = 2 + half * 512 + k
                    P.mm(ps[:, :512], t_ps, dg[:, k * 128:(k + 1) * 128], hc[:, c, seg, b0:b0 + 512],
                         k == 0, k == 30, r=rd)
                P.act(yall[:, c, c0:c0 + w], ps[:, :w], AF.Identity, r=[t_ps, P.t_pp], w=[t_yall[c][ti]],
                      bias=P.ppc("a_conv_b", c))
            else:
                s.add("dve", (lambda c: lambda e: e.memset(yall[:, c, NMAIN:NT], 0.0))(c), w=[t_yall[c][4]])
                pv = ps[:, 0:4].rearrange("p (s t) -> p s t", s=2)
                for k in range(31):
                    P.mm(pv, t_ps, dg[:, k * 128:(k + 1) * 128], hc[:, c, :, k:k + 2],
                         k == 0, k == 30, r=[t_dg, t_hc[c][4]])
                yv = yall[:, c, NMAIN:NT].rearrange("p (s t) -> p s t", s=2)[:, :, 30:32]
                P.act(yv, pv, AF.Identity, r=[t_ps, P.t_pp], w=[t_yall[c][4]], bias=P.ppc("a_conv_b", c))
    sall, t_sall = R2, t_R2
    mu_sb = P.sb("mu_sb", [128, 512], F32)
    t_mu = Tl("mu")
    m2 = P.sb("m2", [128, 512], F32)
    t_m2 = Tl("m2")
    def ln_tile(ti, c0, w):
        for k in range(KC):
            P.act(P.sq[:, k, :w], yall[:, k, c0:c0 + w], AF.Square, r=[t_yall[k][ti]], w=[P.t_sq[k]])
        psm, t_psm = P.psum("stat", [4, 5])
        pss, t_pss = P.psum("stat", [4, 5])
        for k in range(KC):
            P.mm(psm[:, :w], t_psm, P.onesD[:, :], yall[:, k, c0:c0 + w], k == 0, k == KC - 1,
                 r=[t_yall[k][ti], P.t_const])
        for k in range(KC):
            P.mm(pss[:, :w], t_pss, P.onesD[:, :], P.sq[:, k, :w], k == 0, k == KC - 1,
                 r=[P.t_sq[k], P.t_const])
        P.act(mu_sb[:, :w], psm[:, :w], AF.Identity, r=[t_psm], w=[t_mu])
        P.tt("dve", m2[:, :w], mu_sb[:, :w], mu_sb[:, :w], ALU.mult, r=[t_mu], w=[t_m2])
        P.tt("dve", m2[:, :w], pss[:, :w], m2[:, :w], ALU.subtract, r=[t_pss, t_m2], w=[t_m2])
        P.act(P.rt[:, :w], m2[:, :w], AF.Ln, r=[t_m2, P.t_const], w=[P.t_rt], bias=P.epsc[:, 0:1])
        P.act(P.rstd[:, :w], P.rt[:, :w], AF.Exp, r=[P.t_rt], w=[P.t_rstd], scale=-0.5)
        for k in range(KC):
            z, t_z = P.tmpf()
            P.tt("pool", z[:, :w], yall[:, k, c0:c0 + w], mu_sb[:, :w], ALU.subtract,
                 r=[t_yall[k][ti], t_mu], w=[t_z])
            P.tt("dve", z[:, :w], z[:, :w], P.rstd[:, :w], ALU.mult, r=[t_z, P.t_rstd], w=[t_z])
            P.act(sall[:, k, c0:c0 + w], z[:, :w], AF.Silu, r=[t_z, P.t_pp], w=[t_sall[k][ti]],
                  bias=P.ppc("a_ln_b", k), scale=P.ppc("a_ln_g", k))


    wo = [P.next_slab(), P.next_slab(hold=2)]

    def out_tile(ti, c0, w):
        for s2 in range(2):
            wt, t_wt = wo[s2]
            for n in range(4):
                ps, t_ps = P.psum("acc", [0, 1, 2, 3])
                for k in range(KC):
                    P.mm(ps[:, :w], t_ps, wt[:, k * 512 + n * 128: k * 512 + (n + 1) * 128],
                         sall[:, k, c0:c0 + w], k == 0, k == KC - 1, r=[t_wt, t_sall[k][ti]])
                on = s2 * 4 + n
                P.stt(x[:, on, c0:c0 + w], ps[:, :w], P.ppc("a_b_out", on), x[:, on, c0:c0 + w],
                      ALU.add, ALU.add, r=[t_ps, P.t_pp, t_x[on][ti]], w=[t_x[on][ti]])
    for ti, (c0, w) in enumerate(tiles):
        ln_tile(ti, c0, w)
        if ti > 0:
            out_tile(ti - 1, *tiles[ti - 1])
    out_tile(len(tiles) - 1, *tiles[-1])
    P.release(m0)
    mlp_block(P, 0, x, t_x, tiles, R1, t_R1, R2, t_R2)


def phase_L1(P):
    s = P.s
    x, t_x, tiles = P.x, P.t_x, P.tiles
    R1, t_R1, R2, t_R2 = P.R1, P.t_R1, P.R2, P.t_R2
    hm, t_hm = P.hm, P.t_hm
    xn, t_xn = R1, t_R1
    for ti, (c0, w) in enumerate(tiles):
        P.rmsnorm([x[:, k, c0:c0 + w] for k in range(KC)], [t_x[k][ti] for k in range(KC)], "mixn1",
                  [xn[:, k, c0:c0 + w] for k in range(KC)], [t_xn[k][ti] for k in range(KC)], w)
    ub = R2.rearrange("p k (s t) -> p k s t", s=2)
    t_ub = t_R2

    def ub_dst(c, ti):
        if ti < 4:
            seg, half = ti // 2, ti % 2
            return ub[:, c, seg, HALO + half * 512: HALO + half * 512 + 512]
        return ub[:, c, :, 0:HALO]
    for s4 in range(4):
        wt, t_wt = P.next_slab()
        for ti, (c0, w) in enumerate(tiles):
            for cl in range(2):
                c = 2 * s4 + cl
                psc, t_psc = P.psum("acc", [0, 1, 2, 3])
                psh, t_psh = P.psum("acc", [0, 1, 2, 3])
                for k in range(KC):
                    P.mm(psc[:, :w], t_psc, wt[:, k * 512 + cl * 128: k * 512 + (cl + 1) * 128],
                         xn[:, k, c0:c0 + w], k == 0, k == KC - 1, r=[t_wt, t_xn[k][ti]])
                for k in range(KC):
                    P.mm(psh[:, :w], t_psh, wt[:, k * 512 + (2 + cl) * 128: k * 512 + (3 + cl) * 128],
                         xn[:, k, c0:c0 + w], k == 0, k == KC - 1, r=[t_wt, t_xn[k][ti]])
                gc, t_gc = P.tmpf()
                P.act(gc[:, :w], psc[:, :w], AF.Identity, r=[t_psc], w=[t_gc])
                if ti == 4:
                    P.tt("dve", gc[:, :w], gc[:, :w], hm[:, :], ALU.mult, r=[t_gc, t_hm], w=[t_gc])
                    src_h = psh[:, :w].rearrange("p (s t) -> p s t", s=2)
                    src_c = gc[:, :w].rearrange("p (s t) -> p s t", s=2)
                else:
                    src_h = psh[:, :w]
                    src_c = gc[:, :w]
                P.tt("dve", ub_dst(c, ti), src_h, src_c, ALU.mult, r=[t_psh, t_gc], w=[t_ub[c][ti]])
    def conv3(c, ti):
        seg, half = ti // 2, ti % 2
        acc, t_acc = P.tmpf()
        base = HALO + half * 512
        rd = [t_ub[c][ti], t_ub[c][ti - 1 if half else 4], P.t_pp]
        P.ts("dve", acc[:, :], ub[:, c, seg, base - 2: base - 2 + 512], P.ppc("b_conv_w", c * 3 + 0), None,
             ALU.mult, None, r=rd, w=[t_acc])
        P.stt(acc[:, :], ub[:, c, seg, base - 1: base - 1 + 512], P.ppc("b_conv_w", c * 3 + 1), acc[:, :],
              ALU.mult, ALU.add, r=rd + [t_acc], w=[t_acc])
        P.stt(ub[:, c, seg, base: base + 512], ub[:, c, seg, base: base + 512],
              P.ppc("b_conv_w", c * 3 + 2), acc[:, :], ALU.mult, ALU.add,
              r=rd + [t_acc], w=[t_ub[c][ti]])
    mtiles = tiles[:4]
    for s2 in range(2):
        wt, t_wt = P.next_slab()
        for ti in (1, 0, 3, 2):
            (c0, w) = mtiles[ti]
            seg, half = ti // 2, ti % 2
            base = HALO + half * 512
            for n in range(4):
                c = s2 * 4 + n
                conv3(c, ti)
                ps, t_ps = P.psum("acc", [0, 1, 2, 3])
                for k in range(KC):
                    P.mm(ps[:, :w], t_ps, wt[:, k * 512 + n * 128: k * 512 + (n + 1) * 128],
                         xn[:, k, c0:c0 + w], k == 0, k == KC - 1, r=[t_wt, t_xn[k][ti]])
                P.tt("dve", ub[:, c, seg, base:base + 512], ps[:, :w], ub[:, c, seg, base:base + 512], ALU.mult,
                     r=[t_ps, t_ub[c][ti]], w=[t_ub[c][ti]])
    for s2 in range(2):
        wt, t_wt = P.next_slab()
        for ti, (c0, w) in enumerate(mtiles):
            seg, half = ti // 2, ti % 2
            base = HALO + half * 512
            for n in range(4):
                ps, t_ps = P.psum("acc", [0, 1, 2, 3])
                for k in range(KC):
                    P.mm(ps[:, :w], t_ps, wt[:, k * 512 + n * 128: k * 512 + (n + 1) * 128],
                         ub[:, k, seg, base:base + 512], k == 0, k == KC - 1, r=[t_wt, t_ub[k][ti]])
                on = s2 * 4 + n
                P.tt("dve", x[:, on, c0:c0 + w], ps[:, :w], x[:, on, c0:c0 + w], ALU.add,
                     r=[t_ps, t_x[on][ti]], w=[t_x[on][ti]])
    mlp_block(P, 1, x, t_x, mtiles, R1, t_R1, R2, t_R2)


NH = 16
HD = 64
VW = 16 * 65
NEG = -60000.0


def seg_loc(i):
    return (i, 0) if i < 4 else (7 - i, 1)


def phase_L2pre(P, qd, kdst, vdst, cd, t_kd_hp, t_vd_hp, on_written=None):
    s = P.s
    x, t_x = P.x, P.t_x
    mt = P.tiles[:4]
    xn, t_xn = P.R1, P.t_R1
    for ti, (c0, w) in enumerate(mt):
        P.rmsnorm([x[:, k, c0:c0 + w] for k in range(KC)], [t_x[k][ti] for k in range(KC)], "mixn2",
                  [xn[:, k, c0:c0 + w] for k in range(KC)], [t_xn[k][ti] for k in range(KC)], w)
    m0 = P.mark()
    t_c = Tl("l2c")
    s.barrier()
    r2flat = P.R2.rearrange("p k t -> p (k t)")
    r2top = [0]

    def r2alloc(shape, dt):
        n = shape[1]
        ne = n if dt == BF16 else 2 * n
        off = r2top[0]
        r2top[0] += ne
        assert r2top[0] <= KC * NT
        v = r2flat[:, off:off + ne]
        if dt == F32:
            v = v.bitcast(F32)
        return v[0:shape[0]]
    wf = P.sb("wf_bf", [128, KC, 16], BF16)
    s.add("dve", lambda e: e.tensor_copy(out=wf.rearrange("p k h -> p (k h)"), in_=P.ppc("c_w_f", 0, 128)),
          r=[P.t_pp], w=[t_c])
    nbf = P.sb("nbf", [16, 1], F32)
    P.ts("dve", nbf[:, :], P.ppc("c_b_f")[0:16], -1.0, None, ALU.mult, None, r=[P.t_pp], w=[t_c])
    one1 = P.sb("one1", [128, 1], F32)
    s.add("dve", lambda e: e.memset(one1[:, :], 1.0), w=[t_c])
    lbuf = r2alloc([16, NMAIN], F32)
    t_l = Tl("lbuf")
    ones16 = r2alloc([16, SEG], F32)
    s.add("dve", lambda e: e.memset(ones16[:, :], 1.0), w=[t_c])
    for ti, (c0, w) in enumerate(mt):
        ps, t_ps = P.psum("stat", [4, 5])
        for k in range(KC):
            P.mm(ps[0:16, :w], t_ps, wf[:, k, :], xn[:, k, c0:c0 + w], k == 0, k == KC - 1,
                 r=[t_c, t_xn[k][ti]])
        et, t_et = P.tmpf()
        P.act(et[0:16, :w], ps[0:16, :w], AF.Exp, r=[t_ps, t_c], w=[t_et], bias=nbf[:, 0:1], scale=-1.0)
        P.act(lbuf[:, c0:c0 + w], et[0:16, :w], AF.Ln, r=[t_et, t_c], w=[t_l], bias=one1[0:16, 0:1])
    cl = r2alloc([16, NMAIN], F32)
    t_cl = Tl("cl")
    for sg in range(2):
        s.add("dve", (lambda sg: lambda e: e.tensor_tensor_scan(
            cl[:, sg * SEG:(sg + 1) * SEG], ones16[:, :], lbuf[:, sg * SEG:(sg + 1) * SEG], 0.0,
            ALU.mult, ALU.subtract))(sg), r=[t_l, t_c], w=[t_cl])
    t_cd = Tl("cd")
    s.add("sp", lambda e: e.dma_start(out=cd[:, :], in_=cl[:, :]), r=[t_cl], w=[t_cd], dma=True)
    P.out_tiles.append(t_cd)
    P.t_cd = t_cd
    if on_written is not None:
        on_written("c", 0)
    aq = lbuf
    hi = r2alloc([16, NMAIN], BF16)
    lo = r2alloc([16, NMAIN], BF16)
    t_aq = Tl("aq")
    for ti, (c0, w) in enumerate(mt):
        P.ts("dve", aq[:, c0:c0 + w], cl[:, c0:c0 + w], cl[:, c0:c0 + 1], None, ALU.subtract, None,
             r=[t_cl, t_l], w=[t_aq, t_l])
    s.add("dve", lambda e: e.tensor_copy(out=hi[:, :], in_=aq[:, :]), r=[t_aq], w=[t_aq])
    P.tt("dve", lo[:, :], aq[:, :], hi[:, :], ALU.subtract, r=[t_aq], w=[t_aq])
    t_qd = Tl("qd")
    s.add("sp", lambda e: e.dma_start(out=qd[:, 64, :], in_=hi[:, :]), r=[t_aq], w=[t_qd], dma=True)
    s.add("sp", lambda e: e.dma_start(out=qd[:, 65, :], in_=lo[:, :]), r=[t_aq], w=[t_qd], dma=True)
    vst = P.R2.rearrange("p k t -> p (k t)")[:, 0:16 * VW].rearrange("p (h kt c) -> p h kt c", h=16, kt=16)
    t_vst = Tl("vst")
    s.barrier()
    for k in range(KC):
        for ti in range(len(P.t_R2[k])):
            P.t_R2[k][ti] = t_vst
    s.add("dve", lambda e: e.memset(vst[:, :, :, 64:65], 1.0), w=[t_vst])
    ev = 0
    for vs in range(2):
        wt, t_wt = P.next_slab()
        for tb in range(16):
            ps, t_ps = P.psum("acc", [0, 1, 2, 3])
            ti = tb // 4
            for k in range(KC):
                P.mm(ps[:, :], t_ps, xn[:, k, tb * 128:(tb + 1) * 128], wt[:, k * 512:(k + 1) * 512],
                     k == 0, k == KC - 1, r=[t_wt, t_xn[k][ti]])
            dst = vst[:, vs * 8:(vs + 1) * 8, tb, 0:64]
            srcv = ps[:, :].rearrange("p (h c) -> p h c", h=8)
            if ev % 2 == 0:
                P.act(dst, srcv, AF.Identity, r=[t_ps], w=[t_vst])
            else:
                s.add("dve", (lambda dst, srcv: lambda e: e.tensor_copy(out=dst, in_=srcv))(dst, srcv),
                      r=[t_ps], w=[t_vst])
            ev += 1
    vflat = vst.rearrange("p h kt c -> p h (kt c)")
    for hp in range(8):
        s.add("sp", (lambda hp: lambda e: e.dma_start(out=vdst(hp).rearrange("(h p) f -> p h f", h=2),
                                                      in_=vflat[:, 2 * hp:2 * hp + 2, :]))(hp),
              r=[t_vst], w=[t_vd_hp[hp]], dma=True)
        if on_written is not None:
            on_written("v", hp)
    ones64 = P.sb("ones64", [64, 64], BF16)
    s.add("dve", lambda e: e.memset(ones64[:, :], 1.0 / HD), w=[t_c])
    gq = P.sb("gq", [64, 1], F32)
    P.ts("dve", gq[:, :], P.ppc("c_q_norm")[0:64], HD ** -0.5, None, ALU.mult, None, r=[P.t_pp], w=[t_c])
    stq = [P.sb("stq%d" % i, [64, NMAIN], BF16) for i in range(2)]
    stk = [P.sb("stk%d" % i, [66, NMAIN], BF16) for i in range(2)]
    t_stq = [Tl("stq%d" % i) for i in range(2)]
    t_stk = [Tl("stk%d" % i) for i in range(2)]
    for i in range(2):
        s.add("dve", (lambda i: lambda e: e.memset(stk[i][64:66, :], 1.0))(i), w=[t_stk[i]])
    sqh = [P.sb("sqh%d" % i, [64, 512], BF16) for i in range(2)]
    t_sqh = [Tl("sqh%d" % i) for i in range(2)]
    t_kd = Tl("kd")
    cnt = 0
    pending = [None]

    def finish_unit(which, h, stg, t_stg, gcol, ps, t_ps, c0, w, is_last):
        sq_, t_sq_ = sqh[finish_unit.cnt % 2], t_sqh[finish_unit.cnt % 2]
        finish_unit.cnt += 1
        P.act(sq_[:, :w], ps[0:64, :w], AF.Square, r=[t_ps], w=[t_sq_])
        ps2, t_ps2 = P.psum("stat", [4, 5])
        P.mm(ps2[0:64, :w], t_ps2, ones64[:, :], sq_[:, :w], True, True, r=[t_sq_, t_c])
        P.act(P.rt[0:64, :w], ps2[0:64, :w], AF.Ln, r=[t_ps2, P.t_const], w=[P.t_rt],
              bias=P.epsc[0:64, 0:1])
        P.act(P.rstd[0:64, :w], P.rt[0:64, :w], AF.Exp, r=[P.t_rt], w=[P.t_rstd], scale=-0.5)
        P.stt(stg[0:64, c0:c0 + w], ps[0:64, :w], gcol, P.rstd[0:64, :w], ALU.mult, ALU.mult,
              r=[t_ps, P.t_rstd, t_c, P.t_pp], w=[t_stg])
        if is_last:
            if which == 0:
                s.add("sp", lambda e: e.dma_start(out=qd[h, 0:64, :], in_=stg[0:64, :]),
                      r=[t_stg], w=[t_qd], dma=True)
            else:
                s.add("sp", lambda e: e.dma_start(out=kdst(h)[0:64, :], in_=stg[0:64, :]),
                      r=[t_stg], w=[t_kd_hp[h // 2]], dma=True)
                s.add("sp", lambda e: e.dma_start(out=kdst(h)[64:66, :], in_=stg[64:66, :]),
                      r=[t_stg], w=[t_kd_hp[h // 2]], dma=True)
                if on_written is not None and h % 2 == 1:
                    on_written("k", h // 2)
    finish_unit.cnt = 0
    for which in range(2):
        for sl in range(2):
            wt, t_wt = P.next_slab()
            for hl in range(8):
                h = sl * 8 + hl
                if which == 0:
                    stg, t_stg = stq[h % 2], t_stq[h % 2]
                    gcol = gq[:, 0:1]
                else:
                    stg, t_stg = stk[h % 2], t_stk[h % 2]
                    gcol = P.ppc("c_k_norm")[0:64]
                for ti, (c0, w) in enumerate(mt):
                    ps, t_ps = P.psum("acc", [0, 1, 2, 3])
                    for k in range(KC):
                        P.mm(ps[0:64, :w], t_ps, wt[:, k * 512 + hl * 64: k * 512 + (hl + 1) * 64],
                             xn[:, k, c0:c0 + w], k == 0, k == KC - 1, r=[t_wt, t_xn[k][ti]])
                    if pending[0] is not None:
                        finish_unit(*pending[0])
                    pending[0] = (which, h, stg, t_stg, gcol, ps, t_ps, c0, w, ti == len(mt) - 1)
    finish_unit(*pending[0])
    P.t_qd = t_qd
    P.t_cd = t_cd
    P.release(m0)


def phase_L2attn(P, qd, kown, vown, cown, kallf, vallf, call, flexmd, seld, cmaskd, dep):
    s = P.s
    x, t_x = P.x, P.t_x
    mt = P.tiles[:4]
    m0 = P.mark()
    t_k = Tl("l2a_const")
    Tt = P.sb("Tt", [16, 8], F32)
    t_T = Tl("Tt")
    for i in range(8):
        r_, part = seg_loc(i)
        col = part * SEG + SEG - 1
        s.add("sp", (lambda i, r_, col: lambda e: e.dma_start(out=Tt[:, i:i + 1], in_=call[r_, :, col:col + 1], allow_slow_non_contiguous=True))(i, r_, col),
              r=[dep["call"]], w=[t_T], dma=True)
    sel = P.sb("sel", [16, 16], F32)
    s.add("sp", lambda e: e.dma_start(out=sel[:, :], in_=seld[:, :]), w=[t_k], dma=True)
    flexm = P.sb("flexm", [128, 8], F32)
    s.add("sp", lambda e: e.dma_start(out=flexm[:, :], in_=flexmd[:, :]), w=[t_k], dma=True)
    cmask = P.sb("cmask", [128, 4, 512], BF16)
    s.add("pool", lambda e: e.dma_start(out=cmask[:, :, :], in_=cmaskd.rearrange("i p q -> p i q")), w=[t_k], dma=True)
    ones8 = P.sb("ones8", [16, 8], F32)
    s.add("dve", lambda e: e.memset(ones8[:, :], 1.0), w=[t_k])
    Pin = P.sb("Pin", [16, 8], F32)
    Pex = P.sb("Pex", [16, 8], F32)
    t_P = Tl("Pex")
    s.add("dve", lambda e: e.tensor_tensor_scan(Pin[:, :], ones8[:, :], Tt[:, :], 0.0, ALU.mult, ALU.add),
          r=[t_T, t_k], w=[t_P])
    P.tt("dve", Pex[:, :], Pin[:, :], Tt[:, :], ALU.subtract, r=[t_P, t_T], w=[t_P])
    PAB = P.sb("PAB", [16, 2], F32)
    ptmp = P.sb("ptmp", [16, 8], F32)
    for sg in range(2):
        P.tt("dve", ptmp[:, :], Pex[:, :], sel[:, sg * 8:(sg + 1) * 8], ALU.mult, r=[t_P, t_k], w=[t_P])
        s.add("dve", (lambda sg: lambda e: e.reduce_sum(PAB[:, sg:sg + 1], ptmp[:, :], mybir.AxisListType.X))(sg),
              r=[t_P], w=[t_P])
    clq = P.sb("clq", [16, 4], F32)
    t_clq = Tl("clq")
    for qt in range(4):
        s.add("sp", (lambda qt: lambda e: e.dma_start(out=clq[:, qt:qt + 1], in_=cown[:, qt * 512:qt * 512 + 1], allow_slow_non_contiguous=True))(qt),
              r=[dep["cd"]], w=[t_clq], dma=True)
    CQ = P.sb("CQ", [16, 4], F32)
    for qt in range(4):
        P.tt("dve", CQ[:, qt:qt + 1], clq[:, qt:qt + 1], PAB[:, qt // 2:qt // 2 + 1], ALU.add,
             r=[t_clq, t_P], w=[t_P])
    CQd = P.sb("CQd", [16, 64], F32)
    negI = P.sb("negI", [16, 64], F32)
    ones16c = P.sb("ones16c", [16, 128], F32)
    s.add("dve", lambda e: e.memset(ones16c[:, :], 1.0), w=[t_k])
    for qt in range(4):
        P.ts("dve", CQd[:, qt * 16:(qt + 1) * 16], P.ident_f[0:16, 0:16], CQ[:, qt:qt + 1], None, ALU.mult, None,
             r=[P.t_const, t_P], w=[t_P])
        P.ts("dve", negI[:, qt * 16:(qt + 1) * 16], P.ident_f[0:16, 0:16], -1.0, None, ALU.mult, None,
             r=[P.t_const], w=[t_k])
    biasall = P.sb("biasall", [128, 72, 64], F32)
    t_bias = Tl("biasall")
    cgk = [P.sb("cgk%d" % i, [16, SEG], F32) for i in range(2)]
    t_cgk = [Tl("cgk%d" % i) for i in range(2)]
    for ks in range(9):
        cg, t_cg = cgk[ks % 2], t_cgk[ks % 2]
        if ks < 7:
            r_, part = seg_loc(ks)
            src = call[r_, :, part * SEG:(part + 1) * SEG]
            pcol = Pex[:, ks:ks + 1]
        else:
            src = cown[:, (ks - 7) * SEG:(ks - 6) * SEG]
            pcol = PAB[:, ks - 7:ks - 6]
        s.add("sp", (lambda cg, src: lambda e: e.dma_start(out=cg[:, :], in_=src))(cg, src),
              r=[dep["call"], dep["cd"]], w=[t_cg], dma=True)
        P.ts("dve", cg[:, :], cg[:, :], pcol, None, ALU.add, None, r=[t_cg, t_P], w=[t_cg])
        ps, t_ps = P.psum("acc", [4, 5, 6, 7])
        for kt in range(8):
            P.mm(ps[:, kt * 64:(kt + 1) * 64], t_ps, cg[:, kt * 128:(kt + 1) * 128], negI[:, :], True, False,
                 r=[t_cg, t_k])
            P.mm(ps[:, kt * 64:(kt + 1) * 64], t_ps, ones16c[:, :], CQd[:, :], False, True, r=[t_k, t_P])
        s.add("dve", (lambda ks, ps: lambda e: e.tensor_copy(
            out=biasall[:, ks * 8:(ks + 1) * 8, :], in_=ps[:, :].rearrange("p (a b) -> p a b", a=8)))(ks, ps),
            r=[t_ps], w=[t_bias])
    flexbias = P.sb("flexbias", [128, 3, 8, 32], F32)
    for p in range(3):
        P.ts("dve", flexbias[:, p, :, :], biasall[:, p * 8:(p + 1) * 8, 0:32], flexm[:, p:p + 1], None,
             ALU.mult, None, r=[t_bias, t_k], w=[t_bias])
        P.stt(flexbias[:, p, :, :], biasall[:, (6 - p) * 8:(7 - p) * 8, 32:64], flexm[:, 3 + p:4 + p],
              flexbias[:, p, :, :], ALU.mult, ALU.add, r=[t_bias, t_k], w=[t_bias])
    sel65 = P.sb("sel65", [65, 64], F32)
    s.add("dve", lambda e: e.memset(sel65[:, :], 0.0), w=[t_k])
    s.add("dve", lambda e: e.memset(sel65[64:65, :], 1.0), r=[t_k], w=[t_k])
    NRING = 4
    kcs = [P.sb("kc%d" % i, [66, SEG], BF16) for i in range(NRING)]
    vcs = [P.sb("vc%d" % i, [128, 8, 65], BF16) for i in range(NRING)]
    t_kv = [Tl("kv%d" % i) for i in range(NRING)]
    qhs = [P.sb("qh%d" % i, [66, NMAIN], BF16) for i in range(2)]
    t_qh = [Tl("qh%d" % i) for i in range(2)]
    qfb = [P.sb("qflex%d" % i, [66, SEG], BF16) for i in range(2)]
    t_qfb = [Tl("qflex%d" % i) for i in range(2)]
    kfb = [P.sb("kflex%d" % i, [66, SEG], BF16) for i in range(2)]
    vfb = [P.sb("vflex%d" % i, [128, 8, 65], BF16) for i in range(2)]
    t_kfb = [Tl("kvflex%d" % i) for i in range(2)]
    prep_i = [0]
    pts = [P.sb("pt%d" % i, [128, 512], BF16) for i in range(3)]
    t_pt = [Tl("pt%d" % i) for i in range(3)]
    osb = [P.sb("osb%d" % i, [65, 512], F32) for i in range(2)]
    t_osb = [Tl("osb%d" % i) for i in range(2)]
    rden = P.sb("rden", [64, 512], F32)
    t_rden = Tl("rden")
    oT = P.sb("oT", [64, 4, NMAIN], BF16)
    t_oT = [[Tl("oT%d_%d" % (hl, qt)) for qt in range(4)] for hl in range(4)]
    fsum = P.sq.rearrange("p k t -> p (k t)").bitcast(F32)[0:65, 0:2048].rearrange("p (a t) -> p a t", a=4)
    t_fs = [Tl("fsum%d" % i) for i in range(4)]
    ring_i = [0]
    pt_i = [0]
    ob_i = [0]

    def ring_next():
        i = ring_i[0] % NRING
        ring_i[0] += 1
        return kcs[i], vcs[i], t_kv[i]

    def load_chunk(h, ks):
        kc, vc, t = ring_next()
        if ks < 7:
            r_, part = seg_loc(ks)
            ksrc = kallf(r_, h)[:, part * SEG:(part + 1) * SEG]
            vsrc = vallf(r_, h)[:, part * 8 * 65:(part + 1) * 8 * 65]
            rk, rv = [dep["kall"][h // 2]], [dep["vall"][h // 2]]
        else:
            part = ks - 7
            ksrc = kown(h)[:, part * SEG:(part + 1) * SEG]
            vsrc = vown(h)[:, part * 8 * 65:(part + 1) * 8 * 65]
            rk, rv = [dep["kd"][h // 2]], [dep["vd"][h // 2]]
        s.add("sp", lambda e: e.dma_start(out=kc[0:64, :], in_=ksrc[0:64]), r=rk, w=[t], dma=True)
        s.add("sp", lambda e: e.dma_start(out=kc[64:66, :], in_=ksrc[64:66]), r=rk, w=[t], dma=True)
        s.add("sp", lambda e: e.dma_start(out=vc.rearrange("p a b -> p (a b)"), in_=vsrc), r=rv, w=[t], dma=True)
        return kc, vc, t

    SKEW = 2

    def run_blocks(blist):
        pend = []

        def emit_pv(item):
            (vc, t_kvc, kt, pt, tpt, bank, st, sp_) = item
            P.mm(P.ps[bank][0:65, :], P.t_ps[bank], vc[:, kt, :], pt[:, :], st, sp_, r=[t_kvc, tpt])
        for (kc, vc, t_kvc, kt, q_ap, t_q, bcol, mi, bank, st, sp_) in blist:
            psS, t_psS = P.psum("S", [4, 5, 6])
            P.mm(psS[:, :], t_psS, kc[0:66, kt * 128:(kt + 1) * 128], q_ap, True, True, r=[t_kvc, t_q])
            pt, tpt = pts[pt_i[0] % 3], t_pt[pt_i[0] % 3]
            pt_i[0] += 1
            if mi is None:
                P.act(pt[:, :], psS[:, :], AF.Exp, r=[t_psS, t_bias], w=[tpt], bias=bcol)
            else:
                sb_, tsb = P.tmpf()
                P.tt("dve", sb_[:, :], psS[:, :], cmask[:, mi, :], ALU.add, r=[t_psS, t_k], w=[tsb])
                P.act(pt[:, :], sb_[:, :], AF.Exp, r=[tsb, t_bias], w=[tpt], bias=bcol)
            pend.append((vc, t_kvc, kt, pt, tpt, bank, st, sp_))
            if len(pend) > SKEW:
                emit_pv(pend.pop(0))
        while pend:
            emit_pv(pend.pop(0))

    def load_q(h):
        qh, tq = qhs[h % 2], t_qh[h % 2]
        s.add("sp", (lambda h, qh: lambda e: e.dma_start(out=qh[0:64, :], in_=qd[h, 0:64, :]))(h, qh),
              r=[dep["qd"]], w=[tq], dma=True)
        s.add("sp", (lambda h, qh: lambda e: e.dma_start(out=qh[64:66, :], in_=qd[h, 64:66, :]))(h, qh),
              r=[dep["qd"]], w=[tq], dma=True)

    def prepare(h, p):
        i = prep_i[0] % 2
        prep_i[0] += 1
        qh, tq = qhs[h % 2], t_qh[h % 2]
        mA, mB = flexm[:, p:p + 1], flexm[:, 3 + p:4 + p]
        kA, vA, tA = load_chunk(h, p)
        kB, vB, tB = load_chunk(h, 6 - p)
        kf, vf, tf, qf, t_qf = kfb[i], vfb[i], t_kfb[i], qfb[i], t_qfb[i]
        P.ts("dve", kf[:, :], kA[:, :], mA[0:66], None, ALU.mult, None, r=[tA, t_k], w=[tf])
        P.stt(kf[:, :], kB[:, :], mB[0:66], kf[:, :], ALU.mult, ALU.add, r=[tB, t_k, tf], w=[tf])
        vff, vAf, vBf = (v_.rearrange("p a b -> p (a b)") for v_ in (vf, vA, vB))
        P.ts("dve", vff, vAf, mA, None, ALU.mult, None, r=[tA, t_k, tf], w=[tf])
        P.stt(vff, vBf, mB, vff, ALU.mult, ALU.add, r=[tB, t_k, tf], w=[tf])
        P.ts("dve", qf[:, :], qh[:, 0:SEG], mA[0:66], None, ALU.mult, None, r=[tq, t_k], w=[t_qf])
        P.stt(qf[:, :], qh[:, SEG:2 * SEG], mB[0:66], qf[:, :], ALU.mult, ALU.add, r=[tq, t_k, t_qf], w=[t_qf])
        return kf, vf, tf, qf, t_qf

    load_q(0)
    prepared = prepare(0, 0)
    for h in range(NH):
        hl = h % 4
        qh, tq = qhs[h % 2], t_qh[h % 2]
        for p in range(3):
            mA, mB = flexm[:, p:p + 1], flexm[:, 3 + p:4 + p]
            kf, vf, tf, qf, t_qf = prepared
            if p < 2:
                prepared = prepare(h, p + 1)
            elif h + 1 < NH:
                load_q(h + 1)
                prepared = prepare(h + 1, 0)
            bk = 2 * (p % 2)
            bl = []
            for kt in range(8):
                for half in range(2):
                    bl.append((kf, vf, tf, kt, qf[0:66, half * 512:(half + 1) * 512], t_qf,
                               flexbias[:, p, kt, half * 16 + h: half * 16 + h + 1], None, bk + half, kt == 0, kt == 7))
            run_blocks(bl)
            for half in range(2):
                for dest, m in ((0, mA), (1, mB)):
                    fi = 2 * dest + half
                    if p == 0:
                        P.ts("dve", fsum[:, fi, :], P.ps[bk + half][0:65, :], m[0:65], None, ALU.mult, None,
                             r=[P.t_ps[bk + half], t_k], w=[t_fs[fi]])
                    else:
                        P.stt(fsum[:, fi, :], P.ps[bk + half][0:65, :], m[0:65], fsum[:, fi, :], ALU.mult, ALU.add,
                              r=[P.t_ps[bk + half], t_k, t_fs[fi]], w=[t_fs[fi]])
        bl = []
        for ks in range(4):
            kc, vc, tkv = load_chunk(h, ks)
            for kt in range(8):
                for qt in (2, 3):
                    bl.append((kc, vc, tkv, kt, qh[0:66, qt * 512:(qt + 1) * 512], tq,
                               biasall[:, ks * 8 + kt, qt * 16 + h: qt * 16 + h + 1], None, qt,
                               ks == 0 and kt == 0, False))
        for sg in range(2):
            kc, vc, tkv = load_chunk(h, 7 + sg)
            for half in range(2):
                qt = 2 * sg + half
                nk = 4 * (half + 1)
                for kt in range(nk):
                    mi = (kt - 4 * half) if kt >= 4 * half else None
                    bl.append((kc, vc, tkv, kt, qh[0:66, qt * 512:(qt + 1) * 512], tq,
                               biasall[:, (7 + sg) * 8 + kt, qt * 16 + h: qt * 16 + h + 1], mi, qt,
                               sg == 0 and kt == 0, kt == nk - 1))
        run_blocks(bl)
        for qt in range(4):
            ob, tob = osb[ob_i[0] % 2], t_osb[ob_i[0] % 2]
            ob_i[0] += 1
            fi = 2 * (qt // 2) + qt % 2
            P.tt("dve", ob[:, :], P.ps[qt][0:65, :], fsum[:, fi, :], ALU.add, r=[P.t_ps[qt], t_fs[fi]], w=[tob])
            P.mm(P.ps[7][0:64, :], P.t_ps[7], sel65[:, :], ob[:, :], True, True, r=[t_k, tob])
            P.act(rden[:, :], P.ps[7][0:64, :], AF.Ln, r=[P.t_ps[7]], w=[t_rden])
            P.act(rden[:, :], rden[:, :], AF.Exp, r=[t_rden], w=[t_rden], scale=-1.0)
            P.tt("dve", oT[:, hl, qt * 512:(qt + 1) * 512], ob[0:64, :], rden[:, :], ALU.mult,
                 r=[tob, t_rden], w=[t_oT[hl][qt]])
        if hl == 3:
            wt, t_wt = P.next_slab()
            for ti, (c0, w) in enumerate(mt):
                for n in range(KC):
                    ps, t_ps = P.psum("S", [4, 5, 6])
                    for j in range(4):
                        P.mm(ps[:, :w], t_ps, wt[0:64, j * 1024 + n * 128: j * 1024 + (n + 1) * 128],
                             oT[:, j, c0:c0 + w], j == 0, j == 3, r=[t_wt, t_oT[j][ti]])
                    P.tt("dve", x[:, n, c0:c0 + w], ps[:, :w], x[:, n, c0:c0 + w], ALU.add,
                         r=[t_ps, t_x[n][ti]], w=[t_x[n][ti]])
    P.release(m0)


GH = 4
CH = 128
NCH = SEG // CH


def gla_seg(P, sg, mode, sinit=None, gout=None, dout=None, tri=None, sinit_t=None, pre_chunk=None):
    s = P.s
    full = mode == "full"
    x, t_x = P.x, P.t_x
    tis = [2 * sg, 2 * sg + 1]
    m0 = P.mark()
    t_g = Tl("gla_c")
    xn = P.sb("gxn", [128, KC, SEG], BF16)
    t_xn = [[Tl("gxn%d_%d" % (k, i)) for i in range(2)] for k in range(KC)]
    for i in range(2):
        c0 = sg * SEG + i * 512
        P.rmsnorm([x[:, k, c0:c0 + 512] for k in range(KC)], [t_x[k][tis[i]] for k in range(KC)], "mixn3",
                  [xn[:, k, i * 512:(i + 1) * 512] for k in range(KC)], [t_xn[k][i] for k in range(KC)], 512)
    wg1 = P.sb("wg1", [128, KC, 16], BF16)
    s.add("dve", lambda e: e.tensor_copy(out=wg1.rearrange("p k h -> p (k h)"), in_=P.ppc("d_w_g1", 0, 128)),
          r=[P.t_pp], w=[t_g])
    wg2 = P.sb("wg2", [16, 512], BF16)
    s.add("dve", lambda e: e.tensor_copy(out=wg2[:, :], in_=P.ppc("d_w_g2", 0, 512)[0:16]), r=[P.t_pp], w=[t_g])
    nbg = P.sb("nbg", [128, 4], F32)
    P.ts("dve", nbg[:, :], P.ppc("d_b_g", 0, 4), -1.0, None, ALU.mult, None, r=[P.t_pp], w=[t_g])
    one1 = P.sb("gone1", [128, 1], F32)
    s.add("dve", lambda e: e.memset(one1[:, :], 1.0), w=[t_g])
    ones512 = P.sb("ones512", [128, 512], F32)
    s.add("dve", lambda e: e.memset(ones512[:, :], 1.0), w=[t_g])
    vtok = P.sb("vtok", [128, NCH, D], BF16)
    t_v = [Tl("vtok%d" % c) for c in range(NCH)]
    ev = 0
    for vs in range(2):
        wt, t_wt = P.next_slab()
        for c in range(NCH):
            ps, t_ps = P.psum("acc", [0, 1, 2, 3])
            for k in range(KC):
                P.mm(ps[:, :], t_ps, xn[:, k, c * CH:(c + 1) * CH], wt[:, k * 512:(k + 1) * 512],
                     k == 0, k == KC - 1, r=[t_wt, t_xn[k][c // 4]])
            dst = vtok[:, c, vs * 512:(vs + 1) * 512]
            if ev % 2 == 0:
                P.act(dst, ps[:, :], AF.Identity, r=[t_ps], w=[t_v[c]])
            else:
                s.add("dve", (lambda dst, ps: lambda e: e.tensor_copy(out=dst, in_=ps[:, :]))(dst, ps),
                      r=[t_ps], w=[t_v[c]])
            ev += 1
    g1 = P.sb("g1", [16, SEG], BF16)
    t_g1 = Tl("g1")
    for i in range(2):
        ps, t_ps = P.psum("stat", [4, 5])
        for k in range(KC):
            P.mm(ps[0:16, :], t_ps, wg1[:, k, :], xn[:, k, i * 512:(i + 1) * 512], k == 0, k == KC - 1,
                 r=[t_g, t_xn[k][i]])
        P.act(g1[:, i * 512:(i + 1) * 512], ps[0:16, :], AF.Identity, r=[t_ps], w=[t_g1])
    if full:
        wq, t_wq = P.next_slab()
    wk, t_wk = P.next_slab(hold=2 if full else 1)
    if full:
        qt_ = P.sb("gqt", [128, GH, SEG], BF16)
    kt_ = P.sb("gkt", [128, GH, SEG], BF16) if full else None
    kd_ = P.sb("gkd", [128, GH, SEG], BF16)
    t_q = [[Tl("gq%d_%d" % (h, i)) for i in range(2)] for h in range(GH)]
    t_kk = [[Tl("gk%d_%d" % (h, i)) for i in range(2)] for h in range(GH)]
    dl = P.sb("gdl", [128, GH, NCH], F32)
    t_dl = [Tl("gdl%d" % h) for h in range(GH)]
    dtot = P.sb("gdtot", [128, GH], F32)
    t_dt = Tl("gdtot")
    csb = [P.sb("gcs0", [128, 512], F32)] * 2
    t_cs = [Tl("gcs0")] * 2
    e4 = [P.sb("ge4_%d" % i, [128, 4], F32) for i in range(2)]
    ep4 = P.sb("gep4", [128, 4], F32)
    dd4 = P.sb("gdd4", [128, 4], F32)
    t_e = Tl("ge4")
    fb = [P.sb("gfb%d" % i, [128, 512], F32) for i in range(3)]
    t_fb = [Tl("gfb%d" % i) for i in range(3)]
    qscale = float(GLA_DKH ** -0.5)
    for h in range(GH):
        for i in range(2):
            lc = i * 512
            psz, t_psz = P.psum("stat", [4, 5])
            P.mm(psz[:, :], t_psz, wg2[0:16, h * 128:(h + 1) * 128], g1[0:16, lc:lc + 512], True, True,
                 r=[t_g, t_g1])
            P.act(fb[0][:, :], psz[:, :], AF.Exp, r=[t_psz, t_g], w=[t_fb[0]], bias=nbg[:, h:h + 1], scale=-1.0)
            P.act(fb[0][:, :], fb[0][:, :], AF.Ln, r=[t_fb[0], t_g], w=[t_fb[0]], bias=one1[:, 0:1])
            cs, tcs = csb[i], t_cs[i]
            init = 0.0 if i == 0 else e4[0][:, 3:4]
            s.add("dve", (lambda cs, init: lambda e: e.tensor_tensor_scan(
                cs[:, :], ones512[:, :], fb[0][:, :], init, ALU.mult, ALU.add))(cs, init),
                r=[t_fb[0], t_g] + ([t_e] if i else []), w=[tcs])
            csv = cs.rearrange("p (c t) -> p c t", t=CH)
            E4 = e4[i]
            s.add("dve", (lambda E4, csv: lambda e: e.tensor_copy(out=E4[:, :], in_=csv[:, :, CH - 1]))(E4, csv),
                  r=[tcs], w=[t_e])
            if i == 0:
                s.add("dve", lambda e: e.memset(ep4[:, 0:1], 0.0), r=[t_e], w=[t_e])
            else:
                s.add("dve", lambda e: e.tensor_copy(out=ep4[:, 0:1], in_=e4[0][:, 3:4]), r=[t_e], w=[t_e])
            s.add("dve", (lambda E4: lambda e: e.tensor_copy(out=ep4[:, 1:4], in_=E4[:, 0:3]))(E4), r=[t_e], w=[t_e])
            bbv = fb[1].rearrange("p (c t) -> p c t", t=CH)
            bb2v = fb[2].rearrange("p (c t) -> p c t", t=CH)
            s.add("dve", (lambda csv: lambda e: e.tensor_tensor(
                bbv, csv, ep4[:, :].unsqueeze(2).to_broadcast([128, 4, CH]), ALU.subtract))(csv),
                r=[tcs, t_e], w=[t_fb[1]])
            s.add("dve", (lambda csv, E4: lambda e: e.tensor_tensor(
                bb2v, E4[:, :].unsqueeze(2).to_broadcast([128, 4, CH]), csv, ALU.subtract))(csv, E4),
                r=[tcs, t_e], w=[t_fb[2]])
            P.tt("dve", dd4[:, :], E4[:, :], ep4[:, :], ALU.subtract, r=[t_e], w=[t_e])
            P.act(dl[:, h, i * 4:(i + 1) * 4], dd4[:, :], AF.Exp, r=[t_e], w=[t_dl[h]], scale=-1.0 / GLA_TAU)
            if i == 1:
                P.act(dtot[:, h:h + 1], E4[:, 3:4], AF.Exp, r=[t_e], w=[t_dt], scale=-1.0 / GLA_TAU)
            psk, t_psk = P.psum("acc", [0, 1, 2, 3])
            for k in range(KC):
                P.mm(psk[:, :], t_psk, wk[:, k * 512 + h * 128: k * 512 + (h + 1) * 128],
                     xn[:, k, lc:lc + 512], k == 0, k == KC - 1, r=[t_wk, t_xn[k][i]])
            P.act(fb[2][:, :], fb[2][:, :], AF.Exp, r=[t_fb[2]], w=[t_fb[2]], scale=-1.0 / GLA_TAU)
            P.tt("dve", kd_[:, h, lc:lc + 512], psk[:, :], fb[2][:, :], ALU.mult, r=[t_psk, t_fb[2]], w=[t_kk[h][i]])
            if full:
                P.act(fb[0][:, :], fb[1][:, :], AF.Exp, r=[t_fb[1]], w=[t_fb[0]], scale=1.0 / GLA_TAU)
                P.tt("dve", kt_[:, h, lc:lc + 512], psk[:, :], fb[0][:, :], ALU.mult,
                     r=[t_psk, t_fb[0]], w=[t_kk[h][i]])
                P.act(fb[1][:, :], fb[1][:, :], AF.Exp, r=[t_fb[1]], w=[t_fb[1]], scale=-1.0 / GLA_TAU)
                psq, t_psq = P.psum("acc", [0, 1, 2, 3])
                for k in range(KC):
                    P.mm(psq[:, :], t_psq, wq[:, k * 512 + h * 128: k * 512 + (h + 1) * 128],
                         xn[:, k, lc:lc + 512], k == 0, k == KC - 1, r=[t_wq, t_xn[k][i]])
                P.stt(qt_[:, h, lc:lc + 512], psq[:, :], qscale, fb[1][:, :], ALU.mult, ALU.mult,
                      r=[t_psq, t_fb[1]], w=[t_q[h][i]])
    Sf = [P.sb("gS%d" % h, [128, GLA_DVH], F32) for h in range(GH)]
    Sb = [P.sb("gSb%d" % h, [128, GLA_DVH], BF16) for h in range(GH)]
    t_S = [Tl("gS%d" % h) for h in range(GH)]
    t_Sb = [Tl("gSb%d" % h) for h in range(GH)]
    if full:
        oall = P.sb("goall", [128, KC, SEG], BF16)
        t_o = [[Tl("go%d_%d" % (c8, i)) for i in range(2)] for c8 in range(KC)]
        trisb = P.sb("gtri", [128, CH], F32)
        s.add("sp", lambda e: e.dma_start(out=trisb[:, :], in_=tri[:, :]), w=[t_g], dma=True)
    cnt = 0
    if pre_chunk is not None:
        sinit_t = pre_chunk()
        P.t_sinit = sinit_t
    for h in range(GH):
        if full:
            s.add("sp", (lambda h: lambda e: e.dma_start(out=Sf[h][:, :], in_=sinit[h, :, :]))(h),
                  r=[sinit_t], w=[t_S[h]], dma=True)
            s.add("act", (lambda h: lambda e: e.activation(Sb[h][:, :], Sf[h][:, :], AF.Identity))(h),
                  r=[t_S[h]], w=[t_Sb[h]])
        else:
            s.add("dve", (lambda h: lambda e: e.memset(Sf[h][:, :], 0.0))(h), w=[t_S[h]])
    fbb = [fb[j].bitcast(BF16).rearrange("p (i h c) -> p i h c", i=2, h=GH) for j in range(2)]
    attm4 = [fbb[0][:, i] for i in range(2)]
    kdtok4 = [fbb[1][:, i] for i in range(2)]
    t_attm4 = [[Tl("gattm4_%d_%d" % (i, h)) for h in range(GH)] for i in range(2)]
    t_kdtok4 = [[Tl("gkdtok4_%d_%d" % (i, h)) for h in range(GH)] for i in range(2)]
    s.add("dve", lambda e: e.memset(fb[0][:, 0:8], 0.0),
          w=[t_fb[0], t_fb[1]] + [t for l_ in (t_attm4 + t_kdtok4) for t in l_])
    for c in range(NCH):
        i = c // 4
        cc = slice(c * CH, (c + 1) * CH)
        par = c % 2
        psA, t_psA = P.ps[4], P.t_ps[4]
        psT, t_psT = P.ps[5], P.t_ps[5]
        psT_b = psT.bitcast(BF16)
        for h in range(GH):
            if full:
                P.mm(psA[:, h * CH:(h + 1) * CH], t_psA, kt_[:, h, cc], qt_[:, h, cc], True, True,
                     r=[t_kk[h][i], t_q[h][i]])
            s.add("pe", (lambda h, cc, psT_b: lambda e: e.transpose(psT_b[:, h * CH:(h + 1) * CH], kd_[:, h, cc], P.ident_b[:, :]))(h, cc, psT_b),
                  r=[t_kk[h][i], P.t_const], w=[t_psT])
        for h in range(GH):
            if full:
                P.tt("dve", attm4[par][:, h, :], psA[:, h * CH:(h + 1) * CH], trisb[:, :], ALU.mult,
                     r=[t_psA, t_g], w=[t_attm4[par][h]])
            s.add("dve", (lambda h, dst, psT_b: lambda e: e.tensor_copy(out=dst, in_=psT_b[:, h * CH:(h + 1) * CH]))(h, kdtok4[par][:, h, :], psT_b),
                  r=[t_psT], w=[t_kdtok4[par][h]])
        for h in range(GH):
            if full:
                bo = 6 + (h % 2)
                pso, t_pso = P.ps[bo], P.t_ps[bo]
                oc = (h // 2) * 2 * CH if False else 0
                for ec in range(2):
                    P.mm(pso[:, ec * CH:(ec + 1) * CH], t_pso,
                         vtok[:, c, h * GLA_DVH + ec * 128: h * GLA_DVH + (ec + 1) * 128], attm4[par][:, h, :],
                         True, False, r=[t_v[c], t_attm4[par][h]])
                    P.mm(pso[:, ec * CH:(ec + 1) * CH], t_pso, Sb[h][:, ec * 128:(ec + 1) * 128], qt_[:, h, cc],
                         False, True, r=[t_Sb[h], t_q[h][i]])
                s.add("act", (lambda h, cc, pso: lambda e: e.activation(
                    oall[:, 2 * h:2 * h + 2, cc], pso[:, 0:2 * CH].rearrange("p (a t) -> p a t", a=2), AF.Identity))(h, cc, pso),
                    r=[t_pso], w=[t_o[2 * h][i], t_o[2 * h + 1][i]])
            bs = h % 4
            pss, t_pss = P.ps[bs], P.t_ps[bs]
            P.mm(pss[:, 0:GLA_DVH], t_pss, kdtok4[par][:, h, :], vtok[:, c, h * GLA_DVH:(h + 1) * GLA_DVH], True, True,
                 r=[t_kdtok4[par][h], t_v[c]])
            P.stt(Sf[h][:, :], Sf[h][:, :], dl[:, h, c:c + 1], pss[:, 0:GLA_DVH], ALU.mult, ALU.add,
                  r=[t_S[h], t_dl[h], t_pss], w=[t_S[h]])
            if full and c < NCH - 1:
                s.add("act", (lambda h: lambda e: e.activation(Sb[h][:, :], Sf[h][:, :], AF.Identity))(h),
                      r=[t_S[h]], w=[t_Sb[h]])
    if not full:
        for h in range(GH):
            t_go = Tl("gout")
            s.add("sp", (lambda h: lambda e: e.dma_start(out=gout[h, :, :], in_=Sf[h][:, :]))(h),
                  r=[t_S[h]], w=[t_go], dma=True)
            P.out_tiles.append(t_go)
    if not full:
        t_do = Tl("dout")
        s.add("sp", lambda e: e.dma_start(out=dout[:, :], in_=dtot[:, :]), r=[t_dt], w=[t_do], dma=True)
        P.out_tiles.append(t_do)
        P.release(m0)
        return
    if getattr(P, "dbg", None) is not None and sg == 0:
        dd = P.dbg
        allq = [t_q[h][i] for h in range(GH) for i in range(2)]
        allk = [t_kk[h][i] for h in range(GH) for i in range(2)]
        allo = [t_o[c8][i] for c8 in range(KC) for i in range(2)]
        for nm, buf, rd in (("dbg_qt", qt_, allq), ("dbg_kt", kt_, allk), ("dbg_kd", kd_, allk),
                            ("dbg_v", vtok, t_v), ("dbg_o", oall, allo)):
            t_d = Tl(nm)
            s.add("sp", (lambda nm, buf: lambda e: e.dma_start(out=dd[nm], in_=buf))(nm, buf), r=rd, w=[t_d], dma=True)
            P.out_tiles.append(t_d)
        t_d = Tl("dbg_dl")
        s.add("sp", lambda e: e.dma_start(out=dd["dbg_dl"], in_=dl), r=t_dl, w=[t_d], dma=True)
        P.out_tiles.append(t_d)
    ones256 = P.sb("ones256", [128, 128], BF16)
    s.add("dve", lambda e: e.memset(ones256[:, :], 1.0 / GLA_DVH), w=[t_g])
    for i in range(2):
        lc = i * 512
        for h in range(GH):
            for ec in range(2):
                P.act(P.sq[:, ec, :], oall[:, 2 * h + ec, lc:lc + 512], AF.Square, r=[t_o[2 * h + ec][i]], w=[P.t_sq[ec]])
            ps, t_ps = P.psum("stat", [4, 5])
            for ec in range(2):
                P.mm(ps[:, :], t_ps, ones256[:, :], P.sq[:, ec, :], ec == 0, ec == 1, r=[P.t_sq[ec], t_g])
            P.act(P.rt[:, :], ps[:, :], AF.Ln, r=[t_ps, P.t_const], w=[P.t_rt], bias=P.epsc[:, 0:1])
            P.act(P.rstd[:, :], P.rt[:, :], AF.Exp, r=[P.t_rt], w=[P.t_rstd], scale=-0.5)
            for ec in range(2):
                P.stt(oall[:, 2 * h + ec, lc:lc + 512], oall[:, 2 * h + ec, lc:lc + 512], P.ppc("d_o_norm", ec),
                      P.rstd[:, :], ALU.mult, ALU.mult, r=[t_o[2 * h + ec][i], P.t_rstd, P.t_pp],
                      w=[t_o[2 * h + ec][i]])
    for rs in range(2):
        wt, t_wt = P.next_slab()
        for i in range(2):
            lc = i * 512
            for n in range(4):
                ps, t_ps = P.psum("acc", [0, 1, 2, 3])
                for k in range(KC):
                    P.mm(ps[:, :], t_ps, wt[:, k * 512 + n * 128: k * 512 + (n + 1) * 128], xn[:, k, lc:lc + 512],
                         k == 0, k == KC - 1, r=[t_wt, t_xn[k][i]])
                sr, t_sr = P.tmpf()
                P.act(sr[:, :], ps[:, :], AF.Silu, r=[t_ps], w=[t_sr])
                c8 = rs * 4 + n
                P.tt("dve", oall[:, c8, lc:lc + 512], oall[:, c8, lc:lc + 512], sr[:, :], ALU.mult,
                     r=[t_o[c8][i], t_sr], w=[t_o[c8][i]])
    for s2 in range(2):
        wt, t_wt = P.next_slab()
        for i in range(2):
            lc = i * 512
            c0 = sg * SEG + lc
            for n in range(4):
                ps, t_ps = P.psum("acc", [0, 1, 2, 3])
                for k in range(KC):
                    P.mm(ps[:, :], t_ps, wt[:, k * 512 + n * 128: k * 512 + (n + 1) * 128], oall[:, k, lc:lc + 512],
                         k == 0, k == KC - 1, r=[t_wt, t_o[k][i]])
                on = s2 * 4 + n
                P.tt("dve", x[:, on, c0:c0 + 512], ps[:, :], x[:, on, c0:c0 + 512], ALU.add,
                     r=[t_ps, t_x[on][tis[i]]], w=[t_x[on][tis[i]]])
    P.release(m0)


GLA_DKH = 128
GLA_DVH = 256
GLA_TAU = 16.0


def gla_prefix(P, sel128d, sinit, g_ap, d_ap, t_gall, t_dall):
    s = P.s
    flat = P.sq.rearrange("p k t -> p (k t)").bitcast(F32)
    R = flat[:, 0:256]
    SA = flat[:, 256:512]
    SB = flat[:, 512:768]
    G = [flat[:, 768:1024], flat[:, 1024:1280]]
    sel = flat[:, 1280:1296]
    dsb = flat[:, 1296:1328].rearrange("p (s h) -> p s h", s=8)
    t_R, t_SA, t_c = Tl("gR"), Tl("gSAB"), Tl("gpre_c")
    t_G = [Tl("gG0"), Tl("gG1")]
    allt = [t_R, t_SA, t_c] + t_G
    s.add("dve", lambda e: e.memset(flat[:, 0:1328], 0.0), w=list(P.t_sq) + allt)
    s.add("sp", lambda e: e.dma_start(out=sel, in_=sel128d[:, :]), w=[t_c], dma=True)
    for sgi in range(8):
        s.add("sp", (lambda sgi: lambda e: e.dma_start(out=dsb[:, sgi, :], in_=d_ap(sgi)))(sgi),
              r=[t_dall], w=[t_c], dma=True)
    t_out = Tl("sinit")
    gi = 0
    for h in range(GH):
        if h > 0:
            s.add("dve", lambda e: e.memset(flat[:, 0:768], 0.0), w=[t_R, t_SA])
        for sgi in range(8):
            if sgi > 0:
                P.stt(SA, R, sel[:, sgi:sgi + 1], SA, ALU.mult, ALU.add, r=[t_R, t_c, t_SA], w=[t_SA])
                P.stt(SB, R, sel[:, 8 + sgi:9 + sgi], SB, ALU.mult, ALU.add, r=[t_R, t_c, t_SA], w=[t_SA])
            if sgi < 7:
                g, tg = G[gi % 2], t_G[gi % 2]
                gi += 1
                s.add("sp", (lambda g, gsrc: lambda e: e.dma_start(out=g, in_=gsrc))(g, g_ap(sgi, h)),
                      r=[t_gall], w=[tg], dma=True)
                P.stt(R, R, dsb[:, sgi, h:h + 1], g, ALU.mult, ALU.add, r=[t_R, t_c, tg], w=[t_R])
        s.add("sp", (lambda h: lambda e: e.dma_start(out=sinit[0, h, :, :], in_=SA))(h),
              r=[t_SA], w=[t_out], dma=True)
        s.add("sp", (lambda h: lambda e: e.dma_start(out=sinit[1, h, :, :], in_=SB))(h),
              r=[t_SA], w=[t_out], dma=True)
    s.add("dve", lambda e: e.memset(flat[:, 0:16], 0.0), w=list(P.t_sq) + allt)
    return t_out


def std_slabs(W, col0, n):
    return [slab_from(W, np.arange(col0 + i * 512, col0 + (i + 1) * 512)) for i in range(n)]


def attn_out_slabs(W):
    out = []
    Wr = W.reshape(16, 64, 1024)
    for g in range(4):
        sl = np.zeros((128, 4, 1024), np.float32)
        sl[0:64] = Wr[4 * g:4 * g + 4].transpose(1, 0, 2)
        out.append(sl.reshape(128, SLABW))
    return out


def build_pp(inp):
    pp = build_pp_stage1(inp)
    wf = inp["c_w_f"][0]
    pp.put("c_w_f", np.ascontiguousarray(wf.reshape(8, 128, 16).transpose(1, 0, 2).reshape(128, 128)))
    col = np.zeros((128, 1), np.float32)
    col[0:16, 0] = inp["c_b_f"][0]
    pp.put("c_b_f", col)
    col = np.zeros((128, 1), np.float32)
    col[0:64, 0] = inp["c_q_norm"][0]
    pp.put("c_q_norm", col)
    col = np.zeros((128, 1), np.float32)
    col[0:64, 0] = inp["c_k_norm"][0]
    pp.put("c_k_norm", col)
    wg1 = inp["d_w_g1"][0]
    pp.put("d_w_g1", np.ascontiguousarray(wg1.reshape(8, 128, 16).transpose(1, 0, 2).reshape(128, 128)))
    a = np.zeros((128, 512), np.float32)
    a[0:16] = inp["d_w_g2"][0]
    pp.put("d_w_g2", a)
    pp.put("d_b_g", colvec(inp["d_b_g"][0], 4))
    pp.put("d_o_norm", colvec(inp["d_o_norm"][0], 2))
    return pp


def build_fused(n_slabs, pp_off, npp):
    P = Prog(n_slabs, npp,
             out_specs=[("xo", (D, NMAIN), F32)],
             in_specs=[("xt", (D, NT), F32), ("hm", (128, 64), F32), ("ident", (128, 128), F32),
                       ("flexm", (128, 8), F32), ("sel", (16, 16), F32), ("cmask", (4, 128, 512), F32),
                       ("sel128", (128, 16), F32), ("tri", (128, 128), F32)], nslot=3)
    P.pp_off = pp_off
    nc, s = P.nc, P.s
    I = P.ins
    groups = [[0, 1, 2, 3], [4, 5, 6, 7]]
    qd = nc.dram_tensor("qd", [NH, 66, NMAIN], BF16).ap()
    kd_hp = [nc.dram_tensor("kd_hp%d" % i, [2 * 66, NMAIN], BF16).ap() for i in range(8)]
    vd_hp = [nc.dram_tensor("vd_hp%d" % i, [2 * 128, VW], BF16).ap() for i in range(8)]
    kall_hp = [nc.dram_tensor("kall_hp%d" % i, [4 * 2 * 66, NMAIN], BF16).ap() for i in range(8)]
    vall_hp = [nc.dram_tensor("vall_hp%d" % i, [4 * 2 * 128, VW], BF16).ap() for i in range(8)]
    cd2 = nc.dram_tensor("cd2", [NH, NMAIN], F32).ap()
    call2 = nc.dram_tensor("call2", [4 * NH, NMAIN], F32).ap()
    gd2 = nc.dram_tensor("gd2", [2 * GH * 128, GLA_DVH], F32).ap()
    dd2 = nc.dram_tensor("dd2", [2 * 128, GH], F32).ap()
    gall2 = nc.dram_tensor("gall2", [4 * 2 * GH * 128, GLA_DVH], F32).ap()
    dall2 = nc.dram_tensor("dall2", [4 * 2 * 128, GH], F32).ap()
    sinit = nc.dram_tensor("sinit", [2, GH, 128, GLA_DVH], F32).ap()
    call = call2.rearrange("(g h) t -> g h t", g=4)
    gd = gd2.rearrange("(s h p) e -> s h p e", s=2, h=GH)
    dd = dd2.rearrange("(s p) h -> s p h", s=2)
    gall = gall2.rearrange("(g s h p) e -> g s h p e", g=4, s=2, h=GH)
    dall = dall2.rearrange("(g s p) h -> g s p h", g=4, s=2)

    def kdst(h):
        return kd_hp[h // 2][(h % 2) * 66:(h % 2 + 1) * 66, :]

    def vown(h):
        return vd_hp[h // 2][(h % 2) * 128:(h % 2 + 1) * 128, :]

    def kallf(r_, h):
        return kall_hp[h // 2][(r_ * 2 + h % 2) * 66:(r_ * 2 + h % 2 + 1) * 66, :]

    def vallf(r_, h):
        return vall_hp[h // 2][(r_ * 2 + h % 2) * 128:(r_ * 2 + h % 2 + 1) * 128, :]

    t_kd_hp = [Tl("kd_hp%d" % i) for i in range(8)]
    t_vd_hp = [Tl("vd_hp%d" % i) for i in range(8)]
    t_kall = [Tl("kall%d" % i) for i in range(8)]
    t_vall = [Tl("vall%d" % i) for i in range(8)]
    t_call = Tl("call")

    setup_state(P, NT, TILES5)
    P.hm = P.sb("hm_sb", [128, 64], F32)
    P.t_hm = Tl("hm")
    s.add("sp", lambda e: e.dma_start(out=P.hm[:, :], in_=I["hm"][:, :]), w=[P.t_hm], dma=True)
    load_x(P, I["xt"], NT)
    mR = P.mark()
    alloc_R(P, NT, 5)
    phase_L0(P)
    phase_L1(P)
    def allgather(src2, dst2, r, w):
        s.add("pool", lambda e: e.collective_compute("AllGather", ALU.bypass, replica_groups=groups,
                                                     ins=[src2], outs=[dst2]), r=r, w=w, cc=True)

    def on_written(kind, i):
        if kind == "c":
            allgather(cd2, call2, [P.t_cd], [t_call])
        elif kind == "k":
            allgather(kd_hp[i], kall_hp[i], [t_kd_hp[i]], [t_kall[i]])
        else:
            allgather(vd_hp[i], vall_hp[i], [t_vd_hp[i]], [t_vall[i]])
    phase_L2pre(P, qd, kdst, lambda hp: vd_hp[hp], cd2, t_kd_hp, t_vd_hp, on_written=on_written)
    P.out_tiles = []
    P.release(mR)
    dep = {"qd": P.t_qd, "cd": P.t_cd, "call": t_call, "kd": t_kd_hp, "vd": t_vd_hp, "kall": t_kall, "vall": t_vall}
    phase_L2attn(P, qd, kdst, vown, cd2, kallf, vallf, call, I["flexm"], I["sel"], I["cmask"], dep)
    m = P.mark()
    alloc_R(P, NT, 5)
    mlp_block(P, 2, P.x, P.t_x, P.tiles[:4], P.R1, P.t_R1, P.R2, P.t_R2)
    P.release(m)
    for sg in range(2):
        gla_seg(P, sg, "scan", gout=gd[sg], dout=dd[sg])
    P.out_tiles = []
    s.barrier()
    t_gall, t_dall = Tl("gall"), Tl("dall")
    allgather(gd2, gall2, [], [t_gall])
    allgather(dd2, dall2, [], [t_dall])

    def pre_chunk():
        return gla_prefix(P, I["sel128"], sinit,
                          lambda sgi, h: gall[seg_loc(sgi)[0], seg_loc(sgi)[1], h, :, :],
                          lambda sgi: dall[seg_loc(sgi)[0], seg_loc(sgi)[1], :, :], t_gall, t_dall)
    gla_seg(P, 0, "full", sinit=sinit[0], tri=I["tri"], pre_chunk=pre_chunk)
    gla_seg(P, 1, "full", sinit=sinit[1], tri=I["tri"], sinit_t=P.t_sinit)
    m = P.mark()
    alloc_R(P, NT, 5)
    mlp_block(P, 3, P.x, P.t_x, P.tiles[:4], P.R1, P.t_R1, P.R2, P.t_R2)
    P.release(m)
    store_x(P, P.outs["xo"])
    return P.finish()


def run_fused(inp):
    x = np.asarray(inp["x"], np.float32)
    pp = build_pp(inp)
    ppa = pp.array()
    ident = np.eye(128, dtype=np.float32)
    wq = inp["c_w_qkv"][0]
    wd = inp["d_w_in"][0]
    gla_scan = std_slabs(wd, 1024, 2) + std_slabs(wd, 512, 1)
    gla_full = (std_slabs(wd, 1024, 2) + std_slabs(wd, 0, 1) + std_slabs(wd, 512, 1) + std_slabs(wd, 2048, 2)
                + std_slabs(inp["d_w_out"][0], 0, 2))
    slabs = (build_wall_stage1(inp) + std_slabs(wq, 2048, 2) + std_slabs(wq, 0, 2) + std_slabs(wq, 1024, 2)
             + attn_out_slabs(inp["c_w_out"][0]) + mlp_slabs(inp, 2) + gla_scan + gla_scan
             + gla_full + gla_full + mlp_slabs(inp, 3))
    wall = np.stack(slabs, axis=0)
    cm = causal_masks()
    tri = np.triu(np.ones((128, 128), np.float32))
    in_maps = []
    for r in range(8):
        b, j = r // 4, r % 4
        idx = core_token_index(j)
        xt = np.zeros((NT, D), np.float32)
        valid = idx >= 0
        xt[valid] = x[b, idx[valid]]
        hm, flexm, sel = percore_consts(j)
        sel128 = np.zeros((128, 16), np.float32)
        sel128[:, j] = 1.0
        sel128[:, 8 + 7 - j] = 1.0
        in_maps.append({"wall": wall, "pp": ppa, "xt": np.ascontiguousarray(xt.T), "hm": hm, "ident": ident,
                        "flexm": flexm, "sel": sel, "cmask": cm, "sel128": sel128, "tri": tri})
    nc = build_fused(wall.shape[0], pp.off, pp.n)
    res = run_bass_kernel_spmd(nc, in_maps, core_ids=list(range(8))).results
    return gather_x([np.asarray(res[r]["xo"]) for r in range(8)])


def percore_consts(j):
    hm = np.ones((128, 64), np.float32)
    if j == 0:
        hm[:, 0:32] = 0.0
    flexm = np.zeros((128, 8), np.float32)
    for p in range(3):
        flexm[:, p] = 1.0 if j > p else 0.0
        flexm[:, 3 + p] = 0.0 if j > p else 1.0
    sel = np.zeros((16, 16), np.float32)
    sel[:, j] = 1.0
    sel[:, 8 + 7 - j] = 1.0
    return hm, flexm, sel


def causal_masks():
    kl = np.arange(128)[:, None]
    ql = np.arange(512)[None, :]
    return np.stack([np.where(kl - ql <= -128 * i, 0.0, NEG).astype(np.float32) for i in range(4)], axis=0)


_CACHE = {}


def gather_x(xos):
    out = np.zeros((2, 8192, D), np.float32)
    for r in range(8):
        b, j = r // 4, r % 4
        a, bb = seg_tokens(j)
        xo = xos[r].T
        out[b, a] = xo[:SEG]
        out[b, bb] = xo[SEG:]
    return out


def kernel(**inputs):
    return run_fused(inputs)
```

```python
import numpy as np
import concourse.bass as bass
import concourse.mybir as mybir
from concourse.bass_utils import run_bass_kernel_spmd
from contextlib import ExitStack

F32 = mybir.dt.float32
BF16 = mybir.dt.bfloat16
AF = mybir.ActivationFunctionType
ALU = mybir.AluOpType

D = 1024
KC = 8
SEG = 1024
HALO = 32
NMAIN = 2 * SEG
NT = NMAIN + 2 * HALO
EPS = 1e-6
SLABW = 4096
NSLOT = 4
ARENA_W = 53200


class Tl:
    __slots__ = ("name", "lastw", "readers")

    def __init__(self, name):
        self.name = name
        self.lastw = None
        self.readers = []


class Op:
    __slots__ = ("eng", "fn", "deps", "dma", "sig", "waits", "idx", "need", "cc")

    def __init__(self, eng, fn, deps, dma, idx):
        self.cc = False
        self.eng = eng
        self.fn = fn
        self.deps = deps
        self.dma = dma
        self.sig = None
        self.waits = []
        self.idx = idx
        self.need = False


class Sch:
    ENGS = ("pe", "act", "dve", "pool", "sp")
    DMAK = 8

    def __init__(self, nc):
        self.nc = nc
        self.ops = []
        self.last = {e: None for e in self.ENGS}
        self.dmas_open = []

    def add(self, eng, fn, r=(), w=(), dma=False, cc=False):
        idx = len(self.ops)
        deps = set()
        for t in r:
            if t.lastw is not None:
                deps.add(t.lastw)
        for t in w:
            if t.lastw is not None:
                deps.add(t.lastw)
            deps.update(t.readers)
        for t in r:
            t.readers.append(idx)
        for t in w:
            t.lastw = idx
            t.readers = []
        deps.discard(idx)
        op = Op(eng, fn, deps, dma, idx)
        op.cc = cc
        self.ops.append(op)
        self.last[eng] = idx
        if dma or cc:
            self.dmas_open.append(idx)
        return idx

    def barrier(self):
        lasts = [v for v in self.last.values() if v is not None] + list(self.dmas_open)
        self.dmas_open = []
        for e in self.ENGS:
            idx = len(self.ops)
            op = Op(e, None, set(lasts), False, idx)
            self.ops.append(op)
            self.last[e] = idx

    def finalize(self, stack):
        nc = self.nc
        ops = self.ops
        for op in ops:
            keep = set()
            for d in op.deps:
                od = ops[d]
                if od.fn is None:
                    if od.eng == op.eng:
                        continue
                    keep.add(d)
                    continue
                if od.eng == "pe" and op.eng == "pe" and not od.dma:
                    continue
                keep.add(d)
            op.deps = keep
            for d in keep:
                ops[d].need = True
        csem = {e: stack.enter_context(nc.semaphore("c_" + e)) for e in ("pe", "act", "dve", "pool", "sp")}
        dsem = {e: [stack.enter_context(nc.semaphore("d_%s%d" % (e, i))) for i in range(self.DMAK)]
                for e in ("sp", "pool")}
        ccount = {e: 0 for e in csem}
        dcount = {e: 0 for e in dsem}
        ccsem = stack.enter_context(nc.semaphore("cc_sem"))
        ncc = 0
        for op in ops:
            if op.cc:
                ncc += 1
                op.sig = (ccsem, ncc, None)
            elif op.dma:
                j = dcount[op.eng]
                dcount[op.eng] += 1
                sem = dsem[op.eng][j % self.DMAK]
                op.sig = (sem, 16 * (j // self.DMAK + 1), 16)
                if j >= self.DMAK:
                    op.waits.append((sem, 16 * (j // self.DMAK)))
            elif op.need:
                ccount[op.eng] += 1
                op.sig = (csem[op.eng], ccount[op.eng], 1)
        known = {e: {} for e in self.ENGS}
        for op in ops:
            kn = known[op.eng]
            ws = {}
            for (sem, val) in op.waits:
                ws[sem.num] = (sem, max(val, ws.get(sem.num, (None, 0))[1]))
            for d in op.deps:
                sem, val, _ = ops[d].sig
                if ws.get(sem.num, (None, 0))[1] < val:
                    ws[sem.num] = (sem, val)
            out = []
            for num, (sem, val) in ws.items():
                if kn.get(num, 0) >= val:
                    continue
                kn[num] = val
                out.append((sem, val))
            op.waits = out
        self.per_eng = {e: [op for op in ops if op.eng == e] for e in self.ENGS}

    def emit(self, eng_name, e):
        n = 0
        for op in self.per_eng[eng_name]:
            for (sem, val) in op.waits:
                e.wait_ge(sem, val)
            if op.fn is None:
                if op.sig is not None:
                    e.nop().then_inc(op.sig[0], op.sig[2])
                continue
            ins = op.fn(e)
            n += 1
            if op.sig is not None:
                if op.sig[2] is None:
                    ins.then_inc(op.sig[0])
                else:
                    ins.then_inc(op.sig[0], op.sig[2])
        return n


def slab_from(W, cols):
    sub = W[:, cols]
    return np.ascontiguousarray(sub.reshape(8, 128, 512).transpose(1, 0, 2).reshape(128, SLABW))


def colvec(v, nch):
    return np.ascontiguousarray(np.asarray(v, np.float32).reshape(nch, 128).T)


class PP:
    def __init__(self):
        self.cols = []
        self.off = {}
        self.n = 0

    def put(self, name, arr):
        arr = np.asarray(arr, np.float32)
        assert arr.shape[0] == 128
        self.off[name] = (self.n, arr.shape[1])
        self.cols.append(arr)
        self.n += arr.shape[1]

    def array(self):
        return np.ascontiguousarray(np.concatenate(self.cols, axis=1))


def seg_tokens(j):
    a = np.arange(j * SEG, (j + 1) * SEG)
    b = np.arange((7 - j) * SEG, (8 - j) * SEG)
    return a, b


def core_token_index(j):
    a, b = seg_tokens(j)
    ha = np.arange(j * SEG - HALO, j * SEG)
    hb = np.arange((7 - j) * SEG - HALO, (7 - j) * SEG)
    return np.concatenate([a, b, ha, hb])


class Prog:
    def __init__(self, n_slabs, npp, out_specs, in_specs, nslot=NSLOT):
        self.nslot = nslot
        self.nc = nc = bass.Bass("TRN2", target_bir_lowering=False)
        self.st = ExitStack()
        self.s = Sch(nc)
        self.wall = nc.dram_tensor("wall", [n_slabs, 128, SLABW], F32, kind="ExternalInput").ap()
        self.ppd = nc.dram_tensor("pp", [128, npp], F32, kind="ExternalInput").ap()
        self.ins = {}
        for name, shape, dt in in_specs:
            self.ins[name] = nc.dram_tensor(name, list(shape), dt, kind="ExternalInput").ap()
        self.outs = {}
        for name, shape, dt in out_specs:
            self.outs[name] = nc.dram_tensor(name, list(shape), dt, kind="ExternalOutput").ap()
        self.npp = npp
        self.n_slabs = n_slabs
        self.slab_i = 0
        self.slab_issued = 0
        self.out_tiles = []
        self.arena = None

    def sb(self, name, shape, dt):
        if self.arena is None:
            self.arena = self.st.enter_context(self.nc.sbuf_tensor("arena", [128, ARENA_W], F32))
            self.top = 0
        n = 1
        for d in shape[1:]:
            n *= d
        words = n if dt == F32 else (n + 1) // 2
        off = self.top
        self.top += words
        assert self.top <= ARENA_W, ("SBUF arena overflow", name, self.top)
        v = self.arena[:, off:off + words]
        if dt != F32:
            v = v.bitcast(dt)[:, 0:n]
        v = v[0:shape[0]]
        if len(shape) == 3:
            v = v.rearrange("p (a b) -> p a b", a=shape[1])
        elif len(shape) == 4:
            v = v.rearrange("p (a b c) -> p a b c", a=shape[1], b=shape[2])
        return v

    def mark(self):
        return self.top

    def release(self, mark):
        self.top = mark
        self.s.barrier()

    def setup_common(self):
        nc, s = self.nc, self.s
        self.pp = self.sb("pp_sb", [128, self.npp], F32)
        self.t_pp = Tl("pp")
        s.add("sp", lambda e: e.dma_start(out=self.pp[:, :], in_=self.ppd[:, :]), w=[self.t_pp], dma=True)
        self.wring = [self.sb("wslot%d" % i, [128, SLABW], BF16) for i in range(self.nslot)]
        self.t_w = [Tl("w%d" % i) for i in range(self.nslot)]
        self.ps = [self.st.enter_context(nc.psum_tensor("ps%d" % i, [128, 512], F32)) for i in range(8)]
        self.t_ps = [Tl("ps%d" % i) for i in range(8)]
        self.ps_rr = {}
        self.onesD = self.sb("onesD", [128, 128], BF16)
        self.t_const = Tl("const")
        s.add("dve", lambda e: e.memset(self.onesD[:, :], 1.0 / D), w=[self.t_const])
        self.ident_f = self.sb("ident_f", [128, 128], F32)
        self.ident_b = self.sb("ident_b", [128, 128], BF16)
        s.add("sp", lambda e: e.dma_start(out=self.ident_f[:, :], in_=self.ins["ident"][:, :]),
              w=[self.t_const], dma=True)
        s.add("dve", lambda e: e.tensor_copy(out=self.ident_b[:, :], in_=self.ident_f[:, :]),
              r=[self.t_const], w=[self.t_const])

    def psum(self, group, banks):
        i = self.ps_rr.get(group, 0)
        self.ps_rr[group] = i + 1
        b = banks[i % len(banks)]
        return self.ps[b], self.t_ps[b]

    def ppc(self, name, k=0, n=1):
        off, w = self.pp_off[name]
        return self.pp[:, off + k: off + k + n]

    def _issue_slab(self):
        i = self.slab_issued
        if i >= self.n_slabs:
            return
        self.slab_issued += 1
        slot = i % self.nslot
        dst = self.wring[slot]
        src = self.wall[i]
        self.s.add("pool", lambda e: e.dma_start(out=dst[:, :], in_=src), w=[self.t_w[slot]], dma=True)

    def next_slab(self, hold=1):
        i = self.slab_i
        self.slab_i += 1
        while self.slab_issued < min(self.n_slabs, i + self.nslot - (hold - 1)):
            self._issue_slab()
        slot = i % self.nslot
        return self.wring[slot], self.t_w[slot]

    def mm(self, ps_ap, t_ps, lhsT, rhs, start, stop, r):
        self.s.add("pe", lambda e: e.matmul(ps_ap, lhsT, rhs, start=start, stop=stop),
                   r=list(r) + ([] if start else [t_ps]), w=[t_ps])

    def act(self, out, in_, func, r, w, bias=None, scale=None):
        kw = {}
        if bias is not None:
            kw["bias"] = bias
        if scale is not None:
            kw["scale"] = scale
        self.s.add("act", lambda e: e.activation(out, in_, func, **kw), r=r, w=w)

    def tt(self, eng, out, in0, in1, op, r, w):
        self.s.add(eng, lambda e: e.tensor_tensor(out, in0, in1, op), r=r, w=w)

    def ts(self, eng, out, in0, s1, s2, op0, op1, r, w):
        if op1 is None:
            self.s.add(eng, lambda e: e.tensor_scalar(out, in0, s1, None, op0), r=r, w=w)
        else:
            self.s.add(eng, lambda e: e.tensor_scalar(out, in0, s1, s2, op0, op1), r=r, w=w)

    def stt(self, out, in0, scalar, in1, op0, op1, r, w):
        self.s.add("dve", lambda e: e.scalar_tensor_tensor(out, in0, scalar, in1, op0, op1), r=r, w=w)

    def rmsnorm(self, xk, t_xk, gname, outk, t_outk, w):
        sq, t_sq = self.sq, self.t_sq
        for k in range(KC):
            self.act(sq[:, k, :w], xk[k], AF.Square, r=[t_xk[k]], w=[t_sq[k]])
        ps, t_ps = self.psum("stat", [4, 5])
        for k in range(KC):
            self.mm(ps[:, :w], t_ps, self.onesD[:, :], sq[:, k, :w], k == 0, k == KC - 1,
                    r=[t_sq[k], self.t_const])
        self.act(self.rt[:, :w], ps[:, :w], AF.Ln, r=[t_ps, self.t_const], w=[self.t_rt],
                 bias=self.epsc[:, 0:1])
        self.act(self.rstd[:, :w], self.rt[:, :w], AF.Exp, r=[self.t_rt], w=[self.t_rstd], scale=-0.5)
        for k in range(KC):
            self.stt(outk[k], xk[k], self.ppc(gname, k), self.rstd[:, :w], ALU.mult, ALU.mult,
                     r=[t_xk[k], self.t_rstd, self.t_pp], w=[t_outk[k]])

    def finish(self):
        nc, s = self.nc, self.s
        s.add("sp", None, r=self.out_tiles)
        s.finalize(self.st)
        with nc.Block() as block:
            @block.tensor
            def _(e):
                s.emit("pe", e)

            @block.scalar
            def _(e):
                s.emit("act", e)

            @block.vector
            def _(e):
                s.emit("dve", e)

            @block.gpsimd
            def _(e):
                s.emit("pool", e)

            @block.sync
            def _(e):
                s.emit("sp", e)
        self.st.close()
        return nc


TILES5 = [(0, 512), (512, 512), (1024, 512), (1536, 512), (2048, 64)]


def mlp_cols(q, s):
    return np.arange(q * 1024 + s * 512, q * 1024 + (s + 1) * 512)


def build_wall_stage1(inp):
    slabs = []
    w = inp["a_w_in"][0]
    for s4 in range(4):
        cols = np.concatenate([np.arange(2 * s4 * 128, (2 * s4 + 2) * 128),
                               1024 + np.arange(2 * s4 * 128, (2 * s4 + 2) * 128)])
        slabs.append(slab_from(w, cols))
    w = inp["a_w_out"][0]
    for s2 in range(2):
        slabs.append(slab_from(w, np.arange(s2 * 512, (s2 + 1) * 512)))
    slabs += mlp_slabs(inp, 0)
    w = inp["b_w_in"][0]
    for s4 in range(4):
        cols = np.concatenate([1024 + np.arange(2 * s4 * 128, (2 * s4 + 2) * 128),
                               2048 + np.arange(2 * s4 * 128, (2 * s4 + 2) * 128)])
        slabs.append(slab_from(w, cols))
    for s2 in range(2):
        slabs.append(slab_from(w, np.arange(s2 * 512, (s2 + 1) * 512)))
    w = inp["b_w_out"][0]
    for s2 in range(2):
        slabs.append(slab_from(w, np.arange(s2 * 512, (s2 + 1) * 512)))
    slabs += mlp_slabs(inp, 1)
    return slabs


def mlp_slabs(inp, l):
    slabs = []
    w1 = inp["mlp_w1"][l]
    w2 = inp["mlp_w2"][l]
    for q in range(4):
        for s in range(2):
            slabs.append(slab_from(w1, mlp_cols(q, s)))
        for s in range(2):
            slabs.append(slab_from(w2[q * 1024:(q + 1) * 1024], np.arange(s * 512, (s + 1) * 512)))
    return slabs


def build_pp_stage1(inp):
    pp = PP()
    for l in range(4):
        pp.put("mixn%d" % l, colvec(inp["mix_norm"][l], 8))
        pp.put("mlpn%d" % l, colvec(inp["mlp_norm"][l], 8))
    pp.put("a_b_in", colvec(inp["a_b_in"][0], 16))
    cw = inp["a_conv_w"][0]
    pp.put("a_conv_w", np.ascontiguousarray(cw.T.reshape(8, 128, 31).transpose(1, 0, 2).reshape(128, 248)))
    pp.put("a_conv_b", colvec(inp["a_conv_b"][0], 8))
    pp.put("a_ln_g", colvec(inp["a_ln_g"][0], 8))
    pp.put("a_ln_b", colvec(inp["a_ln_b"][0], 8))
    pp.put("a_b_out", colvec(inp["a_b_out"][0], 8))
    bw = inp["b_conv_w"][0]
    pp.put("b_conv_w", np.ascontiguousarray(bw.T.reshape(8, 128, 3).transpose(1, 0, 2).reshape(128, 24)))
    return pp


def mlp_block(P, l, x, t_x, tiles, xn, t_xn, hq, t_hq):
    for ti, (c0, w) in enumerate(tiles):
        P.rmsnorm([x[:, k, c0:c0 + w] for k in range(KC)], [t_x[k][ti] for k in range(KC)], "mlpn%d" % l,
                  [xn[:, k, c0:c0 + w] for k in range(KC)], [t_xn[k][ti] for k in range(KC)], w)
    for q in range(4):
        for s in range(2):
            wt, t_wt = P.next_slab()
            for ti, (c0, w) in enumerate(tiles):
                for n in range(4):
                    ps, t_ps = P.psum("acc", [0, 1, 2, 3])
                    for k in range(KC):
                        P.mm(ps[:, :w], t_ps, wt[:, k * 512 + n * 128: k * 512 + (n + 1) * 128],
                             xn[:, k, c0:c0 + w], k == 0, k == KC - 1, r=[t_wt, t_xn[k][ti]])
                    tmp, t_tmp = P.tmpf()
                    P.act(tmp[:, :w], ps[:, :w], AF.Relu, r=[t_ps], w=[t_tmp])
                    hn = s * 4 + n
                    P.tt("dve", hq[:, hn, c0:c0 + w], tmp[:, :w], tmp[:, :w], ALU.mult,
                         r=[t_tmp], w=[t_hq[hn][ti]])
        for s in range(2):
            wt, t_wt = P.next_slab()
            for ti, (c0, w) in enumerate(tiles):
                for n in range(4):
                    ps, t_ps = P.psum("acc", [0, 1, 2, 3])
                    for k in range(KC):
                        P.mm(ps[:, :w], t_ps, wt[:, k * 512 + n * 128: k * 512 + (n + 1) * 128],
                             hq[:, k, c0:c0 + w], k == 0, k == KC - 1, r=[t_wt, t_hq[k][ti]])
                    on = s * 4 + n
                    P.tt("dve", x[:, on, c0:c0 + w], ps[:, :w], x[:, on, c0:c0 + w], ALU.add,
                         r=[t_ps, t_x[on][ti]], w=[t_x[on][ti]])


def setup_state(P, ncols, tiles):
    s = P.s
    P.setup_common()
    ntl = len(tiles)
    P.tiles = tiles
    P.x = P.sb("xres", [128, KC, ncols], F32)
    P.t_x = [[Tl("x%d_%d" % (k, ti)) for ti in range(ntl)] for k in range(KC)]
    P.sq = P.sb("sq", [128, KC, 512], BF16)
    P.t_sq = [Tl("sq%d" % k) for k in range(KC)]
    P.rt = P.sb("rt", [128, 512], F32)
    P.t_rt = Tl("rt")
    P.rstd = P.sb("rstd", [128, 512], F32)
    P.t_rstd = Tl("rstd")
    P.epsc = P.sb("epsc", [128, 1], F32)
    s.add("dve", lambda e: e.memset(P.epsc[:, :], EPS), w=[P.t_const])
    tmps = [P.sb("tmpf%d" % i, [128, 512], F32) for i in range(3)]
    t_tmps = [Tl("tmpf%d" % i) for i in range(3)]
    rr = [0]

    def tmpf():
        i = rr[0] % 3
        rr[0] += 1
        return tmps[i], t_tmps[i]
    P.tmpf = tmpf


def alloc_R(P, ncols, ntl):
    P.R1 = P.sb("R1", [128, KC, ncols], BF16)
    P.t_R1 = [[Tl("r1_%d_%d" % (k, ti)) for ti in range(ntl)] for k in range(KC)]
    P.R2 = P.sb("R2", [128, KC, ncols], BF16)
    P.t_R2 = [[Tl("r2_%d_%d" % (k, ti)) for ti in range(ntl)] for k in range(KC)]


def load_x(P, xt_ap, ncols):
    s = P.s
    xt_v = xt_ap.rearrange("(k p) t -> p k t", p=128)
    for ti, (c0, w) in enumerate(P.tiles):
        s.add("sp", (lambda c0, w: lambda e: e.dma_start(out=P.x[:, :, c0:c0 + w], in_=xt_v[:, :, c0:c0 + w]))(c0, w),
              w=[P.t_x[k][ti] for k in range(KC)], dma=True)


def store_x(P, xo_ap, ncols=NMAIN):
    s = P.s
    xo_v = xo_ap.rearrange("(k p) t -> p k t", p=128)
    for k in range(KC):
        t_o = Tl("out%d" % k)
        s.add("sp", (lambda k: lambda e: e.dma_start(out=xo_v[:, k, :], in_=P.x[:, k, 0:ncols]))(k),
              r=[P.t_x[k][ti] for ti in range(4)], w=[t_o], dma=True)
        P.out_tiles.append(t_o)


def phase_L0(P):
    s = P.s
    x, t_x, tiles = P.x, P.t_x, P.tiles
    R1, t_R1, R2, t_R2 = P.R1, P.t_R1, P.R2, P.t_R2
    hm, t_hm = P.hm, P.t_hm
    m0 = P.mark()
    xn, t_xn = R1, t_R1
    for ti, (c0, w) in enumerate(tiles):
        P.rmsnorm([x[:, k, c0:c0 + w] for k in range(KC)], [t_x[k][ti] for k in range(KC)], "mixn0",
                  [xn[:, k, c0:c0 + w] for k in range(KC)], [t_xn[k][ti] for k in range(KC)], w)
    hc = R2.rearrange("p k (s t) -> p k s t", s=2)
    t_hc = t_R2

    def hc_dst(c, ti):
        if ti < 4:
            seg, half = ti // 2, ti % 2
            return hc[:, c, seg, HALO + half * 512: HALO + half * 512 + 512]
        return hc[:, c, :, 0:HALO]

    for s4 in range(4):
        wt, t_wt = P.next_slab()
        for ti, (c0, w) in enumerate(tiles):
            for cl in range(2):
                c = 2 * s4 + cl
                psa, t_psa = P.psum("acc", [0, 1, 2, 3])
                psg, t_psg = P.psum("acc", [0, 1, 2, 3])
                for k in range(KC):
                    P.mm(psa[:, :w], t_psa, wt[:, k * 512 + cl * 128: k * 512 + (cl + 1) * 128],
                         xn[:, k, c0:c0 + w], k == 0, k == KC - 1, r=[t_wt, t_xn[k][ti]])
                for k in range(KC):
                    P.mm(psg[:, :w], t_psg, wt[:, k * 512 + (2 + cl) * 128: k * 512 + (3 + cl) * 128],
                         xn[:, k, c0:c0 + w], k == 0, k == KC - 1, r=[t_wt, t_xn[k][ti]])
                sg, t_sg = P.tmpf()
                P.act(sg[:, :w], psg[:, :w], AF.Sigmoid, r=[t_psg, P.t_pp], w=[t_sg],
                      bias=P.ppc("a_b_in", 8 + c))
                if ti == 4:
                    P.tt("dve", sg[:, :w], sg[:, :w], hm[:, :], ALU.mult, r=[t_sg, t_hm], w=[t_sg])
                    src_a = psa[:, :w].rearrange("p (s t) -> p s t", s=2)
                    src_g = sg[:, :w].rearrange("p (s t) -> p s t", s=2)
                else:
                    src_a = psa[:, :w]
                    src_g = sg[:, :w]
                P.stt(hc_dst(c, ti), src_a, P.ppc("a_b_in", c), src_g, ALU.add, ALU.mult,
                      r=[t_psa, t_sg, P.t_pp], w=[t_hc[c][ti]])
    dgs = [P.sb("dg%d" % i, [128, 31 * 128], BF16) for i in range(2)]
    t_dgs = [Tl("dg%d" % i) for i in range(2)]
    yall, t_yall = R1, t_R1
    for c in range(KC):
        dg, t_dg = dgs[c % 2], t_dgs[c % 2]
        for k in range(31):
            P.ts("dve", dg[:, k * 128:(k + 1) * 128], P.ident_f[:, :], P.ppc("a_conv_w", c * 31 + k), None,
                 ALU.mult, None, r=[P.t_const, P.t_pp], w=[t_dg])
        for ti, (c0, w) in enumerate(tiles):
            ps, t_ps = P.psum("conv", [6, 7])
            if ti < 4:
                seg, half = ti // 2, ti % 2
                rd = [t_dg, t_hc[c][ti], t_hc[c][ti - 1 if half else 4]]
                for k in range(31):
                    b0 = 2 + half * 512 + k
                    P.mm(ps[:, :512], t_ps, dg[:, k * 128:(k + 1) * 128], hc[:, c, seg, b0:b0 + 512],
                         k == 0, k == 30, r=rd)
                P.act(yall[:, c, c0:c0 + w], ps[:, :w], AF.Identity, r=[t_ps, P.t_pp], w=[t_yall[c][ti]],
                      bias=P.ppc("a_conv_b", c))
            else:
                s.add("dve", (lambda c: lambda e: e.memset(yall[:, c, NMAIN:NT], 0.0))(c), w=[t_yall[c][4]])
                pv = ps[:, 0:4].rearrange("p (s t) -> p s t", s=2)
                for k in range(31):
                    P.mm(pv, t_ps, dg[:, k * 128:(k + 1) * 128], hc[:, c, :, k:k + 2],
                         k == 0, k == 30, r=[t_dg, t_hc[c][4]])
                yv = yall[:, c, NMAIN:NT].rearrange("p (s t) -> p s t", s=2)[:, :, 30:32]
                P.act(yv, pv, AF.Identity, r=[t_ps, P.t_pp], w=[t_yall[c][4]], bias=P.ppc("a_conv_b", c))
    sall, t_sall = R2, t_R2
    mu_sb = P.sb("mu_sb", [128, 512], F32)
    t_mu = Tl("mu")
    m2 = P.sb("m2", [128, 512], F32)
    t_m2 = Tl("m2")
    def ln_tile(ti, c0, w):
        for k in range(KC):
            P.act(P.sq[:, k, :w], yall[:, k, c0:c0 + w], AF.Square, r=[t_yall[k][ti]], w=[P.t_sq[k]])
        psm, t_psm = P.psum("stat", [4, 5])
        pss, t_pss = P.psum("stat", [4, 5])
        for k in range(KC):
            P.mm(psm[:, :w], t_psm, P.onesD[:, :], yall[:, k, c0:c0 + w], k == 0, k == KC - 1,
                 r=[t_yall[k][ti], P.t_const])
        for k in range(KC):
            P.mm(pss[:, :w], t_pss, P.onesD[:, :], P.sq[:, k, :w], k == 0, k == KC - 1,
                 r=[P.t_sq[k], P.t_const])
        P.act(mu_sb[:, :w], psm[:, :w], AF.Identity, r=[t_psm], w=[t_mu])
        P.tt("dve", m2[:, :w], mu_sb[:, :w], mu_sb[:, :w], ALU.mult, r=[t_mu], w=[t_m2])
        P.tt("dve", m2[:, :w], pss[:, :w], m2[:, :w], ALU.subtract, r=[t_pss, t_m2], w=[t_m2])
        P.act(P.rt[:, :w], m2[:, :w], AF.Ln, r=[t_m2, P.t_const], w=[P.t_rt], bias=P.epsc[:, 0:1])
        P.act(P.rstd[:, :w], P.rt[:, :w], AF.Exp, r=[P.t_rt], w=[P.t_rstd], scale=-0.5)
        for k in range(KC):
            z, t_z = P.tmpf()
            P.tt("pool", z[:, :w], yall[:, k, c0:c0 + w], mu_sb[:, :w], ALU.subtract,
                 r=[t_yall[k][ti], t_mu], w=[t_z])
            P.tt("dve", z[:, :w], z[:, :w], P.rstd[:, :w], ALU.mult, r=[t_z, P.t_rstd], w=[t_z])
            P.act(sall[:, k, c0:c0 + w], z[:, :w], AF.Silu, r=[t_z, P.t_pp], w=[t_sall[k][ti]],
                  bias=P.ppc("a_ln_b", k), scale=P.ppc("a_ln_g", k))


    wo = [P.next_slab(), P.next_slab(hold=2)]

    def out_tile(ti, c0, w):
        for s2 in range(2):
            wt, t_wt = wo[s2]
            for n in range(4):
                ps, t_ps = P.psum("acc", [0, 1, 2, 3])
                for k in range(KC):
                    P.mm(ps[:, :w], t_ps, wt[:, k * 512 + n * 128: k * 512 + (n + 1) * 128],
                         sall[:, k, c0:c0 + w], k == 0, k == KC - 1, r=[t_wt, t_sall[k][ti]])
                on = s2 * 4 + n
                P.stt(x[:, on, c0:c0 + w], ps[:, :w], P.ppc("a_b_out", on), x[:, on, c0:c0 + w],
                      ALU.add, ALU.add, r=[t_ps, P.t_pp, t_x[on][ti]], w=[t_x[on][ti]])
    for ti, (c0, w) in enumerate(tiles):
        ln_tile(ti, c0, w)
        if ti > 0:
            out_tile(ti - 1, *tiles[ti - 1])
    out_tile(len(tiles) - 1, *tiles[-1])
    P.release(m0)
    mlp_block(P, 0, x, t_x, tiles, R1, t_R1, R2, t_R2)


def phase_L1(P):
    s = P.s
    x, t_x, tiles = P.x, P.t_x, P.tiles
    R1, t_R1, R2, t_R2 = P.R1, P.t_R1, P.R2, P.t_R2
    hm, t_hm = P.hm, P.t_hm
    xn, t_xn = R1, t_R1
    for ti, (c0, w) in enumerate(tiles):
        P.rmsnorm([x[:, k, c0:c0 + w] for k in range(KC)], [t_x[k][ti] for k in range(KC)], "mixn1",
                  [xn[:, k, c0:c0 + w] for k in range(KC)], [t_xn[k][ti] for k in range(KC)], w)
    ub = R2.rearrange("p k (s t) -> p k s t", s=2)
    t_ub = t_R2

    def ub_dst(c, ti):
        if ti < 4:
            seg, half = ti // 2, ti % 2
            return ub[:, c, seg, HALO + half * 512: HALO + half * 512 + 512]
        return ub[:, c, :, 0:HALO]
    for s4 in range(4):
        wt, t_wt = P.next_slab()
        for ti, (c0, w) in enumerate(tiles):
            for cl in range(2):
                c = 2 * s4 + cl
                psc, t_psc = P.psum("acc", [0, 1, 2, 3])
                psh, t_psh = P.psum("acc", [0, 1, 2, 3])
                for k in range(KC):
                    P.mm(psc[:, :w], t_psc, wt[:, k * 512 + cl * 128: k * 512 + (cl + 1) * 128],
                         xn[:, k, c0:c0 + w], k == 0, k == KC - 1, r=[t_wt, t_xn[k][ti]])
                for k in range(KC):
                    P.mm(psh[:, :w], t_psh, wt[:, k * 512 + (2 + cl) * 128: k * 512 + (3 + cl) * 128],
                         xn[:, k, c0:c0 + w], k == 0, k == KC - 1, r=[t_wt, t_xn[k][ti]])
                gc, t_gc = P.tmpf()
                P.act(gc[:, :w], psc[:, :w], AF.Identity, r=[t_psc], w=[t_gc])
                if ti == 4:
                    P.tt("dve", gc[:, :w], gc[:, :w], hm[:, :], ALU.mult, r=[t_gc, t_hm], w=[t_gc])
                    src_h = psh[:, :w].rearrange("p (s t) -> p s t", s=2)
                    src_c = gc[:, :w].rearrange("p (s t) -> p s t", s=2)
                else:
                    src_h = psh[:, :w]
                    src_c = gc[:, :w]
                P.tt("dve", ub_dst(c, ti), src_h, src_c, ALU.mult, r=[t_psh, t_gc], w=[t_ub[c][ti]])
    def conv3(c, ti):
        seg, half = ti // 2, ti % 2
        acc, t_acc = P.tmpf()
        base = HALO + half * 512
        rd = [t_ub[c][ti], t_ub[c][ti - 1 if half else 4], P.t_pp]
        P.ts("dve", acc[:, :], ub[:, c, seg, base - 2: base - 2 + 512], P.ppc("b_conv_w", c * 3 + 0), None,
             ALU.mult, None, r=rd, w=[t_acc])
        P.stt(acc[:, :], ub[:, c, seg, base - 1: base - 1 + 512], P.ppc("b_conv_w", c * 3 + 1), acc[:, :],
              ALU.mult, ALU.add, r=rd + [t_acc], w=[t_acc])
        P.stt(ub[:, c, seg, base: base + 512], ub[:, c, seg, base: base + 512],
              P.ppc("b_conv_w", c * 3 + 2), acc[:, :], ALU.mult, ALU.add,
              r=rd + [t_acc], w=[t_ub[c][ti]])
    mtiles = tiles[:4]
    for s2 in range(2):
        wt, t_wt = P.next_slab()
        for ti in (1, 0, 3, 2):
            (c0, w) = mtiles[ti]
            seg, half = ti // 2, ti % 2
            base = HALO + half * 512
            for n in range(4):
                c = s2 * 4 + n
                conv3(c, ti)
                ps, t_ps = P.psum("acc", [0, 1, 2, 3])
                for k in range(KC):
                    P.mm(ps[:, :w], t_ps, wt[:, k * 512 + n * 128: k * 512 + (n + 1) * 128],
                         xn[:, k, c0:c0 + w], k == 0, k == KC - 1, r=[t_wt, t_xn[k][ti]])
                P.tt("dve", ub[:, c, seg, base:base + 512], ps[:, :w], ub[:, c, seg, base:base + 512], ALU.mult,
                     r=[t_ps, t_ub[c][ti]], w=[t_ub[c][ti]])
    for s2 in range(2):
        wt, t_wt = P.next_slab()
        for ti, (c0, w) in enumerate(mtiles):
            seg, half = ti // 2, ti % 2
            base = HALO + half * 512
            for n in range(4):
                ps, t_ps = P.psum("acc", [0, 1, 2, 3])
                for k in range(KC):
                    P.mm(ps[:, :w], t_ps, wt[:, k * 512 + n * 128: k * 512 + (n + 1) * 128],
                         ub[:, k, seg, base:base + 512], k == 0, k == KC - 1, r=[t_wt, t_ub[k][ti]])
                on = s2 * 4 + n
                P.tt("dve", x[:, on, c0:c0 + w], ps[:, :w], x[:, on, c0:c0 + w], ALU.add,
                     r=[t_ps, t_x[on][ti]], w=[t_x[on][ti]])
    mlp_block(P, 1, x, t_x, mtiles, R1, t_R1, R2, t_R2)


NH = 16
HD = 64
VW = 16 * 65
NEG = -60000.0


def seg_loc(i):
    return (i, 0) if i < 4 else (7 - i, 1)


def phase_L2pre(P, qd, kdst, vdst, cd, t_kd_hp, t_vd_hp, on_written=None):
    s = P.s
    x, t_x = P.x, P.t_x
    mt = P.tiles[:4]
    xn, t_xn = P.R1, P.t_R1
    for ti, (c0, w) in enumerate(mt):
        P.rmsnorm([x[:, k, c0:c0 + w] for k in range(KC)], [t_x[k][ti] for k in range(KC)], "mixn2",
                  [xn[:, k, c0:c0 + w] for k in range(KC)], [t_xn[k][ti] for k in range(KC)], w)
    m0 = P.mark()
    t_c = Tl("l2c")
    s.barrier()
    r2flat = P.R2.rearrange("p k t -> p (k t)")
    r2top = [0]

    def r2alloc(shape, dt):
        n = shape[1]
        ne = n if dt == BF16 else 2 * n
        off = r2top[0]
        r2top[0] += ne
        assert r2top[0] <= KC * NT
        v = r2flat[:, off:off + ne]
        if dt == F32:
            v = v.bitcast(F32)
        return v[0:shape[0]]
    wf = P.sb("wf_bf", [128, KC, 16], BF16)
    s.add("dve", lambda e: e.tensor_copy(out=wf.rearrange("p k h -> p (k h)"), in_=P.ppc("c_w_f", 0, 128)),
          r=[P.t_pp], w=[t_c])
    nbf = P.sb("nbf", [16, 1], F32)
    P.ts("dve", nbf[:, :], P.ppc("c_b_f")[0:16], -1.0, None, ALU.mult, None, r=[P.t_pp], w=[t_c])
    one1 = P.sb("one1", [128, 1], F32)
    s.add("dve", lambda e: e.memset(one1[:, :], 1.0), w=[t_c])
    lbuf = r2alloc([16, NMAIN], F32)
    t_l = Tl("lbuf")
    ones16 = r2alloc([16, SEG], F32)
    s.add("dve", lambda e: e.memset(ones16[:, :], 1.0), w=[t_c])
    for ti, (c0, w) in enumerate(mt):
        ps, t_ps = P.psum("stat", [4, 5])
        for k in range(KC):
            P.mm(ps[0:16, :w], t_ps, wf[:, k, :], xn[:, k, c0:c0 + w], k == 0, k == KC - 1,
                 r=[t_c, t_xn[k][ti]])
        et, t_et = P.tmpf()
        P.act(et[0:16, :w], ps[0:16, :w], AF.Exp, r=[t_ps, t_c], w=[t_et], bias=nbf[:, 0:1], scale=-1.0)
        P.act(lbuf[:, c0:c0 + w], et[0:16, :w], AF.Ln, r=[t_et, t_c], w=[t_l], bias=one1[0:16, 0:1])
    cl = r2alloc([16, NMAIN], F32)
    t_cl = Tl("cl")
    for sg in range(2):
        s.add("dve", (lambda sg: lambda e: e.tensor_tensor_scan(
            cl[:, sg * SEG:(sg + 1) * SEG], ones16[:, :], lbuf[:, sg * SEG:(sg + 1) * SEG], 0.0,
            ALU.mult, ALU.subtract))(sg), r=[t_l, t_c], w=[t_cl])
    t_cd = Tl("cd")
    s.add("sp", lambda e: e.dma_start(out=cd[:, :], in_=cl[:, :]), r=[t_cl], w=[t_cd], dma=True)
    P.out_tiles.append(t_cd)
    P.t_cd = t_cd
    if on_written is not None:
        on_written("c", 0)
    aq = lbuf
    hi = r2alloc([16, NMAIN], BF16)
    lo = r2alloc([16, NMAIN], BF16)
    t_aq = Tl("aq")
    for ti, (c0, w) in enumerate(mt):
        P.ts("dve", aq[:, c0:c0 + w], cl[:, c0:c0 + w], cl[:, c0:c0 + 1], None, ALU.subtract, None,
             r=[t_cl, t_l], w=[t_aq, t_l])
    s.add("dve", lambda e: e.tensor_copy(out=hi[:, :], in_=aq[:, :]), r=[t_aq], w=[t_aq])
    P.tt("dve", lo[:, :], aq[:, :], hi[:, :], ALU.subtract, r=[t_aq], w=[t_aq])
    t_qd = Tl("qd")
    s.add("sp", lambda e: e.dma_start(out=qd[:, 64, :], in_=hi[:, :]), r=[t_aq], w=[t_qd], dma=True)
    s.add("sp", lambda e: e.dma_start(out=qd[:, 65, :], in_=lo[:, :]), r=[t_aq], w=[t_qd], dma=True)
    vst = P.R2.rearrange("p k t -> p (k t)")[:, 0:16 * VW].rearrange("p (h kt c) -> p h kt c", h=16, kt=16)
    t_vst = Tl("vst")
    s.barrier()
    for k in range(KC):
        for ti in range(len(P.t_R2[k])):
            P.t_R2[k][ti] = t_vst
    s.add("dve", lambda e: e.memset(vst[:, :, :, 64:65], 1.0), w=[t_vst])
    ev = 0
    for vs in range(2):
        wt, t_wt = P.next_slab()
        for tb in range(16):
            ps, t_ps = P.psum("acc", [0, 1, 2, 3])
            ti = tb // 4
            for k in range(KC):
                P.mm(ps[:, :], t_ps, xn[:, k, tb * 128:(tb + 1) * 128], wt[:, k * 512:(k + 1) * 512],
                     k == 0, k == KC - 1, r=[t_wt, t_xn[k][ti]])
            dst = vst[:, vs * 8:(vs + 1) * 8, tb, 0:64]
            srcv = ps[:, :].rearrange("p (h c) -> p h c", h=8)
            if ev % 2 == 0:
                P.act(dst, srcv, AF.Identity, r=[t_ps], w=[t_vst])
            else:
                s.add("dve", (lambda dst, srcv: lambda e: e.tensor_copy(out=dst, in_=srcv))(dst, srcv),
                      r=[t_ps], w=[t_vst])
            ev += 1
    vflat = vst.rearrange("p h kt c -> p h (kt c)")
    for hp in range(8):
        s.add("sp", (lambda hp: lambda e: e.dma_start(out=vdst(hp).rearrange("(h p) f -> p h f", h=2),
                                                      in_=vflat[:, 2 * hp:2 * hp + 2, :]))(hp),
              r=[t_vst], w=[t_vd_hp[hp]], dma=True)
        if on_written is not None:
            on_written("v", hp)
    ones64 = P.sb("ones64", [64, 64], BF16)
    s.add("dve", lambda e: e.memset(ones64[:, :], 1.0 / HD), w=[t_c])
    gq = P.sb("gq", [64, 1], F32)
    P.ts("dve", gq[:, :], P.ppc("c_q_norm")[0:64], HD ** -0.5, None, ALU.mult, None, r=[P.t_pp], w=[t_c])
    stq = [P.sb("stq%d" % i, [64, NMAIN], BF16) for i in range(2)]
    stk = [P.sb("stk%d" % i, [66, NMAIN], BF16) for i in range(2)]
    t_stq = [Tl("stq%d" % i) for i in range(2)]
    t_stk = [Tl("stk%d" % i) for i in range(2)]
    for i in range(2):
        s.add("dve", (lambda i: lambda e: e.memset(stk[i][64:66, :], 1.0))(i), w=[t_stk[i]])
    sqh = [P.sb("sqh%d" % i, [64, 512], BF16) for i in range(2)]
    t_sqh = [Tl("sqh%d" % i) for i in range(2)]
    t_kd = Tl("kd")
    cnt = 0
    pending = [None]

    def finish_unit(which, h, stg, t_stg, gcol, ps, t_ps, c0, w, is_last):
        sq_, t_sq_ = sqh[finish_unit.cnt % 2], t_sqh[finish_unit.cnt % 2]
        finish_unit.cnt += 1
        P.act(sq_[:, :w], ps[0:64, :w], AF.Square, r=[t_ps], w=[t_sq_])
        ps2, t_ps2 = P.psum("stat", [4, 5])
        P.mm(ps2[0:64, :w], t_ps2, ones64[:, :], sq_[:, :w], True, True, r=[t_sq_, t_c])
        P.act(P.rt[0:64, :w], ps2[0:64, :w], AF.Ln, r=[t_ps2, P.t_const], w=[P.t_rt],
              bias=P.epsc[0:64, 0:1])
        P.act(P.rstd[0:64, :w], P.rt[0:64, :w], AF.Exp, r=[P.t_rt], w=[P.t_rstd], scale=-0.5)
        P.stt(stg[0:64, c0:c0 + w], ps[0:64, :w], gcol, P.rstd[0:64, :w], ALU.mult, ALU.mult,
              r=[t_ps, P.t_rstd, t_c, P.t_pp], w=[t_stg])
        if is_last:
            if which == 0:
                s.add("sp", lambda e: e.dma_start(out=qd[h, 0:64, :], in_=stg[0:64, :]),
                      r=[t_stg], w=[t_qd], dma=True)
            else:
                s.add("sp", lambda e: e.dma_start(out=kdst(h)[0:64, :], in_=stg[0:64, :]),
                      r=[t_stg], w=[t_kd_hp[h // 2]], dma=True)
                s.add("sp", lambda e: e.dma_start(out=kdst(h)[64:66, :], in_=stg[64:66, :]),
                      r=[t_stg], w=[t_kd_hp[h // 2]], dma=True)
                if on_written is not None and h % 2 == 1:
                    on_written("k", h // 2)
    finish_unit.cnt = 0
    for which in range(2):
        for sl in range(2):
            wt, t_wt = P.next_slab()
            for hl in range(8):
                h = sl * 8 + hl
                if which == 0:
                    stg, t_stg = stq[h % 2], t_stq[h % 2]
                    gcol = gq[:, 0:1]
                else:
                    stg, t_stg = stk[h % 2], t_stk[h % 2]
                    gcol = P.ppc("c_k_norm")[0:64]
                for ti, (c0, w) in enumerate(mt):
                    ps, t_ps = P.psum("acc", [0, 1, 2, 3])
                    for k in range(KC):
                        P.mm(ps[0:64, :w], t_ps, wt[:, k * 512 + hl * 64: k * 512 + (hl + 1) * 64],
                             xn[:, k, c0:c0 + w], k == 0, k == KC - 1, r=[t_wt, t_xn[k][ti]])
                    if pending[0] is not None:
                        finish_unit(*pending[0])
                    pending[0] = (which, h, stg, t_stg, gcol, ps, t_ps, c0, w, ti == len(mt) - 1)
    finish_unit(*pending[0])
    P.t_qd = t_qd
    P.t_cd = t_cd
    P.release(m0)


def phase_L2attn(P, qd, kown, vown, cown, kallf, vallf, call, flexmd, seld, cmaskd, dep):
    s = P.s
    x, t_x = P.x, P.t_x
    mt = P.tiles[:4]
    m0 = P.mark()
    t_k = Tl("l2a_const")
    Tt = P.sb("Tt", [16, 8], F32)
    t_T = Tl("Tt")
    for i in range(8):
        r_, part = seg_loc(i)
        col = part * SEG + SEG - 1
        s.add("sp", (lambda i, r_, col: lambda e: e.dma_start(out=Tt[:, i:i + 1], in_=call[r_, :, col:col + 1], allow_slow_non_contiguous=True))(i, r_, col),
              r=[dep["call"]], w=[t_T], dma=True)
    sel = P.sb("sel", [16, 16], F32)
    s.add("sp", lambda e: e.dma_start(out=sel[:, :], in_=seld[:, :]), w=[t_k], dma=True)
    flexm = P.sb("flexm", [128, 8], F32)
    s.add("sp", lambda e: e.dma_start(out=flexm[:, :], in_=flexmd[:, :]), w=[t_k], dma=True)
    cmask = P.sb("cmask", [128, 4, 512], BF16)
    s.add("pool", lambda e: e.dma_start(out=cmask[:, :, :], in_=cmaskd.rearrange("i p q -> p i q")), w=[t_k], dma=True)
    ones8 = P.sb("ones8", [16, 8], F32)
    s.add("dve", lambda e: e.memset(ones8[:, :], 1.0), w=[t_k])
    Pin = P.sb("Pin", [16, 8], F32)
    Pex = P.sb("Pex", [16, 8], F32)
    t_P = Tl("Pex")
    s.add("dve", lambda e: e.tensor_tensor_scan(Pin[:, :], ones8[:, :], Tt[:, :], 0.0, ALU.mult, ALU.add),
          r=[t_T, t_k], w=[t_P])
    P.tt("dve", Pex[:, :], Pin[:, :], Tt[:, :], ALU.subtract, r=[t_P, t_T], w=[t_P])
    PAB = P.sb("PAB", [16, 2], F32)
    ptmp = P.sb("ptmp", [16, 8], F32)
    for sg in range(2):
        P.tt("dve", ptmp[:, :], Pex[:, :], sel[:, sg * 8:(sg + 1) * 8], ALU.mult, r=[t_P, t_k], w=[t_P])
        s.add("dve", (lambda sg: lambda e: e.reduce_sum(PAB[:, sg:sg + 1], ptmp[:, :], mybir.AxisListType.X))(sg),
              r=[t_P], w=[t_P])
    clq = P.sb("clq", [16, 4], F32)
    t_clq = Tl("clq")
    for qt in range(4):
        s.add("sp", (lambda qt: lambda e: e.dma_start(out=clq[:, qt:qt + 1], in_=cown[:, qt * 512:qt * 512 + 1], allow_slow_non_contiguous=True))(qt),
              r=[dep["cd"]], w=[t_clq], dma=True)
    CQ = P.sb("CQ", [16, 4], F32)
    for qt in range(4):
        P.tt("dve", CQ[:, qt:qt + 1], clq[:, qt:qt + 1], PAB[:, qt // 2:qt // 2 + 1], ALU.add,
             r=[t_clq, t_P], w=[t_P])
    CQd = P.sb("CQd", [16, 64], F32)
    negI = P.sb("negI", [16, 64], F32)
    ones16c = P.sb("ones16c", [16, 128], F32)
    s.add("dve", lambda e: e.memset(ones16c[:, :], 1.0), w=[t_k])
    for qt in range(4):
        P.ts("dve", CQd[:, qt * 16:(qt + 1) * 16], P.ident_f[0:16, 0:16], CQ[:, qt:qt + 1], None, ALU.mult, None,
             r=[P.t_const, t_P], w=[t_P])
        P.ts("dve", negI[:, qt * 16:(qt + 1) * 16], P.ident_f[0:16, 0:16], -1.0, None, ALU.mult, None,
             r=[P.t_const], w=[t_k])
    biasall = P.sb("biasall", [128, 72, 64], F32)
    t_bias = Tl("biasall")
    cgk = [P.sb("cgk%d" % i, [16, SEG], F32) for i in range(2)]
    t_cgk = [Tl("cgk%d" % i) for i in range(2)]
    for ks in range(9):
        cg, t_cg = cgk[ks % 2], t_cgk[ks % 2]
        if ks < 7:
            r_, part = seg_loc(ks)
            src = call[r_, :, part * SEG:(part + 1) * SEG]
            pcol = Pex[:, ks:ks + 1]
        else:
            src = cown[:, (ks - 7) * SEG:(ks - 6) * SEG]
            pcol = PAB[:, ks - 7:ks - 6]
        s.add("sp", (lambda cg, src: lambda e: e.dma_start(out=cg[:, :], in_=src))(cg, src),
              r=[dep["call"], dep["cd"]], w=[t_cg], dma=True)
        P.ts("dve", cg[:, :], cg[:, :], pcol, None, ALU.add, None, r=[t_cg, t_P], w=[t_cg])
        ps, t_ps = P.psum("acc", [4, 5, 6, 7])
        for kt in range(8):
            P.mm(ps[:, kt * 64:(kt + 1) * 64], t_ps, cg[:, kt * 128:(kt + 1) * 128], negI[:, :], True, False,
                 r=[t_cg, t_k])
            P.mm(ps[:, kt * 64:(kt + 1) * 64], t_ps, ones16c[:, :], CQd[:, :], False, True, r=[t_k, t_P])
        s.add("dve", (lambda ks, ps: lambda e: e.tensor_copy(
            out=biasall[:, ks * 8:(ks + 1) * 8, :], in_=ps[:, :].rearrange("p (a b) -> p a b", a=8)))(ks, ps),
            r=[t_ps], w=[t_bias])
    flexbias = P.sb("flexbias", [128, 3, 8, 32], F32)
    for p in range(3):
        P.ts("dve", flexbias[:, p, :, :], biasall[:, p * 8:(p + 1) * 8, 0:32], flexm[:, p:p + 1], None,
             ALU.mult, None, r=[t_bias, t_k], w=[t_bias])
        P.stt(flexbias[:, p, :, :], biasall[:, (6 - p) * 8:(7 - p) * 8, 32:64], flexm[:, 3 + p:4 + p],
              flexbias[:, p, :, :], ALU.mult, ALU.add, r=[t_bias, t_k], w=[t_bias])
    sel65 = P.sb("sel65", [65, 64], F32)
    s.add("dve", lambda e: e.memset(sel65[:, :], 0.0), w=[t_k])
    s.add("dve", lambda e: e.memset(sel65[64:65, :], 1.0), r=[t_k], w=[t_k])
    NRING = 4
    kcs = [P.sb("kc%d" % i, [66, SEG], BF16) for i in range(NRING)]
    vcs = [P.sb("vc%d" % i, [128, 8, 65], BF16) for i in range(NRING)]
    t_kv = [Tl("kv%d" % i) for i in range(NRING)]
    qhs = [P.sb("qh%d" % i, [66, NMAIN], BF16) for i in range(2)]
    t_qh = [Tl("qh%d" % i) for i in range(2)]
    qfb = [P.sb("qflex%d" % i, [66, SEG], BF16) for i in range(2)]
    t_qfb = [Tl("qflex%d" % i) for i in range(2)]
    kfb = [P.sb("kflex%d" % i, [66, SEG], BF16) for i in range(2)]
    vfb = [P.sb("vflex%d" % i, [128, 8, 65], BF16) for i in range(2)]
    t_kfb = [Tl("kvflex%d" % i) for i in range(2)]
    prep_i = [0]
    NPT = 4
    pts = [P.sb("pt%d" % i, [128, 512], BF16) for i in range(NPT)]
    t_pt = [Tl("pt%d" % i) for i in range(NPT)]
    osb = [P.sb("osb%d" % i, [65, 512], F32) for i in range(2)]
    t_osb = [Tl("osb%d" % i) for i in range(2)]
    rden = P.sb("rden", [64, 512], F32)
    t_rden = Tl("rden")
    oT = P.sb("oT", [64, 4, NMAIN], BF16)
    t_oT = [[Tl("oT%d_%d" % (hl, qt)) for qt in range(4)] for hl in range(4)]
    fsum = P.sq.rearrange("p k t -> p (k t)").bitcast(F32)[0:65, 0:2048].rearrange("p (a t) -> p a t", a=4)
    t_fs = [Tl("fsum%d" % i) for i in range(4)]
    ring_i = [0]
    pt_i = [0]
    ob_i = [0]

    def ring_next():
        i = ring_i[0] % NRING
        ring_i[0] += 1
        return kcs[i], vcs[i], t_kv[i]

    def load_chunk(h, ks):
        kc, vc, t = ring_next()
        if ks < 7:
            r_, part = seg_loc(ks)
            ksrc = kallf(r_, h)[:, part * SEG:(part + 1) * SEG]
            vsrc = vallf(r_, h)[:, part * 8 * 65:(part + 1) * 8 * 65]
            rk, rv = [dep["kall"][h // 2]], [dep["vall"][h // 2]]
        else:
            part = ks - 7
            ksrc = kown(h)[:, part * SEG:(part + 1) * SEG]
            vsrc = vown(h)[:, part * 8 * 65:(part + 1) * 8 * 65]
            rk, rv = [dep["kd"][h // 2]], [dep["vd"][h // 2]]
        s.add("sp", lambda e: e.dma_start(out=kc[0:64, :], in_=ksrc[0:64]), r=rk, w=[t], dma=True)
        s.add("sp", lambda e: e.dma_start(out=kc[64:66, :], in_=ksrc[64:66]), r=rk, w=[t], dma=True)
        s.add("sp", lambda e: e.dma_start(out=vc.rearrange("p a b -> p (a b)"), in_=vsrc), r=rv, w=[t], dma=True)
        return kc, vc, t

    SKEW = 3

    def run_blocks(blist):
        pend = []

        def emit_pv(item):
            (vc, t_kvc, kt, pt, tpt, bank, st, sp_) = item
            P.mm(P.ps[bank][0:65, :], P.t_ps[bank], vc[:, kt, :], pt[:, :], st, sp_, r=[t_kvc, tpt])
        for (kc, vc, t_kvc, kt, q_ap, t_q, bcol, mi, bank, st, sp_) in blist:
            psS, t_psS = P.psum("S", [4, 5, 6, 7])
            P.mm(psS[:, :], t_psS, kc[0:66, kt * 128:(kt + 1) * 128], q_ap, True, True, r=[t_kvc, t_q])
            pt, tpt = pts[pt_i[0] % NPT], t_pt[pt_i[0] % NPT]
            pt_i[0] += 1
            if mi is None:
                P.act(pt[:, :], psS[:, :], AF.Exp, r=[t_psS, t_bias], w=[tpt], bias=bcol)
            else:
                sb_, tsb = P.tmpf()
                P.tt("dve", sb_[:, :], psS[:, :], cmask[:, mi, :], ALU.add, r=[t_psS, t_k], w=[tsb])
                P.act(pt[:, :], sb_[:, :], AF.Exp, r=[tsb, t_bias], w=[tpt], bias=bcol)
            pend.append((vc, t_kvc, kt, pt, tpt, bank, st, sp_))
            if len(pend) > SKEW:
                emit_pv(pend.pop(0))
        while pend:
            emit_pv(pend.pop(0))

    def load_q(h):
        qh, tq = qhs[h % 2], t_qh[h % 2]
        s.add("sp", (lambda h, qh: lambda e: e.dma_start(out=qh[0:64, :], in_=qd[h, 0:64, :]))(h, qh),
              r=[dep["qd"]], w=[tq], dma=True)
        s.add("sp", (lambda h, qh: lambda e: e.dma_start(out=qh[64:66, :], in_=qd[h, 64:66, :]))(h, qh),
              r=[dep["qd"]], w=[tq], dma=True)

    def prepare(h, p):
        i = prep_i[0] % 2
        prep_i[0] += 1
        qh, tq = qhs[h % 2], t_qh[h % 2]
        mA, mB = flexm[:, p:p + 1], flexm[:, 3 + p:4 + p]
        kA, vA, tA = load_chunk(h, p)
        kB, vB, tB = load_chunk(h, 6 - p)
        kf, vf, tf, qf, t_qf = kfb[i], vfb[i], t_kfb[i], qfb[i], t_qfb[i]
        P.ts("dve", kf[:, :], kA[:, :], mA[0:66], None, ALU.mult, None, r=[tA, t_k], w=[tf])
        P.stt(kf[:, :], kB[:, :], mB[0:66], kf[:, :], ALU.mult, ALU.add, r=[tB, t_k, tf], w=[tf])
        vff, vAf, vBf = (v_.rearrange("p a b -> p (a b)") for v_ in (vf, vA, vB))
        P.ts("dve", vff, vAf, mA, None, ALU.mult, None, r=[tA, t_k, tf], w=[tf])
        P.stt(vff, vBf, mB, vff, ALU.mult, ALU.add, r=[tB, t_k, tf], w=[tf])
        P.ts("dve", qf[:, :], qh[:, 0:SEG], mA[0:66], None, ALU.mult, None, r=[tq, t_k], w=[t_qf])
        P.stt(qf[:, :], qh[:, SEG:2 * SEG], mB[0:66], qf[:, :], ALU.mult, ALU.add, r=[tq, t_k, t_qf], w=[t_qf])
        return kf, vf, tf, qf, t_qf

    load_q(0)
    prepared = prepare(0, 0)
    for h in range(NH):
        hl = h % 4
        qh, tq = qhs[h % 2], t_qh[h % 2]
        for p in range(3):
            mA, mB = flexm[:, p:p + 1], flexm[:, 3 + p:4 + p]
            kf, vf, tf, qf, t_qf = prepared
            if p < 2:
                prepared = prepare(h, p + 1)
            elif h + 1 < NH:
                load_q(h + 1)
                prepared = prepare(h + 1, 0)
            bk = 2 * (p % 2)
            bl = []
            for kt in range(8):
                for half in range(2):
                    bl.append((kf, vf, tf, kt, qf[0:66, half * 512:(half + 1) * 512], t_qf,
                               flexbias[:, p, kt, half * 16 + h: half * 16 + h + 1], None, bk + half, kt == 0, kt == 7))
            run_blocks(bl)
            for half in range(2):
                for dest, m in ((0, mA), (1, mB)):
                    fi = 2 * dest + half
                    if p == 0:
                        P.ts("dve", fsum[:, fi, :], P.ps[bk + half][0:65, :], m[0:65], None, ALU.mult, None,
                             r=[P.t_ps[bk + half], t_k], w=[t_fs[fi]])
                    else:
                        P.stt(fsum[:, fi, :], P.ps[bk + half][0:65, :], m[0:65], fsum[:, fi, :], ALU.mult, ALU.add,
                              r=[P.t_ps[bk + half], t_k, t_fs[fi]], w=[t_fs[fi]])
        bl = []
        for ks in range(4):
            kc, vc, tkv = load_chunk(h, ks)
            for kt in range(8):
                for qt in (2, 3):
                    bl.append((kc, vc, tkv, kt, qh[0:66, qt * 512:(qt + 1) * 512], tq,
                               biasall[:, ks * 8 + kt, qt * 16 + h: qt * 16 + h + 1], None, qt,
                               ks == 0 and kt == 0, False))
        for sg in range(2):
            kc, vc, tkv = load_chunk(h, 7 + sg)
            for half in range(2):
                qt = 2 * sg + half
                nk = 4 * (half + 1)
                for kt in range(nk):
                    mi = (kt - 4 * half) if kt >= 4 * half else None
                    bl.append((kc, vc, tkv, kt, qh[0:66, qt * 512:(qt + 1) * 512], tq,
                               biasall[:, (7 + sg) * 8 + kt, qt * 16 + h: qt * 16 + h + 1], mi, qt,
                               sg == 0 and kt == 0, kt == nk - 1))
        run_blocks(bl)
        for qt in range(4):
            ob, tob = osb[ob_i[0] % 2], t_osb[ob_i[0] % 2]
            ob_i[0] += 1
            fi = 2 * (qt // 2) + qt % 2
            P.tt("dve", ob[:, :], P.ps[qt][0:65, :], fsum[:, fi, :], ALU.add, r=[P.t_ps[qt], t_fs[fi]], w=[tob])
            P.mm(P.ps[7][0:64, :], P.t_ps[7], sel65[:, :], ob[:, :], True, True, r=[t_k, tob])
            P.act(rden[:, :], P.ps[7][0:64, :], AF.Ln, r=[P.t_ps[7]], w=[t_rden])
            P.act(rden[:, :], rden[:, :], AF.Exp, r=[t_rden], w=[t_rden], scale=-1.0)
            P.tt("dve", oT[:, hl, qt * 512:(qt + 1) * 512], ob[0:64, :], rden[:, :], ALU.mult,
                 r=[tob, t_rden], w=[t_oT[hl][qt]])
        if hl == 3:
            wt, t_wt = P.next_slab()
            for ti, (c0, w) in enumerate(mt):
                for n in range(KC):
                    ps, t_ps = P.psum("S", [4, 5, 6, 7])
                    for j in range(4):
                        P.mm(ps[:, :w], t_ps, wt[0:64, j * 1024 + n * 128: j * 1024 + (n + 1) * 128],
                             oT[:, j, c0:c0 + w], j == 0, j == 3, r=[t_wt, t_oT[j][ti]])
                    P.tt("dve", x[:, n, c0:c0 + w], ps[:, :w], x[:, n, c0:c0 + w], ALU.add,
                         r=[t_ps, t_x[n][ti]], w=[t_x[n][ti]])
    P.release(m0)


GH = 4
CH = 128
NCH = SEG // CH


def gla_seg(P, sg, mode, sinit=None, gout=None, dout=None, tri=None, sinit_t=None, pre_chunk=None):
    s = P.s
    full = mode == "full"
    x, t_x = P.x, P.t_x
    tis = [2 * sg, 2 * sg + 1]
    m0 = P.mark()
    t_g = Tl("gla_c")
    xn = P.sb("gxn", [128, KC, SEG], BF16)
    t_xn = [[Tl("gxn%d_%d" % (k, i)) for i in range(2)] for k in range(KC)]
    for i in range(2):
        c0 = sg * SEG + i * 512
        P.rmsnorm([x[:, k, c0:c0 + 512] for k in range(KC)], [t_x[k][tis[i]] for k in range(KC)], "mixn3",
                  [xn[:, k, i * 512:(i + 1) * 512] for k in range(KC)], [t_xn[k][i] for k in range(KC)], 512)
    wg1 = P.sb("wg1", [128, KC, 16], BF16)
    s.add("dve", lambda e: e.tensor_copy(out=wg1.rearrange("p k h -> p (k h)"), in_=P.ppc("d_w_g1", 0, 128)),
          r=[P.t_pp], w=[t_g])
    wg2 = P.sb("wg2", [16, 512], BF16)
    s.add("dve", lambda e: e.tensor_copy(out=wg2[:, :], in_=P.ppc("d_w_g2", 0, 512)[0:16]), r=[P.t_pp], w=[t_g])
    nbg = P.sb("nbg", [128, 4], F32)
    P.ts("dve", nbg[:, :], P.ppc("d_b_g", 0, 4), -1.0, None, ALU.mult, None, r=[P.t_pp], w=[t_g])
    one1 = P.sb("gone1", [128, 1], F32)
    s.add("dve", lambda e: e.memset(one1[:, :], 1.0), w=[t_g])
    ones512 = P.sb("ones512", [128, 512], F32)
    s.add("dve", lambda e: e.memset(ones512[:, :], 1.0), w=[t_g])
    vtok = P.sb("vtok", [128, NCH, D], BF16)
    t_v = [Tl("vtok%d" % c) for c in range(NCH)]
    ev = 0
    for vs in range(2):
        wt, t_wt = P.next_slab()
        for c in range(NCH):
            ps, t_ps = P.psum("acc", [0, 1, 2, 3])
            for k in range(KC):
                P.mm(ps[:, :], t_ps, xn[:, k, c * CH:(c + 1) * CH], wt[:, k * 512:(k + 1) * 512],
                     k == 0, k == KC - 1, r=[t_wt, t_xn[k][c // 4]])
            dst = vtok[:, c, vs * 512:(vs + 1) * 512]
            if ev % 2 == 0:
                P.act(dst, ps[:, :], AF.Identity, r=[t_ps], w=[t_v[c]])
            else:
                s.add("dve", (lambda dst, ps: lambda e: e.tensor_copy(out=dst, in_=ps[:, :]))(dst, ps),
                      r=[t_ps], w=[t_v[c]])
            ev += 1
    g1 = P.sb("g1", [16, SEG], BF16)
    t_g1 = Tl("g1")
    for i in range(2):
        ps, t_ps = P.psum("stat", [4, 5])
        for k in range(KC):
            P.mm(ps[0:16, :], t_ps, wg1[:, k, :], xn[:, k, i * 512:(i + 1) * 512], k == 0, k == KC - 1,
                 r=[t_g, t_xn[k][i]])
        P.act(g1[:, i * 512:(i + 1) * 512], ps[0:16, :], AF.Identity, r=[t_ps], w=[t_g1])
    if full:
        wq, t_wq = P.next_slab()
    wk, t_wk = P.next_slab(hold=2 if full else 1)
    if full:
        qt_ = P.sb("gqt", [128, GH, SEG], BF16)
    kt_ = P.sb("gkt", [128, GH, SEG], BF16) if full else None
    kd_ = P.sb("gkd", [128, GH, SEG], BF16)
    t_q = [[Tl("gq%d_%d" % (h, i)) for i in range(2)] for h in range(GH)]
    t_kk = [[Tl("gk%d_%d" % (h, i)) for i in range(2)] for h in range(GH)]
    dl = P.sb("gdl", [128, GH, NCH], F32)
    t_dl = [Tl("gdl%d" % h) for h in range(GH)]
    dtot = P.sb("gdtot", [128, GH], F32)
    t_dt = Tl("gdtot")
    csb = [P.sb("gcs0", [128, 512], F32)] * 2
    t_cs = [Tl("gcs0")] * 2
    e4 = [P.sb("ge4_%d" % i, [128, 4], F32) for i in range(2)]
    ep4 = P.sb("gep4", [128, 4], F32)
    dd4 = P.sb("gdd4", [128, 4], F32)
    t_e = Tl("ge4")
    fb = [P.sb("gfb%d" % i, [128, 512], F32) for i in range(3)]
    t_fb = [Tl("gfb%d" % i) for i in range(3)]
    qscale = float(GLA_DKH ** -0.5)
    for h in range(GH):
        for i in range(2):
            lc = i * 512
            psz, t_psz = P.psum("stat", [4, 5])
            P.mm(psz[:, :], t_psz, wg2[0:16, h * 128:(h + 1) * 128], g1[0:16, lc:lc + 512], True, True,
                 r=[t_g, t_g1])
            P.act(fb[0][:, :], psz[:, :], AF.Exp, r=[t_psz, t_g], w=[t_fb[0]], bias=nbg[:, h:h + 1], scale=-1.0)
            P.act(fb[0][:, :], fb[0][:, :], AF.Ln, r=[t_fb[0], t_g], w=[t_fb[0]], bias=one1[:, 0:1])
            cs, tcs = csb[i], t_cs[i]
            init = 0.0 if i == 0 else e4[0][:, 3:4]
            s.add("dve", (lambda cs, init: lambda e: e.tensor_tensor_scan(
                cs[:, :], ones512[:, :], fb[0][:, :], init, ALU.mult, ALU.add))(cs, init),
                r=[t_fb[0], t_g] + ([t_e] if i else []), w=[tcs])
            csv = cs.rearrange("p (c t) -> p c t", t=CH)
            E4 = e4[i]
            s.add("dve", (lambda E4, csv: lambda e: e.tensor_copy(out=E4[:, :], in_=csv[:, :, CH - 1]))(E4, csv),
                  r=[tcs], w=[t_e])
            if i == 0:
                s.add("dve", lambda e: e.memset(ep4[:, 0:1], 0.0), r=[t_e], w=[t_e])
            else:
                s.add("dve", lambda e: e.tensor_copy(out=ep4[:, 0:1], in_=e4[0][:, 3:4]), r=[t_e], w=[t_e])
            s.add("dve", (lambda E4: lambda e: e.tensor_copy(out=ep4[:, 1:4], in_=E4[:, 0:3]))(E4), r=[t_e], w=[t_e])
            bbv = fb[1].rearrange("p (c t) -> p c t", t=CH)
            bb2v = fb[2].rearrange("p (c t) -> p c t", t=CH)
            s.add("dve", (lambda csv: lambda e: e.tensor_tensor(
                bbv, csv, ep4[:, :].unsqueeze(2).to_broadcast([128, 4, CH]), ALU.subtract))(csv),
                r=[tcs, t_e], w=[t_fb[1]])
            s.add("dve", (lambda csv, E4: lambda e: e.tensor_tensor(
                bb2v, E4[:, :].unsqueeze(2).to_broadcast([128, 4, CH]), csv, ALU.subtract))(csv, E4),
                r=[tcs, t_e], w=[t_fb[2]])
            P.tt("dve", dd4[:, :], E4[:, :], ep4[:, :], ALU.subtract, r=[t_e], w=[t_e])
            P.act(dl[:, h, i * 4:(i + 1) * 4], dd4[:, :], AF.Exp, r=[t_e], w=[t_dl[h]], scale=-1.0 / GLA_TAU)
            if i == 1:
                P.act(dtot[:, h:h + 1], E4[:, 3:4], AF.Exp, r=[t_e], w=[t_dt], scale=-1.0 / GLA_TAU)
            psk, t_psk = P.psum("acc", [0, 1, 2, 3])
            for k in range(KC):
                P.mm(psk[:, :], t_psk, wk[:, k * 512 + h * 128: k * 512 + (h + 1) * 128],
                     xn[:, k, lc:lc + 512], k == 0, k == KC - 1, r=[t_wk, t_xn[k][i]])
            P.act(fb[2][:, :], fb[2][:, :], AF.Exp, r=[t_fb[2]], w=[t_fb[2]], scale=-1.0 / GLA_TAU)
            P.tt("dve", kd_[:, h, lc:lc + 512], psk[:, :], fb[2][:, :], ALU.mult, r=[t_psk, t_fb[2]], w=[t_kk[h][i]])
            if full:
                P.act(fb[0][:, :], fb[1][:, :], AF.Exp, r=[t_fb[1]], w=[t_fb[0]], scale=1.0 / GLA_TAU)
                P.tt("dve", kt_[:, h, lc:lc + 512], psk[:, :], fb[0][:, :], ALU.mult,
                     r=[t_psk, t_fb[0]], w=[t_kk[h][i]])
                P.act(fb[1][:, :], fb[1][:, :], AF.Exp, r=[t_fb[1]], w=[t_fb[1]], scale=-1.0 / GLA_TAU)
                psq, t_psq = P.psum("acc", [0, 1, 2, 3])
                for k in range(KC):
                    P.mm(psq[:, :], t_psq, wq[:, k * 512 + h * 128: k * 512 + (h + 1) * 128],
                         xn[:, k, lc:lc + 512], k == 0, k == KC - 1, r=[t_wq, t_xn[k][i]])
                P.stt(qt_[:, h, lc:lc + 512], psq[:, :], qscale, fb[1][:, :], ALU.mult, ALU.mult,
                      r=[t_psq, t_fb[1]], w=[t_q[h][i]])
    Sf = [P.sb("gS%d" % h, [128, GLA_DVH], F32) for h in range(GH)]
    Sb = [P.sb("gSb%d" % h, [128, GLA_DVH], BF16) for h in range(GH)]
    t_S = [Tl("gS%d" % h) for h in range(GH)]
    t_Sb = [Tl("gSb%d" % h) for h in range(GH)]
    if full:
        oall = P.sb("goall", [128, KC, SEG], BF16)
        t_o = [[Tl("go%d_%d" % (c8, i)) for i in range(2)] for c8 in range(KC)]
        trisb = P.sb("gtri", [128, CH], F32)
        s.add("sp", lambda e: e.dma_start(out=trisb[:, :], in_=tri[:, :]), w=[t_g], dma=True)
    cnt = 0
    if pre_chunk is not None:
        sinit_t = pre_chunk()
        P.t_sinit = sinit_t
    for h in range(GH):
        if full:
            s.add("sp", (lambda h: lambda e: e.dma_start(out=Sf[h][:, :], in_=sinit[h, :, :]))(h),
                  r=[sinit_t], w=[t_S[h]], dma=True)
            s.add("act", (lambda h: lambda e: e.activation(Sb[h][:, :], Sf[h][:, :], AF.Identity))(h),
                  r=[t_S[h]], w=[t_Sb[h]])
        else:
            s.add("dve", (lambda h: lambda e: e.memset(Sf[h][:, :], 0.0))(h), w=[t_S[h]])
    fbb = [fb[j].bitcast(BF16).rearrange("p (i h c) -> p i h c", i=2, h=GH) for j in range(2)]
    attm4 = [fbb[0][:, i] for i in range(2)]
    kdtok4 = [fbb[1][:, i] for i in range(2)]
    t_attm4 = [[Tl("gattm4_%d_%d" % (i, h)) for h in range(GH)] for i in range(2)]
    t_kdtok4 = [[Tl("gkdtok4_%d_%d" % (i, h)) for h in range(GH)] for i in range(2)]
    s.add("dve", lambda e: e.memset(fb[0][:, 0:8], 0.0),
          w=[t_fb[0], t_fb[1]] + [t for l_ in (t_attm4 + t_kdtok4) for t in l_])
    for c in range(NCH):
        i = c // 4
        cc = slice(c * CH, (c + 1) * CH)
        par = c % 2
        psA, t_psA = P.ps[4], P.t_ps[4]
        psT, t_psT = P.ps[5], P.t_ps[5]
        psT_b = psT.bitcast(BF16)
        for h in range(GH):
            if full:
                P.mm(psA[:, h * CH:(h + 1) * CH], t_psA, kt_[:, h, cc], qt_[:, h, cc], True, True,
                     r=[t_kk[h][i], t_q[h][i]])
            s.add("pe", (lambda h, cc, psT_b: lambda e: e.transpose(psT_b[:, h * CH:(h + 1) * CH], kd_[:, h, cc], P.ident_b[:, :]))(h, cc, psT_b),
                  r=[t_kk[h][i], P.t_const], w=[t_psT])
        for h in range(GH):
            if full:
                P.tt("dve", attm4[par][:, h, :], psA[:, h * CH:(h + 1) * CH], trisb[:, :], ALU.mult,
                     r=[t_psA, t_g], w=[t_attm4[par][h]])
            s.add("dve", (lambda h, dst, psT_b: lambda e: e.tensor_copy(out=dst, in_=psT_b[:, h * CH:(h + 1) * CH]))(h, kdtok4[par][:, h, :], psT_b),
                  r=[t_psT], w=[t_kdtok4[par][h]])
        for h in range(GH):
            if full:
                bo = 6 + (h % 2)
                pso, t_pso = P.ps[bo], P.t_ps[bo]
                oc = (h // 2) * 2 * CH if False else 0
                for ec in range(2):
                    P.mm(pso[:, ec * CH:(ec + 1) * CH], t_pso,
                         vtok[:, c, h * GLA_DVH + ec * 128: h * GLA_DVH + (ec + 1) * 128], attm4[par][:, h, :],
                         True, False, r=[t_v[c], t_attm4[par][h]])
                    P.mm(pso[:, ec * CH:(ec + 1) * CH], t_pso, Sb[h][:, ec * 128:(ec + 1) * 128], qt_[:, h, cc],
                         False, True, r=[t_Sb[h], t_q[h][i]])
                s.add("act", (lambda h, cc, pso: lambda e: e.activation(
                    oall[:, 2 * h:2 * h + 2, cc], pso[:, 0:2 * CH].rearrange("p (a t) -> p a t", a=2), AF.Identity))(h, cc, pso),
                    r=[t_pso], w=[t_o[2 * h][i], t_o[2 * h + 1][i]])
            bs = h % 4
            pss, t_pss = P.ps[bs], P.t_ps[bs]
            P.mm(pss[:, 0:GLA_DVH], t_pss, kdtok4[par][:, h, :], vtok[:, c, h * GLA_DVH:(h + 1) * GLA_DVH], True, True,
                 r=[t_kdtok4[par][h], t_v[c]])
            P.stt(Sf[h][:, :], Sf[h][:, :], dl[:, h, c:c + 1], pss[:, 0:GLA_DVH], ALU.mult, ALU.add,
                  r=[t_S[h], t_dl[h], t_pss], w=[t_S[h]])
            if full and c < NCH - 1:
                s.add("act", (lambda h: lambda e: e.activation(Sb[h][:, :], Sf[h][:, :], AF.Identity))(h),
                      r=[t_S[h]], w=[t_Sb[h]])
    if not full:
        for h in range(GH):
            t_go = Tl("gout")
            s.add("sp", (lambda h: lambda e: e.dma_start(out=gout[h, :, :], in_=Sf[h][:, :]))(h),
                  r=[t_S[h]], w=[t_go], dma=True)
            P.out_tiles.append(t_go)
    if not full:
        t_do = Tl("dout")
        s.add("sp", lambda e: e.dma_start(out=dout[:, :], in_=dtot[:, :]), r=[t_dt], w=[t_do], dma=True)
        P.out_tiles.append(t_do)
        P.release(m0)
        return
    if getattr(P, "dbg", None) is not None and sg == 0:
        dd = P.dbg
        allq = [t_q[h][i] for h in range(GH) for i in range(2)]
        allk = [t_kk[h][i] for h in range(GH) for i in range(2)]
        allo = [t_o[c8][i] for c8 in range(KC) for i in range(2)]
        for nm, buf, rd in (("dbg_qt", qt_, allq), ("dbg_kt", kt_, allk), ("dbg_kd", kd_, allk),
                            ("dbg_v", vtok, t_v), ("dbg_o", oall, allo)):
            t_d = Tl(nm)
            s.add("sp", (lambda nm, buf: lambda e: e.dma_start(out=dd[nm], in_=buf))(nm, buf), r=rd, w=[t_d], dma=True)
            P.out_tiles.append(t_d)
        t_d = Tl("dbg_dl")
        s.add("sp", lambda e: e.dma_start(out=dd["dbg_dl"], in_=dl), r=t_dl, w=[t_d], dma=True)
        P.out_tiles.append(t_d)
    ones256 = P.sb("ones256", [128, 128], BF16)
    s.add("dve", lambda e: e.memset(ones256[:, :], 1.0 / GLA_DVH), w=[t_g])
    for i in range(2):
        lc = i * 512
        for h in range(GH):
            for ec in range(2):
                P.act(P.sq[:, ec, :], oall[:, 2 * h + ec, lc:lc + 512], AF.Square, r=[t_o[2 * h + ec][i]], w=[P.t_sq[ec]])
            ps, t_ps = P.psum("stat", [4, 5])
            for ec in range(2):
                P.mm(ps[:, :], t_ps, ones256[:, :], P.sq[:, ec, :], ec == 0, ec == 1, r=[P.t_sq[ec], t_g])
            P.act(P.rt[:, :], ps[:, :], AF.Ln, r=[t_ps, P.t_const], w=[P.t_rt], bias=P.epsc[:, 0:1])
            P.act(P.rstd[:, :], P.rt[:, :], AF.Exp, r=[P.t_rt], w=[P.t_rstd], scale=-0.5)
            for ec in range(2):
                P.stt(oall[:, 2 * h + ec, lc:lc + 512], oall[:, 2 * h + ec, lc:lc + 512], P.ppc("d_o_norm", ec),
                      P.rstd[:, :], ALU.mult, ALU.mult, r=[t_o[2 * h + ec][i], P.t_rstd, P.t_pp],
                      w=[t_o[2 * h + ec][i]])
    for rs in range(2):
        wt, t_wt = P.next_slab()
        for i in range(2):
            lc = i * 512
            for n in range(4):
                ps, t_ps = P.psum("acc", [0, 1, 2, 3])
                for k in range(KC):
                    P.mm(ps[:, :], t_ps, wt[:, k * 512 + n * 128: k * 512 + (n + 1) * 128], xn[:, k, lc:lc + 512],
                         k == 0, k == KC - 1, r=[t_wt, t_xn[k][i]])
                sr, t_sr = P.tmpf()
                P.act(sr[:, :], ps[:, :], AF.Silu, r=[t_ps], w=[t_sr])
                c8 = rs * 4 + n
                P.tt("dve", oall[:, c8, lc:lc + 512], oall[:, c8, lc:lc + 512], sr[:, :], ALU.mult,
                     r=[t_o[c8][i], t_sr], w=[t_o[c8][i]])
    for s2 in range(2):
        wt, t_wt = P.next_slab()
        for i in range(2):
            lc = i * 512
            c0 = sg * SEG + lc
            for n in range(4):
                ps, t_ps = P.psum("acc", [0, 1, 2, 3])
                for k in range(KC):
                    P.mm(ps[:, :], t_ps, wt[:, k * 512 + n * 128: k * 512 + (n + 1) * 128], oall[:, k, lc:lc + 512],
                         k == 0, k == KC - 1, r=[t_wt, t_o[k][i]])
                on = s2 * 4 + n
                P.tt("dve", x[:, on, c0:c0 + 512], ps[:, :], x[:, on, c0:c0 + 512], ALU.add,
                     r=[t_ps, t_x[on][tis[i]]], w=[t_x[on][tis[i]]])
    P.release(m0)


GLA_DKH = 128
GLA_DVH = 256
GLA_TAU = 16.0


def gla_prefix(P, sel128d, sinit, g_ap, d_ap, t_gall, t_dall):
    s = P.s
    flat = P.sq.rearrange("p k t -> p (k t)").bitcast(F32)
    R = flat[:, 0:256]
    SA = flat[:, 256:512]
    SB = flat[:, 512:768]
    G = [flat[:, 768:1024], flat[:, 1024:1280]]
    sel = flat[:, 1280:1296]
    dsb = flat[:, 1296:1328].rearrange("p (s h) -> p s h", s=8)
    t_R, t_SA, t_c = Tl("gR"), Tl("gSAB"), Tl("gpre_c")
    t_G = [Tl("gG0"), Tl("gG1")]
    allt = [t_R, t_SA, t_c] + t_G
    s.add("dve", lambda e: e.memset(flat[:, 0:1328], 0.0), w=list(P.t_sq) + allt)
    s.add("sp", lambda e: e.dma_start(out=sel, in_=sel128d[:, :]), w=[t_c], dma=True)
    for sgi in range(8):
        s.add("sp", (lambda sgi: lambda e: e.dma_start(out=dsb[:, sgi, :], in_=d_ap(sgi)))(sgi),
              r=[t_dall], w=[t_c], dma=True)
    t_out = Tl("sinit")
    gi = 0
    for h in range(GH):
        if h > 0:
            s.add("dve", lambda e: e.memset(flat[:, 0:768], 0.0), w=[t_R, t_SA])
        for sgi in range(8):
            if sgi > 0:
                P.stt(SA, R, sel[:, sgi:sgi + 1], SA, ALU.mult, ALU.add, r=[t_R, t_c, t_SA], w=[t_SA])
                P.stt(SB, R, sel[:, 8 + sgi:9 + sgi], SB, ALU.mult, ALU.add, r=[t_R, t_c, t_SA], w=[t_SA])
            if sgi < 7:
                g, tg = G[gi % 2], t_G[gi % 2]
                gi += 1
                s.add("sp", (lambda g, gsrc: lambda e: e.dma_start(out=g, in_=gsrc))(g, g_ap(sgi, h)),
                      r=[t_gall], w=[tg], dma=True)
                P.stt(R, R, dsb[:, sgi, h:h + 1], g, ALU.mult, ALU.add, r=[t_R, t_c, tg], w=[t_R])
        s.add("sp", (lambda h: lambda e: e.dma_start(out=sinit[0, h, :, :], in_=SA))(h),
              r=[t_SA], w=[t_out], dma=True)
        s.add("sp", (lambda h: lambda e: e.dma_start(out=sinit[1, h, :, :], in_=SB))(h),
              r=[t_SA], w=[t_out], dma=True)
    s.add("dve", lambda e: e.memset(flat[:, 0:16], 0.0), w=list(P.t_sq) + allt)
    return t_out


def std_slabs(W, col0, n):
    return [slab_from(W, np.arange(col0 + i * 512, col0 + (i + 1) * 512)) for i in range(n)]


def attn_out_slabs(W):
    out = []
    Wr = W.reshape(16, 64, 1024)
    for g in range(4):
        sl = np.zeros((128, 4, 1024), np.float32)
        sl[0:64] = Wr[4 * g:4 * g + 4].transpose(1, 0, 2)
        out.append(sl.reshape(128, SLABW))
    return out


def build_pp(inp):
    pp = build_pp_stage1(inp)
    wf = inp["c_w_f"][0]
    pp.put("c_w_f", np.ascontiguousarray(wf.reshape(8, 128, 16).transpose(1, 0, 2).reshape(128, 128)))
    col = np.zeros((128, 1), np.float32)
    col[0:16, 0] = inp["c_b_f"][0]
    pp.put("c_b_f", col)
    col = np.zeros((128, 1), np.float32)
    col[0:64, 0] = inp["c_q_norm"][0]
    pp.put("c_q_norm", col)
    col = np.zeros((128, 1), np.float32)
    col[0:64, 0] = inp["c_k_norm"][0]
    pp.put("c_k_norm", col)
    wg1 = inp["d_w_g1"][0]
    pp.put("d_w_g1", np.ascontiguousarray(wg1.reshape(8, 128, 16).transpose(1, 0, 2).reshape(128, 128)))
    a = np.zeros((128, 512), np.float32)
    a[0:16] = inp["d_w_g2"][0]
    pp.put("d_w_g2", a)
    pp.put("d_b_g", colvec(inp["d_b_g"][0], 4))
    pp.put("d_o_norm", colvec(inp["d_o_norm"][0], 2))
    return pp


def build_fused(n_slabs, pp_off, npp):
    P = Prog(n_slabs, npp,
             out_specs=[("xo", (D, NMAIN), F32)],
             in_specs=[("xt", (D, NT), F32), ("hm", (128, 64), F32), ("ident", (128, 128), F32),
                       ("flexm", (128, 8), F32), ("sel", (16, 16), F32), ("cmask", (4, 128, 512), F32),
                       ("sel128", (128, 16), F32), ("tri", (128, 128), F32)], nslot=3)
    P.pp_off = pp_off
    nc, s = P.nc, P.s
    I = P.ins
    groups = [[0, 1, 2, 3], [4, 5, 6, 7]]
    qd = nc.dram_tensor("qd", [NH, 66, NMAIN], BF16).ap()
    kd_hp = [nc.dram_tensor("kd_hp%d" % i, [2 * 66, NMAIN], BF16).ap() for i in range(8)]
    vd_hp = [nc.dram_tensor("vd_hp%d" % i, [2 * 128, VW], BF16).ap() for i in range(8)]
    kall_hp = [nc.dram_tensor("kall_hp%d" % i, [4 * 2 * 66, NMAIN], BF16).ap() for i in range(8)]
    vall_hp = [nc.dram_tensor("vall_hp%d" % i, [4 * 2 * 128, VW], BF16).ap() for i in range(8)]
    cd2 = nc.dram_tensor("cd2", [NH, NMAIN], F32).ap()
    call2 = nc.dram_tensor("call2", [4 * NH, NMAIN], F32).ap()
    gd2 = nc.dram_tensor("gd2", [2 * GH * 128, GLA_DVH], F32).ap()
    dd2 = nc.dram_tensor("dd2", [2 * 128, GH], F32).ap()
    gall2 = nc.dram_tensor("gall2", [4 * 2 * GH * 128, GLA_DVH], F32).ap()
    dall2 = nc.dram_tensor("dall2", [4 * 2 * 128, GH], F32).ap()
    sinit = nc.dram_tensor("sinit", [2, GH, 128, GLA_DVH], F32).ap()
    call = call2.rearrange("(g h) t -> g h t", g=4)
    gd = gd2.rearrange("(s h p) e -> s h p e", s=2, h=GH)
    dd = dd2.rearrange("(s p) h -> s p h", s=2)
    gall = gall2.rearrange("(g s h p) e -> g s h p e", g=4, s=2, h=GH)
    dall = dall2.rearrange("(g s p) h -> g s p h", g=4, s=2)

    def kdst(h):
        return kd_hp[h // 2][(h % 2) * 66:(h % 2 + 1) * 66, :]

    def vown(h):
        return vd_hp[h // 2][(h % 2) * 128:(h % 2 + 1) * 128, :]

    def kallf(r_, h):
        return kall_hp[h // 2][(r_ * 2 + h % 2) * 66:(r_ * 2 + h % 2 + 1) * 66, :]

    def vallf(r_, h):
        return vall_hp[h // 2][(r_ * 2 + h % 2) * 128:(r_ * 2 + h % 2 + 1) * 128, :]

    t_kd_hp = [Tl("kd_hp%d" % i) for i in range(8)]
    t_vd_hp = [Tl("vd_hp%d" % i) for i in range(8)]
    t_kall = [Tl("kall%d" % i) for i in range(8)]
    t_vall = [Tl("vall%d" % i) for i in range(8)]
    t_call = Tl("call")

    setup_state(P, NT, TILES5)
    P.hm = P.sb("hm_sb", [128, 64], F32)
    P.t_hm = Tl("hm")
    s.add("sp", lambda e: e.dma_start(out=P.hm[:, :], in_=I["hm"][:, :]), w=[P.t_hm], dma=True)
    load_x(P, I["xt"], NT)
    mR = P.mark()
    alloc_R(P, NT, 5)
    phase_L0(P)
    phase_L1(P)
    def allgather(src2, dst2, r, w):
        s.add("pool", lambda e: e.collective_compute("AllGather", ALU.bypass, replica_groups=groups,
                                                     ins=[src2], outs=[dst2]), r=r, w=w, cc=True)

    def on_written(kind, i):
        if kind == "c":
            allgather(cd2, call2, [P.t_cd], [t_call])
        elif kind == "k":
            allgather(kd_hp[i], kall_hp[i], [t_kd_hp[i]], [t_kall[i]])
        else:
            allgather(vd_hp[i], vall_hp[i], [t_vd_hp[i]], [t_vall[i]])
    phase_L2pre(P, qd, kdst, lambda hp: vd_hp[hp], cd2, t_kd_hp, t_vd_hp, on_written=on_written)
    P.out_tiles = []
    P.release(mR)
    dep = {"qd": P.t_qd, "cd": P.t_cd, "call": t_call, "kd": t_kd_hp, "vd": t_vd_hp, "kall": t_kall, "vall": t_vall}
    phase_L2attn(P, qd, kdst, vown, cd2, kallf, vallf, call, I["flexm"], I["sel"], I["cmask"], dep)
    m = P.mark()
    alloc_R(P, NT, 5)
    mlp_block(P, 2, P.x, P.t_x, P.tiles[:4], P.R1, P.t_R1, P.R2, P.t_R2)
    P.release(m)
    for sg in range(2):
        gla_seg(P, sg, "scan", gout=gd[sg], dout=dd[sg])
    P.out_tiles = []
    s.barrier()
    t_gall, t_dall = Tl("gall"), Tl("dall")
    allgather(gd2, gall2, [], [t_gall])
    allgather(dd2, dall2, [], [t_dall])

    def pre_chunk():
        return gla_prefix(P, I["sel128"], sinit,
                          lambda sgi, h: gall[seg_loc(sgi)[0], seg_loc(sgi)[1], h, :, :],
                          lambda sgi: dall[seg_loc(sgi)[0], seg_loc(sgi)[1], :, :], t_gall, t_dall)
    gla_seg(P, 0, "full", sinit=sinit[0], tri=I["tri"], pre_chunk=pre_chunk)
    gla_seg(P, 1, "full", sinit=sinit[1], tri=I["tri"], sinit_t=P.t_sinit)
    m = P.mark()
    alloc_R(P, NT, 5)
    mlp_block(P, 3, P.x, P.t_x, P.tiles[:4], P.R1, P.t_R1, P.R2, P.t_R2)
    P.release(m)
    store_x(P, P.outs["xo"])
    return P.finish()


def run_fused(inp):
    x = np.asarray(inp["x"], np.float32)
    pp = build_pp(inp)
    ppa = pp.array()
    ident = np.eye(128, dtype=np.float32)
    wq = inp["c_w_qkv"][0]
    wd = inp["d_w_in"][0]
    gla_scan = std_slabs(wd, 1024, 2) + std_slabs(wd, 512, 1)
    gla_full = (std_slabs(wd, 1024, 2) + std_slabs(wd, 0, 1) + std_slabs(wd, 512, 1) + std_slabs(wd, 2048, 2)
                + std_slabs(inp["d_w_out"][0], 0, 2))
    slabs = (build_wall_stage1(inp) + std_slabs(wq, 2048, 2) + std_slabs(wq, 0, 2) + std_slabs(wq, 1024, 2)
             + attn_out_slabs(inp["c_w_out"][0]) + mlp_slabs(inp, 2) + gla_scan + gla_scan
             + gla_full + gla_full + mlp_slabs(inp, 3))
    wall = np.stack(slabs, axis=0)
    cm = causal_masks()
    tri = np.triu(np.ones((128, 128), np.float32))
    in_maps = []
    for r in range(8):
        b, j = r // 4, r % 4
        idx = core_token_index(j)
        xt = np.zeros((NT, D), np.float32)
        valid = idx >= 0
        xt[valid] = x[b, idx[valid]]
        hm, flexm, sel = percore_consts(j)
        sel128 = np.zeros((128, 16), np.float32)
        sel128[:, j] = 1.0
        sel128[:, 8 + 7 - j] = 1.0
        in_maps.append({"wall": wall, "pp": ppa, "xt": np.ascontiguousarray(xt.T), "hm": hm, "ident": ident,
                        "flexm": flexm, "sel": sel, "cmask": cm, "sel128": sel128, "tri": tri})
    nc = build_fused(wall.shape[0], pp.off, pp.n)
    res = run_bass_kernel_spmd(nc, in_maps, core_ids=list(range(8))).results
    return gather_x([np.asarray(res[r]["xo"]) for r in range(8)])


def percore_consts(j):
    hm = np.ones((128, 64), np.float32)
    if j == 0:
        hm[:, 0:32] = 0.0
    flexm = np.zeros((128, 8), np.float32)
    for p in range(3):
        flexm[:, p] = 1.0 if j > p else 0.0
        flexm[:, 3 + p] = 0.0 if j > p else 1.0
    sel = np.zeros((16, 16), np.float32)
    sel[:, j] = 1.0
    sel[:, 8 + 7 - j] = 1.0
    return hm, flexm, sel


def causal_masks():
    kl = np.arange(128)[:, None]
    ql = np.arange(512)[None, :]
    return np.stack([np.where(kl - ql <= -128 * i, 0.0, NEG).astype(np.float32) for i in range(4)], axis=0)


_CACHE = {}


def gather_x(xos):
    out = np.zeros((2, 8192, D), np.float32)
    for r in range(8):
        b, j = r // 4, r % 4
        a, bb = seg_tokens(j)
        xo = xos[r].T
        out[b, a] = xo[:SEG]
        out[b, bb] = xo[SEG:]
    return out


def kernel(**inputs):
    return run_fused(inputs)
```
